# Optimizing a Trainium2 kernel written in Bass

```python
import math
import jax, jax.numpy as jnp
from jax import lax
import numpy as np

D_MODEL = 1024
BATCH = 4
SEQ = 8192
DEPTH = 2

GRID_W = 64
CTX_LEN = 256
HEAD_DIM = 64
N_Q_HEADS = 8
N_KV_HEADS = 4
GROUP = N_Q_HEADS // N_KV_HEADS
Q_W = N_Q_HEADS * HEAD_DIM
KV_W = N_KV_HEADS * HEAD_DIM
QKV_W = Q_W + 2 * KV_W
CONV_W = D_MODEL // 2
MIX_IN_W = QKV_W + 3 * CONV_W
MIX_OUT_W = Q_W + CONV_W
Q_BLOCK = 128
WINDOW = 128
ROPE_THETA = 10000.0
FILT_EMB = 33
FILT_WIDTH = 64
DECAY_TARGET = 1e-2
FAST_DECAY_PCT = 0.3
SLOW_DECAY_PCT = 1.5
DECAY_SHIFT = 0.05
D_FF = 2816
N_EVEN = (DEPTH + 1) // 2
N_ODD = DEPTH // 2
NEG_INF = -1e30
RMS_EPS = 1e-6

kernel_name = 'hybrid_flow_backbone'


def rmsnorm(x, g):
    xf = x.astype(jnp.float32)
    xf = xf * lax.rsqrt(jnp.mean(xf * xf, axis=-1, keepdims=True) + RMS_EPS)
    return xf.astype(x.dtype) * g


def modulate(x, g, shift, scale):
    return rmsnorm(x, g) * (1 + scale) + shift


def dwconv3(x, w):
    xp = jnp.pad(x, ((0, 0), (1, 1), (0, 0)))
    return xp[:, :-2] * w[0] + xp[:, 1:-1] * w[1] + xp[:, 2:] * w[2]


def axial_angles(rows):
    row = jnp.repeat(jnp.arange(rows), GRID_W).astype(jnp.float32)
    col = jnp.tile(jnp.arange(GRID_W), rows).astype(jnp.float32)
    n_freq = HEAD_DIM // 4
    inv = ROPE_THETA ** (-jnp.arange(n_freq, dtype=jnp.float32) / n_freq)
    return row[:, None] * inv, col[:, None] * inv


def rope_half(x, ang):
    x1, x2 = jnp.split(x, 2, axis=-1)
    cos = jnp.cos(ang).astype(x.dtype)
    sin = jnp.sin(ang).astype(x.dtype)
    return jnp.concatenate([x1 * cos - x2 * sin, x1 * sin + x2 * cos], axis=-1)


def rope_2d(x, ang_row, ang_col):
    xr, xc = jnp.split(x, 2, axis=-1)
    return jnp.concatenate([rope_half(xr, ang_row), rope_half(xc, ang_col)], axis=-1)


def split_kv(z):
    b, s, _ = z.shape
    k = z[..., :KV_W].reshape(b, s, N_KV_HEADS, HEAD_DIM).transpose(0, 2, 1, 3)
    v = z[..., KV_W:].reshape(b, s, N_KV_HEADS, HEAD_DIM).transpose(0, 2, 1, 3)
    return k, v


def split_qkv(z):
    b, s, _ = z.shape
    q = z[..., :Q_W].reshape(b, s, N_KV_HEADS, GROUP, HEAD_DIM).transpose(0, 2, 3, 1, 4)
    k, v = split_kv(z[..., Q_W:QKV_W])
    return q, k, v


def merge_heads(o):
    b, h, g, s, d = o.shape
    return o.transpose(0, 3, 1, 2, 4).reshape(b, s, h * g * d)


def attend(q, k, v, sink=None):
    s = jnp.einsum('bhgqd,bhkd->bhgqk', q, k).astype(jnp.float32) * HEAD_DIM ** -0.5
    if sink is None:
        p = jax.nn.softmax(s, axis=-1)
    else:
        sk = jnp.broadcast_to(sink.astype(jnp.float32)[None, :, :, None, None], s.shape[:-1] + (1,))
        p = jax.nn.softmax(jnp.concatenate([s, sk], axis=-1), axis=-1)[..., :-1]
    return jnp.einsum('bhgqk,bhkd->bhgqd', p.astype(v.dtype), v)


def dense_block_attention(q, k, v, kc, vc):
    b, h, g, s, d = q.shape
    nb = s // Q_BLOCK
    kk = jnp.concatenate([k, kc], axis=2)
    vv = jnp.concatenate([v, vc], axis=2)
    qb = q.reshape(b, h, g, nb, Q_BLOCK, d).transpose(3, 0, 1, 2, 4, 5)
    out = lax.map(lambda qblk: attend(qblk, kk, vv), qb)
    return out.transpose(1, 2, 3, 0, 4, 5).reshape(b, h, g, s, d)


def banded_attention(q, k, v, kc, vc, sink):
    b, h, g, s, d = q.shape
    nb = s // Q_BLOCK
    qb = q.reshape(b, h, g, nb, Q_BLOCK, d)

    def band(t):
        tp = jnp.pad(t.reshape(b, h, nb, Q_BLOCK, d), ((0, 0), (0, 0), (1, 1), (0, 0), (0, 0)))
        return jnp.concatenate([tp[:, :, :-2], tp[:, :, 1:-1], tp[:, :, 2:]], axis=3)

    kb, vb = band(k), band(v)
    scale = HEAD_DIM ** -0.5
    s_loc = jnp.einsum('bhgnqd,bhnkd->bhgnqk', qb, kb).astype(jnp.float32) * scale
    s_ctx = jnp.einsum('bhgnqd,bhkd->bhgnqk', qb, kc).astype(jnp.float32) * scale
    blk = jnp.arange(nb)[:, None, None]
    qi = blk * Q_BLOCK + jnp.arange(Q_BLOCK)[None, :, None]
    kj = (blk - 1) * Q_BLOCK + jnp.arange(3 * Q_BLOCK)[None, None, :]
    ok = (jnp.abs(kj - qi) <= WINDOW) & (kj >= 0) & (kj < s)
    s_loc = jnp.where(ok, s_loc, NEG_INF)
    sk = jnp.broadcast_to(sink.astype(jnp.float32)[None, :, :, None, None, None], s_loc.shape[:-1] + (1,))
    p = jax.nn.softmax(jnp.concatenate([s_loc, s_ctx, sk], axis=-1), axis=-1).astype(v.dtype)
    n_loc = 3 * Q_BLOCK
    o = (jnp.einsum('bhgnqk,bhnkd->bhgnqd', p[..., :n_loc], vb)
         + jnp.einsum('bhgnqk,bhkd->bhgnqd', p[..., n_loc:-1], vc))
    return o.reshape(b, h, g, s, d)


def implicit_filter(n, w1, b1, w2, b2, w3, b3, w4, freq):
    t01 = jnp.linspace(0.0, 1.0, n, dtype=jnp.float32)[:, None]
    bands = (FILT_EMB - 1) // 2
    w = 2.0 * math.pi * jnp.arange(n, dtype=jnp.float32)[:, None] / n
    f = jnp.linspace(1e-4, bands - 1, bands, dtype=jnp.float32)[None, :]
    feats = jnp.concatenate([t01, jnp.cos(f * w), -jnp.sin(f * w)], axis=-1)
    hid = jnp.sin(freq * (feats @ w1 + b1))
    hid = jnp.sin(freq * (hid @ w2 + b2))
    hid = jnp.sin(freq * (hid @ w3 + b3))
    hf = (hid @ w4).astype(jnp.float32)
    deltas = jnp.abs(jnp.linspace(math.log(DECAY_TARGET) / SLOW_DECAY_PCT,
                                  math.log(DECAY_TARGET) / FAST_DECAY_PCT, CONV_W, dtype=jnp.float32))
    window = jnp.exp(-t01 * deltas) + DECAY_SHIFT
    h_fwd = hf[:, :CONV_W] * window
    h_bwd = hf[:, CONV_W:] * window
    k = jnp.concatenate([h_fwd, jnp.zeros((1, CONV_W), jnp.float32), h_bwd[:0:-1]], axis=0)
    return k / jnp.sum(jnp.abs(k), axis=0, keepdims=True)


def hyena(z, conv_w, conv_b, w1, b1, w2, b2, w3, b3, w4, freq, bias_d):
    n = z.shape[1]
    z = dwconv3(z, conv_w) + conv_b
    x0, x1, v = jnp.split(z, 3, axis=-1)
    k = implicit_filter(n, w1, b1, w2, b2, w3, b3, w4, freq)
    u = (v * x1).astype(jnp.float32)
    y = jnp.fft.irfft(jnp.fft.rfft(u, n=2 * n, axis=1) * jnp.fft.rfft(k, n=2 * n, axis=0)[None],
                      n=2 * n, axis=1)[:, :n]
    y = y + u * bias_d
    return (y * x0.astype(jnp.float32)).astype(z.dtype)


def short_conv_mixer(z, conv_w):
    bg, cg, xv = jnp.split(z, 3, axis=-1)
    return bg * dwconv3(cg * xv, conv_w)


def conv_ffn(h, w_up, conv_w, conv_b, w_down):
    a, v = jnp.split(h @ w_up, 2, axis=-1)
    a = dwconv3(a, conv_w) + conv_b
    return (jax.nn.gelu(a) * v) @ w_down


def setup_inputs(seed: int = 0) -> dict:
    key = jax.random.key(seed)
    keys = iter(jax.random.split(key, 40))

    def nrm(shape, scale):
        return jax.random.normal(next(keys), shape, jnp.float32) * scale

    def gain(shape):
        return 1.0 + nrm(shape, 0.02)

    return {
        'x': nrm((BATCH, SEQ, D_MODEL), 1.0),
        'c': nrm((BATCH, D_MODEL), 1.0),
        'ctx': nrm((BATCH, CTX_LEN, D_MODEL), 1.0),
        'c_ctx': nrm((D_MODEL,), 1.0),
        'ada_w': nrm((DEPTH, D_MODEL, 6 * D_MODEL), 0.5 * D_MODEL ** -0.5),
        'ada_b': nrm((DEPTH, 6 * D_MODEL), 0.01),
        'norm_mix': gain((DEPTH, D_MODEL)),
        'norm_ffn': gain((DEPTH, D_MODEL)),
        'mix_w_in': nrm((DEPTH, D_MODEL, MIX_IN_W), D_MODEL ** -0.5),
        'mix_w_out': nrm((DEPTH, MIX_OUT_W, D_MODEL), MIX_OUT_W ** -0.5),
        'attn_q_norm': gain((DEPTH, HEAD_DIM)),
        'attn_k_norm': gain((DEPTH, HEAD_DIM)),
        'swa_sink': nrm((N_ODD, N_Q_HEADS), 0.5),
        'hy_conv_w': nrm((N_EVEN, 3, 3 * CONV_W), 3 ** -0.5),
        'hy_conv_b': nrm((N_EVEN, 3 * CONV_W), 0.01),
        'hy_w1': nrm((N_EVEN, FILT_EMB, FILT_WIDTH), FILT_EMB ** -0.5),
        'hy_b1': nrm((N_EVEN, FILT_WIDTH), 0.1),
        'hy_w2': nrm((N_EVEN, FILT_WIDTH, FILT_WIDTH), FILT_WIDTH ** -0.5),
        'hy_b2': nrm((N_EVEN, FILT_WIDTH), 0.1),
        'hy_w3': nrm((N_EVEN, FILT_WIDTH, FILT_WIDTH), FILT_WIDTH ** -0.5),
        'hy_b3': nrm((N_EVEN, FILT_WIDTH), 0.1),
        'hy_w4': nrm((N_EVEN, FILT_WIDTH, 2 * CONV_W), FILT_WIDTH ** -0.5),
        'hy_freq': gain((N_EVEN, FILT_WIDTH)),
        'hy_bias_d': nrm((N_EVEN, CONV_W), 1.0),
        'sc_conv_w': nrm((N_ODD, 3, CONV_W), 3 ** -0.5),
        'ffn_w_up': nrm((DEPTH, D_MODEL, 2 * D_FF), D_MODEL ** -0.5),
        'ffn_conv_w': nrm((DEPTH, 3, D_FF), 3 ** -0.5),
        'ffn_conv_b': nrm((DEPTH, D_FF), 0.01),
        'ffn_w_down': nrm((DEPTH, D_FF, D_MODEL), D_FF ** -0.5),
    }


def reference(x, c, ctx, c_ctx, ada_w, ada_b, norm_mix, norm_ffn, mix_w_in, mix_w_out,
              attn_q_norm, attn_k_norm, swa_sink, hy_conv_w, hy_conv_b, hy_w1, hy_b1, hy_w2, hy_b2,
              hy_w3, hy_b3, hy_w4, hy_freq, hy_bias_d, sc_conv_w, ffn_w_up, ffn_conv_w, ffn_conv_b,
              ffn_w_down):
    rows = x.shape[1] // GRID_W
    ang_r, ang_c = axial_angles(rows)
    xc = ctx
    for i in range(DEPTH):
        last = i == DEPTH - 1
        j = i // 2
        mod = jax.nn.silu(c) @ ada_w[i] + ada_b[i]
        mod_c = jax.nn.silu(c_ctx) @ ada_w[i] + ada_b[i]
        sh1, sc1, g1, sh2, sc2, g2 = [m[:, None, :] for m in jnp.split(mod, 6, axis=-1)]
        sh1c, sc1c, g1c, sh2c, sc2c, g2c = jnp.split(mod_c, 6, axis=-1)
        w_in = mix_w_in[i]

        h = modulate(x, norm_mix[i], sh1, sc1)
        z = h @ w_in
        q, k, v = split_qkv(z)
        q = rope_2d(rmsnorm(q, attn_q_norm[i]), ang_r, ang_c)
        k = rope_2d(rmsnorm(k, attn_k_norm[i]), ang_r, ang_c)

        hc = modulate(xc, norm_mix[i], sh1c, sc1c)
        if last:
            kc, vc = split_kv(hc @ w_in[:, Q_W:QKV_W])
        else:
            zc = hc @ w_in
            qc, kc, vc = split_qkv(zc)
            qc = rmsnorm(qc, attn_q_norm[i])
        kc = rmsnorm(kc, attn_k_norm[i])

        if i % 2 == 0:
            hy_args = (hy_conv_w[j], hy_conv_b[j], hy_w1[j], hy_b1[j], hy_w2[j], hy_b2[j],
                       hy_w3[j], hy_b3[j], hy_w4[j], hy_freq[j], hy_bias_d[j])
            o_attn = dense_block_attention(q, k, v, kc, vc)
            o_conv = hyena(z[..., QKV_W:], *hy_args)
            if not last:
                oc_attn = attend(qc, kc, vc)
                oc_conv = hyena(zc[..., QKV_W:], *hy_args)
        else:
            sink = swa_sink[j].reshape(N_KV_HEADS, GROUP)
            o_attn = banded_attention(q, k, v, kc, vc, sink)
            o_conv = short_conv_mixer(z[..., QKV_W:], sc_conv_w[j])
            if not last:
                oc_attn = attend(qc, kc, vc, sink)
                oc_conv = short_conv_mixer(zc[..., QKV_W:], sc_conv_w[j])

        y = jnp.concatenate([merge_heads(o_attn), o_conv], axis=-1) @ mix_w_out[i]
        x = x + g1 * y
        x = x + g2 * conv_ffn(modulate(x, norm_ffn[i], sh2, sc2),
                              ffn_w_up[i], ffn_conv_w[i], ffn_conv_b[i], ffn_w_down[i])
        if not last:
            yc = jnp.concatenate([merge_heads(oc_attn), oc_conv], axis=-1) @ mix_w_out[i]
            xc = xc + g1c * yc
            xc = xc + g2c * conv_ffn(modulate(xc, norm_ffn[i], sh2c, sc2c),
                                     ffn_w_up[i], ffn_conv_w[i], ffn_conv_b[i], ffn_w_down[i])
    return x
```

```python
import numpy as np
import ml_dtypes
from contextlib import ExitStack
import concourse.bass as bass
import concourse.mybir as mybir
from concourse.bass_utils import run_bass_kernel_spmd

F32 = mybir.dt.float32
BF16 = mybir.dt.bfloat16
I32 = mybir.dt.int32
AF = mybir.ActivationFunctionType
ALU = mybir.AluOpType
NPBF = ml_dtypes.bfloat16

D = 1024; SEQ = 8192; CTX = 256; TT = SEQ + CTX; E = 4608; OWN = 4096
DFF = 2816; NM = 22
SAME_ENGINE_SYNC = {"act", "pool", "dve"}


class Res:
    __slots__ = ("name", "last_w", "reads", "excl")
    def __init__(self, name, excl=False):
        self.name = name; self.last_w = None; self.reads = {}; self.excl = excl


class Tl:
    def __init__(self, t, r):
        self.t = t; self.r = r
    def __getitem__(self, idx):
        return self.t[idx]


class K:
    ENGS = ("pe", "act", "dve", "pool", "sp")

    def __init__(self, nc):
        self.nc = nc
        self.es = ExitStack()
        self.sem = {}; self.cnt = {}
        for e in self.ENGS:
            self.sem[e] = self.es.enter_context(nc.semaphore("s_" + e))
            self.cnt[e] = 0
        self.dma_sems = {}
        self.dma_key = {}
        self.dma_rr = {}
        self.NDMASEM = {"sp": 32, "pool": 24, "act": 8, "pe": 4, "dve": 4}
        self.seen = {e: {} for e in self.ENGS}
        self.ops = {e: [] for e in self.ENGS}
        self.phase_es = None
        self.nres = 0
        self.ndma = 0

    def sbuf(self, shape, dt, name=None, persist=False):
        self.nres += 1
        name = (name or "t") + "_%d" % self.nres
        es = self.es if persist else self.phase_es
        t = es.enter_context(self.nc.sbuf_tensor(name, list(shape), dt))
        return Tl(t, Res(name))

    def psum(self, shape, dt, name=None):
        self.nres += 1
        name = (name or "p") + "_%d" % self.nres
        t = self.phase_es.enter_context(self.nc.psum_tensor(name, list(shape), dt))
        return Tl(t, Res(name, excl=True))

    def _need(self, reads, writes, eng=None):
        evs = []
        for r in reads:
            if r.last_w is not None: evs.append(r.last_w)
            if r.excl:
                evs.extend((kk[0], kk[1], v) for kk, v in r.reads.items() if not (kk[0] == "eng" and kk[1] == eng))
        for w in writes:
            if w.last_w is not None: evs.append(w.last_w)
            evs.extend((kk[0], kk[1], v) for kk, v in w.reads.items())
        return evs

    def _emit_waits(self, eng, evs):
        need = {}
        for kind, key, val in evs:
            if kind == "eng":
                if key == eng and eng not in SAME_ENGINE_SYNC: continue
                v = val
            else:
                v = val
            if self.seen[eng].get((kind, key), 0) >= v: continue
            if need.get((kind, key), 0) < v: need[(kind, key)] = v
        for (kind, key), v in need.items():
            self.seen[eng][(kind, key)] = v
            sem = self.sem[key] if kind == "eng" else self.dma_sems[key][0]
            self.ops[eng].append(lambda e, sem=sem, v=v: e.wait_ge(sem, v))

    def _commit(self, ev, reads, writes):
        for r in reads:
            kk = (ev[0], ev[1])
            if r.reads.get(kk, 0) < ev[2]: r.reads[kk] = ev[2]
        for w in writes:
            w.last_w = ev; w.reads = {}

    def op(self, eng, fn, reads=(), writes=()):
        reads = [x.r if isinstance(x, Tl) else x for x in reads]
        writes = [x.r if isinstance(x, Tl) else x for x in writes]
        self._emit_waits(eng, self._need(reads, writes, eng))
        self.cnt[eng] += 1
        sem = self.sem[eng]
        self.ops[eng].append(lambda e, fn=fn, sem=sem: fn(e).then_inc(sem, 1))
        self._commit(("eng", eng, self.cnt[eng]), reads, writes)

    def dma(self, q, out, in_, reads=(), writes=(), **kw):
        reads = [x.r if isinstance(x, Tl) else x for x in reads]
        writes = [x.r if isinstance(x, Tl) else x for x in writes]
        npool = self.NDMASEM[q]
        idx = (q, self.dma_rr.get(q, 0) % npool)
        self.dma_rr[q] = self.dma_rr.get(q, 0) + 1
        if idx not in self.dma_sems:
            s_ = self.es.enter_context(self.nc.semaphore("d_%s_%d" % idx))
            self.dma_sems[idx] = [s_, 0]
        ent = self.dma_sems[idx]
        evs = self._need(reads, writes, q)
        if ent[1] > 0:
            evs.append(("dma", idx, ent[1] * 16))
        self._emit_waits(q, evs)
        ent[1] += 1
        sem = ent[0]
        self.ndma += 1
        self.ops[q].append(lambda e, out=out, in_=in_, sem=sem, kw=kw: e.dma_start(out=out, in_=in_, **kw).then_inc(sem, 16))
        self._commit(("dma", idx, ent[1] * 16), reads, writes)

    def barrier(self):
        evs = [("eng", e, self.cnt[e]) for e in self.ENGS if self.cnt[e] > 0]
        evs += [("dma", kk, v[1] * 16) for kk, v in self.dma_sems.items() if v[1] > 0]
        for e in self.ENGS:
            self._emit_waits(e, [ev for ev in evs if not (ev[0] == "eng" and ev[1] == e)])

    def phase(self, body):
        with ExitStack() as pes:
            self.phase_es = pes
            body()
            self.barrier()
            ops = self.ops
            self.ops = {e: [] for e in self.ENGS}
            with self.nc.Block() as block:
                @block.tensor
                def _(e):
                    for f in ops["pe"]: f(e)
                @block.scalar
                def _(e):
                    for f in ops["act"]: f(e)
                @block.vector
                def _(e):
                    for f in ops["dve"]: f(e)
                @block.gpsimd
                def _(e):
                    for f in ops["pool"]: f(e)
                @block.sync
                def _(e):
                    for f in ops["sp"]: f(e)
        self.phase_es = None

    def close(self):
        self.es.close()

    def ts(self, eng, out, in0, s1, s2, op0, op1, r, w):
        if s2 is None:
            s2 = 0.0; op1 = ALU.add
        self.op(eng, lambda e: e.tensor_scalar(out=out, in0=in0, scalar1=s1, scalar2=s2, op0=op0, op1=op1), r, w)
    def stt(self, eng, out, in0, sc, in1, op0, op1, r, w):
        self.op(eng, lambda e: e.scalar_tensor_tensor(out=out, in0=in0, scalar=sc, in1=in1, op0=op0, op1=op1), r, w)
    def tt(self, eng, out, in0, in1, op, r, w):
        self.op(eng, lambda e: e.tensor_tensor(out=out, in0=in0, in1=in1, op=op), r, w)
    def act(self, out, in_, func, r, w, bias=None, scale=None, accum=None):
        kw = {}
        if bias is not None: kw["bias"] = bias
        if scale is not None: kw["scale"] = scale
        if accum is not None: kw["accum_out"] = accum
        self.op("act", lambda e: e.activation(out=out, in_=in_, func=func, **kw), r, w)
    def mm(self, out, lhsT, rhs, start, stop, r, w):
        self.op("pe", lambda e: e.matmul(out, lhsT=lhsT, rhs=rhs, start=start, stop=stop), r, w)
    def copy(self, eng, out, in_, r, w):
        if eng == "act":
            self.op("act", lambda e: e.copy(out=out, in_=in_), r, w)
        else:
            self.op(eng, lambda e: e.tensor_copy(out=out, in_=in_), r, w)
    def memset(self, eng, ap, val, w):
        self.op(eng, lambda e: e.memset(ap, val), [], w)
    def recip(self, out, in_, r, w):
        self.op("dve", lambda e: e.reciprocal(out=out, in_=in_), r, w)

def _consts():
    c = {}
    nf = 16
    inv = 10000.0 ** (-np.arange(nf, dtype=np.float64) / nf)
    t = np.arange(SEQ)
    row = (t // 64).astype(np.float64); col = (t % 64).astype(np.float64)
    ar = row[None, :] * inv[:, None]; ac = col[None, :] * inv[:, None]
    cos64 = np.concatenate([np.cos(ar), np.cos(ar), np.cos(ac), np.cos(ac)], 0)
    sin64 = np.concatenate([-np.sin(ar), np.sin(ar), -np.sin(ac), np.sin(ac)], 0)
    c["cos64"] = cos64.astype(np.float32); c["sin64"] = sin64.astype(np.float32)
    perm = np.concatenate([np.arange(16) + 16, np.arange(16), np.arange(16) + 48, np.arange(16) + 32])
    c["perm64"] = perm
    p = np.arange(128)[:, None]; f = np.arange(512)[None, :]
    c["bmask"] = np.stack([(np.abs(128 * r + p - f) <= 128) for r in range(-1, 5)], 0).astype(NPBF)
    return c


def _fft_consts(NA):
    N = 128 * NA
    c = {}
    a = np.arange(NA)[:, None]; f1 = np.arange(NA)[None, :]
    th = 2 * np.pi * a * f1 / NA
    c["f1cs"] = np.concatenate([np.cos(th), -np.sin(th)], 1).astype(NPBF)
    p = np.arange(128)[:, None]; f2 = np.arange(128)[None, :]
    th2 = 2 * np.pi * p * f2 / 128
    C = np.cos(th2); S = np.sin(th2)
    c["fC"] = C.astype(NPBF); c["fS"] = S.astype(NPBF); c["fnS"] = (-S).astype(NPBF)
    c["fCS"] = np.concatenate([C, S], 1).astype(NPBF)
    c["fnSC"] = np.concatenate([-S, C], 1).astype(NPBF)
    tw = 2 * np.pi * np.arange(128)[:, None] * np.arange(NA)[None, :] / N
    G = 512 // (2 * NA)
    tc_ = np.cos(tw); ts_ = np.sin(tw)
    A = np.concatenate([tc_, tc_], 1)
    B = np.concatenate([ts_, -ts_], 1)
    c["twA"] = np.tile(A[:, None, :], (1, G, 1)).reshape(128, 512).astype(np.float32)
    c["twB"] = np.tile(B[:, None, :], (1, G, 1)).reshape(128, 512).astype(np.float32)
    tcT = np.cos(tw).T; tsT = np.sin(tw).T
    A2 = np.stack([tcT, tcT], 1)
    B2 = np.stack([tsT, -tsT], 1)
    c["twA2"] = np.tile(A2[:, None], (1, 4, 1, 1)).reshape(NA, 1024).astype(np.float32)
    c["twB2"] = np.tile(B2[:, None], (1, 4, 1, 1)).reshape(NA, 1024).astype(np.float32)
    th = 2 * np.pi * np.arange(NA)[:, None] * np.arange(NA)[None, :] / NA
    c["g3C"] = (np.cos(th) / N).astype(NPBF); c["g3nS"] = (-np.sin(th) / N).astype(NPBF)
    return c


def _filter_consts(n):
    q = np.arange(2 * n)
    tap = np.where(q < n, q, 2 * n - q).astype(np.int64)
    tap = np.minimum(tap, n - 1)
    t01 = np.linspace(0.0, 1.0, n, dtype=np.float32)
    bands = 16
    w = (2.0 * np.pi * np.arange(n, dtype=np.float32) / n).astype(np.float32)
    f = np.linspace(1e-4, bands - 1, bands, dtype=np.float32)[None, :]
    feats = np.concatenate([t01[:, None], np.cos(f * w[:, None]), -np.sin(f * w[:, None])], -1).astype(np.float32)
    featsT = np.ascontiguousarray(feats[tap].T)
    t01b = np.ascontiguousarray(np.tile(t01[tap][None, :], (128, 1)))
    deltas = np.abs(np.linspace(np.log(1e-2) / 1.5, np.log(1e-2) / 0.3, 512, dtype=np.float32))
    ndel = np.ascontiguousarray((-deltas).reshape(4, 128).T)
    return featsT.astype(np.float32), t01b.astype(np.float32), ndel.astype(np.float32)


def _pm(v, n):
    return np.ascontiguousarray(np.asarray(v, np.float32).reshape(n, 128).T)


def _host_prep(inp, b, hh, C):
    fl = (hh == 1)
    m = {}
    x = inp["x"][b]; cx = inp["ctx"][b]
    if fl: x = x[::-1]; cx = cx[::-1]
    m["xt0"] = np.ascontiguousarray(np.concatenate([x, cx], 0).T)
    cv = np.stack([inp["c"][b], inp["c_ctx"]], 1)
    m["cvec"] = np.ascontiguousarray(cv.reshape(8, 128, 2).transpose(1, 0, 2))
    perm = C["perm64"]
    for i in range(2):
        m[f"ada_w{i}"] = inp["ada_w"][i]
        m[f"ada_b{i}"] = _pm(inp["ada_b"][i], 48)
        m[f"nmix{i}"] = _pm(inp["norm_mix"][i], 8)
        m[f"nffn{i}"] = _pm(inp["norm_ffn"][i], 8)
        w = inp["mix_w_in"][i]
        qk = w[:, :768].reshape(D, 12, 64)[:, :, perm].reshape(D, 768)
        m[f"w_in{i}"] = np.ascontiguousarray(np.concatenate([w, qk], 1))
        m[f"w_out{i}"] = inp["mix_w_out"][i]
        gq = inp["attn_q_norm"][i]; gk = inp["attn_k_norm"][i]
        m[f"qkg{i}"] = np.ascontiguousarray(np.stack([np.tile(gq, 2), np.tile(gq[perm], 2), np.tile(gk, 2), np.tile(gk[perm], 2)], 1).astype(np.float32))
        m[f"w_up{i}"] = inp["ffn_w_up"][i]
        m[f"w_dn{i}"] = inp["ffn_w_down"][i]
        fw = inp["ffn_conv_w"][i]
        if fl: fw = fw[::-1]
        m[f"fcw{i}"] = np.ascontiguousarray(fw.reshape(3, NM, 128).transpose(2, 1, 0))
        m[f"fcb{i}"] = _pm(inp["ffn_conv_b"][i], NM)
    hw = inp["hy_conv_w"][0]
    if fl: hw = hw[::-1]
    m["hcw"] = np.ascontiguousarray(hw.reshape(3, 12, 128).transpose(2, 1, 0))
    m["hcb"] = _pm(inp["hy_conv_b"][0], 12)
    sw = inp["sc_conv_w"][0]
    if fl: sw = sw[::-1]
    m["scw"] = np.ascontiguousarray(sw.reshape(3, 4, 128).transpose(2, 1, 0))
    m["hw1"] = inp["hy_w1"][0]; m["hw2"] = inp["hy_w2"][0]; m["hw3"] = inp["hy_w3"][0]
    w4 = inp["hy_w4"][0]
    if fl: w4 = np.concatenate([w4[:, 512:], w4[:, :512]], 1)
    m["hw4"] = np.ascontiguousarray(w4)
    m["hb"] = np.ascontiguousarray(np.stack([inp["hy_b1"][0], inp["hy_b2"][0], inp["hy_b3"][0], inp["hy_freq"][0]], 1).astype(np.float32))
    m["hbd"] = _pm(inp["hy_bias_d"][0], 4)
    m["sink"] = np.ascontiguousarray(np.tile(inp["swa_sink"][0][None, :], (128, 1)).astype(np.float32))
    cos = C["cos64"]; sin = C["sin64"]
    if fl: cos = cos[:, ::-1]; sin = sin[:, ::-1]
    cosx = np.concatenate([cos, np.ones((64, CTX), np.float32)], 1)
    sinx = np.concatenate([sin, np.zeros((64, CTX), np.float32)], 1)
    m["cosT"] = np.ascontiguousarray(np.concatenate([cosx, cosx], 0))
    m["sinT"] = np.ascontiguousarray(np.concatenate([sinx, sinx], 0))
    m["bmask"] = C["bmask"]
    for tag, NA in (("L", 128), ("C", 4)):
        for kk, v in C["fft" + tag].items():
            m[kk + tag] = v
    for tag in ("L", "C"):
        ft, t01b, ndel = C["filt" + tag]
        m["featsT" + tag] = ft; m["t01b" + tag] = t01b
    m["ndel"] = C["filtL"][2]
    return m

def _build(stop_after=None, dbg=()):
    nc = bass.Bass("TRN2", target_bir_lowering=False)
    k = K(nc)
    def din(name, shape, dt=F32):
        return nc.dram_tensor(name, list(shape), dt, kind="ExternalInput").ap()
    def dscr(name, shape, dt):
        kind = "ExternalOutput" if name in dbg else "Internal"
        return nc.dram_tensor(name, list(shape), dt, kind=kind).ap()
    I = {}
    I["xt0"] = din("xt0", [D, TT]); I["cvec"] = din("cvec", [128, 8, 2])
    for i in range(2):
        I[f"ada_w{i}"] = din(f"ada_w{i}", [D, 6 * D]); I[f"ada_b{i}"] = din(f"ada_b{i}", [128, 48])
        I[f"nmix{i}"] = din(f"nmix{i}", [128, 8]); I[f"nffn{i}"] = din(f"nffn{i}", [128, 8])
        I[f"w_in{i}"] = din(f"w_in{i}", [D, 3328]); I[f"w_out{i}"] = din(f"w_out{i}", [D, D])
        I[f"qkg{i}"] = din(f"qkg{i}", [128, 4])
        I[f"w_up{i}"] = din(f"w_up{i}", [D, 2 * DFF]); I[f"w_dn{i}"] = din(f"w_dn{i}", [DFF, D])
        I[f"fcw{i}"] = din(f"fcw{i}", [128, NM, 3]); I[f"fcb{i}"] = din(f"fcb{i}", [128, NM])
    I["hcw"] = din("hcw", [128, 12, 3]); I["hcb"] = din("hcb", [128, 12]); I["scw"] = din("scw", [128, 4, 3])
    I["hw1"] = din("hw1", [33, 64]); I["hw2"] = din("hw2", [64, 64]); I["hw3"] = din("hw3", [64, 64])
    I["hw4"] = din("hw4", [64, 1024]); I["hb"] = din("hb", [64, 4]); I["hbd"] = din("hbd", [128, 4])
    I["sink"] = din("sink", [128, 8])
    I["cosT"] = din("cosT", [128, TT]); I["sinT"] = din("sinT", [128, TT])
    I["bmask"] = din("bmask", [6, 128, 512], BF16)
    for tag, NA in (("L", 128), ("C", 4)):
        I["f1cs" + tag] = din("f1cs" + tag, [NA, 2 * NA], BF16)
        for nm in ("fC", "fS", "fnS"): I[nm + tag] = din(nm + tag, [128, 128], BF16)
        for nm in ("fCS", "fnSC"): I[nm + tag] = din(nm + tag, [128, 256], BF16)
        for nm in ("twA", "twB"): I[nm + tag] = din(nm + tag, [128, 512])
        for nm in ("twA2", "twB2"): I[nm + tag] = din(nm + tag, [NA, 1024])
        for nm in ("g3C", "g3nS"): I[nm + tag] = din(nm + tag, [NA, NA], BF16)
        n = 64 * NA
        I["featsT" + tag] = din("featsT" + tag, [33, 2 * n]); I["t01b" + tag] = din("t01b" + tag, [128, 2 * n])
    I["ndel"] = din("ndel", [128, 4])
    OUT = nc.dram_tensor("out", [D, OWN], F32, kind="ExternalOutput").ap()

    S = {}
    for i in range(2):
        S[f"bw_in{i}"] = dscr(f"bw_in{i}", [D, 3328], BF16); S[f"bw_out{i}"] = dscr(f"bw_out{i}", [D, D], BF16)
        S[f"bw_up{i}"] = dscr(f"bw_up{i}", [D, 2 * DFF], BF16); S[f"bw_dn{i}"] = dscr(f"bw_dn{i}", [DFF, D], BF16)
    S["QT"] = dscr("QT", [8, 64, TT], BF16); S["KT"] = dscr("KT", [4, 64, TT], BF16); S["V"] = dscr("V", [TT, 256], BF16)
    S["UT"] = dscr("UT", [512, TT], BF16); S["X0T"] = dscr("X0T", [512, TT], F32)
    S["UF"] = dscr("UF", [512, TT], F32)
    S["YT"] = dscr("YT", [512, TT], F32)
    S["OAT"] = dscr("OAT", [512, TT], BF16); S["OCT"] = dscr("OCT", [512, TT], BF16)
    S["XM"] = dscr("XM", [D, TT], F32); S["X1"] = dscr("X1", [D, TT], F32); S["XM1"] = dscr("XM1", [D, TT], F32)
    S["KRAWL"] = dscr("KRAWL", [512, 2 * SEQ], F32); S["KNL"] = dscr("KNL", [512, 2 * SEQ], BF16)
    S["KRAWC"] = dscr("KRAWC", [512, 2 * CTX], F32); S["KNC"] = dscr("KNC", [512, 2 * CTX], BF16)

    MOD = k.sbuf([128, 2, 48, 2], F32, "mod", persist=True)
    AB = k.sbuf([128, 2, 2, 2, 8, 2], F32, "ab", persist=True)
    ONES = k.sbuf([128, 128], BF16, "ones", persist=True)
    BONES = k.sbuf([128, 128], BF16, "bones", persist=True)
    SEL = k.sbuf([128, 64], F32, "sel", persist=True)
    ESINK = k.sbuf([128, 8], F32, "esink", persist=True)

    phases = []

    def ph_setup():
        k.memset("dve", ONES[:], 1.0, [ONES])
        k.memset("dve", BONES[:], 0.0, [BONES])
        k.memset("dve", BONES[0:64, 0:64], 1.0, [BONES])
        k.memset("dve", BONES[64:128, 64:128], 1.0, [BONES])
        k.memset("dve", SEL[:], 0.0, [SEL])
        k.memset("dve", SEL[64:65, :], 1.0, [SEL])
        snk = k.sbuf([128, 8], F32)
        k.dma("sp", snk[:], I["sink"][:, :], writes=[snk])
        k.act(ESINK[:], snk[:], AF.Exp, [snk], [ESINK])
        stg = [k.sbuf([128, 2048], F32, "stg") for _ in range(3)]
        stb = [k.sbuf([128, 2048], BF16, "stb") for _ in range(3)]
        n = 0
        for i in range(2):
            for src, dst, rows, cols in ((f"w_in{i}", f"bw_in{i}", D, 3328), (f"w_out{i}", f"bw_out{i}", D, D),
                                         (f"w_up{i}", f"bw_up{i}", D, 2 * DFF), (f"w_dn{i}", f"bw_dn{i}", DFF, D)):
                for r0 in range(0, rows, 128):
                    for c0 in range(0, cols, 2048):
                        cw = min(2048, cols - c0)
                        a = stg[n % 3]; bt = stb[n % 3]
                        k.dma("sp", a[:, 0:cw], I[src][r0:r0 + 128, c0:c0 + cw], writes=[a])
                        k.copy("dve" if n % 2 == 0 else "pool", bt[:, 0:cw], a[:, 0:cw], [a], [bt])
                        k.dma("pool" if n % 2 == 0 else "sp", S[dst][r0:r0 + 128, c0:c0 + cw], bt[:, 0:cw], reads=[bt])
                        n += 1
        cv = k.sbuf([128, 8, 2], F32)
        k.dma("sp", cv[:], I["cvec"][:, :, :], writes=[cv])
        sc = k.sbuf([128, 8, 2], F32)
        k.act(sc[:], cv[:], AF.Silu, [cv], [sc])
        wst = [k.sbuf([128, 8, 512], F32, "wst") for _ in range(2)]
        ps = [k.psum([128, 512], F32) for _ in range(2)]
        n = 0
        for i in range(2):
            adb = k.sbuf([128, 48], F32)
            k.dma("sp", adb[:], I[f"ada_b{i}"][:, :], writes=[adb])
            for cb in range(12):
                wt = wst[n % 2]; n += 1
                k.dma("sp", wt[:], I[f"ada_w{i}"][:, cb * 512:(cb + 1) * 512].rearrange("(k p) f -> p k f", p=128), writes=[wt])
                for mi in range(4):
                    m = cb * 4 + mi
                    pt = ps[m % 2]
                    for kk in range(8):
                        k.mm(pt[:, 0:2], wt[:, kk, mi * 128:(mi + 1) * 128], sc[:, kk, :], kk == 0, kk == 7, [wt, sc], [pt])
                    k.ts("dve", MOD[:, i, m, :], pt[:, 0:2], adb[:, m:m + 1], None, ALU.add, None, [pt, adb], [MOD])
            for wh, (nm, sh0, sc0) in enumerate(((f"nmix{i}", 0, 8), (f"nffn{i}", 24, 32))):
                g = k.sbuf([128, 8], F32)
                k.dma("sp", g[:], I[nm][:, :], writes=[g])
                for j in range(2):
                    k.stt("dve", AB[:, i, wh, 0, :, j], MOD[:, i, sc0:sc0 + 8, j], 1.0, g[:], ALU.add, ALU.mult, [MOD, g], [AB])
                    k.copy("dve", AB[:, i, wh, 1, :, j], MOD[:, i, sh0:sh0 + 8, j], [MOD], [AB])
    phases.append(("setup", ph_setup))

    def norm_mod(xh, Wc, i, wh, j, sq, ssps, rstd, tmp, h):
        for kk in range(8):
            k.act(sq[:, kk, 0:Wc], xh[:, kk, 0:Wc], AF.Square, [xh], [sq])
        for (c0, c1) in ((0, min(512, Wc)), (512, Wc)):
            if c1 <= c0: continue
            for kk in range(8):
                k.mm(ssps[:, c0:c1], ONES[:, :], sq[:, kk, c0:c1], kk == 0, kk == 7, [ONES, sq], [ssps])
        k.act(rstd[:, 0:Wc], ssps[:, 0:Wc], AF.Sqrt, [ssps], [rstd], bias=1e-6, scale=1.0 / D)
        k.recip(rstd[:, 0:Wc], rstd[:, 0:Wc], [rstd], [rstd])
        for kk in range(8):
            t = tmp[kk % 2]
            k.stt("dve", t[:, 0:Wc], xh[:, kk, 0:Wc], AB[:, i, wh, 0, kk, j:j + 1], rstd[:, 0:Wc], ALU.mult, ALU.mult, [xh, AB, rstd], [t])
            k.act(h[:, kk, 0:Wc], t[:, 0:Wc], AF.Identity, [t, AB], [h], bias=AB[:, i, wh, 1, kk, j:j + 1], scale=1.0)

    CHUNKS_ALL = [(c * 512, 512, c == 0, c == 15, 0) for c in range(16)] + [(SEQ, CTX, True, True, 1)]
    CHUNKS_E = [(c * 512, 512, c == 0, False, 0) for c in range(9)]
    CTXCH = (SEQ, CTX, True, True, 1)

    def load_xh(xh, src, t0, W, ledge, redge, q="sp"):
        lo = 2 if ledge else 1
        hi = W + 2 if redge else W + 3
        if ledge: k.memset("pool", xh[:, :, 1:2], 0.0, [xh])
        if redge: k.memset("pool", xh[:, :, W + 2:W + 3], 0.0, [xh])
        k.dma(q, xh[:, :, lo:hi], src[:, t0 - 2 + lo:t0 - 2 + hi].rearrange("(k p) t -> p k t", p=128), writes=[xh])

    def make_inproj(i, XIN, chunks):
        def ph():
            w = k.sbuf([128, 8, 3328], BF16, "w_in")
            for kk in range(8):
                k.dma("sp", w[:, kk, :], S[f"bw_in{i}"][kk * 128:(kk + 1) * 128, :], writes=[w])
            qkg = k.sbuf([128, 4], F32); k.dma("sp", qkg[:], I[f"qkg{i}"][:, :], writes=[qkg])
            if i == 0:
                cw = k.sbuf([128, 12, 3], F32); k.dma("sp", cw[:], I["hcw"][:, :, :], writes=[cw])
                cb = k.sbuf([128, 12], F32); k.dma("sp", cb[:], I["hcb"][:, :], writes=[cb])
            else:
                cw = k.sbuf([128, 4, 3], F32); k.dma("sp", cw[:], I["scw"][:, :, :], writes=[cw])
            xhs = [k.sbuf([128, 8, 516], F32, "xh") for _ in range(2)]
            for t_ in xhs: k.memset("pool", t_[:], 0.0, [t_])
            sq = k.sbuf([128, 8, 516], BF16, "sq")
            h = k.sbuf([128, 8, 516], BF16, "h")
            rstd = k.sbuf([128, 516], F32, "rstd")
            tmp = [k.sbuf([128, 516], F32, "tmp") for _ in range(2)]
            ssps = k.psum([128, 1024], F32, "ssps")
            zps = [k.psum([128, 512], F32, "zps") for _ in range(4)]
            cps = k.psum([128, 1024], F32, "cps")
            cosb = k.sbuf([128, 512], F32, "cos"); sinb = k.sbuf([128, 512], F32, "sin")
            sq2 = [k.sbuf([128, 512], BF16, "sq2") for _ in range(2)]
            rs = [k.sbuf([128, 512], F32, "rs") for _ in range(2)]
            ta = [k.sbuf([128, 512], F32, "ta") for _ in range(2)]
            tb = [k.sbuf([128, 512], F32, "tb") for _ in range(2)]
            qo = [k.sbuf([128, 512], BF16, "qo") for _ in range(2)]
            vo = [k.sbuf([128, 256], BF16, "vo") for _ in range(2)]
            asb = [k.sbuf([128, 516], F32, "asb") for _ in range(3)]
            c1 = [k.sbuf([128, 512], F32, "c1") for _ in range(3)]
            uo = [k.sbuf([128, 512], F32, "uo") for _ in range(2)]
            ub = [k.sbuf([128, 512], BF16, "ub") for _ in range(2)]
            pp = [k.sbuf([128, 516], F32, "pp") for _ in range(2)]
            nq = [0]
            import os
            PARTS = os.environ.get("INPROJ_PARTS", "nqvc")
            NCHK = int(os.environ.get("INPROJ_NCH", "99"))
            for ci, (t0, W, le, re, j) in enumerate(chunks[:NCHK]):
                Wc = W + 4
                do_q = (t0 < E) or j == 1
                if i == 1 and j == 1: do_q = False
                xh = xhs[ci % 2]
                load_xh(xh, XIN, t0, W, le, re)
                norm_mod(xh, Wc, i, 0, j, sq, ssps, rstd, tmp, h)
                k.dma("sp", cosb[:, 0:W], I["cosT"][:, t0:t0 + W], writes=[cosb])
                k.dma("sp", sinb[:, 0:W], I["sinT"][:, t0:t0 + W], writes=[sinb])
                for pr in range(6):
                    if "q" not in PARTS: continue
                    if pr < 4 and not do_q: continue
                    n = nq[0]; nq[0] += 1
                    zp = zps[(2 * n) % 4]; zsp = zps[(2 * n + 1) % 4]
                    for kk in range(8):
                        k.mm(zp[:, 0:W], w[:, kk, pr * 128:(pr + 1) * 128], h[:, kk, 2:W + 2], kk == 0, kk == 7, [w, h], [zp])
                    for kk in range(8):
                        k.mm(zsp[:, 0:W], w[:, kk, 2560 + pr * 128:2560 + (pr + 1) * 128], h[:, kk, 2:W + 2], kk == 0, kk == 7, [w, h], [zsp])
                    s2 = sq2[n % 2]; r_ = rs[n % 2]; a_ = ta[n % 2]; b_ = tb[n % 2]; q_ = qo[n % 2]
                    gi = 0 if pr < 4 else 2
                    k.act(s2[:, 0:W], zp[:, 0:W], AF.Square, [zp], [s2])
                    k.stt("dve", a_[:, 0:W], zp[:, 0:W], qkg[:, gi:gi + 1], cosb[:, 0:W], ALU.mult, ALU.mult, [zp, qkg, cosb], [a_])
                    k.stt("dve", b_[:, 0:W], zsp[:, 0:W], qkg[:, gi + 1:gi + 2], sinb[:, 0:W], ALU.mult, ALU.mult, [zsp, qkg, sinb], [b_])
                    k.mm(zp[:, 0:W], BONES[:, :], s2[:, 0:W], True, True, [BONES, s2], [zp])
                    k.act(r_[:, 0:W], zp[:, 0:W], AF.Sqrt, [zp], [r_], bias=1e-6, scale=1.0 / 64)
                    k.recip(r_[:, 0:W], r_[:, 0:W], [r_], [r_])
                    QS = os.environ.get("QSKIP", "")
                    pe_ = "dve" if "pool" in QS else "pool"
                    k.tt(pe_, a_[:, 0:W], a_[:, 0:W], b_[:, 0:W], ALU.add, [a_, b_], [a_])
                    k.tt(pe_, q_[:, 0:W], a_[:, 0:W], r_[:, 0:W], ALU.mult, [a_, r_], [q_])
                    for hf in range(2):
                        if "dma" in QS: continue
                        if pr < 4:
                            dst = S["QT"][2 * pr + hf, :, t0:t0 + W]
                        else:
                            dst = S["KT"][2 * (pr - 4) + hf, :, t0:t0 + W]
                        k.dma("pool", dst, q_[hf * 64:(hf + 1) * 64, 0:W], reads=[q_])
                for tj in range(W // 128):
                    if "v" not in PARTS: continue
                    n = nq[0]; nq[0] += 1
                    vp = zps[n % 4]
                    for kk in range(8):
                        k.mm(vp[:, 0:256], h[:, kk, 2 + tj * 128:2 + (tj + 1) * 128], w[:, kk, 768:1024], kk == 0, kk == 7, [w, h], [vp])
                    v_ = vo[n % 2]
                    k.copy("act", v_[:, :], vp[:, 0:256], [vp], [v_])
                    k.dma("pool", S["V"][t0 + tj * 128:t0 + (tj + 1) * 128, :], v_[:, :], reads=[v_])
                def convproj(m, dst):
                    for (c0, c1_) in ((0, min(512, Wc)), (512, Wc)):
                        if c1_ <= c0: continue
                        for kk in range(8):
                            k.mm(cps[:, c0:c1_], w[:, kk, 1024 + m * 128:1024 + (m + 1) * 128], h[:, kk, c0:c1_], kk == 0, kk == 7, [w, h], [cps])
                    k.copy("act", dst[:, 0:Wc], cps[:, 0:Wc], [cps], [dst])
                    if le: k.memset("pool", dst[:, 1:2], 0.0, [dst])
                    if re: k.memset("pool", dst[:, W + 2:W + 3], 0.0, [dst])
                def conv3(out, a, wts, m, bias, eng="dve"):
                    if bias is not None:
                        k.ts(eng, out[:, 0:W], a[:, 2:W + 2], wts[:, m, 1:2], bias, ALU.mult, ALU.add, [a, wts, cb], [out])
                    else:
                        k.ts(eng, out[:, 0:W], a[:, 2:W + 2], wts[:, m, 1:2], None, ALU.mult, None, [a, wts], [out])
                    k.stt(eng, out[:, 0:W], a[:, 1:W + 1], wts[:, m, 0:1], out[:, 0:W], ALU.mult, ALU.add, [a, wts, out], [out])
                    k.stt(eng, out[:, 0:W], a[:, 3:W + 3], wts[:, m, 2:3], out[:, 0:W], ALU.mult, ALU.add, [a, wts, out], [out])
                for jc in range(4):
                    if "c" not in PARTS: continue
                    if i == 0:
                        convproj(4 + jc, asb[0]); conv3(c1[0], asb[0], cw, 4 + jc, cb[:, 4 + jc:5 + jc])
                        convproj(8 + jc, asb[1]); conv3(c1[1], asb[1], cw, 8 + jc, cb[:, 8 + jc:9 + jc])
                        u_ = uo[jc % 2]; ub_ = ub[jc % 2]
                        k.tt("dve", u_[:, 0:W], c1[0][:, 0:W], c1[1][:, 0:W], ALU.mult, [c1[0], c1[1]], [u_])
                        k.copy("pool", ub_[:, 0:W], u_[:, 0:W], [u_], [ub_])
                        k.dma("pool", S["UF"][jc * 128:(jc + 1) * 128, t0:t0 + W], u_[:, 0:W], reads=[u_])
                        k.dma("pool", S["UT"][jc * 128:(jc + 1) * 128, t0:t0 + W], ub_[:, 0:W], reads=[ub_])
                        if do_q:
                            convproj(jc, asb[2]); conv3(c1[2], asb[2], cw, jc, cb[:, jc:jc + 1])
                            k.dma("pool", S["X0T"][jc * 128:(jc + 1) * 128, t0:t0 + W], c1[2][:, 0:W], reads=[c1[2]])
                    elif j == 0:
                        convproj(4 + jc, asb[0]); convproj(8 + jc, asb[1])
                        p_ = pp[jc % 2]
                        k.tt("dve", p_[:, 0:Wc], asb[0][:, 0:Wc], asb[1][:, 0:Wc], ALU.mult, [asb[0], asb[1]], [p_])
                        conv3(c1[0], p_, cw, jc, None)
                        convproj(jc, asb[2])
                        ub_ = ub[jc % 2]
                        k.tt("pool", ub_[:, 0:W], c1[0][:, 0:W], asb[2][:, 2:W + 2], ALU.mult, [c1[0], asb[2]], [ub_])
                        k.dma("pool", S["OCT"][jc * 128:(jc + 1) * 128, t0:t0 + W], ub_[:, 0:W], reads=[ub_])
        return ph

    def make_attn(i):
        def ph():
            NKT = TT // 128
            kT = k.sbuf([64, TT], BF16, "kT")
            va = k.sbuf([128, NKT, 65], BF16, "va")
            qT = [k.sbuf([64, 512], BF16, "qT") for _ in range(2)]
            sps = [k.psum([128, 512], F32, "sps") for _ in range(4)]
            ops_ = [k.psum([128, 512], F32, "ops") for _ in range(2)]
            bps = k.psum([128, 512], F32, "bps")
            pT = [k.sbuf([128, 512], BF16, "pT") for _ in range(4)]
            osb = [k.sbuf([65, 512], F32, "osb") for _ in range(2)]
            rb = [k.sbuf([64, 512], F32, "rb") for _ in range(2)]
            ob = [k.sbuf([64, 512], BF16, "ob") for _ in range(2)]
            if i == 1:
                bm = k.sbuf([128, 6, 512], BF16, "bm")
                for r in range(6):
                    k.dma("sp", bm[:, r, :], I["bmask"][r, :, :], writes=[bm])
            qch = []
            for c in range(9):
                t0 = c * 512
                if i == 0:
                    kts = [(kt, 0, 512, None) for kt in range(NKT)]
                else:
                    kts = []
                    for r in range(-1, 5):
                        kt = 4 * c + r
                        if kt < 0 or kt >= E // 128: continue
                        f0 = max(0, 128 * (r - 1)); f1 = min(512, 128 * (r + 2))
                        kts.append((kt, f0, f1, r + 1))
                    kts += [(64, 0, 512, None), (65, 0, 512, None)]
                qch.append((t0, 512, kts))
            if i == 0:
                qch.append((SEQ, CTX, [(64, 0, CTX, None), (65, 0, CTX, None)]))
            n = 0; nh = 0
            for jkv in range(4):
                k.dma("sp", kT[:, :], S["KT"][jkv, :, :], writes=[kT])
                k.dma("sp", va[:, :, 0:64], S["V"][:, jkv * 64:(jkv + 1) * 64].rearrange("(n p) d -> p n d", p=128), writes=[va])
                k.memset("pool", va[:, :, 64:65], 1.0, [va])
                for g in range(2):
                    hq = 2 * jkv + g
                    for (t0, W, kts) in qch:
                        q_ = qT[nh % 2]; op_ = ops_[nh % 2]; o_ = osb[nh % 2]; r_ = rb[nh % 2]; b_ = ob[nh % 2]
                        nh += 1
                        k.dma("sp", q_[:, 0:W], S["QT"][hq, :, t0:t0 + W], writes=[q_])
                        if i == 1:
                            pass
                        nk = len(kts)
                        def smm(idx):
                            kt, f0, f1, mi = kts[idx]
                            sp_ = sps[(n + idx) % 4]
                            k.mm(sp_[:, f0:f1], kT[:, kt * 128:(kt + 1) * 128], q_[:, f0:f1], True, True, [kT, q_], [sp_])
                        smm(0)
                        if nk > 1: smm(1)
                        for idx in range(nk):
                            if idx + 2 < nk: smm(idx + 2)
                            kt, f0, f1, mi = kts[idx]
                            sp_ = sps[(n + idx) % 4]; p_ = pT[(n + idx) % 4]
                            if mi is not None and (f0 > 0 or f1 < W):
                                k.memset("pool", p_[:, 0:W], 0.0, [p_])
                            k.act(p_[:, f0:f1], sp_[:, f0:f1], AF.Exp, [sp_], [p_], scale=0.125)
                            if mi is not None:
                                k.tt("pool", p_[:, f0:f1], p_[:, f0:f1], bm[:, mi, f0:f1], ALU.mult, [p_, bm], [p_])
                            k.mm(op_[0:65, 0:W], va[:, kt, :], p_[:, 0:W], idx == 0, idx == nk - 1, [va, p_], [op_])
                        n += nk
                        k.copy("act", o_[:, 0:W], op_[0:65, 0:W], [op_], [o_])
                        k.mm(bps[0:64, 0:W], SEL[0:65, :], o_[0:65, 0:W], True, True, [SEL, o_], [bps])
                        if i == 1:
                            k.ts("dve", r_[:, 0:W], bps[0:64, 0:W], ESINK[0:64, hq:hq + 1], None, ALU.add, None, [bps, ESINK], [r_])
                            k.recip(r_[:, 0:W], r_[:, 0:W], [r_], [r_])
                        else:
                            k.recip(r_[:, 0:W], bps[0:64, 0:W], [bps], [r_])
                        k.tt("dve", b_[:, 0:W], o_[0:64, 0:W], r_[:, 0:W], ALU.mult, [o_, r_], [b_])
                        k.dma("pool", S["OAT"][hq * 64:(hq + 1) * 64, t0:t0 + W], b_[:, 0:W], reads=[b_])
        return ph

    def make_outproj(i, XIN, XOUT, chunks):
        def ph():
            wa = k.sbuf([64, 8, D], BF16, "woa")
            wc = k.sbuf([128, 4, D], BF16, "woc")
            k.dma("sp", wa[:], S[f"bw_out{i}"][0:512, :].rearrange("(h d) f -> d h f", d=64), writes=[wa])
            k.dma("sp", wc[:], S[f"bw_out{i}"][512:1024, :].rearrange("(c p) f -> p c f", p=128), writes=[wc])
            oa = [k.sbuf([64, 8, 512], BF16, "oa") for _ in range(2)]
            oc = [k.sbuf([128, 4, 512], BF16, "oc") for _ in range(2)]
            xs = [k.sbuf([128, 8, 512], F32, "xs") for _ in range(2)]
            xo = [k.sbuf([128, 8, 512], F32, "xo") for _ in range(2)]
            ps = [k.psum([128, 512], F32, "ps") for _ in range(4)]
            n = 0
            for ci, (t0, W, le, re, j) in enumerate(chunks):
                a_ = oa[ci % 2]; c_ = oc[ci % 2]; x_ = xs[ci % 2]; o_ = xo[ci % 2]
                k.dma("sp", a_[:, :, 0:W], S["OAT"][:, t0:t0 + W].rearrange("(h d) t -> d h t", d=64), writes=[a_])
                k.dma("sp", c_[:, :, 0:W], S["OCT"][:, t0:t0 + W].rearrange("(c p) t -> p c t", p=128), writes=[c_])
                k.dma("sp", x_[:, :, 0:W], XIN[:, t0:t0 + W].rearrange("(k p) t -> p k t", p=128), writes=[x_])
                for m in range(8):
                    p_ = ps[n % 4]; n += 1
                    for hh_ in range(8):
                        k.mm(p_[:, 0:W], wa[:, hh_, m * 128:(m + 1) * 128], a_[:, hh_, 0:W], hh_ == 0, False, [wa, a_], [p_])
                    for cc in range(4):
                        k.mm(p_[:, 0:W], wc[:, cc, m * 128:(m + 1) * 128], c_[:, cc, 0:W], False, cc == 3, [wc, c_], [p_])
                    k.stt("dve", o_[:, m, 0:W], p_[:, 0:W], MOD[:, i, 16 + m, j:j + 1], x_[:, m, 0:W], ALU.mult, ALU.add, [p_, MOD, x_], [o_])
                k.dma("pool", XOUT[:, t0:t0 + W].rearrange("(k p) t -> p k t", p=128), o_[:, :, 0:W], reads=[o_])
        return ph

    def make_ffn(i, XIN, XOUT, chunks, final=False):
        def ph():
            wu = k.sbuf([128, 8, 2 * DFF], BF16, "wu")
            for kk in range(8):
                k.dma("sp", wu[:, kk, :], S[f"bw_up{i}"][kk * 128:(kk + 1) * 128, :], writes=[wu])
            wd = [k.sbuf([128, NM, 128], BF16, "wd") for _ in range(2)]
            cw = k.sbuf([128, NM, 3], F32); k.dma("sp", cw[:], I[f"fcw{i}"][:, :, :], writes=[cw])
            cb = k.sbuf([128, NM], F32); k.dma("sp", cb[:], I[f"fcb{i}"][:, :], writes=[cb])
            xh = k.sbuf([128, 8, 516], F32, "xh")
            k.memset("pool", xh[:], 0.0, [xh])
            sq = k.sbuf([128, 8, 516], BF16, "sq")
            h = k.sbuf([128, 8, 516], BF16, "h")
            rstd = k.sbuf([128, 516], F32, "rstd")
            tmp = [k.sbuf([128, 516], F32, "tmp") for _ in range(2)]
            gg = k.sbuf([128, NM, 512], BF16, "gg")
            asb = [k.sbuf([128, 516], F32, "asb") for _ in range(2)]
            c1 = [k.sbuf([128, 512], F32, "c1") for _ in range(2)]
            xo = [k.sbuf([128, 512], F32, "xo") for _ in range(2)]
            ssps = k.psum([128, 1024], F32, "ssps")
            aps = [k.psum([128, 1024], F32, "aps") for _ in range(2)]
            vps = [k.psum([128, 512], F32, "vps") for _ in range(2)]
            n = 0; nd = 0
            for ci, (t0, W, le, re, j) in enumerate(chunks):
                Wc = W + 4
                load_xh(xh, XIN, t0, W, le, re)
                norm_mod(xh, Wc, i, 1, j, sq, ssps, rstd, tmp, h)
                for m in range(NM):
                    ap_ = aps[n % 2]; vp_ = vps[n % 2]; a_ = asb[n % 2]; c_ = c1[n % 2]; n += 1
                    for (c0, c1_) in ((0, min(512, Wc)), (512, Wc)):
                        if c1_ <= c0: continue
                        for kk in range(8):
                            k.mm(ap_[:, c0:c1_], wu[:, kk, m * 128:(m + 1) * 128], h[:, kk, c0:c1_], kk == 0, kk == 7, [wu, h], [ap_])
                    for kk in range(8):
                        k.mm(vp_[:, 0:W], wu[:, kk, DFF + m * 128:DFF + (m + 1) * 128], h[:, kk, 2:W + 2], kk == 0, kk == 7, [wu, h], [vp_])
                    k.copy("act", a_[:, 0:Wc], ap_[:, 0:Wc], [ap_], [a_])
                    if le: k.memset("pool", a_[:, 1:2], 0.0, [a_])
                    if re: k.memset("pool", a_[:, W + 2:W + 3], 0.0, [a_])
                    eng = "dve"
                    k.ts(eng, c_[:, 0:W], a_[:, 2:W + 2], cw[:, m, 1:2], cb[:, m:m + 1], ALU.mult, ALU.add, [a_, cw, cb], [c_])
                    k.stt(eng, c_[:, 0:W], a_[:, 1:W + 1], cw[:, m, 0:1], c_[:, 0:W], ALU.mult, ALU.add, [a_, cw, c_], [c_])
                    k.stt(eng, c_[:, 0:W], a_[:, 3:W + 3], cw[:, m, 2:3], c_[:, 0:W], ALU.mult, ALU.add, [a_, cw, c_], [c_])
                    k.act(c_[:, 0:W], c_[:, 0:W], AF.Gelu_apprx_tanh, [c_], [c_])
                    k.tt("dve", gg[:, m, 0:W], c_[:, 0:W], vp_[:, 0:W], ALU.mult, [c_, vp_], [gg])
                for mo in range(8):
                    wd_ = wd[nd % 2]; o_ = xo[nd % 2]; p_ = vps[nd % 2]; nd += 1
                    k.dma("sp", wd_[:], S[f"bw_dn{i}"][:, mo * 128:(mo + 1) * 128].rearrange("(m p) f -> p m f", p=128), writes=[wd_])
                    for m in range(NM):
                        k.mm(p_[:, 0:W], wd_[:, m, :], gg[:, m, 0:W], m == 0, m == NM - 1, [wd_, gg], [p_])
                    k.stt("dve", o_[:, 0:W], p_[:, 0:W], MOD[:, i, 40 + mo, j:j + 1], xh[:, mo, 2:W + 2], ALU.mult, ALU.add, [p_, MOD, xh], [o_])
                    k.dma("pool", XOUT[mo * 128:(mo + 1) * 128, t0:t0 + W], o_[:, 0:W], reads=[o_])
        return ph

    def make_filter(tag, n):
        KRAW = S["KRAW" + tag]; KN = S["KN" + tag]
        def ph():
            w1 = k.sbuf([33, 64], F32); k.dma("sp", w1[:], I["hw1"][:, :], writes=[w1])
            w2 = k.sbuf([64, 64], F32); k.dma("sp", w2[:], I["hw2"][:, :], writes=[w2])
            w3 = k.sbuf([64, 64], F32); k.dma("sp", w3[:], I["hw3"][:, :], writes=[w3])
            w4 = k.sbuf([64, 1024], F32); k.dma("sp", w4[:], I["hw4"][:, :], writes=[w4])
            hb = k.sbuf([64, 4], F32); k.dma("sp", hb[:], I["hb"][:, :], writes=[hb])
            ndel = k.sbuf([128, 4], F32); k.dma("sp", ndel[:], I["ndel"][:, :], writes=[ndel])
            asum = k.sbuf([128, 4, 40], F32, "asum")
            k.memset("dve", asum[:], 0.0, [asum])
            ft = [k.sbuf([33, 512], F32, "ft") for _ in range(2)]
            t01 = [k.sbuf([128, 512], F32, "t01") for _ in range(2)]
            hid = [k.sbuf([64, 512], F32, "hid") for _ in range(3)]
            ki = k.sbuf([64, 512], I32, "ki")
            win = [k.sbuf([128, 512], F32, "win") for _ in range(2)]
            kr = [k.sbuf([128, 512], F32, "kr") for _ in range(2)]
            junk = k.sbuf([128, 512], F32, "junk")
            ps = [k.psum([128, 512], F32, "ps") for _ in range(4)]
            NCH = (2 * n) // 512
            n_ = 0
            for c in range(NCH):
                q0 = c * 512
                f_ = ft[c % 2]; t_ = t01[c % 2]
                k.dma("sp", f_[:, :], I["featsT" + tag][:, q0:q0 + 512], writes=[f_])
                k.dma("sp", t_[:, :], I["t01b" + tag][:, q0:q0 + 512], writes=[t_])
                src = f_; srcK = 33
                for li, wl in enumerate((w1, w2, w3)):
                    p_ = ps[n_ % 4]; n_ += 1
                    k.mm(p_[0:64, :], wl[0:srcK, :], src[0:srcK, :], True, True, [wl, src], [p_])
                    hd = hid[li]
                    k.ts("dve", hd[:, :], p_[0:64, :], hb[:, li:li + 1], hb[:, 3:4], ALU.add, ALU.mult, [p_, hb], [hd])
                    k.ts("dve", ki[:, :], hd[:, :], float(1.0 / (2 * np.pi)), None, ALU.mult, None, [hd], [ki])
                    k.stt("dve", hd[:, :], ki[:, :], float(-2 * np.pi), hd[:, :], ALU.mult, ALU.add, [ki, hd], [hd])
                    k.act(hd[:, :], hd[:, :], AF.Sin, [hd], [hd])
                    src = hd; srcK = 64
                segs = []
                if q0 + 512 <= n: segs = [(0, 512, 0)]
                elif q0 >= n: segs = [(0, 512, 512)]
                else: segs = [(0, n - q0, 0), (n - q0, 512, 512)]
                for jc in range(4):
                    p_ = ps[n_ % 4]; n_ += 1
                    for (a0, a1, off) in segs:
                        k.mm(p_[:, a0:a1], w4[:, off + jc * 128:off + (jc + 1) * 128], hid[2][:, a0:a1], True, True, [w4, hid[2]], [p_])
                    wn = win[jc % 2]; kr_ = kr[jc % 2]
                    k.act(wn[:, :], t_[:, :], AF.Exp, [t_, ndel], [wn], scale=ndel[:, jc:jc + 1])
                    k.stt("dve", kr_[:, :], wn[:, :], 0.05, p_[:, :], ALU.add, ALU.mult, [wn, p_], [kr_])
                    if q0 <= n < q0 + 512:
                        k.memset("dve", kr_[:, n - q0:n - q0 + 1], 0.0, [kr_])
                    k.act(junk[:, :], kr_[:, :], AF.Abs, [kr_], [junk, asum], accum=asum[:, jc, c:c + 1])
                    k.dma("pool", KRAW[jc * 128:(jc + 1) * 128, q0:q0 + 512], kr_[:, :], reads=[kr_])
            rn = k.sbuf([128, 4], F32, "rn")
            for jc in range(4):
                k.op("dve", lambda e, jc=jc: e.reduce_sum(out=rn[:, jc:jc + 1], in_=asum[:, jc, 0:NCH], axis=mybir.AxisListType.X), [asum], [rn])
            k.recip(rn[:, :], rn[:, :], [rn], [rn])
            k.barrier()
            big = [k.sbuf([128, 2048], F32, "big") for _ in range(2)]
            bigb = [k.sbuf([128, 2048], BF16, "bigb") for _ in range(2)]
            n2 = 0
            for jc in range(4):
                for q0 in range(0, 2 * n, 2048):
                    qw = min(2048, 2 * n - q0)
                    b_ = big[n2 % 2]; bb_ = bigb[n2 % 2]; n2 += 1
                    k.dma("sp", b_[:, 0:qw], KRAW[jc * 128:(jc + 1) * 128, q0:q0 + qw], writes=[b_])
                    k.ts("dve", bb_[:, 0:qw], b_[:, 0:qw], rn[:, jc:jc + 1], None, ALU.mult, None, [b_, rn], [bb_])
                    k.dma("pool", KN[jc * 128:(jc + 1) * 128, q0:q0 + qw], bb_[:, 0:qw], reads=[bb_])
        return ph

    def make_fftconv(tag, NA, n, tok0, n_out_blocks):
        KN = S["KN" + tag]
        NR = NA // 2
        GF = 512 // NA
        GB = 512 // (2 * NA)
        def ph():
            cst = {}
            for nm, shp, dt in (("f1cs", [NA, 2 * NA], BF16), ("fC", [128, 128], BF16), ("fS", [128, 128], BF16), ("fnS", [128, 128], BF16),
                                ("fCS", [128, 256], BF16), ("fnSC", [128, 256], BF16), ("twA", [128, 512], F32), ("twB", [128, 512], F32),
                                ("twA2", [NA, 1024], F32), ("twB2", [NA, 1024], F32), ("g3C", [NA, NA], BF16), ("g3nS", [NA, NA], BF16)):
                cst[nm] = k.sbuf(shp, dt, nm)
                k.dma("sp", cst[nm][:], I[nm + tag][tuple(slice(None) for _ in shp)], writes=[cst[nm]])
            LG = 32 if NA == 128 else 128
            xu = [k.sbuf([NR, LG, 128], BF16, "xu") for _ in range(2)]
            xk = [k.sbuf([NA, LG, 128], BF16, "xk") for _ in range(2)]
            s1 = [k.psum([128, 512], F32, "s1") for _ in range(2)]
            s2 = [k.psum([128, 512], F32, "s2") for _ in range(2)]
            s3 = [k.psum([128, 1024], F32, "s3") for _ in range(1)]
            s4 = [k.psum([128, 512], F32, "s4") for _ in range(2)]
            ta = k.sbuf([128, 512], F32, "ta"); tb = k.sbuf([128, 512], F32, "tb")
            bu = k.sbuf([128, 2, GF, NA], BF16, "bu"); bk = k.sbuf([128, 2, GF, NA], BF16, "bk")
            kh = k.sbuf([128, 2, 512], F32, "kh")
            m1 = k.sbuf([128, 512], F32, "m1"); m2 = k.sbuf([128, 512], F32, "m2")
            m3 = k.sbuf([128, 512], F32, "m3"); m4 = k.sbuf([128, 512], F32, "m4")
            yh = k.sbuf([128, 2, GF, NA], BF16, "yh")
            t2a = k.sbuf([NA, 1024], F32, "t2a"); t2b = k.sbuf([NA, 1024], F32, "t2b")
            y3 = k.sbuf([NA, 2, 4, 128], BF16, "y3")
            yo = [k.sbuf([n_out_blocks, 4, 128], F32, "yo") for _ in range(2)]
            MO = n_out_blocks
            ng = 0
            def fwd(x, K_, cbase, bdst):
                for half in range(2):
                    bank = s1[half]
                    for cc in range(GB):
                        ch = cbase + half * GB + cc
                        k.mm(bank[:, cc * 2 * NA:(cc + 1) * 2 * NA], x[0:K_, ch, :], cst["f1cs"][0:K_, :], True, True, [x, cst["f1cs"]], [bank])
                    k.tt("dve", ta[:, :], bank[:, :], cst["twA"][:, :], ALU.mult, [bank, cst["twA"]], [ta])
                    k.tt("dve", tb[:, :], bank[:, :], cst["twB"][:, :], ALU.mult, [bank, cst["twB"]], [tb])
                    tav = ta[:, :].rearrange("p (g r f) -> p g r f", g=GB, r=2)
                    tbv = tb[:, :].rearrange("p (g r f) -> p g r f", g=GB, r=2)
                    k.tt("pool", bdst[:, 0, half * GB:(half + 1) * GB, :], tav[:, :, 0, :], tbv[:, :, 1, :], ALU.subtract, [ta, tb], [bdst])
                    k.tt("pool", bdst[:, 1, half * GB:(half + 1) * GB, :], tav[:, :, 1, :], tbv[:, :, 0, :], ALU.subtract, [ta, tb], [bdst])
                bre = bdst[:, 0, :, :].rearrange("p g f -> p (g f)"); bim = bdst[:, 1, :, :].rearrange("p g f -> p (g f)")
                k.mm(s2[0][:, :], cst["fC"][:, :], bre, True, False, [cst["fC"], bdst], [s2[0]])
                k.mm(s2[0][:, :], cst["fS"][:, :], bim, False, True, [cst["fS"], bdst], [s2[0]])
                k.mm(s2[1][:, :], cst["fC"][:, :], bim, True, False, [cst["fC"], bdst], [s2[1]])
                k.mm(s2[1][:, :], cst["fnS"][:, :], bre, False, True, [cst["fnS"], bdst], [s2[1]])
            for lg in range(512 // LG):
                xu_ = xu[lg % 2]; xk_ = xk[lg % 2]
                k.dma("sp", xu_[:, :, :], S["UT"][lg * LG:(lg + 1) * LG, tok0:tok0 + n].rearrange("c (a p) -> a c p", p=128), writes=[xu_])
                k.dma("sp", xk_[:, :, :], KN[lg * LG:(lg + 1) * LG, :].rearrange("c (a p) -> a c p", p=128), writes=[xk_])
                for gf in range(LG // GF):
                    cbase = gf * GF
                    fwd(xk_, NA, cbase, bk)
                    k.copy("act", kh[:, 0, :], s2[0][:, :], [s2[0]], [kh])
                    k.copy("act", kh[:, 1, :], s2[1][:, :], [s2[1]], [kh])
                    fwd(xu_, NR, cbase, bu)
                    k.tt("dve", m1[:, :], s2[0][:, :], kh[:, 0, :], ALU.mult, [s2[0], kh], [m1])
                    k.tt("dve", m2[:, :], s2[1][:, :], kh[:, 1, :], ALU.mult, [s2[1], kh], [m2])
                    k.tt("dve", m3[:, :], s2[0][:, :], kh[:, 1, :], ALU.mult, [s2[0], kh], [m3])
                    k.tt("dve", m4[:, :], s2[1][:, :], kh[:, 0, :], ALU.mult, [s2[1], kh], [m4])
                    k.tt("pool", yh[:, 0, :, :].rearrange("p g f -> p (g f)"), m1[:, :], m2[:, :], ALU.subtract, [m1, m2], [yh])
                    k.tt("pool", yh[:, 1, :, :].rearrange("p g f -> p (g f)"), m3[:, :], m4[:, :], ALU.add, [m3, m4], [yh])
                    for sg in range(GF // 4):
                        b3 = s3[0]
                        for cc in range(4):
                            ch = sg * 4 + cc
                            k.mm(b3[0:NA, cc * 256:(cc + 1) * 256], yh[:, 0, ch, :], cst["fCS"][:, :], True, False, [yh, cst["fCS"]], [b3])
                            k.mm(b3[0:NA, cc * 256:(cc + 1) * 256], yh[:, 1, ch, :], cst["fnSC"][:, :], False, True, [yh, cst["fnSC"]], [b3])
                        k.tt("dve", t2a[:, :], b3[0:NA, :], cst["twA2"][:, :], ALU.mult, [b3, cst["twA2"]], [t2a])
                        k.tt("dve", t2b[:, :], b3[0:NA, :], cst["twB2"][:, :], ALU.mult, [b3, cst["twB2"]], [t2b])
                        av = t2a[:, :].rearrange("p (g r f) -> p g r f", g=4, r=2)
                        bv = t2b[:, :].rearrange("p (g r f) -> p g r f", g=4, r=2)
                        k.tt("pool", y3[:, 0, :, :], av[:, :, 0, :], bv[:, :, 1, :], ALU.add, [t2a, t2b], [y3])
                        k.tt("pool", y3[:, 1, :, :], av[:, :, 1, :], bv[:, :, 0, :], ALU.add, [t2a, t2b], [y3])
                        p4 = s4[ng % 2]; yo_ = yo[ng % 2]; ng += 1
                        k.mm(p4[0:MO, :], cst["g3C"][:, 0:MO], y3[:, 0, :, :].rearrange("p g f -> p (g f)"), True, False, [cst["g3C"], y3], [p4])
                        k.mm(p4[0:MO, :], cst["g3nS"][:, 0:MO], y3[:, 1, :, :].rearrange("p g f -> p (g f)"), False, True, [cst["g3nS"], y3], [p4])
                        k.copy("act", yo_[:, :, :].rearrange("p g f -> p (g f)"), p4[0:MO, :], [p4], [yo_])
                        c0 = lg * LG + cbase + sg * 4
                        k.dma("pool", S["YT"][c0:c0 + 4, tok0:tok0 + MO * 128].rearrange("c (a p) -> a c p", p=128), yo_[:, :, :], reads=[yo_])
        return ph

    def make_hycombine(chunks):
        def ph():
            bd = k.sbuf([128, 4], F32); k.dma("sp", bd[:], I["hbd"][:, :], writes=[bd])
            yt = [k.sbuf([128, 512], F32, "yt") for _ in range(2)]
            ut = [k.sbuf([128, 512], F32, "ut") for _ in range(2)]
            x0 = [k.sbuf([128, 512], F32, "x0") for _ in range(2)]
            ob = [k.sbuf([128, 512], BF16, "ob") for _ in range(2)]
            n = 0
            for (t0, W, le, re, j) in chunks:
                for jc in range(4):
                    y_ = yt[n % 2]; u_ = ut[n % 2]; x_ = x0[n % 2]; o_ = ob[n % 2]; n += 1
                    rows = slice(jc * 128, (jc + 1) * 128)
                    k.dma("sp", y_[:, 0:W], S["YT"][rows, t0:t0 + W], writes=[y_])
                    k.dma("sp", u_[:, 0:W], S["UF"][rows, t0:t0 + W], writes=[u_])
                    k.dma("sp", x_[:, 0:W], S["X0T"][rows, t0:t0 + W], writes=[x_])
                    k.stt("dve", y_[:, 0:W], u_[:, 0:W], bd[:, jc:jc + 1], y_[:, 0:W], ALU.mult, ALU.add, [u_, bd, y_], [y_])
                    k.tt("dve", o_[:, 0:W], y_[:, 0:W], x_[:, 0:W], ALU.mult, [y_, x_], [o_])
                    k.dma("pool", S["OCT"][rows, t0:t0 + W], o_[:, 0:W], reads=[o_])
        return ph

    CH_E = [(c * 512, 512, c == 0, c == 8, 0) for c in range(9)]
    CH_OWN = [(c * 512, 512, c == 0, False, 0) for c in range(8)]
    phases.append(("inproj0", make_inproj(0, I["xt0"], CHUNKS_ALL)))
    phases.append(("filterL", make_filter("L", SEQ)))
    phases.append(("filterC", make_filter("C", CTX)))
    phases.append(("fftL", make_fftconv("L", 128, SEQ, 0, E // 128)))
    phases.append(("fftC", make_fftconv("C", 4, CTX, SEQ, 2)))
    phases.append(("hycomb", make_hycombine(CH_E + [CTXCH])))
    phases.append(("attn0", make_attn(0)))
    phases.append(("outproj0", make_outproj(0, I["xt0"], S["XM"], CH_E + [CTXCH])))
    phases.append(("ffn0", make_ffn(0, S["XM"], S["X1"], CH_E + [CTXCH])))
    phases.append(("inproj1", make_inproj(1, S["X1"], CH_E + [CTXCH])))
    phases.append(("attn1", make_attn(1)))
    phases.append(("outproj1", make_outproj(1, S["X1"], S["XM1"], CH_E)))
    phases.append(("ffn1", make_ffn(1, S["XM1"], OUT, CH_OWN)))
    for nm, ph in phases:
        k.phase(ph)
        if stop_after == nm:
            break
    k.close()
    return nc


_CACHE = {}


def kernel(**inputs):
    inp = {kk: np.asarray(v) for kk, v in inputs.items()}
    if "C" not in _CACHE:
        C = _consts()
        C["fftL"] = _fft_consts(128); C["fftC"] = _fft_consts(4)
        C["filtL"] = _filter_consts(SEQ); C["filtC"] = _filter_consts(CTX)
        _CACHE["C"] = C
    C = _CACHE["C"]
    nc = _build()
    in_maps = []
    for core in range(8):
        b, hh = core // 2, core % 2
        in_maps.append(_host_prep(inp, b, hh, C))
    res = run_bass_kernel_spmd(nc, in_maps, core_ids=list(range(8)))
    out = np.empty((4, SEQ, D), np.float32)
    for core in range(8):
        b, hh = core // 2, core % 2
        o = np.asarray(res.results[core]["out"]).T
        if hh == 0:
            out[b, :OWN] = o
        else:
            out[b, OWN:] = o[::-1]
    return out
```

```python
import numpy as np
import ml_dtypes
from contextlib import ExitStack
import concourse.bass as bass
import concourse.mybir as mybir
from concourse.bass_utils import run_bass_kernel_spmd

F32 = mybir.dt.float32
BF16 = mybir.dt.bfloat16
I32 = mybir.dt.int32
AF = mybir.ActivationFunctionType
ALU = mybir.AluOpType
NPBF = ml_dtypes.bfloat16

D = 1024; SEQ = 8192; CTX = 256; TT = SEQ + CTX; E = 4608; OWN = 4096
DFF = 2816; NM = 22
SAME_ENGINE_SYNC = {"act", "pool", "dve"}


class Res:
    __slots__ = ("name", "last_w", "reads", "excl")
    def __init__(self, name, excl=False):
        self.name = name; self.last_w = None; self.reads = {}; self.excl = excl


class Tl:
    def __init__(self, t, r):
        self.t = t; self.r = r
    def __getitem__(self, idx):
        return self.t[idx]


class K:
    ENGS = ("pe", "act", "dve", "pool", "sp")

    def __init__(self, nc):
        self.nc = nc
        self.es = ExitStack()
        self.sem = {}; self.cnt = {}
        for e in self.ENGS:
            self.sem[e] = self.es.enter_context(nc.semaphore("s_" + e))
            self.cnt[e] = 0
        self.dma_sems = {}
        self.dma_key = {}
        self.dma_rr = {}
        self.NDMASEM = {"sp": 32, "pool": 24, "act": 8, "pe": 4, "dve": 4}
        self.seen = {e: {} for e in self.ENGS}
        self.ops = {e: [] for e in self.ENGS}
        self.phase_es = None
        self.nres = 0
        self.ndma = 0

    def sbuf(self, shape, dt, name=None, persist=False):
        self.nres += 1
        name = (name or "t") + "_%d" % self.nres
        es = self.es if persist else self.phase_es
        t = es.enter_context(self.nc.sbuf_tensor(name, list(shape), dt))
        return Tl(t, Res(name))

    def psum(self, shape, dt, name=None):
        self.nres += 1
        name = (name or "p") + "_%d" % self.nres
        t = self.phase_es.enter_context(self.nc.psum_tensor(name, list(shape), dt))
        return Tl(t, Res(name, excl=True))

    def _need(self, reads, writes, eng=None):
        evs = []
        for r in reads:
            if r.last_w is not None: evs.append(r.last_w)
            if r.excl:
                evs.extend((kk[0], kk[1], v) for kk, v in r.reads.items() if not (kk[0] == "eng" and kk[1] == eng))
        for w in writes:
            if w.last_w is not None: evs.append(w.last_w)
            evs.extend((kk[0], kk[1], v) for kk, v in w.reads.items())
        return evs

    def _emit_waits(self, eng, evs):
        need = {}
        for kind, key, val in evs:
            if kind == "eng":
                if key == eng and eng not in SAME_ENGINE_SYNC: continue
                v = val
            else:
                v = val
            if self.seen[eng].get((kind, key), 0) >= v: continue
            if need.get((kind, key), 0) < v: need[(kind, key)] = v
        for (kind, key), v in need.items():
            self.seen[eng][(kind, key)] = v
            sem = self.sem[key] if kind == "eng" else self.dma_sems[key][0]
            self.ops[eng].append(lambda e, sem=sem, v=v: e.wait_ge(sem, v))

    def _commit(self, ev, reads, writes):
        for r in reads:
            kk = (ev[0], ev[1])
            if r.reads.get(kk, 0) < ev[2]: r.reads[kk] = ev[2]
        for w in writes:
            w.last_w = ev; w.reads = {}

    def op(self, eng, fn, reads=(), writes=()):
        reads = [x.r if isinstance(x, Tl) else x for x in reads]
        writes = [x.r if isinstance(x, Tl) else x for x in writes]
        self._emit_waits(eng, self._need(reads, writes, eng))
        self.cnt[eng] += 1
        sem = self.sem[eng]
        self.ops[eng].append(lambda e, fn=fn, sem=sem: fn(e).then_inc(sem, 1))
        self._commit(("eng", eng, self.cnt[eng]), reads, writes)

    def dma(self, q, out, in_, reads=(), writes=(), **kw):
        reads = [x.r if isinstance(x, Tl) else x for x in reads]
        writes = [x.r if isinstance(x, Tl) else x for x in writes]
        npool = self.NDMASEM[q]
        idx = (q, self.dma_rr.get(q, 0) % npool)
        self.dma_rr[q] = self.dma_rr.get(q, 0) + 1
        if idx not in self.dma_sems:
            s_ = self.es.enter_context(self.nc.semaphore("d_%s_%d" % idx))
            self.dma_sems[idx] = [s_, 0]
        ent = self.dma_sems[idx]
        evs = self._need(reads, writes, q)
        if ent[1] > 0:
            evs.append(("dma", idx, ent[1] * 16))
        self._emit_waits(q, evs)
        ent[1] += 1
        sem = ent[0]
        self.ndma += 1
        self.ops[q].append(lambda e, out=out, in_=in_, sem=sem, kw=kw: e.dma_start(out=out, in_=in_, **kw).then_inc(sem, 16))
        self._commit(("dma", idx, ent[1] * 16), reads, writes)

    def barrier(self):
        evs = [("eng", e, self.cnt[e]) for e in self.ENGS if self.cnt[e] > 0]
        evs += [("dma", kk, v[1] * 16) for kk, v in self.dma_sems.items() if v[1] > 0]
        for e in self.ENGS:
            self._emit_waits(e, [ev for ev in evs if not (ev[0] == "eng" and ev[1] == e)])

    def phase(self, body):
        with ExitStack() as pes:
            self.phase_es = pes
            body()
            self.barrier()
            ops = self.ops
            self.ops = {e: [] for e in self.ENGS}
            with self.nc.Block() as block:
                @block.tensor
                def _(e):
                    for f in ops["pe"]: f(e)
                @block.scalar
                def _(e):
                    for f in ops["act"]: f(e)
                @block.vector
                def _(e):
                    for f in ops["dve"]: f(e)
                @block.gpsimd
                def _(e):
                    for f in ops["pool"]: f(e)
                @block.sync
                def _(e):
                    for f in ops["sp"]: f(e)
        self.phase_es = None

    def close(self):
        self.es.close()

    def ts(self, eng, out, in0, s1, s2, op0, op1, r, w):
        if s2 is None:
            s2 = 0.0; op1 = ALU.add
        self.op(eng, lambda e: e.tensor_scalar(out=out, in0=in0, scalar1=s1, scalar2=s2, op0=op0, op1=op1), r, w)
    def stt(self, eng, out, in0, sc, in1, op0, op1, r, w):
        self.op(eng, lambda e: e.scalar_tensor_tensor(out=out, in0=in0, scalar=sc, in1=in1, op0=op0, op1=op1), r, w)
    def tt(self, eng, out, in0, in1, op, r, w):
        self.op(eng, lambda e: e.tensor_tensor(out=out, in0=in0, in1=in1, op=op), r, w)
    def act(self, out, in_, func, r, w, bias=None, scale=None, accum=None):
        kw = {}
        if bias is not None: kw["bias"] = bias
        if scale is not None: kw["scale"] = scale
        if accum is not None: kw["accum_out"] = accum
        self.op("act", lambda e: e.activation(out=out, in_=in_, func=func, **kw), r, w)
    def mm(self, out, lhsT, rhs, start, stop, r, w):
        self.op("pe", lambda e: e.matmul(out, lhsT=lhsT, rhs=rhs, start=start, stop=stop), r, w)
    def copy(self, eng, out, in_, r, w):
        if eng == "act":
            self.op("act", lambda e: e.copy(out=out, in_=in_), r, w)
        else:
            self.op(eng, lambda e: e.tensor_copy(out=out, in_=in_), r, w)
    def memset(self, eng, ap, val, w):
        self.op(eng, lambda e: e.memset(ap, val), [], w)
    def recip(self, out, in_, r, w):
        self.op("dve", lambda e: e.reciprocal(out=out, in_=in_), r, w)

def _consts():
    c = {}
    nf = 16
    inv = 10000.0 ** (-np.arange(nf, dtype=np.float64) / nf)
    t = np.arange(SEQ)
    row = (t // 64).astype(np.float64); col = (t % 64).astype(np.float64)
    ar = row[None, :] * inv[:, None]; ac = col[None, :] * inv[:, None]
    cos64 = np.concatenate([np.cos(ar), np.cos(ar), np.cos(ac), np.cos(ac)], 0)
    sin64 = np.concatenate([-np.sin(ar), np.sin(ar), -np.sin(ac), np.sin(ac)], 0)
    c["cos64"] = cos64.astype(np.float32); c["sin64"] = sin64.astype(np.float32)
    perm = np.concatenate([np.arange(16) + 16, np.arange(16), np.arange(16) + 48, np.arange(16) + 32])
    c["perm64"] = perm
    p = np.arange(128)[:, None]; f = np.arange(512)[None, :]
    c["bmask"] = np.stack([(np.abs(128 * r + p - f) <= 128) for r in range(-1, 5)], 0).astype(NPBF)
    return c


def _fft_consts(NA):
    N = 128 * NA
    c = {}
    a = np.arange(NA)[:, None]; f1 = np.arange(NA)[None, :]
    th = 2 * np.pi * a * f1 / NA
    c["f1cs"] = np.concatenate([np.cos(th), -np.sin(th)], 1).astype(NPBF)
    p = np.arange(128)[:, None]; f2 = np.arange(128)[None, :]
    th2 = 2 * np.pi * p * f2 / 128
    C = np.cos(th2); S = np.sin(th2)
    c["fC"] = C.astype(NPBF); c["fS"] = S.astype(NPBF); c["fnS"] = (-S).astype(NPBF)
    c["fCS"] = np.concatenate([C, S], 1).astype(NPBF)
    c["fnSC"] = np.concatenate([-S, C], 1).astype(NPBF)
    tw = 2 * np.pi * np.arange(128)[:, None] * np.arange(NA)[None, :] / N
    G = 512 // (2 * NA)
    tc_ = np.cos(tw); ts_ = np.sin(tw)
    A = np.concatenate([tc_, tc_], 1)
    B = np.concatenate([ts_, -ts_], 1)
    c["twA"] = np.tile(A[:, None, :], (1, G, 1)).reshape(128, 512).astype(np.float32)
    c["twB"] = np.tile(B[:, None, :], (1, G, 1)).reshape(128, 512).astype(np.float32)
    tcT = np.cos(tw).T; tsT = np.sin(tw).T
    A2 = np.stack([tcT, tcT], 1)
    B2 = np.stack([tsT, -tsT], 1)
    c["twA2"] = np.tile(A2[:, None], (1, 4, 1, 1)).reshape(NA, 1024).astype(np.float32)
    c["twB2"] = np.tile(B2[:, None], (1, 4, 1, 1)).reshape(NA, 1024).astype(np.float32)
    th = 2 * np.pi * np.arange(NA)[:, None] * np.arange(NA)[None, :] / NA
    c["g3C"] = (np.cos(th) / N).astype(NPBF); c["g3nS"] = (-np.sin(th) / N).astype(NPBF)
    return c


def _filter_consts(n):
    q = np.arange(2 * n)
    tap = np.where(q < n, q, 2 * n - q).astype(np.int64)
    tap = np.minimum(tap, n - 1)
    t01 = np.linspace(0.0, 1.0, n, dtype=np.float32)
    bands = 16
    w = (2.0 * np.pi * np.arange(n, dtype=np.float32) / n).astype(np.float32)
    f = np.linspace(1e-4, bands - 1, bands, dtype=np.float32)[None, :]
    feats = np.concatenate([t01[:, None], np.cos(f * w[:, None]), -np.sin(f * w[:, None])], -1).astype(np.float32)
    featsT = np.ascontiguousarray(feats[tap].T)
    t01b = np.ascontiguousarray(np.tile(t01[tap][None, :], (128, 1)))
    deltas = np.abs(np.linspace(np.log(1e-2) / 1.5, np.log(1e-2) / 0.3, 512, dtype=np.float32))
    ndel = np.ascontiguousarray((-deltas).reshape(4, 128).T)
    return featsT.astype(np.float32), t01b.astype(np.float32), ndel.astype(np.float32)


def _pm(v, n):
    return np.ascontiguousarray(np.asarray(v, np.float32).reshape(n, 128).T)


def _host_prep(inp, b, hh, C):
    fl = (hh == 1)
    m = {}
    x = inp["x"][b]; cx = inp["ctx"][b]
    if fl: x = x[::-1]; cx = cx[::-1]
    m["xt0"] = np.ascontiguousarray(np.concatenate([x, cx], 0).T)
    cv = np.stack([inp["c"][b], inp["c_ctx"]], 1)
    m["cvec"] = np.ascontiguousarray(cv.reshape(8, 128, 2).transpose(1, 0, 2))
    perm = C["perm64"]
    for i in range(2):
        m[f"ada_w{i}"] = inp["ada_w"][i]
        m[f"ada_b{i}"] = _pm(inp["ada_b"][i], 48)
        m[f"nmix{i}"] = _pm(inp["norm_mix"][i], 8)
        m[f"nffn{i}"] = _pm(inp["norm_ffn"][i], 8)
        w = inp["mix_w_in"][i]
        qk = w[:, :768].reshape(D, 12, 64)[:, :, perm].reshape(D, 768)
        m[f"w_in{i}"] = np.ascontiguousarray(np.concatenate([w, qk], 1))
        m[f"w_out{i}"] = inp["mix_w_out"][i]
        gq = inp["attn_q_norm"][i]; gk = inp["attn_k_norm"][i]
        m[f"qkg{i}"] = np.ascontiguousarray(np.stack([np.tile(gq, 2), np.tile(gq[perm], 2), np.tile(gk, 2), np.tile(gk[perm], 2)], 1).astype(np.float32))
        m[f"w_up{i}"] = inp["ffn_w_up"][i]
        m[f"w_dn{i}"] = inp["ffn_w_down"][i]
        fw = inp["ffn_conv_w"][i]
        if fl: fw = fw[::-1]
        m[f"fcw{i}"] = np.ascontiguousarray(fw.reshape(3, NM, 128).transpose(2, 1, 0))
        m[f"fcb{i}"] = _pm(inp["ffn_conv_b"][i], NM)
    hw = inp["hy_conv_w"][0]
    if fl: hw = hw[::-1]
    m["hcw"] = np.ascontiguousarray(hw.reshape(3, 12, 128).transpose(2, 1, 0))
    m["hcb"] = _pm(inp["hy_conv_b"][0], 12)
    sw = inp["sc_conv_w"][0]
    if fl: sw = sw[::-1]
    m["scw"] = np.ascontiguousarray(sw.reshape(3, 4, 128).transpose(2, 1, 0))
    m["hw1"] = inp["hy_w1"][0]; m["hw2"] = inp["hy_w2"][0]; m["hw3"] = inp["hy_w3"][0]
    w4 = inp["hy_w4"][0]
    if fl: w4 = np.concatenate([w4[:, 512:], w4[:, :512]], 1)
    m["hw4"] = np.ascontiguousarray(w4)
    m["hb"] = np.ascontiguousarray(np.stack([inp["hy_b1"][0], inp["hy_b2"][0], inp["hy_b3"][0], inp["hy_freq"][0]], 1).astype(np.float32))
    m["hbd"] = _pm(inp["hy_bias_d"][0], 4)
    m["sink"] = np.ascontiguousarray(np.tile(inp["swa_sink"][0][None, :], (128, 1)).astype(np.float32))
    cos = C["cos64"]; sin = C["sin64"]
    if fl: cos = cos[:, ::-1]; sin = sin[:, ::-1]
    cosx = np.concatenate([cos, np.ones((64, CTX), np.float32)], 1)
    sinx = np.concatenate([sin, np.zeros((64, CTX), np.float32)], 1)
    m["cosT"] = np.ascontiguousarray(np.concatenate([cosx, cosx], 0))
    m["sinT"] = np.ascontiguousarray(np.concatenate([sinx, sinx], 0))
    m["bmask"] = C["bmask"]
    for tag, NA in (("L", 128), ("C", 4)):
        for kk, v in C["fft" + tag].items():
            m[kk + tag] = v
    for tag in ("L", "C"):
        ft, t01b, ndel = C["filt" + tag]
        m["featsT" + tag] = ft; m["t01b" + tag] = t01b
    m["ndel"] = C["filtL"][2]
    return m

def _build(stop_after=None, dbg=()):
    nc = bass.Bass("TRN2", target_bir_lowering=False)
    k = K(nc)
    def din(name, shape, dt=F32):
        return nc.dram_tensor(name, list(shape), dt, kind="ExternalInput").ap()
    def dscr(name, shape, dt):
        kind = "ExternalOutput" if name in dbg else "Internal"
        return nc.dram_tensor(name, list(shape), dt, kind=kind).ap()
    I = {}
    I["xt0"] = din("xt0", [D, TT]); I["cvec"] = din("cvec", [128, 8, 2])
    for i in range(2):
        I[f"ada_w{i}"] = din(f"ada_w{i}", [D, 6 * D]); I[f"ada_b{i}"] = din(f"ada_b{i}", [128, 48])
        I[f"nmix{i}"] = din(f"nmix{i}", [128, 8]); I[f"nffn{i}"] = din(f"nffn{i}", [128, 8])
        I[f"w_in{i}"] = din(f"w_in{i}", [D, 3328]); I[f"w_out{i}"] = din(f"w_out{i}", [D, D])
        I[f"qkg{i}"] = din(f"qkg{i}", [128, 4])
        I[f"w_up{i}"] = din(f"w_up{i}", [D, 2 * DFF]); I[f"w_dn{i}"] = din(f"w_dn{i}", [DFF, D])
        I[f"fcw{i}"] = din(f"fcw{i}", [128, NM, 3]); I[f"fcb{i}"] = din(f"fcb{i}", [128, NM])
    I["hcw"] = din("hcw", [128, 12, 3]); I["hcb"] = din("hcb", [128, 12]); I["scw"] = din("scw", [128, 4, 3])
    I["hw1"] = din("hw1", [33, 64]); I["hw2"] = din("hw2", [64, 64]); I["hw3"] = din("hw3", [64, 64])
    I["hw4"] = din("hw4", [64, 1024]); I["hb"] = din("hb", [64, 4]); I["hbd"] = din("hbd", [128, 4])
    I["sink"] = din("sink", [128, 8])
    I["cosT"] = din("cosT", [128, TT]); I["sinT"] = din("sinT", [128, TT])
    I["bmask"] = din("bmask", [6, 128, 512], BF16)
    for tag, NA in (("L", 128), ("C", 4)):
        I["f1cs" + tag] = din("f1cs" + tag, [NA, 2 * NA], BF16)
        for nm in ("fC", "fS", "fnS"): I[nm + tag] = din(nm + tag, [128, 128], BF16)
        for nm in ("fCS", "fnSC"): I[nm + tag] = din(nm + tag, [128, 256], BF16)
        for nm in ("twA", "twB"): I[nm + tag] = din(nm + tag, [128, 512])
        for nm in ("twA2", "twB2"): I[nm + tag] = din(nm + tag, [NA, 1024])
        for nm in ("g3C", "g3nS"): I[nm + tag] = din(nm + tag, [NA, NA], BF16)
        n = 64 * NA
        I["featsT" + tag] = din("featsT" + tag, [33, 2 * n]); I["t01b" + tag] = din("t01b" + tag, [128, 2 * n])
    I["ndel"] = din("ndel", [128, 4])
    OUT = nc.dram_tensor("out", [D, OWN], F32, kind="ExternalOutput").ap()

    S = {}
    for i in range(2):
        S[f"bw_in{i}"] = dscr(f"bw_in{i}", [D, 3328], BF16); S[f"bw_out{i}"] = dscr(f"bw_out{i}", [D, D], BF16)
        S[f"bw_up{i}"] = dscr(f"bw_up{i}", [D, 2 * DFF], BF16); S[f"bw_dn{i}"] = dscr(f"bw_dn{i}", [DFF, D], BF16)
    S["QT"] = dscr("QT", [8, 64, TT], BF16); S["KT"] = dscr("KT", [4, 64, TT], BF16); S["V"] = dscr("V", [TT, 256], BF16)
    S["UT"] = dscr("UT", [512, TT], BF16); S["X0T"] = dscr("X0T", [512, TT], F32)
    S["UF"] = dscr("UF", [512, TT], F32)
    S["YT"] = dscr("YT", [512, TT], F32)
    S["OAT"] = dscr("OAT", [512, TT], BF16); S["OCT"] = dscr("OCT", [512, TT], BF16)
    S["XM"] = dscr("XM", [D, TT], F32); S["X1"] = dscr("X1", [D, TT], F32); S["XM1"] = dscr("XM1", [D, TT], F32)
    S["KRAWL"] = dscr("KRAWL", [512, 2 * SEQ], F32); S["KNL"] = dscr("KNL", [512, 2 * SEQ], BF16)
    S["KRAWC"] = dscr("KRAWC", [512, 2 * CTX], F32); S["KNC"] = dscr("KNC", [512, 2 * CTX], BF16)

    MOD = k.sbuf([128, 2, 48, 2], F32, "mod", persist=True)
    AB = k.sbuf([128, 2, 2, 2, 8, 2], F32, "ab", persist=True)
    ONES = k.sbuf([128, 128], BF16, "ones", persist=True)
    BONES = k.sbuf([128, 128], BF16, "bones", persist=True)
    SEL = k.sbuf([128, 64], F32, "sel", persist=True)
    ESINK = k.sbuf([128, 8], F32, "esink", persist=True)

    phases = []

    def ph_setup():
        k.memset("dve", ONES[:], 1.0, [ONES])
        k.memset("dve", BONES[:], 0.0, [BONES])
        k.memset("dve", BONES[0:64, 0:64], 1.0, [BONES])
        k.memset("dve", BONES[64:128, 64:128], 1.0, [BONES])
        k.memset("dve", SEL[:], 0.0, [SEL])
        k.memset("dve", SEL[64:65, :], 1.0, [SEL])
        snk = k.sbuf([128, 8], F32)
        k.dma("sp", snk[:], I["sink"][:, :], writes=[snk])
        k.act(ESINK[:], snk[:], AF.Exp, [snk], [ESINK])
        stg = [k.sbuf([128, 2048], F32, "stg") for _ in range(3)]
        stb = [k.sbuf([128, 2048], BF16, "stb") for _ in range(3)]
        n = 0
        for i in range(2):
            for src, dst, rows, cols in ((f"w_in{i}", f"bw_in{i}", D, 3328), (f"w_out{i}", f"bw_out{i}", D, D),
                                         (f"w_up{i}", f"bw_up{i}", D, 2 * DFF), (f"w_dn{i}", f"bw_dn{i}", DFF, D)):
                for r0 in range(0, rows, 128):
                    for c0 in range(0, cols, 2048):
                        cw = min(2048, cols - c0)
                        a = stg[n % 3]; bt = stb[n % 3]
                        k.dma("sp", a[:, 0:cw], I[src][r0:r0 + 128, c0:c0 + cw], writes=[a])
                        k.copy("dve" if n % 2 == 0 else "pool", bt[:, 0:cw], a[:, 0:cw], [a], [bt])
                        k.dma("pool" if n % 2 == 0 else "sp", S[dst][r0:r0 + 128, c0:c0 + cw], bt[:, 0:cw], reads=[bt])
                        n += 1
        cv = k.sbuf([128, 8, 2], F32)
        k.dma("sp", cv[:], I["cvec"][:, :, :], writes=[cv])
        sc = k.sbuf([128, 8, 2], F32)
        k.act(sc[:], cv[:], AF.Silu, [cv], [sc])
        wst = [k.sbuf([128, 8, 512], F32, "wst") for _ in range(2)]
        ps = [k.psum([128, 512], F32) for _ in range(2)]
        n = 0
        for i in range(2):
            adb = k.sbuf([128, 48], F32)
            k.dma("sp", adb[:], I[f"ada_b{i}"][:, :], writes=[adb])
            for cb in range(12):
                wt = wst[n % 2]; n += 1
                k.dma("sp", wt[:], I[f"ada_w{i}"][:, cb * 512:(cb + 1) * 512].rearrange("(k p) f -> p k f", p=128), writes=[wt])
                for mi in range(4):
                    m = cb * 4 + mi
                    pt = ps[m % 2]
                    for kk in range(8):
                        k.mm(pt[:, 0:2], wt[:, kk, mi * 128:(mi + 1) * 128], sc[:, kk, :], kk == 0, kk == 7, [wt, sc], [pt])
                    k.ts("dve", MOD[:, i, m, :], pt[:, 0:2], adb[:, m:m + 1], None, ALU.add, None, [pt, adb], [MOD])
            for wh, (nm, sh0, sc0) in enumerate(((f"nmix{i}", 0, 8), (f"nffn{i}", 24, 32))):
                g = k.sbuf([128, 8], F32)
                k.dma("sp", g[:], I[nm][:, :], writes=[g])
                for j in range(2):
                    k.stt("dve", AB[:, i, wh, 0, :, j], MOD[:, i, sc0:sc0 + 8, j], 1.0, g[:], ALU.add, ALU.mult, [MOD, g], [AB])
                    k.copy("dve", AB[:, i, wh, 1, :, j], MOD[:, i, sh0:sh0 + 8, j], [MOD], [AB])
    phases.append(("setup", ph_setup))

    def norm_mod(xh, Wc, i, wh, j, sq, ssps, rstd, tmp, h):
        for kk in range(8):
            k.act(sq[:, kk, 0:Wc], xh[:, kk, 0:Wc], AF.Square, [xh], [sq])
        for (c0, c1) in ((0, min(512, Wc)), (512, Wc)):
            if c1 <= c0: continue
            for kk in range(8):
                k.mm(ssps[:, c0:c1], ONES[:, :], sq[:, kk, c0:c1], kk == 0, kk == 7, [ONES, sq], [ssps])
        k.act(rstd[:, 0:Wc], ssps[:, 0:Wc], AF.Sqrt, [ssps], [rstd], bias=1e-6, scale=1.0 / D)
        k.recip(rstd[:, 0:Wc], rstd[:, 0:Wc], [rstd], [rstd])
        for kk in range(8):
            t = tmp[kk % 2]
            k.stt("dve", t[:, 0:Wc], xh[:, kk, 0:Wc], AB[:, i, wh, 0, kk, j:j + 1], rstd[:, 0:Wc], ALU.mult, ALU.mult, [xh, AB, rstd], [t])
            k.act(h[:, kk, 0:Wc], t[:, 0:Wc], AF.Identity, [t, AB], [h], bias=AB[:, i, wh, 1, kk, j:j + 1], scale=1.0)

    CHUNKS_ALL = [(c * 512, 512, c == 0, c == 15, 0) for c in range(16)] + [(SEQ, CTX, True, True, 1)]
    CHUNKS_E = [(c * 512, 512, c == 0, False, 0) for c in range(9)]
    CTXCH = (SEQ, CTX, True, True, 1)

    def load_xh(xh, src, t0, W, ledge, redge, q="sp"):
        lo = 2 if ledge else 1
        hi = W + 2 if redge else W + 3
        if ledge: k.memset("pool", xh[:, :, 1:2], 0.0, [xh])
        if redge: k.memset("pool", xh[:, :, W + 2:W + 3], 0.0, [xh])
        k.dma(q, xh[:, :, lo:hi], src[:, t0 - 2 + lo:t0 - 2 + hi].rearrange("(k p) t -> p k t", p=128), writes=[xh])

    def make_inproj(i, XIN, chunks):
        def ph():
            w = k.sbuf([128, 8, 3328], BF16, "w_in")
            for kk in range(8):
                k.dma("sp", w[:, kk, :], S[f"bw_in{i}"][kk * 128:(kk + 1) * 128, :], writes=[w])
            qkg = k.sbuf([128, 4], F32); k.dma("sp", qkg[:], I[f"qkg{i}"][:, :], writes=[qkg])
            if i == 0:
                cw = k.sbuf([128, 12, 3], F32); k.dma("sp", cw[:], I["hcw"][:, :, :], writes=[cw])
                cb = k.sbuf([128, 12], F32); k.dma("sp", cb[:], I["hcb"][:, :], writes=[cb])
            else:
                cw = k.sbuf([128, 4, 3], F32); k.dma("sp", cw[:], I["scw"][:, :, :], writes=[cw])
            xhs = [k.sbuf([128, 8, 516], F32, "xh") for _ in range(2)]
            for t_ in xhs: k.memset("pool", t_[:], 0.0, [t_])
            sq = k.sbuf([128, 8, 516], BF16, "sq")
            h = k.sbuf([128, 8, 516], BF16, "h")
            rstd = k.sbuf([128, 516], F32, "rstd")
            tmp = [k.sbuf([128, 516], F32, "tmp") for _ in range(2)]
            ssps = k.psum([128, 1024], F32, "ssps")
            zps = [k.psum([128, 512], F32, "zps") for _ in range(4)]
            cps = k.psum([128, 1024], F32, "cps")
            cosb = k.sbuf([128, 512], F32, "cos"); sinb = k.sbuf([128, 512], F32, "sin")
            sq2 = [k.sbuf([128, 512], BF16, "sq2") for _ in range(2)]
            rs = [k.sbuf([128, 512], F32, "rs") for _ in range(2)]
            ta = [k.sbuf([128, 512], F32, "ta") for _ in range(2)]
            tb = [k.sbuf([128, 512], F32, "tb") for _ in range(2)]
            qo = [k.sbuf([128, 512], BF16, "qo") for _ in range(2)]
            vo = [k.sbuf([128, 256], BF16, "vo") for _ in range(2)]
            asb = [k.sbuf([128, 516], F32, "asb") for _ in range(3)]
            c1 = [k.sbuf([128, 512], F32, "c1") for _ in range(3)]
            uo = [k.sbuf([128, 512], F32, "uo") for _ in range(2)]
            ub = [k.sbuf([128, 512], BF16, "ub") for _ in range(2)]
            pp = [k.sbuf([128, 516], F32, "pp") for _ in range(2)]
            nq = [0]
            import os
            PARTS = os.environ.get("INPROJ_PARTS", "nqvc")
            NCHK = int(os.environ.get("INPROJ_NCH", "99"))
            for ci, (t0, W, le, re, j) in enumerate(chunks[:NCHK]):
                Wc = W + 4
                do_q = (t0 < E) or j == 1
                if i == 1 and j == 1: do_q = False
                xh = xhs[ci % 2]
                load_xh(xh, XIN, t0, W, le, re)
                norm_mod(xh, Wc, i, 0, j, sq, ssps, rstd, tmp, h)
                k.dma("sp", cosb[:, 0:W], I["cosT"][:, t0:t0 + W], writes=[cosb])
                k.dma("sp", sinb[:, 0:W], I["sinT"][:, t0:t0 + W], writes=[sinb])
                for pr in range(6):
                    if "q" not in PARTS: continue
                    if pr < 4 and not do_q: continue
                    n = nq[0]; nq[0] += 1
                    zp = zps[(2 * n) % 4]; zsp = zps[(2 * n + 1) % 4]
                    for kk in range(8):
                        k.mm(zp[:, 0:W], w[:, kk, pr * 128:(pr + 1) * 128], h[:, kk, 2:W + 2], kk == 0, kk == 7, [w, h], [zp])
                    for kk in range(8):
                        k.mm(zsp[:, 0:W], w[:, kk, 2560 + pr * 128:2560 + (pr + 1) * 128], h[:, kk, 2:W + 2], kk == 0, kk == 7, [w, h], [zsp])
                    s2 = sq2[n % 2]; r_ = rs[n % 2]; a_ = ta[n % 2]; b_ = tb[n % 2]; q_ = qo[n % 2]
                    gi = 0 if pr < 4 else 2
                    k.act(s2[:, 0:W], zp[:, 0:W], AF.Square, [zp], [s2])
                    k.stt("dve", a_[:, 0:W], zp[:, 0:W], qkg[:, gi:gi + 1], cosb[:, 0:W], ALU.mult, ALU.mult, [zp, qkg, cosb], [a_])
                    k.stt("dve", b_[:, 0:W], zsp[:, 0:W], qkg[:, gi + 1:gi + 2], sinb[:, 0:W], ALU.mult, ALU.mult, [zsp, qkg, sinb], [b_])
                    k.mm(zp[:, 0:W], BONES[:, :], s2[:, 0:W], True, True, [BONES, s2], [zp])
                    k.act(r_[:, 0:W], zp[:, 0:W], AF.Sqrt, [zp], [r_], bias=1e-6, scale=1.0 / 64)
                    k.recip(r_[:, 0:W], r_[:, 0:W], [r_], [r_])
                    QS = os.environ.get("QSKIP", "")
                    pe_ = "dve" if "pool" in QS else "pool"
                    k.tt(pe_, a_[:, 0:W], a_[:, 0:W], b_[:, 0:W], ALU.add, [a_, b_], [a_])
                    k.tt(pe_, q_[:, 0:W], a_[:, 0:W], r_[:, 0:W], ALU.mult, [a_, r_], [q_])
                    for hf in range(2):
                        if "dma" in QS: continue
                        if pr < 4:
                            dst = S["QT"][2 * pr + hf, :, t0:t0 + W]
                        else:
                            dst = S["KT"][2 * (pr - 4) + hf, :, t0:t0 + W]
                        k.dma("pool", dst, q_[hf * 64:(hf + 1) * 64, 0:W], reads=[q_])
                for tj in range(W // 128):
                    if "v" not in PARTS: continue
                    n = nq[0]; nq[0] += 1
                    vp = zps[n % 4]
                    for kk in range(8):
                        k.mm(vp[:, 0:256], h[:, kk, 2 + tj * 128:2 + (tj + 1) * 128], w[:, kk, 768:1024], kk == 0, kk == 7, [w, h], [vp])
                    v_ = vo[n % 2]
                    k.copy("act", v_[:, :], vp[:, 0:256], [vp], [v_])
                    k.dma("pool", S["V"][t0 + tj * 128:t0 + (tj + 1) * 128, :], v_[:, :], reads=[v_])
                def convproj(m, dst):
                    for (c0, c1_) in ((0, min(512, Wc)), (512, Wc)):
                        if c1_ <= c0: continue
                        for kk in range(8):
                            k.mm(cps[:, c0:c1_], w[:, kk, 1024 + m * 128:1024 + (m + 1) * 128], h[:, kk, c0:c1_], kk == 0, kk == 7, [w, h], [cps])
                    k.copy("act", dst[:, 0:Wc], cps[:, 0:Wc], [cps], [dst])
                    if le: k.memset("pool", dst[:, 1:2], 0.0, [dst])
                    if re: k.memset("pool", dst[:, W + 2:W + 3], 0.0, [dst])
                def conv3(out, a, wts, m, bias, eng="dve"):
                    if bias is not None:
                        k.ts(eng, out[:, 0:W], a[:, 2:W + 2], wts[:, m, 1:2], bias, ALU.mult, ALU.add, [a, wts, cb], [out])
                    else:
                        k.ts(eng, out[:, 0:W], a[:, 2:W + 2], wts[:, m, 1:2], None, ALU.mult, None, [a, wts], [out])
                    k.stt(eng, out[:, 0:W], a[:, 1:W + 1], wts[:, m, 0:1], out[:, 0:W], ALU.mult, ALU.add, [a, wts, out], [out])
                    k.stt(eng, out[:, 0:W], a[:, 3:W + 3], wts[:, m, 2:3], out[:, 0:W], ALU.mult, ALU.add, [a, wts, out], [out])
                for jc in range(4):
                    if "c" not in PARTS: continue
                    if i == 0:
                        convproj(4 + jc, asb[0]); conv3(c1[0], asb[0], cw, 4 + jc, cb[:, 4 + jc:5 + jc])
                        convproj(8 + jc, asb[1]); conv3(c1[1], asb[1], cw, 8 + jc, cb[:, 8 + jc:9 + jc])
                        u_ = uo[jc % 2]; ub_ = ub[jc % 2]
                        k.tt("dve", u_[:, 0:W], c1[0][:, 0:W], c1[1][:, 0:W], ALU.mult, [c1[0], c1[1]], [u_])
                        k.copy("pool", ub_[:, 0:W], u_[:, 0:W], [u_], [ub_])
                        k.dma("pool", S["UF"][jc * 128:(jc + 1) * 128, t0:t0 + W], u_[:, 0:W], reads=[u_])
                        k.dma("pool", S["UT"][jc * 128:(jc + 1) * 128, t0:t0 + W], ub_[:, 0:W], reads=[ub_])
                        if do_q:
                            convproj(jc, asb[2]); conv3(c1[2], asb[2], cw, jc, cb[:, jc:jc + 1])
                            k.dma("pool", S["X0T"][jc * 128:(jc + 1) * 128, t0:t0 + W], c1[2][:, 0:W], reads=[c1[2]])
                    elif j == 0:
                        convproj(4 + jc, asb[0]); convproj(8 + jc, asb[1])
                        p_ = pp[jc % 2]
                        k.tt("dve", p_[:, 0:Wc], asb[0][:, 0:Wc], asb[1][:, 0:Wc], ALU.mult, [asb[0], asb[1]], [p_])
                        conv3(c1[0], p_, cw, jc, None)
                        convproj(jc, asb[2])
                        ub_ = ub[jc % 2]
                        k.tt("pool", ub_[:, 0:W], c1[0][:, 0:W], asb[2][:, 2:W + 2], ALU.mult, [c1[0], asb[2]], [ub_])
                        k.dma("pool", S["OCT"][jc * 128:(jc + 1) * 128, t0:t0 + W], ub_[:, 0:W], reads=[ub_])
        return ph

    def make_attn(i):
        def ph():
            NKT = TT // 128
            kT = k.sbuf([128, TT], BF16, "kT")
            k.memset("pool", kT[64:128, :], 0.0, [kT])
            va = k.sbuf([128, NKT, 65], BF16, "va")
            qT = [k.sbuf([128, 512], BF16, "qT") for _ in range(2)]
            for t_ in qT: k.memset("pool", t_[64:128, :], 0.0, [t_])
            sps = [k.psum([128, 512], F32, "sps") for _ in range(4)]
            ops_ = [k.psum([128, 512], F32, "ops") for _ in range(2)]
            bps = k.psum([128, 512], F32, "bps")
            pT = [k.sbuf([128, 512], BF16, "pT") for _ in range(4)]
            osb = [k.sbuf([65, 512], F32, "osb") for _ in range(2)]
            rb = [k.sbuf([64, 512], F32, "rb") for _ in range(2)]
            ob = [k.sbuf([64, 512], BF16, "ob") for _ in range(2)]
            if i == 1:
                bm = k.sbuf([128, 6, 512], BF16, "bm")
                for r in range(6):
                    k.dma("sp", bm[:, r, :], I["bmask"][r, :, :], writes=[bm])
            qch = []
            for c in range(9):
                t0 = c * 512
                if i == 0:
                    kts = [(kt, 0, 512, None) for kt in range(NKT)]
                else:
                    kts = []
                    for r in range(-1, 5):
                        kt = 4 * c + r
                        if kt < 0 or kt >= E // 128: continue
                        f0 = max(0, 128 * (r - 1)); f1 = min(512, 128 * (r + 2))
                        kts.append((kt, f0, f1, r + 1))
                    kts += [(64, 0, 512, None), (65, 0, 512, None)]
                qch.append((t0, 512, kts))
            if i == 0:
                qch.append((SEQ, CTX, [(64, 0, CTX, None), (65, 0, CTX, None)]))
            n = 0; nh = 0
            for jkv in range(4):
                k.dma("sp", kT[0:64, :], S["KT"][jkv, :, :], writes=[kT])
                k.dma("sp", va[:, :, 0:64], S["V"][:, jkv * 64:(jkv + 1) * 64].rearrange("(n p) d -> p n d", p=128), writes=[va])
                k.memset("pool", va[:, :, 64:65], 1.0, [va])
                for g in range(2):
                    hq = 2 * jkv + g
                    for (t0, W, kts) in qch:
                        q_ = qT[nh % 2]; op_ = ops_[nh % 2]; o_ = osb[nh % 2]; r_ = rb[nh % 2]; b_ = ob[nh % 2]
                        nh += 1
                        k.dma("sp", q_[0:64, 0:W], S["QT"][hq, :, t0:t0 + W], writes=[q_])
                        if i == 1:
                            pass
                        nk = len(kts)
                        def smm(idx):
                            kt, f0, f1, mi = kts[idx]
                            sp_ = sps[(n + idx) % 4]
                            k.mm(sp_[:, f0:f1], kT[:, kt * 128:(kt + 1) * 128], q_[:, f0:f1], True, True, [kT, q_], [sp_])
                        smm(0)
                        if nk > 1: smm(1)
                        for idx in range(nk):
                            if idx + 2 < nk: smm(idx + 2)
                            kt, f0, f1, mi = kts[idx]
                            sp_ = sps[(n + idx) % 4]; p_ = pT[(n + idx) % 4]
                            if mi is not None and (f0 > 0 or f1 < W):
                                k.memset("pool", p_[:, 0:W], 0.0, [p_])
                            k.act(p_[:, f0:f1], sp_[:, f0:f1], AF.Exp, [sp_], [p_], scale=0.125)
                            if mi is not None:
                                k.tt("pool", p_[:, f0:f1], p_[:, f0:f1], bm[:, mi, f0:f1], ALU.mult, [p_, bm], [p_])
                            k.mm(op_[0:65, 0:W], va[:, kt, :], p_[:, 0:W], idx == 0, idx == nk - 1, [va, p_], [op_])
                        n += nk
                        k.copy("act", o_[:, 0:W], op_[0:65, 0:W], [op_], [o_])
                        k.mm(bps[0:64, 0:W], SEL[0:65, :], o_[0:65, 0:W], True, True, [SEL, o_], [bps])
                        if i == 1:
                            k.ts("dve", r_[:, 0:W], bps[0:64, 0:W], ESINK[0:64, hq:hq + 1], None, ALU.add, None, [bps, ESINK], [r_])
                            k.recip(r_[:, 0:W], r_[:, 0:W], [r_], [r_])
                        else:
                            k.recip(r_[:, 0:W], bps[0:64, 0:W], [bps], [r_])
                        k.tt("dve", b_[:, 0:W], o_[0:64, 0:W], r_[:, 0:W], ALU.mult, [o_, r_], [b_])
                        k.dma("pool", S["OAT"][hq * 64:(hq + 1) * 64, t0:t0 + W], b_[:, 0:W], reads=[b_])
        return ph

    def make_outproj(i, XIN, XOUT, chunks):
        def ph():
            wa = k.sbuf([128, 4, D], BF16, "woa")
            wc = k.sbuf([128, 4, D], BF16, "woc")
            k.dma("sp", wa[:], S[f"bw_out{i}"][0:512, :].rearrange("(c p) f -> p c f", p=128), writes=[wa])
            k.dma("sp", wc[:], S[f"bw_out{i}"][512:1024, :].rearrange("(c p) f -> p c f", p=128), writes=[wc])
            oa = [k.sbuf([128, 4, 512], BF16, "oa") for _ in range(2)]
            oc = [k.sbuf([128, 4, 512], BF16, "oc") for _ in range(2)]
            xs = [k.sbuf([128, 8, 512], F32, "xs") for _ in range(2)]
            xo = [k.sbuf([128, 8, 512], F32, "xo") for _ in range(2)]
            ps = [k.psum([128, 512], F32, "ps") for _ in range(4)]
            n = 0
            for ci, (t0, W, le, re, j) in enumerate(chunks):
                a_ = oa[ci % 2]; c_ = oc[ci % 2]; x_ = xs[ci % 2]; o_ = xo[ci % 2]
                k.dma("sp", a_[:, :, 0:W], S["OAT"][:, t0:t0 + W].rearrange("(c p) t -> p c t", p=128), writes=[a_])
                k.dma("sp", c_[:, :, 0:W], S["OCT"][:, t0:t0 + W].rearrange("(c p) t -> p c t", p=128), writes=[c_])
                k.dma("sp", x_[:, :, 0:W], XIN[:, t0:t0 + W].rearrange("(k p) t -> p k t", p=128), writes=[x_])
                for m in range(8):
                    p_ = ps[n % 4]; n += 1
                    for hh_ in range(4):
                        k.mm(p_[:, 0:W], wa[:, hh_, m * 128:(m + 1) * 128], a_[:, hh_, 0:W], hh_ == 0, False, [wa, a_], [p_])
                    for cc in range(4):
                        k.mm(p_[:, 0:W], wc[:, cc, m * 128:(m + 1) * 128], c_[:, cc, 0:W], False, cc == 3, [wc, c_], [p_])
                    k.stt("dve", o_[:, m, 0:W], p_[:, 0:W], MOD[:, i, 16 + m, j:j + 1], x_[:, m, 0:W], ALU.mult, ALU.add, [p_, MOD, x_], [o_])
                k.dma("pool", XOUT[:, t0:t0 + W].rearrange("(k p) t -> p k t", p=128), o_[:, :, 0:W], reads=[o_])
        return ph

    def make_ffn(i, XIN, XOUT, chunks, final=False):
        def ph():
            wu = k.sbuf([128, 8, 2 * DFF], BF16, "wu")
            for kk in range(8):
                k.dma("sp", wu[:, kk, :], S[f"bw_up{i}"][kk * 128:(kk + 1) * 128, :], writes=[wu])
            wd = [k.sbuf([128, NM, 128], BF16, "wd") for _ in range(2)]
            cw = k.sbuf([128, NM, 3], F32); k.dma("sp", cw[:], I[f"fcw{i}"][:, :, :], writes=[cw])
            cb = k.sbuf([128, NM], F32); k.dma("sp", cb[:], I[f"fcb{i}"][:, :], writes=[cb])
            xh = k.sbuf([128, 8, 516], F32, "xh")
            k.memset("pool", xh[:], 0.0, [xh])
            sq = k.sbuf([128, 8, 516], BF16, "sq")
            h = k.sbuf([128, 8, 516], BF16, "h")
            rstd = k.sbuf([128, 516], F32, "rstd")
            tmp = [k.sbuf([128, 516], F32, "tmp") for _ in range(2)]
            gg = k.sbuf([128, NM, 512], BF16, "gg")
            asb = [k.sbuf([128, 516], F32, "asb") for _ in range(2)]
            c1 = [k.sbuf([128, 512], F32, "c1") for _ in range(2)]
            xo = [k.sbuf([128, 512], F32, "xo") for _ in range(2)]
            ssps = k.psum([128, 1024], F32, "ssps")
            aps = [k.psum([128, 1024], F32, "aps") for _ in range(2)]
            vps = [k.psum([128, 512], F32, "vps") for _ in range(2)]
            n = 0; nd = 0
            for ci, (t0, W, le, re, j) in enumerate(chunks):
                Wc = W + 4
                load_xh(xh, XIN, t0, W, le, re)
                norm_mod(xh, Wc, i, 1, j, sq, ssps, rstd, tmp, h)
                for m in range(NM):
                    ap_ = aps[n % 2]; vp_ = vps[n % 2]; a_ = asb[n % 2]; c_ = c1[n % 2]; n += 1
                    for (c0, c1_) in ((0, min(512, Wc)), (512, Wc)):
                        if c1_ <= c0: continue
                        for kk in range(8):
                            k.mm(ap_[:, c0:c1_], wu[:, kk, m * 128:(m + 1) * 128], h[:, kk, c0:c1_], kk == 0, kk == 7, [wu, h], [ap_])
                    for kk in range(8):
                        k.mm(vp_[:, 0:W], wu[:, kk, DFF + m * 128:DFF + (m + 1) * 128], h[:, kk, 2:W + 2], kk == 0, kk == 7, [wu, h], [vp_])
                    k.copy("act", a_[:, 0:Wc], ap_[:, 0:Wc], [ap_], [a_])
                    if le: k.memset("pool", a_[:, 1:2], 0.0, [a_])
                    if re: k.memset("pool", a_[:, W + 2:W + 3], 0.0, [a_])
                    eng = "dve"
                    k.ts(eng, c_[:, 0:W], a_[:, 2:W + 2], cw[:, m, 1:2], cb[:, m:m + 1], ALU.mult, ALU.add, [a_, cw, cb], [c_])
                    k.stt(eng, c_[:, 0:W], a_[:, 1:W + 1], cw[:, m, 0:1], c_[:, 0:W], ALU.mult, ALU.add, [a_, cw, c_], [c_])
                    k.stt(eng, c_[:, 0:W], a_[:, 3:W + 3], cw[:, m, 2:3], c_[:, 0:W], ALU.mult, ALU.add, [a_, cw, c_], [c_])
                    k.act(c_[:, 0:W], c_[:, 0:W], AF.Gelu_apprx_tanh, [c_], [c_])
                    k.tt("dve", gg[:, m, 0:W], c_[:, 0:W], vp_[:, 0:W], ALU.mult, [c_, vp_], [gg])
                for mo in range(8):
                    wd_ = wd[nd % 2]; o_ = xo[nd % 2]; p_ = vps[nd % 2]; nd += 1
                    k.dma("sp", wd_[:], S[f"bw_dn{i}"][:, mo * 128:(mo + 1) * 128].rearrange("(m p) f -> p m f", p=128), writes=[wd_])
                    for m in range(NM):
                        k.mm(p_[:, 0:W], wd_[:, m, :], gg[:, m, 0:W], m == 0, m == NM - 1, [wd_, gg], [p_])
                    k.stt("dve", o_[:, 0:W], p_[:, 0:W], MOD[:, i, 40 + mo, j:j + 1], xh[:, mo, 2:W + 2], ALU.mult, ALU.add, [p_, MOD, xh], [o_])
                    k.dma("pool", XOUT[mo * 128:(mo + 1) * 128, t0:t0 + W], o_[:, 0:W], reads=[o_])
        return ph

    def make_filter(tag, n):
        KRAW = S["KRAW" + tag]; KN = S["KN" + tag]
        def ph():
            w1 = k.sbuf([33, 64], F32); k.dma("sp", w1[:], I["hw1"][:, :], writes=[w1])
            w2 = k.sbuf([64, 64], F32); k.dma("sp", w2[:], I["hw2"][:, :], writes=[w2])
            w3 = k.sbuf([64, 64], F32); k.dma("sp", w3[:], I["hw3"][:, :], writes=[w3])
            w4 = k.sbuf([64, 1024], F32); k.dma("sp", w4[:], I["hw4"][:, :], writes=[w4])
            hb = k.sbuf([64, 4], F32); k.dma("sp", hb[:], I["hb"][:, :], writes=[hb])
            ndel = k.sbuf([128, 4], F32); k.dma("sp", ndel[:], I["ndel"][:, :], writes=[ndel])
            asum = k.sbuf([128, 4, 40], F32, "asum")
            k.memset("dve", asum[:], 0.0, [asum])
            ft = [k.sbuf([33, 512], F32, "ft") for _ in range(2)]
            t01 = [k.sbuf([128, 512], F32, "t01") for _ in range(2)]
            hid = [k.sbuf([64, 512], F32, "hid") for _ in range(3)]
            ki = k.sbuf([64, 512], I32, "ki")
            win = [k.sbuf([128, 512], F32, "win") for _ in range(2)]
            kr = [k.sbuf([128, 512], F32, "kr") for _ in range(2)]
            junk = k.sbuf([128, 512], F32, "junk")
            ps = [k.psum([128, 512], F32, "ps") for _ in range(4)]
            NCH = (2 * n) // 512
            n_ = 0
            for c in range(NCH):
                q0 = c * 512
                f_ = ft[c % 2]; t_ = t01[c % 2]
                k.dma("sp", f_[:, :], I["featsT" + tag][:, q0:q0 + 512], writes=[f_])
                k.dma("sp", t_[:, :], I["t01b" + tag][:, q0:q0 + 512], writes=[t_])
                src = f_; srcK = 33
                for li, wl in enumerate((w1, w2, w3)):
                    p_ = ps[n_ % 4]; n_ += 1
                    k.mm(p_[0:64, :], wl[0:srcK, :], src[0:srcK, :], True, True, [wl, src], [p_])
                    hd = hid[li]
                    k.ts("dve", hd[:, :], p_[0:64, :], hb[:, li:li + 1], hb[:, 3:4], ALU.add, ALU.mult, [p_, hb], [hd])
                    k.ts("dve", ki[:, :], hd[:, :], float(1.0 / (2 * np.pi)), None, ALU.mult, None, [hd], [ki])
                    k.stt("dve", hd[:, :], ki[:, :], float(-2 * np.pi), hd[:, :], ALU.mult, ALU.add, [ki, hd], [hd])
                    k.act(hd[:, :], hd[:, :], AF.Sin, [hd], [hd])
                    src = hd; srcK = 64
                segs = []
                if q0 + 512 <= n: segs = [(0, 512, 0)]
                elif q0 >= n: segs = [(0, 512, 512)]
                else: segs = [(0, n - q0, 0), (n - q0, 512, 512)]
                for jc in range(4):
                    p_ = ps[n_ % 4]; n_ += 1
                    for (a0, a1, off) in segs:
                        k.mm(p_[:, a0:a1], w4[:, off + jc * 128:off + (jc + 1) * 128], hid[2][:, a0:a1], True, True, [w4, hid[2]], [p_])
                    wn = win[jc % 2]; kr_ = kr[jc % 2]
                    k.act(wn[:, :], t_[:, :], AF.Exp, [t_, ndel], [wn], scale=ndel[:, jc:jc + 1])
                    k.stt("dve", kr_[:, :], wn[:, :], 0.05, p_[:, :], ALU.add, ALU.mult, [wn, p_], [kr_])
                    if q0 <= n < q0 + 512:
                        k.memset("dve", kr_[:, n - q0:n - q0 + 1], 0.0, [kr_])
                    k.act(junk[:, :], kr_[:, :], AF.Abs, [kr_], [junk, asum], accum=asum[:, jc, c:c + 1])
                    k.dma("pool", KRAW[jc * 128:(jc + 1) * 128, q0:q0 + 512], kr_[:, :], reads=[kr_])
            rn = k.sbuf([128, 4], F32, "rn")
            for jc in range(4):
                k.op("dve", lambda e, jc=jc: e.reduce_sum(out=rn[:, jc:jc + 1], in_=asum[:, jc, 0:NCH], axis=mybir.AxisListType.X), [asum], [rn])
            k.recip(rn[:, :], rn[:, :], [rn], [rn])
            k.barrier()
            big = [k.sbuf([128, 2048], F32, "big") for _ in range(2)]
            bigb = [k.sbuf([128, 2048], BF16, "bigb") for _ in range(2)]
            n2 = 0
            for jc in range(4):
                for q0 in range(0, 2 * n, 2048):
                    qw = min(2048, 2 * n - q0)
                    b_ = big[n2 % 2]; bb_ = bigb[n2 % 2]; n2 += 1
                    k.dma("sp", b_[:, 0:qw], KRAW[jc * 128:(jc + 1) * 128, q0:q0 + qw], writes=[b_])
                    k.ts("dve", bb_[:, 0:qw], b_[:, 0:qw], rn[:, jc:jc + 1], None, ALU.mult, None, [b_, rn], [bb_])
                    k.dma("pool", KN[jc * 128:(jc + 1) * 128, q0:q0 + qw], bb_[:, 0:qw], reads=[bb_])
        return ph

    def make_fftconv(tag, NA, n, tok0, n_out_blocks):
        KN = S["KN" + tag]
        NR = NA // 2
        GF = 512 // NA
        GB = 512 // (2 * NA)
        def ph():
            cst = {}
            for nm, shp, dt in (("f1cs", [NA, 2 * NA], BF16), ("fC", [128, 128], BF16), ("fS", [128, 128], BF16), ("fnS", [128, 128], BF16),
                                ("fCS", [128, 256], BF16), ("fnSC", [128, 256], BF16), ("twA", [128, 512], F32), ("twB", [128, 512], F32),
                                ("twA2", [NA, 1024], F32), ("twB2", [NA, 1024], F32), ("g3C", [NA, NA], BF16), ("g3nS", [NA, NA], BF16)):
                cst[nm] = k.sbuf(shp, dt, nm)
                k.dma("sp", cst[nm][:], I[nm + tag][tuple(slice(None) for _ in shp)], writes=[cst[nm]])
            LG = 32 if NA == 128 else 128
            NXB = 2 if NA == 128 else 1
            xu = [k.sbuf([NA, LG, 128], BF16, "xu") for _ in range(NXB)]
            for t_ in xu: k.memset("pool", t_[:], 0.0, [t_])
            xk = [k.sbuf([NA, LG, 128], BF16, "xk") for _ in range(NXB)]
            s1 = [k.psum([128, 512], F32, "s1") for _ in range(2)]
            s2 = [k.psum([128, 512], F32, "s2") for _ in range(2)]
            s3 = [k.psum([128, 1024], F32, "s3") for _ in range(1)]
            s4 = [k.psum([128, 512], F32, "s4") for _ in range(2)]
            NB = 2
            ta = [k.sbuf([128, 512], F32, "ta") for _ in range(NB)]; tb = [k.sbuf([128, 512], F32, "tb") for _ in range(NB)]
            bu = [k.sbuf([128, 2, GF, NA], BF16, "bu") for _ in range(NB)]; bk = [k.sbuf([128, 2, GF, NA], BF16, "bk") for _ in range(NB)]
            kh = [k.sbuf([128, 2, 512], F32, "kh") for _ in range(NB)]
            m1 = [k.sbuf([128, 512], F32, "m1") for _ in range(NB)]; m2 = [k.sbuf([128, 512], F32, "m2") for _ in range(NB)]
            m3 = [k.sbuf([128, 512], F32, "m3") for _ in range(NB)]; m4 = [k.sbuf([128, 512], F32, "m4") for _ in range(NB)]
            yh = [k.sbuf([128, 2, GF, NA], BF16, "yh") for _ in range(NB)]
            t2a = [k.sbuf([NA, 1024], F32, "t2a") for _ in range(NB)]; t2b = [k.sbuf([NA, 1024], F32, "t2b") for _ in range(NB)]
            y3 = [k.sbuf([NA, 2, 4, 128], BF16, "y3") for _ in range(NB)]
            yo = [k.sbuf([n_out_blocks, 4, 128], F32, "yo") for _ in range(2)]
            MO = n_out_blocks
            cnt = {"tw": 0, "inv": 0, "g": 0}
            def fwd(x, cbase, bdst):
                for half in range(2):
                    bank = s1[half]
                    ta_ = ta[cnt["tw"] % NB]; tb_ = tb[cnt["tw"] % NB]; cnt["tw"] += 1
                    for cc in range(GB):
                        ch = cbase + half * GB + cc
                        k.mm(bank[:, cc * 2 * NA:(cc + 1) * 2 * NA], x[0:NA, ch, :], cst["f1cs"][0:NA, :], True, True, [x, cst["f1cs"]], [bank])
                    k.tt("dve", ta_[:, :], bank[:, :], cst["twA"][:, :], ALU.mult, [bank, cst["twA"]], [ta_])
                    k.tt("dve", tb_[:, :], bank[:, :], cst["twB"][:, :], ALU.mult, [bank, cst["twB"]], [tb_])
                    tav = ta_[:, :].rearrange("p (g r f) -> p g r f", g=GB, r=2)
                    tbv = tb_[:, :].rearrange("p (g r f) -> p g r f", g=GB, r=2)
                    k.tt("pool", bdst[:, 0, half * GB:(half + 1) * GB, :], tav[:, :, 0, :], tbv[:, :, 1, :], ALU.subtract, [ta_, tb_], [bdst])
                    k.tt("pool", bdst[:, 1, half * GB:(half + 1) * GB, :], tav[:, :, 1, :], tbv[:, :, 0, :], ALU.subtract, [ta_, tb_], [bdst])
                bre = bdst[:, 0, :, :].rearrange("p g f -> p (g f)"); bim = bdst[:, 1, :, :].rearrange("p g f -> p (g f)")
                k.mm(s2[0][:, :], cst["fC"][:, :], bre, True, False, [cst["fC"], bdst], [s2[0]])
                k.mm(s2[0][:, :], cst["fS"][:, :], bim, False, True, [cst["fS"], bdst], [s2[0]])
                k.mm(s2[1][:, :], cst["fC"][:, :], bim, True, False, [cst["fC"], bdst], [s2[1]])
                k.mm(s2[1][:, :], cst["fnS"][:, :], bre, False, True, [cst["fnS"], bdst], [s2[1]])
            for lg in range(512 // LG):
                xu_ = xu[lg % NXB]; xk_ = xk[lg % NXB]
                k.dma("sp", xu_[0:NR, :, :], S["UT"][lg * LG:(lg + 1) * LG, tok0:tok0 + n].rearrange("c (a p) -> a c p", p=128), writes=[xu_])
                k.dma("sp", xk_[:, :, :], KN[lg * LG:(lg + 1) * LG, :].rearrange("c (a p) -> a c p", p=128), writes=[xk_])
                for gf in range(LG // GF):
                    cbase = gf * GF
                    g_ = cnt["g"] % NB; cnt["g"] += 1
                    kh_ = kh[g_]; yh_ = yh[g_]
                    fwd(xk_, cbase, bk[g_])
                    k.copy("act", kh_[:, 0, :], s2[0][:, :], [s2[0]], [kh_])
                    k.copy("act", kh_[:, 1, :], s2[1][:, :], [s2[1]], [kh_])
                    fwd(xu_, cbase, bu[g_])
                    k.tt("dve", m1[g_][:, :], s2[0][:, :], kh_[:, 0, :], ALU.mult, [s2[0], kh_], [m1[g_]])
                    k.tt("dve", m3[g_][:, :], s2[0][:, :], kh_[:, 1, :], ALU.mult, [s2[0], kh_], [m3[g_]])
                    k.tt("dve", m2[g_][:, :], s2[1][:, :], kh_[:, 1, :], ALU.mult, [s2[1], kh_], [m2[g_]])
                    k.tt("dve", m4[g_][:, :], s2[1][:, :], kh_[:, 0, :], ALU.mult, [s2[1], kh_], [m4[g_]])
                    k.tt("pool", yh_[:, 0, :, :].rearrange("p g f -> p (g f)"), m1[g_][:, :], m2[g_][:, :], ALU.subtract, [m1[g_], m2[g_]], [yh_])
                    k.tt("pool", yh_[:, 1, :, :].rearrange("p g f -> p (g f)"), m3[g_][:, :], m4[g_][:, :], ALU.add, [m3[g_], m4[g_]], [yh_])
                    for sg in range(GF // 4):
                        b3 = s3[0]
                        iv = cnt["inv"] % NB; cnt["inv"] += 1
                        t2a_ = t2a[iv]; t2b_ = t2b[iv]; y3_ = y3[iv]
                        for cc in range(4):
                            ch = sg * 4 + cc
                            k.mm(b3[0:NA, cc * 256:(cc + 1) * 256], yh_[:, 0, ch, :], cst["fCS"][:, :], True, False, [yh_, cst["fCS"]], [b3])
                            k.mm(b3[0:NA, cc * 256:(cc + 1) * 256], yh_[:, 1, ch, :], cst["fnSC"][:, :], False, True, [yh_, cst["fnSC"]], [b3])
                        k.tt("dve", t2a_[:, :], b3[0:NA, :], cst["twA2"][:, :], ALU.mult, [b3, cst["twA2"]], [t2a_])
                        k.tt("dve", t2b_[:, :], b3[0:NA, :], cst["twB2"][:, :], ALU.mult, [b3, cst["twB2"]], [t2b_])
                        av = t2a_[:, :].rearrange("p (g r f) -> p g r f", g=4, r=2)
                        bv = t2b_[:, :].rearrange("p (g r f) -> p g r f", g=4, r=2)
                        k.tt("pool", y3_[:, 0, :, :], av[:, :, 0, :], bv[:, :, 1, :], ALU.add, [t2a_, t2b_], [y3_])
                        k.tt("pool", y3_[:, 1, :, :], av[:, :, 1, :], bv[:, :, 0, :], ALU.add, [t2a_, t2b_], [y3_])
                        p4 = s4[iv % 2]; yo_ = yo[iv % 2]
                        k.mm(p4[0:MO, :], cst["g3C"][:, 0:MO], y3_[:, 0, :, :].rearrange("p g f -> p (g f)"), True, False, [cst["g3C"], y3_], [p4])
                        k.mm(p4[0:MO, :], cst["g3nS"][:, 0:MO], y3_[:, 1, :, :].rearrange("p g f -> p (g f)"), False, True, [cst["g3nS"], y3_], [p4])
                        k.copy("act", yo_[:, :, :].rearrange("p g f -> p (g f)"), p4[0:MO, :], [p4], [yo_])
                        c0 = lg * LG + cbase + sg * 4
                        k.dma("sp", S["YT"][c0:c0 + 4, tok0:tok0 + MO * 128].rearrange("c (a p) -> a c p", p=128), yo_[:, :, :], reads=[yo_])
        return ph

    def make_hycombine(chunks):
        def ph():
            bd = k.sbuf([128, 4], F32); k.dma("sp", bd[:], I["hbd"][:, :], writes=[bd])
            yt = [k.sbuf([128, 512], F32, "yt") for _ in range(2)]
            ut = [k.sbuf([128, 512], F32, "ut") for _ in range(2)]
            x0 = [k.sbuf([128, 512], F32, "x0") for _ in range(2)]
            ob = [k.sbuf([128, 512], BF16, "ob") for _ in range(2)]
            n = 0
            for (t0, W, le, re, j) in chunks:
                for jc in range(4):
                    y_ = yt[n % 2]; u_ = ut[n % 2]; x_ = x0[n % 2]; o_ = ob[n % 2]; n += 1
                    rows = slice(jc * 128, (jc + 1) * 128)
                    k.dma("sp", y_[:, 0:W], S["YT"][rows, t0:t0 + W], writes=[y_])
                    k.dma("sp", u_[:, 0:W], S["UF"][rows, t0:t0 + W], writes=[u_])
                    k.dma("sp", x_[:, 0:W], S["X0T"][rows, t0:t0 + W], writes=[x_])
                    k.stt("dve", y_[:, 0:W], u_[:, 0:W], bd[:, jc:jc + 1], y_[:, 0:W], ALU.mult, ALU.add, [u_, bd, y_], [y_])
                    k.tt("dve", o_[:, 0:W], y_[:, 0:W], x_[:, 0:W], ALU.mult, [y_, x_], [o_])
                    k.dma("pool", S["OCT"][rows, t0:t0 + W], o_[:, 0:W], reads=[o_])
        return ph

    CH_E = [(c * 512, 512, c == 0, c == 8, 0) for c in range(9)]
    CH_OWN = [(c * 512, 512, c == 0, False, 0) for c in range(8)]
    phases.append(("inproj0", make_inproj(0, I["xt0"], CHUNKS_ALL)))
    phases.append(("filterL", make_filter("L", SEQ)))
    phases.append(("filterC", make_filter("C", CTX)))
    phases.append(("fftL", make_fftconv("L", 128, SEQ, 0, E // 128)))
    phases.append(("fftC", make_fftconv("C", 4, CTX, SEQ, 2)))
    phases.append(("hycomb", make_hycombine(CH_E + [CTXCH])))
    phases.append(("attn0", make_attn(0)))
    phases.append(("outproj0", make_outproj(0, I["xt0"], S["XM"], CH_E + [CTXCH])))
    phases.append(("ffn0", make_ffn(0, S["XM"], S["X1"], CH_E + [CTXCH])))
    phases.append(("inproj1", make_inproj(1, S["X1"], CH_E + [CTXCH])))
    phases.append(("attn1", make_attn(1)))
    phases.append(("outproj1", make_outproj(1, S["X1"], S["XM1"], CH_E)))
    phases.append(("ffn1", make_ffn(1, S["XM1"], OUT, CH_OWN)))
    for nm, ph in phases:
        k.phase(ph)
        if stop_after == nm:
            break
    k.close()
    return nc


_CACHE = {}


def kernel(**inputs):
    inp = {kk: np.asarray(v) for kk, v in inputs.items()}
    if "C" not in _CACHE:
        C = _consts()
        C["fftL"] = _fft_consts(128); C["fftC"] = _fft_consts(4)
        C["filtL"] = _filter_consts(SEQ); C["filtC"] = _filter_consts(CTX)
        _CACHE["C"] = C
    C = _CACHE["C"]
    nc = _build()
    in_maps = []
    for core in range(8):
        b, hh = core // 2, core % 2
        in_maps.append(_host_prep(inp, b, hh, C))
    res = run_bass_kernel_spmd(nc, in_maps, core_ids=list(range(8)))
    out = np.empty((4, SEQ, D), np.float32)
    for core in range(8):
        b, hh = core // 2, core % 2
        o = np.asarray(res.results[core]["out"]).T
        if hh == 0:
            out[b, :OWN] = o
        else:
            out[b, OWN:] = o[::-1]
    return out
```

```python
import numpy as np
import ml_dtypes
from contextlib import ExitStack
import concourse.bass as bass
import concourse.mybir as mybir
from concourse.bass_utils import run_bass_kernel_spmd

F32 = mybir.dt.float32
BF16 = mybir.dt.bfloat16
I32 = mybir.dt.int32
AF = mybir.ActivationFunctionType
ALU = mybir.AluOpType
NPBF = ml_dtypes.bfloat16

D = 1024; SEQ = 8192; CTX = 256; TT = SEQ + CTX; E = 4608; OWN = 4096
DFF = 2816; NM = 22
SAME_ENGINE_SYNC = {"act", "pool"}


class Res:
    __slots__ = ("name", "last_w", "reads", "excl")
    def __init__(self, name, excl=False):
        self.name = name; self.last_w = None; self.reads = {}; self.excl = excl


class Tl:
    def __init__(self, t, r):
        self.t = t; self.r = r
    def __getitem__(self, idx):
        return self.t[idx]


class K:
    ENGS = ("pe", "act", "dve", "pool", "sp")

    def __init__(self, nc):
        self.nc = nc
        self.es = ExitStack()
        self.sem = {}; self.cnt = {}
        for e in self.ENGS:
            self.sem[e] = self.es.enter_context(nc.semaphore("s_" + e))
            self.cnt[e] = 0
        self.dma_sems = {}
        self.dma_key = {}
        self.dma_rr = {}
        self.NDMASEM = {"sp": 32, "pool": 24, "act": 8, "pe": 4, "dve": 4}
        self.seen = {e: {} for e in self.ENGS}
        self.ops = {e: [] for e in self.ENGS}
        self.phase_es = None
        self.nres = 0
        self.ndma = 0

    def sbuf(self, shape, dt, name=None, persist=False):
        self.nres += 1
        name = (name or "t") + "_%d" % self.nres
        es = self.es if persist else self.phase_es
        t = es.enter_context(self.nc.sbuf_tensor(name, list(shape), dt))
        return Tl(t, Res(name))

    def psum(self, shape, dt, name=None):
        self.nres += 1
        name = (name or "p") + "_%d" % self.nres
        t = self.phase_es.enter_context(self.nc.psum_tensor(name, list(shape), dt))
        return Tl(t, Res(name, excl=True))

    def _need(self, reads, writes, eng=None):
        evs = []
        for r in reads:
            if r.last_w is not None: evs.append(r.last_w)
            if r.excl:
                evs.extend((kk[0], kk[1], v) for kk, v in r.reads.items() if not (kk[0] == "eng" and kk[1] == eng))
        for w in writes:
            if w.last_w is not None: evs.append(w.last_w)
            evs.extend((kk[0], kk[1], v) for kk, v in w.reads.items())
        return evs

    def _emit_waits(self, eng, evs, force_self=False):
        need = {}
        for kind, key, val in evs:
            if kind == "eng":
                if key == eng and eng not in SAME_ENGINE_SYNC and not force_self: continue
                v = val
            else:
                v = val
            if self.seen[eng].get((kind, key), 0) >= v: continue
            if need.get((kind, key), 0) < v: need[(kind, key)] = v
        for (kind, key), v in need.items():
            self.seen[eng][(kind, key)] = v
            sem = self.sem[key] if kind == "eng" else self.dma_sems[key][0]
            self.ops[eng].append(lambda e, sem=sem, v=v: e.wait_ge(sem, v))

    def _commit(self, ev, reads, writes):
        for r in reads:
            kk = (ev[0], ev[1])
            if r.reads.get(kk, 0) < ev[2]: r.reads[kk] = ev[2]
        for w in writes:
            w.last_w = ev; w.reads = {}

    def op(self, eng, fn, reads=(), writes=(), force_self=False):
        reads = [x.r if isinstance(x, Tl) else x for x in reads]
        writes = [x.r if isinstance(x, Tl) else x for x in writes]
        self._emit_waits(eng, self._need(reads, writes, eng), force_self)
        self.cnt[eng] += 1
        sem = self.sem[eng]
        self.ops[eng].append(lambda e, fn=fn, sem=sem: fn(e).then_inc(sem, 1))
        self._commit(("eng", eng, self.cnt[eng]), reads, writes)

    def dma(self, q, out, in_, reads=(), writes=(), **kw):
        reads = [x.r if isinstance(x, Tl) else x for x in reads]
        writes = [x.r if isinstance(x, Tl) else x for x in writes]
        npool = self.NDMASEM[q]
        idx = (q, self.dma_rr.get(q, 0) % npool)
        self.dma_rr[q] = self.dma_rr.get(q, 0) + 1
        if idx not in self.dma_sems:
            s_ = self.es.enter_context(self.nc.semaphore("d_%s_%d" % idx))
            self.dma_sems[idx] = [s_, 0]
        ent = self.dma_sems[idx]
        evs = self._need(reads, writes, q)
        if ent[1] > 0:
            evs.append(("dma", idx, ent[1] * 16))
        self._emit_waits(q, evs)
        ent[1] += 1
        sem = ent[0]
        self.ndma += 1
        self.ops[q].append(lambda e, out=out, in_=in_, sem=sem, kw=kw: e.dma_start(out=out, in_=in_, **kw).then_inc(sem, 16))
        self._commit(("dma", idx, ent[1] * 16), reads, writes)

    def barrier(self):
        evs = [("eng", e, self.cnt[e]) for e in self.ENGS if self.cnt[e] > 0]
        evs += [("dma", kk, v[1] * 16) for kk, v in self.dma_sems.items() if v[1] > 0]
        for e in self.ENGS:
            self._emit_waits(e, [ev for ev in evs if not (ev[0] == "eng" and ev[1] == e)])

    def phase(self, body):
        with ExitStack() as pes:
            self.phase_es = pes
            body()
            self.barrier()
            ops = self.ops
            self.ops = {e: [] for e in self.ENGS}
            with self.nc.Block() as block:
                @block.tensor
                def _(e):
                    for f in ops["pe"]: f(e)
                @block.scalar
                def _(e):
                    for f in ops["act"]: f(e)
                @block.vector
                def _(e):
                    for f in ops["dve"]: f(e)
                @block.gpsimd
                def _(e):
                    for f in ops["pool"]: f(e)
                @block.sync
                def _(e):
                    for f in ops["sp"]: f(e)
        self.phase_es = None

    def close(self):
        self.es.close()

    def ts(self, eng, out, in0, s1, s2, op0, op1, r, w, force_self=False):
        if s2 is None:
            s2 = 0.0; op1 = ALU.add
        self.op(eng, lambda e: e.tensor_scalar(out=out, in0=in0, scalar1=s1, scalar2=s2, op0=op0, op1=op1), r, w, force_self)
    def stt(self, eng, out, in0, sc, in1, op0, op1, r, w):
        self.op(eng, lambda e: e.scalar_tensor_tensor(out=out, in0=in0, scalar=sc, in1=in1, op0=op0, op1=op1), r, w)
    def tt(self, eng, out, in0, in1, op, r, w):
        self.op(eng, lambda e: e.tensor_tensor(out=out, in0=in0, in1=in1, op=op), r, w)
    def act(self, out, in_, func, r, w, bias=None, scale=None, accum=None):
        kw = {}
        if bias is not None: kw["bias"] = bias
        if scale is not None: kw["scale"] = scale
        if accum is not None: kw["accum_out"] = accum
        self.op("act", lambda e: e.activation(out=out, in_=in_, func=func, **kw), r, w)
    def mm(self, out, lhsT, rhs, start, stop, r, w):
        self.op("pe", lambda e: e.matmul(out, lhsT=lhsT, rhs=rhs, start=start, stop=stop), r, w)
    def copy(self, eng, out, in_, r, w):
        if eng == "act":
            self.op("act", lambda e: e.copy(out=out, in_=in_), r, w)
        else:
            self.op(eng, lambda e: e.tensor_copy(out=out, in_=in_), r, w)
    def memset(self, eng, ap, val, w):
        self.op(eng, lambda e: e.memset(ap, val), [], w)
    def recip(self, out, in_, r, w, force_self=False):
        self.op("dve", lambda e: e.reciprocal(out=out, in_=in_), r, w, force_self)

def _consts():
    c = {}
    nf = 16
    inv = 10000.0 ** (-np.arange(nf, dtype=np.float64) / nf)
    t = np.arange(SEQ)
    row = (t // 64).astype(np.float64); col = (t % 64).astype(np.float64)
    ar = row[None, :] * inv[:, None]; ac = col[None, :] * inv[:, None]
    cos64 = np.concatenate([np.cos(ar), np.cos(ar), np.cos(ac), np.cos(ac)], 0)
    sin64 = np.concatenate([-np.sin(ar), np.sin(ar), -np.sin(ac), np.sin(ac)], 0)
    c["cos64"] = cos64.astype(np.float32); c["sin64"] = sin64.astype(np.float32)
    perm = np.concatenate([np.arange(16) + 16, np.arange(16), np.arange(16) + 48, np.arange(16) + 32])
    c["perm64"] = perm
    p = np.arange(128)[:, None]; f = np.arange(512)[None, :]
    c["bmask"] = np.stack([(np.abs(128 * r + p - f) <= 128) for r in range(-1, 5)], 0).astype(NPBF)
    return c


def _fft_consts(NA):
    N = 128 * NA
    c = {}
    a = np.arange(NA)[:, None]; f1 = np.arange(NA)[None, :]
    th = 2 * np.pi * a * f1 / NA
    c["f1cs"] = np.concatenate([np.cos(th), -np.sin(th)], 1).astype(NPBF)
    p = np.arange(128)[:, None]; f2 = np.arange(128)[None, :]
    th2 = 2 * np.pi * p * f2 / 128
    C = np.cos(th2); S = np.sin(th2)
    c["fC"] = C.astype(NPBF); c["fS"] = S.astype(NPBF); c["fnS"] = (-S).astype(NPBF)
    c["fCS"] = np.concatenate([C, S], 1).astype(NPBF)
    c["fnSC"] = np.concatenate([-S, C], 1).astype(NPBF)
    tw = 2 * np.pi * np.arange(128)[:, None] * np.arange(NA)[None, :] / N
    G = 512 // (2 * NA)
    tc_ = np.cos(tw); ts_ = np.sin(tw)
    A = np.concatenate([tc_, tc_], 1)
    B = np.concatenate([ts_, -ts_], 1)
    c["twA"] = np.tile(A[:, None, :], (1, G, 1)).reshape(128, 512).astype(np.float32)
    c["twB"] = np.tile(B[:, None, :], (1, G, 1)).reshape(128, 512).astype(np.float32)
    tcT = np.cos(tw).T; tsT = np.sin(tw).T
    A2 = np.stack([tcT, tcT], 1)
    B2 = np.stack([tsT, -tsT], 1)
    c["twA2"] = np.tile(A2[:, None], (1, 4, 1, 1)).reshape(NA, 1024).astype(np.float32)
    c["twB2"] = np.tile(B2[:, None], (1, 4, 1, 1)).reshape(NA, 1024).astype(np.float32)
    th = 2 * np.pi * np.arange(NA)[:, None] * np.arange(NA)[None, :] / NA
    c["g3C"] = (np.cos(th) / N).astype(NPBF); c["g3nS"] = (-np.sin(th) / N).astype(NPBF)
    return c


def _filter_consts(n):
    q = np.arange(2 * n)
    tap = np.where(q < n, q, 2 * n - q).astype(np.int64)
    tap = np.minimum(tap, n - 1)
    t01 = np.linspace(0.0, 1.0, n, dtype=np.float32)
    bands = 16
    w = (2.0 * np.pi * np.arange(n, dtype=np.float32) / n).astype(np.float32)
    f = np.linspace(1e-4, bands - 1, bands, dtype=np.float32)[None, :]
    feats = np.concatenate([t01[:, None], np.cos(f * w[:, None]), -np.sin(f * w[:, None])], -1).astype(np.float32)
    featsT = np.ascontiguousarray(feats[tap].T)
    t01b = np.ascontiguousarray(np.tile(t01[tap][None, :], (128, 1)))
    deltas = np.abs(np.linspace(np.log(1e-2) / 1.5, np.log(1e-2) / 0.3, 512, dtype=np.float32))
    ndel = np.ascontiguousarray((-deltas).reshape(4, 128).T)
    return featsT.astype(np.float32), t01b.astype(np.float32), ndel.astype(np.float32)


def _pm(v, n):
    return np.ascontiguousarray(np.asarray(v, np.float32).reshape(n, 128).T)


def _host_prep(inp, b, hh, C):
    fl = (hh == 1)
    m = {}
    x = inp["x"][b]; cx = inp["ctx"][b]
    if fl: x = x[::-1]; cx = cx[::-1]
    m["xt0"] = np.ascontiguousarray(np.concatenate([x, cx], 0).T)
    cv = np.stack([inp["c"][b], inp["c_ctx"]], 1)
    m["cvec"] = np.ascontiguousarray(cv.reshape(8, 128, 2).transpose(1, 0, 2))
    perm = C["perm64"]
    for i in range(2):
        m[f"ada_w{i}"] = inp["ada_w"][i]
        m[f"ada_b{i}"] = _pm(inp["ada_b"][i], 48)
        m[f"nmix{i}"] = _pm(inp["norm_mix"][i], 8)
        m[f"nffn{i}"] = _pm(inp["norm_ffn"][i], 8)
        w = inp["mix_w_in"][i]
        qk = w[:, :768].reshape(D, 12, 64)[:, :, perm].reshape(D, 768)
        m[f"w_in{i}"] = np.ascontiguousarray(np.concatenate([w, qk], 1))
        m[f"w_out{i}"] = inp["mix_w_out"][i]
        gq = inp["attn_q_norm"][i]; gk = inp["attn_k_norm"][i]
        m[f"qkg{i}"] = np.ascontiguousarray(np.stack([np.tile(gq, 2), np.tile(gq[perm], 2), np.tile(gk, 2), np.tile(gk[perm], 2)], 1).astype(np.float32))
        m[f"w_up{i}"] = inp["ffn_w_up"][i]
        m[f"w_dn{i}"] = inp["ffn_w_down"][i]
        fw = inp["ffn_conv_w"][i]
        if fl: fw = fw[::-1]
        m[f"fcw{i}"] = np.ascontiguousarray(fw.reshape(3, NM, 128).transpose(2, 1, 0))
        m[f"fcb{i}"] = _pm(inp["ffn_conv_b"][i], NM)
    hw = inp["hy_conv_w"][0]
    if fl: hw = hw[::-1]
    m["hcw"] = np.ascontiguousarray(hw.reshape(3, 12, 128).transpose(2, 1, 0))
    m["hcb"] = _pm(inp["hy_conv_b"][0], 12)
    sw = inp["sc_conv_w"][0]
    if fl: sw = sw[::-1]
    m["scw"] = np.ascontiguousarray(sw.reshape(3, 4, 128).transpose(2, 1, 0))
    m["hw1"] = inp["hy_w1"][0]; m["hw2"] = inp["hy_w2"][0]; m["hw3"] = inp["hy_w3"][0]
    w4 = inp["hy_w4"][0]
    if fl: w4 = np.concatenate([w4[:, 512:], w4[:, :512]], 1)
    m["hw4"] = np.ascontiguousarray(w4)
    m["hb"] = np.ascontiguousarray(np.stack([inp["hy_b1"][0], inp["hy_b2"][0], inp["hy_b3"][0], inp["hy_freq"][0]], 1).astype(np.float32))
    m["hbd"] = _pm(inp["hy_bias_d"][0], 4)
    m["sink"] = np.ascontiguousarray(np.tile(inp["swa_sink"][0][None, :], (128, 1)).astype(np.float32))
    cos = C["cos64"]; sin = C["sin64"]
    if fl: cos = cos[:, ::-1]; sin = sin[:, ::-1]
    cosx = np.concatenate([cos, np.ones((64, CTX), np.float32)], 1)
    sinx = np.concatenate([sin, np.zeros((64, CTX), np.float32)], 1)
    m["cosT"] = np.ascontiguousarray(np.concatenate([cosx, cosx], 0))
    m["sinT"] = np.ascontiguousarray(np.concatenate([sinx, sinx], 0))
    m["bmask"] = C["bmask"]
    for tag, NA in (("L", 128), ("C", 4)):
        for kk, v in C["fft" + tag].items():
            m[kk + tag] = v
    for tag in ("L", "C"):
        ft, t01b, ndel = C["filt" + tag]
        m["featsT" + tag] = ft; m["t01b" + tag] = t01b
    m["ndel"] = C["filtL"][2]
    return m

def _build(stop_after=None, dbg=()):
    nc = bass.Bass("TRN2", target_bir_lowering=False)
    k = K(nc)
    def din(name, shape, dt=F32):
        return nc.dram_tensor(name, list(shape), dt, kind="ExternalInput").ap()
    def dscr(name, shape, dt):
        kind = "ExternalOutput" if name in dbg else "Internal"
        return nc.dram_tensor(name, list(shape), dt, kind=kind).ap()
    I = {}
    I["xt0"] = din("xt0", [D, TT]); I["cvec"] = din("cvec", [128, 8, 2])
    for i in range(2):
        I[f"ada_w{i}"] = din(f"ada_w{i}", [D, 6 * D]); I[f"ada_b{i}"] = din(f"ada_b{i}", [128, 48])
        I[f"nmix{i}"] = din(f"nmix{i}", [128, 8]); I[f"nffn{i}"] = din(f"nffn{i}", [128, 8])
        I[f"w_in{i}"] = din(f"w_in{i}", [D, 3328]); I[f"w_out{i}"] = din(f"w_out{i}", [D, D])
        I[f"qkg{i}"] = din(f"qkg{i}", [128, 4])
        I[f"w_up{i}"] = din(f"w_up{i}", [D, 2 * DFF]); I[f"w_dn{i}"] = din(f"w_dn{i}", [DFF, D])
        I[f"fcw{i}"] = din(f"fcw{i}", [128, NM, 3]); I[f"fcb{i}"] = din(f"fcb{i}", [128, NM])
    I["hcw"] = din("hcw", [128, 12, 3]); I["hcb"] = din("hcb", [128, 12]); I["scw"] = din("scw", [128, 4, 3])
    I["hw1"] = din("hw1", [33, 64]); I["hw2"] = din("hw2", [64, 64]); I["hw3"] = din("hw3", [64, 64])
    I["hw4"] = din("hw4", [64, 1024]); I["hb"] = din("hb", [64, 4]); I["hbd"] = din("hbd", [128, 4])
    I["sink"] = din("sink", [128, 8])
    I["cosT"] = din("cosT", [128, TT]); I["sinT"] = din("sinT", [128, TT])
    I["bmask"] = din("bmask", [6, 128, 512], BF16)
    for tag, NA in (("L", 128), ("C", 4)):
        I["f1cs" + tag] = din("f1cs" + tag, [NA, 2 * NA], BF16)
        for nm in ("fC", "fS", "fnS"): I[nm + tag] = din(nm + tag, [128, 128], BF16)
        for nm in ("fCS", "fnSC"): I[nm + tag] = din(nm + tag, [128, 256], BF16)
        for nm in ("twA", "twB"): I[nm + tag] = din(nm + tag, [128, 512])
        for nm in ("twA2", "twB2"): I[nm + tag] = din(nm + tag, [NA, 1024])
        for nm in ("g3C", "g3nS"): I[nm + tag] = din(nm + tag, [NA, NA], BF16)
        n = 64 * NA
        I["featsT" + tag] = din("featsT" + tag, [33, 2 * n]); I["t01b" + tag] = din("t01b" + tag, [128, 2 * n])
    I["ndel"] = din("ndel", [128, 4])
    OUT = nc.dram_tensor("out", [D, OWN], F32, kind="ExternalOutput").ap()

    S = {}
    for i in range(2):
        S[f"bw_in{i}"] = dscr(f"bw_in{i}", [D, 3328], BF16); S[f"bw_out{i}"] = dscr(f"bw_out{i}", [D, D], BF16)
        S[f"bw_up{i}"] = dscr(f"bw_up{i}", [D, 2 * DFF], BF16); S[f"bw_dn{i}"] = dscr(f"bw_dn{i}", [DFF, D], BF16)
    S["QT"] = dscr("QT", [8, 64, TT], BF16); S["KT"] = dscr("KT", [4, 64, TT], BF16); S["V"] = dscr("V", [TT, 256], BF16)
    S["UT"] = dscr("UT", [512, TT], BF16); S["X0T"] = dscr("X0T", [512, TT], F32)
    S["UF"] = dscr("UF", [512, TT], F32)
    S["YT"] = dscr("YT", [512, TT], F32)
    S["OAT"] = dscr("OAT", [512, TT], BF16); S["OCT"] = dscr("OCT", [512, TT], BF16)
    S["XM"] = dscr("XM", [D, TT], F32); S["X1"] = dscr("X1", [D, TT], F32); S["XM1"] = dscr("XM1", [D, TT], F32)
    S["KRAWL"] = dscr("KRAWL", [512, 2 * SEQ], F32); S["KNL"] = dscr("KNL", [512, 2 * SEQ], BF16)
    S["KRAWC"] = dscr("KRAWC", [512, 2 * CTX], F32); S["KNC"] = dscr("KNC", [512, 2 * CTX], BF16)

    MOD = k.sbuf([128, 2, 48, 2], F32, "mod", persist=True)
    AB = k.sbuf([128, 2, 2, 2, 8, 2], F32, "ab", persist=True)
    ONES = k.sbuf([128, 128], BF16, "ones", persist=True)
    BONES = k.sbuf([128, 128], BF16, "bones", persist=True)
    SEL = k.sbuf([128, 64], F32, "sel", persist=True)
    ESINK = k.sbuf([128, 8], F32, "esink", persist=True)

    phases = []

    conv_items = []
    for i in range(2):
        for src, dst, rows, cols in ((f"w_in{i}", f"bw_in{i}", D, 3328), (f"w_out{i}", f"bw_out{i}", D, D),
                                     (f"w_up{i}", f"bw_up{i}", D, 2 * DFF), (f"w_dn{i}", f"bw_dn{i}", DFF, D)):
            for r0 in range(0, rows, 128):
                for c0 in range(0, cols, 2048):
                    conv_items.append((src, dst, r0, c0, min(2048, cols - c0)))
    CONV_EARLY = sum(1 for it in conv_items if it[0] in ("w_in0", "w_out0"))
    conv_pos = [0]

    class make_converter:
        def __init__(self):
            self.stg = [k.sbuf([128, 2048], F32, "stg") for _ in range(3)]
            self.stb = [k.sbuf([128, 2048], BF16, "stb") for _ in range(3)]
        def step(self, eng=None, q="sp"):
            n = conv_pos[0]
            if n >= len(conv_items): return False
            conv_pos[0] += 1
            src, dst, r0, c0, cw = conv_items[n]
            a = self.stg[n % 3]; bt = self.stb[n % 3]
            k.dma(q, a[:, 0:cw], I[src][r0:r0 + 128, c0:c0 + cw], writes=[a])
            k.copy(eng or ("dve" if n % 2 == 0 else "act"), bt[:, 0:cw], a[:, 0:cw], [a], [bt])
            k.dma(q, S[dst][r0:r0 + 128, c0:c0 + cw], bt[:, 0:cw], reads=[bt])
            return True

    def ph_setup():
        k.memset("dve", ONES[:], 1.0, [ONES])
        k.memset("dve", BONES[:], 0.0, [BONES])
        k.memset("dve", BONES[0:64, 0:64], 1.0, [BONES])
        k.memset("dve", BONES[64:128, 64:128], 1.0, [BONES])
        k.memset("dve", SEL[:], 0.0, [SEL])
        k.memset("dve", SEL[64:65, :], 1.0, [SEL])
        snk = k.sbuf([128, 8], F32)
        k.dma("sp", snk[:], I["sink"][:, :], writes=[snk])
        k.act(ESINK[:], snk[:], AF.Exp, [snk], [ESINK])
        cv_ = make_converter()
        for _ in range(CONV_EARLY):
            cv_.step()
        cv = k.sbuf([128, 8, 2], F32)
        k.dma("sp", cv[:], I["cvec"][:, :, :], writes=[cv])
        sc = k.sbuf([128, 8, 2], F32)
        k.act(sc[:], cv[:], AF.Silu, [cv], [sc])
        wst = [k.sbuf([128, 8, 512], F32, "wst") for _ in range(2)]
        ps = [k.psum([128, 512], F32) for _ in range(2)]
        n = 0
        for i in range(2):
            adb = k.sbuf([128, 48], F32)
            k.dma("sp", adb[:], I[f"ada_b{i}"][:, :], writes=[adb])
            for cb in range(12):
                wt = wst[n % 2]; n += 1
                k.dma("sp", wt[:], I[f"ada_w{i}"][:, cb * 512:(cb + 1) * 512].rearrange("(k p) f -> p k f", p=128), writes=[wt])
                for mi in range(4):
                    m = cb * 4 + mi
                    pt = ps[m % 2]
                    for kk in range(8):
                        k.mm(pt[:, 0:2], wt[:, kk, mi * 128:(mi + 1) * 128], sc[:, kk, :], kk == 0, kk == 7, [wt, sc], [pt])
                    k.ts("dve", MOD[:, i, m, :], pt[:, 0:2], adb[:, m:m + 1], None, ALU.add, None, [pt, adb], [MOD])
            for wh, (nm, sh0, sc0) in enumerate(((f"nmix{i}", 0, 8), (f"nffn{i}", 24, 32))):
                g = k.sbuf([128, 8], F32)
                k.dma("sp", g[:], I[nm][:, :], writes=[g])
                for j in range(2):
                    k.stt("dve", AB[:, i, wh, 0, :, j], MOD[:, i, sc0:sc0 + 8, j], 1.0, g[:], ALU.add, ALU.mult, [MOD, g], [AB])
                    k.copy("dve", AB[:, i, wh, 1, :, j], MOD[:, i, sh0:sh0 + 8, j], [MOD], [AB])
    phases.append(("setup", ph_setup))

    def norm_mod(xh, Wc, i, wh, j, sq, ssps, rstd, tmp, h):
        for kk in range(8):
            k.act(sq[:, kk, 0:Wc], xh[:, kk, 0:Wc], AF.Square, [xh], [sq])
        for (c0, c1) in ((0, min(512, Wc)), (512, Wc)):
            if c1 <= c0: continue
            for kk in range(8):
                k.mm(ssps[:, c0:c1], ONES[:, :], sq[:, kk, c0:c1], kk == 0, kk == 7, [ONES, sq], [ssps])
        k.act(rstd[:, 0:Wc], ssps[:, 0:Wc], AF.Sqrt, [ssps], [rstd], bias=1e-6, scale=1.0 / D)
        k.recip(rstd[:, 0:Wc], rstd[:, 0:Wc], [rstd], [rstd])
        for kk in range(8):
            t = tmp[kk % 2]
            k.stt("dve", t[:, 0:Wc], xh[:, kk, 0:Wc], AB[:, i, wh, 0, kk, j:j + 1], rstd[:, 0:Wc], ALU.mult, ALU.mult, [xh, AB, rstd], [t])
            k.act(h[:, kk, 0:Wc], t[:, 0:Wc], AF.Identity, [t, AB], [h], bias=AB[:, i, wh, 1, kk, j:j + 1], scale=1.0)

    CHUNKS_ALL = [(c * 512, 512, c == 0, c == 15, 0) for c in range(16)] + [(SEQ, CTX, True, True, 1)]
    CHUNKS_E = [(c * 512, 512, c == 0, False, 0) for c in range(9)]
    CTXCH = (SEQ, CTX, True, True, 1)

    def load_xh(xh, src, t0, W, ledge, redge, q="sp"):
        lo = 2 if ledge else 1
        hi = W + 2 if redge else W + 3
        if ledge: k.memset("pool", xh[:, :, 1:2], 0.0, [xh])
        if redge: k.memset("pool", xh[:, :, W + 2:W + 3], 0.0, [xh])
        k.dma(q, xh[:, :, lo:hi], src[:, t0 - 2 + lo:t0 - 2 + hi].rearrange("(k p) t -> p k t", p=128), writes=[xh])

    def make_inproj(i, XIN, chunks):
        def ph():
            w = k.sbuf([128, 8, 3328], BF16, "w_in")
            for kk in range(8):
                k.dma("sp", w[:, kk, :], S[f"bw_in{i}"][kk * 128:(kk + 1) * 128, :], writes=[w])
            qkg = k.sbuf([128, 4], F32); k.dma("sp", qkg[:], I[f"qkg{i}"][:, :], writes=[qkg])
            if i == 0:
                cw = k.sbuf([128, 12, 3], F32); k.dma("sp", cw[:], I["hcw"][:, :, :], writes=[cw])
                cb = k.sbuf([128, 12], F32); k.dma("sp", cb[:], I["hcb"][:, :], writes=[cb])
            else:
                cw = k.sbuf([128, 4, 3], F32); k.dma("sp", cw[:], I["scw"][:, :, :], writes=[cw])
            xhs = [k.sbuf([128, 8, 516], F32, "xh") for _ in range(2)]
            for t_ in xhs: k.memset("pool", t_[:], 0.0, [t_])
            sq = k.sbuf([128, 8, 516], BF16, "sq")
            h = k.sbuf([128, 8, 516], BF16, "h")
            rstd = k.sbuf([128, 516], F32, "rstd")
            tmp = [k.sbuf([128, 516], F32, "tmp") for _ in range(2)]
            ssps = k.psum([128, 1024], F32, "ssps")
            zps = [k.psum([128, 512], F32, "zps") for _ in range(4)]
            cps = k.psum([128, 1024], F32, "cps")
            cosb = k.sbuf([128, 512], F32, "cos"); sinb = k.sbuf([128, 512], F32, "sin")
            sq2 = [k.sbuf([128, 512], BF16, "sq2") for _ in range(2)]
            rs = [k.sbuf([128, 512], F32, "rs") for _ in range(2)]
            ta = [k.sbuf([128, 512], F32, "ta") for _ in range(2)]
            tb = [k.sbuf([128, 512], F32, "tb") for _ in range(2)]
            qo = [k.sbuf([128, 512], BF16, "qo") for _ in range(2)]
            vo = [k.sbuf([128, 256], BF16, "vo") for _ in range(2)]
            asb = [k.sbuf([128, 516], F32, "asb") for _ in range(3)]
            c1 = [k.sbuf([128, 512], F32, "c1") for _ in range(3)]
            uo = [k.sbuf([128, 512], F32, "uo") for _ in range(2)]
            ub = [k.sbuf([128, 512], BF16, "ub") for _ in range(2)]
            pp = [k.sbuf([128, 516], F32, "pp") for _ in range(2)]
            nq = [0]
            import os
            PARTS = os.environ.get("INPROJ_PARTS", "nqvc")
            NCHK = int(os.environ.get("INPROJ_NCH", "99"))
            for ci, (t0, W, le, re, j) in enumerate(chunks[:NCHK]):
                Wc = W + 4
                do_q = (t0 < E) or j == 1
                if i == 1 and j == 1: do_q = False
                xh = xhs[ci % 2]
                load_xh(xh, XIN, t0, W, le, re)
                norm_mod(xh, Wc, i, 0, j, sq, ssps, rstd, tmp, h)
                k.dma("sp", cosb[:, 0:W], I["cosT"][:, t0:t0 + W], writes=[cosb])
                k.dma("sp", sinb[:, 0:W], I["sinT"][:, t0:t0 + W], writes=[sinb])
                for pr in range(6):
                    if "q" not in PARTS: continue
                    if pr < 4 and not do_q: continue
                    n = nq[0]; nq[0] += 1
                    zp = zps[(2 * n) % 4]; zsp = zps[(2 * n + 1) % 4]
                    for kk in range(8):
                        k.mm(zp[:, 0:W], w[:, kk, pr * 128:(pr + 1) * 128], h[:, kk, 2:W + 2], kk == 0, kk == 7, [w, h], [zp])
                    for kk in range(8):
                        k.mm(zsp[:, 0:W], w[:, kk, 2560 + pr * 128:2560 + (pr + 1) * 128], h[:, kk, 2:W + 2], kk == 0, kk == 7, [w, h], [zsp])
                    s2 = sq2[n % 2]; r_ = rs[n % 2]; a_ = ta[n % 2]; b_ = tb[n % 2]; q_ = qo[n % 2]
                    gi = 0 if pr < 4 else 2
                    k.act(s2[:, 0:W], zp[:, 0:W], AF.Square, [zp], [s2])
                    k.stt("dve", a_[:, 0:W], zp[:, 0:W], qkg[:, gi:gi + 1], cosb[:, 0:W], ALU.mult, ALU.mult, [zp, qkg, cosb], [a_])
                    k.stt("dve", b_[:, 0:W], zsp[:, 0:W], qkg[:, gi + 1:gi + 2], sinb[:, 0:W], ALU.mult, ALU.mult, [zsp, qkg, sinb], [b_])
                    k.mm(zp[:, 0:W], BONES[:, :], s2[:, 0:W], True, True, [BONES, s2], [zp])
                    k.act(r_[:, 0:W], zp[:, 0:W], AF.Sqrt, [zp], [r_], bias=1e-6, scale=1.0 / 64)
                    k.recip(r_[:, 0:W], r_[:, 0:W], [r_], [r_])
                    QS = os.environ.get("QSKIP", "")
                    pe_ = "dve" if "pool" in QS else "pool"
                    k.tt(pe_, a_[:, 0:W], a_[:, 0:W], b_[:, 0:W], ALU.add, [a_, b_], [a_])
                    k.tt(pe_, q_[:, 0:W], a_[:, 0:W], r_[:, 0:W], ALU.mult, [a_, r_], [q_])
                    for hf in range(2):
                        if "dma" in QS: continue
                        if pr < 4:
                            dst = S["QT"][2 * pr + hf, :, t0:t0 + W]
                        else:
                            dst = S["KT"][2 * (pr - 4) + hf, :, t0:t0 + W]
                        k.dma("pool", dst, q_[hf * 64:(hf + 1) * 64, 0:W], reads=[q_])
                for tj in range(W // 128):
                    if "v" not in PARTS: continue
                    n = nq[0]; nq[0] += 1
                    vp = zps[n % 4]
                    for kk in range(8):
                        k.mm(vp[:, 0:256], h[:, kk, 2 + tj * 128:2 + (tj + 1) * 128], w[:, kk, 768:1024], kk == 0, kk == 7, [w, h], [vp])
                    v_ = vo[n % 2]
                    k.copy("act", v_[:, :], vp[:, 0:256], [vp], [v_])
                    k.dma("pool", S["V"][t0 + tj * 128:t0 + (tj + 1) * 128, :], v_[:, :], reads=[v_])
                def convproj(m, dst):
                    for (c0, c1_) in ((0, min(512, Wc)), (512, Wc)):
                        if c1_ <= c0: continue
                        for kk in range(8):
                            k.mm(cps[:, c0:c1_], w[:, kk, 1024 + m * 128:1024 + (m + 1) * 128], h[:, kk, c0:c1_], kk == 0, kk == 7, [w, h], [cps])
                    k.copy("act", dst[:, 0:Wc], cps[:, 0:Wc], [cps], [dst])
                    if le: k.memset("pool", dst[:, 1:2], 0.0, [dst])
                    if re: k.memset("pool", dst[:, W + 2:W + 3], 0.0, [dst])
                def conv3(out, a, wts, m, bias, eng="dve"):
                    if bias is not None:
                        k.ts(eng, out[:, 0:W], a[:, 2:W + 2], wts[:, m, 1:2], bias, ALU.mult, ALU.add, [a, wts, cb], [out])
                    else:
                        k.ts(eng, out[:, 0:W], a[:, 2:W + 2], wts[:, m, 1:2], None, ALU.mult, None, [a, wts], [out])
                    k.stt(eng, out[:, 0:W], a[:, 1:W + 1], wts[:, m, 0:1], out[:, 0:W], ALU.mult, ALU.add, [a, wts, out], [out])
                    k.stt(eng, out[:, 0:W], a[:, 3:W + 3], wts[:, m, 2:3], out[:, 0:W], ALU.mult, ALU.add, [a, wts, out], [out])
                for jc in range(4):
                    if "c" not in PARTS: continue
                    if i == 0:
                        convproj(4 + jc, asb[0]); conv3(c1[0], asb[0], cw, 4 + jc, cb[:, 4 + jc:5 + jc])
                        convproj(8 + jc, asb[1]); conv3(c1[1], asb[1], cw, 8 + jc, cb[:, 8 + jc:9 + jc])
                        u_ = uo[jc % 2]; ub_ = ub[jc % 2]
                        k.tt("dve", u_[:, 0:W], c1[0][:, 0:W], c1[1][:, 0:W], ALU.mult, [c1[0], c1[1]], [u_])
                        k.copy("pool", ub_[:, 0:W], u_[:, 0:W], [u_], [ub_])
                        k.dma("pool", S["UF"][jc * 128:(jc + 1) * 128, t0:t0 + W], u_[:, 0:W], reads=[u_])
                        k.dma("pool", S["UT"][jc * 128:(jc + 1) * 128, t0:t0 + W], ub_[:, 0:W], reads=[ub_])
                        if do_q:
                            convproj(jc, asb[2]); conv3(c1[2], asb[2], cw, jc, cb[:, jc:jc + 1])
                            k.dma("pool", S["X0T"][jc * 128:(jc + 1) * 128, t0:t0 + W], c1[2][:, 0:W], reads=[c1[2]])
                    elif j == 0:
                        convproj(4 + jc, asb[0]); convproj(8 + jc, asb[1])
                        p_ = pp[jc % 2]
                        k.tt("dve", p_[:, 0:Wc], asb[0][:, 0:Wc], asb[1][:, 0:Wc], ALU.mult, [asb[0], asb[1]], [p_])
                        conv3(c1[0], p_, cw, jc, None)
                        convproj(jc, asb[2])
                        ub_ = ub[jc % 2]
                        k.tt("pool", ub_[:, 0:W], c1[0][:, 0:W], asb[2][:, 2:W + 2], ALU.mult, [c1[0], asb[2]], [ub_])
                        k.dma("pool", S["OCT"][jc * 128:(jc + 1) * 128, t0:t0 + W], ub_[:, 0:W], reads=[ub_])
        return ph

    def make_attn(i):
        def ph():
            NKT = TT // 128
            kT = k.sbuf([128, TT], BF16, "kT")
            k.memset("pool", kT[64:128, :], 0.0, [kT])
            va = k.sbuf([128, NKT, 65], BF16, "va")
            qT = [k.sbuf([128, 512], BF16, "qT") for _ in range(2)]
            for t_ in qT: k.memset("pool", t_[64:128, :], 0.0, [t_])
            sps = [k.psum([128, 512], F32, "sps") for _ in range(4)]
            ops_ = [k.psum([128, 512], F32, "ops") for _ in range(2)]
            bps = k.psum([128, 512], F32, "bps")
            pT = [k.sbuf([128, 512], BF16, "pT") for _ in range(4)]
            osb = [k.sbuf([65, 512], F32, "osb") for _ in range(2)]
            rb = [k.sbuf([64, 512], F32, "rb") for _ in range(2)]
            ob = [k.sbuf([64, 512], BF16, "ob") for _ in range(2)]
            if i == 1:
                bm = k.sbuf([128, 6, 512], BF16, "bm")
                for r in range(6):
                    k.dma("sp", bm[:, r, :], I["bmask"][r, :, :], writes=[bm])
            qch = []
            for c in range(9):
                t0 = c * 512
                if i == 0:
                    kts = [(kt, 0, 512, None) for kt in range(NKT)]
                else:
                    kts = []
                    for r in range(-1, 5):
                        kt = 4 * c + r
                        if kt < 0 or kt >= E // 128: continue
                        f0 = max(0, 128 * (r - 1)); f1 = min(512, 128 * (r + 2))
                        kts.append((kt, f0, f1, r + 1))
                    kts += [(64, 0, 512, None), (65, 0, 512, None)]
                qch.append((t0, 512, kts))
            if i == 0:
                qch.append((SEQ, CTX, [(64, 0, CTX, None), (65, 0, CTX, None)]))
            n = 0; nh = 0
            cvt = make_converter() if i == 0 else None
            for jkv in range(4):
                k.dma("sp", kT[0:64, :], S["KT"][jkv, :, :], writes=[kT])
                k.dma("sp", va[:, :, 0:64], S["V"][:, jkv * 64:(jkv + 1) * 64].rearrange("(n p) d -> p n d", p=128), writes=[va])
                k.memset("pool", va[:, :, 64:65], 1.0, [va])
                for g in range(2):
                    hq = 2 * jkv + g
                    for (t0, W, kts) in qch:
                        q_ = qT[nh % 2]; op_ = ops_[nh % 2]; o_ = osb[nh % 2]; r_ = rb[nh % 2]; b_ = ob[nh % 2]
                        nh += 1
                        k.dma("sp", q_[0:64, 0:W], S["QT"][hq, :, t0:t0 + W], writes=[q_])
                        if i == 1:
                            pass
                        nk = len(kts)
                        def smm(idx):
                            kt, f0, f1, mi = kts[idx]
                            sp_ = sps[(n + idx) % 4]
                            k.mm(sp_[:, f0:f1], kT[:, kt * 128:(kt + 1) * 128], q_[:, f0:f1], True, True, [kT, q_], [sp_])
                        smm(0)
                        if nk > 1: smm(1)
                        for idx in range(nk):
                            if idx + 2 < nk: smm(idx + 2)
                            kt, f0, f1, mi = kts[idx]
                            sp_ = sps[(n + idx) % 4]; p_ = pT[(n + idx) % 4]
                            if mi is not None and (f0 > 0 or f1 < W):
                                k.memset("pool", p_[:, 0:W], 0.0, [p_])
                            k.act(p_[:, f0:f1], sp_[:, f0:f1], AF.Exp, [sp_], [p_], scale=0.125)
                            if mi is not None:
                                k.tt("pool", p_[:, f0:f1], p_[:, f0:f1], bm[:, mi, f0:f1], ALU.mult, [p_, bm], [p_])
                            k.mm(op_[0:65, 0:W], va[:, kt, :], p_[:, 0:W], idx == 0, idx == nk - 1, [va, p_], [op_])
                        n += nk
                        k.copy("act", o_[:, 0:W], op_[0:65, 0:W], [op_], [o_])
                        k.mm(bps[0:64, 0:W], SEL[0:65, :], o_[0:65, 0:W], True, True, [SEL, o_], [bps])
                        if i == 1:
                            k.ts("dve", r_[:, 0:W], bps[0:64, 0:W], ESINK[0:64, hq:hq + 1], None, ALU.add, None, [bps, ESINK], [r_])
                            k.recip(r_[:, 0:W], r_[:, 0:W], [r_], [r_])
                        else:
                            k.recip(r_[:, 0:W], bps[0:64, 0:W], [bps], [r_])
                        k.tt("dve", b_[:, 0:W], o_[0:64, 0:W], r_[:, 0:W], ALU.mult, [o_, r_], [b_])
                        k.dma("pool", S["OAT"][hq * 64:(hq + 1) * 64, t0:t0 + W], b_[:, 0:W], reads=[b_])
                        if cvt is not None:
                            cvt.step("dve", "pool"); cvt.step("dve", "pool")
            if cvt is not None:
                while cvt.step("dve", "pool"): pass
        return ph

    def make_outproj(i, XIN, XOUT, chunks):
        def ph():
            wa = k.sbuf([128, 4, D], BF16, "woa")
            wc = k.sbuf([128, 4, D], BF16, "woc")
            k.dma("sp", wa[:], S[f"bw_out{i}"][0:512, :].rearrange("(c p) f -> p c f", p=128), writes=[wa])
            k.dma("sp", wc[:], S[f"bw_out{i}"][512:1024, :].rearrange("(c p) f -> p c f", p=128), writes=[wc])
            oa = [k.sbuf([128, 4, 512], BF16, "oa") for _ in range(2)]
            oc = [k.sbuf([128, 4, 512], BF16, "oc") for _ in range(2)]
            xs = [k.sbuf([128, 8, 512], F32, "xs") for _ in range(2)]
            xo = [k.sbuf([128, 8, 512], F32, "xo") for _ in range(2)]
            ps = [k.psum([128, 512], F32, "ps") for _ in range(4)]
            n = 0
            for ci, (t0, W, le, re, j) in enumerate(chunks):
                a_ = oa[ci % 2]; c_ = oc[ci % 2]; x_ = xs[ci % 2]; o_ = xo[ci % 2]
                k.dma("sp", a_[:, :, 0:W], S["OAT"][:, t0:t0 + W].rearrange("(c p) t -> p c t", p=128), writes=[a_])
                k.dma("sp", c_[:, :, 0:W], S["OCT"][:, t0:t0 + W].rearrange("(c p) t -> p c t", p=128), writes=[c_])
                k.dma("sp", x_[:, :, 0:W], XIN[:, t0:t0 + W].rearrange("(k p) t -> p k t", p=128), writes=[x_])
                for m in range(8):
                    p_ = ps[n % 4]; n += 1
                    for hh_ in range(4):
                        k.mm(p_[:, 0:W], wa[:, hh_, m * 128:(m + 1) * 128], a_[:, hh_, 0:W], hh_ == 0, False, [wa, a_], [p_])
                    for cc in range(4):
                        k.mm(p_[:, 0:W], wc[:, cc, m * 128:(m + 1) * 128], c_[:, cc, 0:W], False, cc == 3, [wc, c_], [p_])
                    k.stt("dve", o_[:, m, 0:W], p_[:, 0:W], MOD[:, i, 16 + m, j:j + 1], x_[:, m, 0:W], ALU.mult, ALU.add, [p_, MOD, x_], [o_])
                k.dma("pool", XOUT[:, t0:t0 + W].rearrange("(k p) t -> p k t", p=128), o_[:, :, 0:W], reads=[o_])
        return ph

    def make_ffn(i, XIN, XOUT, chunks, final=False):
        def ph():
            wu = k.sbuf([128, 8, 2 * DFF], BF16, "wu")
            for kk in range(8):
                k.dma("sp", wu[:, kk, :], S[f"bw_up{i}"][kk * 128:(kk + 1) * 128, :], writes=[wu])
            wd = [k.sbuf([128, NM, 128], BF16, "wd") for _ in range(2)]
            cw = k.sbuf([128, NM, 3], F32); k.dma("sp", cw[:], I[f"fcw{i}"][:, :, :], writes=[cw])
            cb = k.sbuf([128, NM], F32); k.dma("sp", cb[:], I[f"fcb{i}"][:, :], writes=[cb])
            xh = k.sbuf([128, 8, 516], F32, "xh")
            k.memset("pool", xh[:], 0.0, [xh])
            sq = k.sbuf([128, 8, 516], BF16, "sq")
            h = k.sbuf([128, 8, 516], BF16, "h")
            rstd = k.sbuf([128, 516], F32, "rstd")
            tmp = [k.sbuf([128, 516], F32, "tmp") for _ in range(2)]
            gg = k.sbuf([128, NM, 512], BF16, "gg")
            asb = [k.sbuf([128, 516], F32, "asb") for _ in range(2)]
            c1 = [k.sbuf([128, 512], F32, "c1") for _ in range(2)]
            xo = [k.sbuf([128, 512], F32, "xo") for _ in range(2)]
            ssps = k.psum([128, 1024], F32, "ssps")
            aps = [k.psum([128, 1024], F32, "aps") for _ in range(2)]
            vps = [k.psum([128, 512], F32, "vps") for _ in range(2)]
            n = 0; nd = 0
            for ci, (t0, W, le, re, j) in enumerate(chunks):
                Wc = W + 4
                load_xh(xh, XIN, t0, W, le, re)
                norm_mod(xh, Wc, i, 1, j, sq, ssps, rstd, tmp, h)
                for m in range(NM):
                    ap_ = aps[n % 2]; vp_ = vps[n % 2]; a_ = asb[n % 2]; c_ = c1[n % 2]; n += 1
                    for (c0, c1_) in ((0, min(512, Wc)), (512, Wc)):
                        if c1_ <= c0: continue
                        for kk in range(8):
                            k.mm(ap_[:, c0:c1_], wu[:, kk, m * 128:(m + 1) * 128], h[:, kk, c0:c1_], kk == 0, kk == 7, [wu, h], [ap_])
                    for kk in range(8):
                        k.mm(vp_[:, 0:W], wu[:, kk, DFF + m * 128:DFF + (m + 1) * 128], h[:, kk, 2:W + 2], kk == 0, kk == 7, [wu, h], [vp_])
                    k.copy("act", a_[:, 0:Wc], ap_[:, 0:Wc], [ap_], [a_])
                    if le: k.memset("pool", a_[:, 1:2], 0.0, [a_])
                    if re: k.memset("pool", a_[:, W + 2:W + 3], 0.0, [a_])
                    eng = "dve"
                    k.ts(eng, c_[:, 0:W], a_[:, 2:W + 2], cw[:, m, 1:2], cb[:, m:m + 1], ALU.mult, ALU.add, [a_, cw, cb], [c_])
                    k.stt(eng, c_[:, 0:W], a_[:, 1:W + 1], cw[:, m, 0:1], c_[:, 0:W], ALU.mult, ALU.add, [a_, cw, c_], [c_])
                    k.stt(eng, c_[:, 0:W], a_[:, 3:W + 3], cw[:, m, 2:3], c_[:, 0:W], ALU.mult, ALU.add, [a_, cw, c_], [c_])
                    k.act(c_[:, 0:W], c_[:, 0:W], AF.Gelu_apprx_tanh, [c_], [c_])
                    k.tt("dve", gg[:, m, 0:W], c_[:, 0:W], vp_[:, 0:W], ALU.mult, [c_, vp_], [gg])
                for mo in range(8):
                    wd_ = wd[nd % 2]; o_ = xo[nd % 2]; p_ = vps[nd % 2]; nd += 1
                    k.dma("sp", wd_[:], S[f"bw_dn{i}"][:, mo * 128:(mo + 1) * 128].rearrange("(m p) f -> p m f", p=128), writes=[wd_])
                    for m in range(NM):
                        k.mm(p_[:, 0:W], wd_[:, m, :], gg[:, m, 0:W], m == 0, m == NM - 1, [wd_, gg], [p_])
                    k.stt("dve", o_[:, 0:W], p_[:, 0:W], MOD[:, i, 40 + mo, j:j + 1], xh[:, mo, 2:W + 2], ALU.mult, ALU.add, [p_, MOD, xh], [o_])
                    k.dma("pool", XOUT[mo * 128:(mo + 1) * 128, t0:t0 + W], o_[:, 0:W], reads=[o_])
        return ph

    def make_filter(tag, n):
        KRAW = S["KRAW" + tag]; KN = S["KN" + tag]
        def ph():
            w1 = k.sbuf([33, 64], F32); k.dma("sp", w1[:], I["hw1"][:, :], writes=[w1])
            w2 = k.sbuf([64, 64], F32); k.dma("sp", w2[:], I["hw2"][:, :], writes=[w2])
            w3 = k.sbuf([64, 64], F32); k.dma("sp", w3[:], I["hw3"][:, :], writes=[w3])
            w4 = k.sbuf([64, 1024], F32); k.dma("sp", w4[:], I["hw4"][:, :], writes=[w4])
            hb = k.sbuf([64, 4], F32); k.dma("sp", hb[:], I["hb"][:, :], writes=[hb])
            ndel = k.sbuf([128, 4], F32); k.dma("sp", ndel[:], I["ndel"][:, :], writes=[ndel])
            asum = k.sbuf([128, 4, 40], F32, "asum")
            k.memset("dve", asum[:], 0.0, [asum])
            ft = [k.sbuf([33, 512], F32, "ft") for _ in range(2)]
            t01 = [k.sbuf([128, 512], F32, "t01") for _ in range(2)]
            hid2 = [[k.sbuf([64, 512], F32, "hid") for _ in range(3)] for _ in range(2)]
            ki2 = [k.sbuf([64, 512], I32, "ki") for _ in range(2)]
            win = [k.sbuf([128, 512], F32, "win") for _ in range(2)]
            kr = [k.sbuf([128, 512], F32, "kr") for _ in range(2)]
            junk = k.sbuf([128, 512], F32, "junk")
            ps = [k.psum([128, 512], F32, "ps") for _ in range(4)]
            NCH = (2 * n) // 512
            n_ = 0
            for c in range(NCH):
                q0 = c * 512
                f_ = ft[c % 2]; t_ = t01[c % 2]
                k.dma("sp", f_[:, :], I["featsT" + tag][:, q0:q0 + 512], writes=[f_])
                k.dma("sp", t_[:, :], I["t01b" + tag][:, q0:q0 + 512], writes=[t_])
                src = f_; srcK = 33
                hid = hid2[c % 2]; ki = ki2[c % 2]
                for li, wl in enumerate((w1, w2, w3)):
                    p_ = ps[n_ % 4]; n_ += 1
                    k.mm(p_[0:64, :], wl[0:srcK, :], src[0:srcK, :], True, True, [wl, src], [p_])
                    hd = hid[li]
                    k.ts("dve", hd[:, :], p_[0:64, :], hb[:, li:li + 1], hb[:, 3:4], ALU.add, ALU.mult, [p_, hb], [hd])
                    k.ts("dve", ki[:, :], hd[:, :], float(1.0 / (2 * np.pi)), None, ALU.mult, None, [hd], [ki])
                    k.stt("dve", hd[:, :], ki[:, :], float(-2 * np.pi), hd[:, :], ALU.mult, ALU.add, [ki, hd], [hd])
                    k.act(hd[:, :], hd[:, :], AF.Sin, [hd], [hd])
                    src = hd; srcK = 64
                segs = []
                if q0 + 512 <= n: segs = [(0, 512, 0)]
                elif q0 >= n: segs = [(0, 512, 512)]
                else: segs = [(0, n - q0, 0), (n - q0, 512, 512)]
                for jc in range(4):
                    p_ = ps[n_ % 4]; n_ += 1
                    for (a0, a1, off) in segs:
                        k.mm(p_[:, a0:a1], w4[:, off + jc * 128:off + (jc + 1) * 128], hid[2][:, a0:a1], True, True, [w4, hid[2]], [p_])
                    wn = win[jc % 2]; kr_ = kr[jc % 2]
                    k.act(wn[:, :], t_[:, :], AF.Exp, [t_, ndel], [wn], scale=ndel[:, jc:jc + 1])
                    k.stt("dve", kr_[:, :], wn[:, :], 0.05, p_[:, :], ALU.add, ALU.mult, [wn, p_], [kr_])
                    if q0 <= n < q0 + 512:
                        k.memset("dve", kr_[:, n - q0:n - q0 + 1], 0.0, [kr_])
                    k.act(junk[:, :], kr_[:, :], AF.Abs, [kr_], [junk, asum], accum=asum[:, jc, c:c + 1])
                    k.dma("pool", KRAW[jc * 128:(jc + 1) * 128, q0:q0 + 512], kr_[:, :], reads=[kr_])
            rn = k.sbuf([128, 4], F32, "rn")
            for jc in range(4):
                k.op("dve", lambda e, jc=jc: e.reduce_sum(out=rn[:, jc:jc + 1], in_=asum[:, jc, 0:NCH], axis=mybir.AxisListType.X), [asum], [rn])
            k.recip(rn[:, :], rn[:, :], [rn], [rn], force_self=True)
            k.barrier()
            big = [k.sbuf([128, 2048], F32, "big") for _ in range(2)]
            bigb = [k.sbuf([128, 2048], BF16, "bigb") for _ in range(2)]
            n2 = 0
            for jc in range(4):
                for q0 in range(0, 2 * n, 2048):
                    qw = min(2048, 2 * n - q0)
                    b_ = big[n2 % 2]; bb_ = bigb[n2 % 2]; n2 += 1
                    k.dma("sp", b_[:, 0:qw], KRAW[jc * 128:(jc + 1) * 128, q0:q0 + qw], writes=[b_])
                    k.ts("dve", bb_[:, 0:qw], b_[:, 0:qw], rn[:, jc:jc + 1], None, ALU.mult, None, [b_, rn], [bb_], force_self=True)
                    k.dma("pool", KN[jc * 128:(jc + 1) * 128, q0:q0 + qw], bb_[:, 0:qw], reads=[bb_])
        return ph

    def make_fftconv(tag, NA, n, tok0, n_out_blocks):
        KN = S["KN" + tag]
        NR = NA // 2
        GF = 512 // NA
        GB = 512 // (2 * NA)
        def ph():
            cst = {}
            for nm, shp, dt in (("f1cs", [NA, 2 * NA], BF16), ("fC", [128, 128], BF16), ("fS", [128, 128], BF16), ("fnS", [128, 128], BF16),
                                ("fCS", [128, 256], BF16), ("fnSC", [128, 256], BF16), ("twA", [128, 512], F32), ("twB", [128, 512], F32),
                                ("twA2", [NA, 1024], F32), ("twB2", [NA, 1024], F32), ("g3C", [NA, NA], BF16), ("g3nS", [NA, NA], BF16)):
                cst[nm] = k.sbuf(shp, dt, nm)
                k.dma("sp", cst[nm][:], I[nm + tag][tuple(slice(None) for _ in shp)], writes=[cst[nm]])
            LG = 32 if NA == 128 else 128
            NXB = 2 if NA == 128 else 1
            xu = [k.sbuf([NA, LG, 128], BF16, "xu") for _ in range(NXB)]
            for t_ in xu: k.memset("pool", t_[:], 0.0, [t_])
            xk = [k.sbuf([NA, LG, 128], BF16, "xk") for _ in range(NXB)]
            s1 = [k.psum([128, 512], F32, "s1") for _ in range(2)]
            s2 = [k.psum([128, 512], F32, "s2") for _ in range(2)]
            s3 = [k.psum([128, 1024], F32, "s3") for _ in range(1)]
            s4 = [k.psum([128, 512], F32, "s4") for _ in range(2)]
            NB = 2
            ta = [k.sbuf([128, 512], F32, "ta") for _ in range(NB)]; tb = [k.sbuf([128, 512], F32, "tb") for _ in range(NB)]
            bu = [k.sbuf([128, 2, GF, NA], BF16, "bu") for _ in range(NB)]; bk = [k.sbuf([128, 2, GF, NA], BF16, "bk") for _ in range(NB)]
            kh = [k.sbuf([128, 2, 512], F32, "kh") for _ in range(NB)]
            m1 = [k.sbuf([128, 512], F32, "m1") for _ in range(NB)]; m2 = [k.sbuf([128, 512], F32, "m2") for _ in range(NB)]
            m3 = [k.sbuf([128, 512], F32, "m3") for _ in range(NB)]; m4 = [k.sbuf([128, 512], F32, "m4") for _ in range(NB)]
            yh = [k.sbuf([128, 2, GF, NA], BF16, "yh") for _ in range(NB)]
            t2a = [k.sbuf([NA, 1024], F32, "t2a") for _ in range(NB)]; t2b = [k.sbuf([NA, 1024], F32, "t2b") for _ in range(NB)]
            y3 = [k.sbuf([NA, 2, 4, 128], BF16, "y3") for _ in range(NB)]
            yo = [k.sbuf([n_out_blocks, 4, 128], F32, "yo") for _ in range(2)]
            MO = n_out_blocks
            cnt = {"tw": 0, "inv": 0, "g": 0}
            def fwd(x, cbase, bdst):
                for half in range(2):
                    bank = s1[half]
                    ta_ = ta[cnt["tw"] % NB]; tb_ = tb[cnt["tw"] % NB]; cnt["tw"] += 1
                    for cc in range(GB):
                        ch = cbase + half * GB + cc
                        k.mm(bank[:, cc * 2 * NA:(cc + 1) * 2 * NA], x[0:NA, ch, :], cst["f1cs"][0:NA, :], True, True, [x, cst["f1cs"]], [bank])
                    k.tt("dve", ta_[:, :], bank[:, :], cst["twA"][:, :], ALU.mult, [bank, cst["twA"]], [ta_])
                    k.tt("dve", tb_[:, :], bank[:, :], cst["twB"][:, :], ALU.mult, [bank, cst["twB"]], [tb_])
                    tav = ta_[:, :].rearrange("p (g r f) -> p g r f", g=GB, r=2)
                    tbv = tb_[:, :].rearrange("p (g r f) -> p g r f", g=GB, r=2)
                    k.tt("dve", bdst[:, 0, half * GB:(half + 1) * GB, :], tav[:, :, 0, :], tbv[:, :, 1, :], ALU.subtract, [ta_, tb_], [bdst])
                    k.tt("dve", bdst[:, 1, half * GB:(half + 1) * GB, :], tav[:, :, 1, :], tbv[:, :, 0, :], ALU.subtract, [ta_, tb_], [bdst])
                bre = bdst[:, 0, :, :].rearrange("p g f -> p (g f)"); bim = bdst[:, 1, :, :].rearrange("p g f -> p (g f)")
                k.mm(s2[0][:, :], cst["fC"][:, :], bre, True, False, [cst["fC"], bdst], [s2[0]])
                k.mm(s2[0][:, :], cst["fS"][:, :], bim, False, True, [cst["fS"], bdst], [s2[0]])
                k.mm(s2[1][:, :], cst["fC"][:, :], bim, True, False, [cst["fC"], bdst], [s2[1]])
                k.mm(s2[1][:, :], cst["fnS"][:, :], bre, False, True, [cst["fnS"], bdst], [s2[1]])
            for lg in range(512 // LG):
                xu_ = xu[lg % NXB]; xk_ = xk[lg % NXB]
                k.dma("sp", xu_[0:NR, :, :], S["UT"][lg * LG:(lg + 1) * LG, tok0:tok0 + n].rearrange("c (a p) -> a c p", p=128), writes=[xu_])
                k.dma("sp", xk_[:, :, :], KN[lg * LG:(lg + 1) * LG, :].rearrange("c (a p) -> a c p", p=128), writes=[xk_])
                for gf in range(LG // GF):
                    cbase = gf * GF
                    g_ = cnt["g"] % NB; cnt["g"] += 1
                    kh_ = kh[g_]; yh_ = yh[g_]
                    fwd(xk_, cbase, bk[g_])
                    k.copy("act", kh_[:, 0, :], s2[0][:, :], [s2[0]], [kh_])
                    k.copy("act", kh_[:, 1, :], s2[1][:, :], [s2[1]], [kh_])
                    fwd(xu_, cbase, bu[g_])
                    k.tt("dve", m1[g_][:, :], s2[0][:, :], kh_[:, 0, :], ALU.mult, [s2[0], kh_], [m1[g_]])
                    k.tt("dve", m3[g_][:, :], s2[0][:, :], kh_[:, 1, :], ALU.mult, [s2[0], kh_], [m3[g_]])
                    k.tt("dve", m2[g_][:, :], s2[1][:, :], kh_[:, 1, :], ALU.mult, [s2[1], kh_], [m2[g_]])
                    k.tt("dve", m4[g_][:, :], s2[1][:, :], kh_[:, 0, :], ALU.mult, [s2[1], kh_], [m4[g_]])
                    k.tt("pool", yh_[:, 0, :, :].rearrange("p g f -> p (g f)"), m1[g_][:, :], m2[g_][:, :], ALU.subtract, [m1[g_], m2[g_]], [yh_])
                    k.tt("pool", yh_[:, 1, :, :].rearrange("p g f -> p (g f)"), m3[g_][:, :], m4[g_][:, :], ALU.add, [m3[g_], m4[g_]], [yh_])
                    for sg in range(GF // 4):
                        b3 = s3[0]
                        iv = cnt["inv"] % NB; cnt["inv"] += 1
                        t2a_ = t2a[iv]; t2b_ = t2b[iv]; y3_ = y3[iv]
                        for cc in range(4):
                            ch = sg * 4 + cc
                            k.mm(b3[0:NA, cc * 256:(cc + 1) * 256], yh_[:, 0, ch, :], cst["fCS"][:, :], True, False, [yh_, cst["fCS"]], [b3])
                            k.mm(b3[0:NA, cc * 256:(cc + 1) * 256], yh_[:, 1, ch, :], cst["fnSC"][:, :], False, True, [yh_, cst["fnSC"]], [b3])
                        k.tt("dve", t2a_[:, :], b3[0:NA, :], cst["twA2"][:, :], ALU.mult, [b3, cst["twA2"]], [t2a_])
                        k.tt("dve", t2b_[:, :], b3[0:NA, :], cst["twB2"][:, :], ALU.mult, [b3, cst["twB2"]], [t2b_])
                        av = t2a_[:, :].rearrange("p (g r f) -> p g r f", g=4, r=2)
                        bv = t2b_[:, :].rearrange("p (g r f) -> p g r f", g=4, r=2)
                        k.tt("pool", y3_[:, 0, :, :], av[:, :, 0, :], bv[:, :, 1, :], ALU.add, [t2a_, t2b_], [y3_])
                        k.tt("pool", y3_[:, 1, :, :], av[:, :, 1, :], bv[:, :, 0, :], ALU.add, [t2a_, t2b_], [y3_])
                        p4 = s4[iv % 2]; yo_ = yo[iv % 2]
                        k.mm(p4[0:MO, :], cst["g3C"][:, 0:MO], y3_[:, 0, :, :].rearrange("p g f -> p (g f)"), True, False, [cst["g3C"], y3_], [p4])
                        k.mm(p4[0:MO, :], cst["g3nS"][:, 0:MO], y3_[:, 1, :, :].rearrange("p g f -> p (g f)"), False, True, [cst["g3nS"], y3_], [p4])
                        k.copy("act", yo_[:, :, :].rearrange("p g f -> p (g f)"), p4[0:MO, :], [p4], [yo_])
                        c0 = lg * LG + cbase + sg * 4
                        k.dma("sp", S["YT"][c0:c0 + 4, tok0:tok0 + MO * 128].rearrange("c (a p) -> a c p", p=128), yo_[:, :, :], reads=[yo_])
        return ph

    def make_hycombine(chunks):
        def ph():
            bd = k.sbuf([128, 4], F32); k.dma("sp", bd[:], I["hbd"][:, :], writes=[bd])
            yt = [k.sbuf([128, 512], F32, "yt") for _ in range(2)]
            ut = [k.sbuf([128, 512], F32, "ut") for _ in range(2)]
            x0 = [k.sbuf([128, 512], F32, "x0") for _ in range(2)]
            ob = [k.sbuf([128, 512], BF16, "ob") for _ in range(2)]
            n = 0
            for (t0, W, le, re, j) in chunks:
                for jc in range(4):
                    y_ = yt[n % 2]; u_ = ut[n % 2]; x_ = x0[n % 2]; o_ = ob[n % 2]; n += 1
                    rows = slice(jc * 128, (jc + 1) * 128)
                    k.dma("sp", y_[:, 0:W], S["YT"][rows, t0:t0 + W], writes=[y_])
                    k.dma("sp", u_[:, 0:W], S["UF"][rows, t0:t0 + W], writes=[u_])
                    k.dma("sp", x_[:, 0:W], S["X0T"][rows, t0:t0 + W], writes=[x_])
                    k.stt("dve", y_[:, 0:W], u_[:, 0:W], bd[:, jc:jc + 1], y_[:, 0:W], ALU.mult, ALU.add, [u_, bd, y_], [y_])
                    k.tt("dve", o_[:, 0:W], y_[:, 0:W], x_[:, 0:W], ALU.mult, [y_, x_], [o_])
                    k.dma("pool", S["OCT"][rows, t0:t0 + W], o_[:, 0:W], reads=[o_])
        return ph

    CH_E = [(c * 512, 512, c == 0, c == 8, 0) for c in range(9)]
    CH_OWN = [(c * 512, 512, c == 0, False, 0) for c in range(8)]
    phases.append(("inproj0", make_inproj(0, I["xt0"], CHUNKS_ALL)))
    phases.append(("filterL", make_filter("L", SEQ)))
    phases.append(("filterC", make_filter("C", CTX)))
    phases.append(("fftL", make_fftconv("L", 128, SEQ, 0, E // 128)))
    phases.append(("fftC", make_fftconv("C", 4, CTX, SEQ, 2)))
    phases.append(("hycomb", make_hycombine(CH_E + [CTXCH])))
    phases.append(("attn0", make_attn(0)))
    phases.append(("outproj0", make_outproj(0, I["xt0"], S["XM"], CH_E + [CTXCH])))
    phases.append(("ffn0", make_ffn(0, S["XM"], S["X1"], CH_E + [CTXCH])))
    phases.append(("inproj1", make_inproj(1, S["X1"], CH_E + [CTXCH])))
    phases.append(("attn1", make_attn(1)))
    phases.append(("outproj1", make_outproj(1, S["X1"], S["XM1"], CH_E)))
    phases.append(("ffn1", make_ffn(1, S["XM1"], OUT, CH_OWN)))
    for nm, ph in phases:
        k.phase(ph)
        if stop_after == nm:
            break
    k.close()
    return nc


_CACHE = {}


def kernel(**inputs):
    inp = {kk: np.asarray(v) for kk, v in inputs.items()}
    if "C" not in _CACHE:
        C = _consts()
        C["fftL"] = _fft_consts(128); C["fftC"] = _fft_consts(4)
        C["filtL"] = _filter_consts(SEQ); C["filtC"] = _filter_consts(CTX)
        _CACHE["C"] = C
    C = _CACHE["C"]
    nc = _build()
    in_maps = []
    for core in range(8):
        b, hh = core // 2, core % 2
        in_maps.append(_host_prep(inp, b, hh, C))
    res = run_bass_kernel_spmd(nc, in_maps, core_ids=list(range(8)))
    out = np.empty((4, SEQ, D), np.float32)
    for core in range(8):
        b, hh = core // 2, core % 2
        o = np.asarray(res.results[core]["out"]).T
        if hh == 0:
            out[b, :OWN] = o
        else:
            out[b, OWN:] = o[::-1]
    return out
```

```python
import numpy as np
import ml_dtypes
from contextlib import ExitStack
import concourse.bass as bass
import concourse.mybir as mybir
from concourse.bass_utils import run_bass_kernel_spmd

F32 = mybir.dt.float32
BF16 = mybir.dt.bfloat16
I32 = mybir.dt.int32
AF = mybir.ActivationFunctionType
ALU = mybir.AluOpType
NPBF = ml_dtypes.bfloat16

D = 1024; SEQ = 8192; CTX = 256; TT = SEQ + CTX; E = 4608; OWN = 4096
DFF = 2816; NM = 22
SAME_ENGINE_SYNC = {"act", "pool"}


class Res:
    __slots__ = ("name", "last_w", "reads", "excl")
    def __init__(self, name, excl=False):
        self.name = name; self.last_w = None; self.reads = {}; self.excl = excl


class Tl:
    def __init__(self, t, r):
        self.t = t; self.r = r
    def __getitem__(self, idx):
        return self.t[idx]


class K:
    ENGS = ("pe", "act", "dve", "pool", "sp")

    def __init__(self, nc):
        self.nc = nc
        self.es = ExitStack()
        self.sem = {}; self.cnt = {}
        for e in self.ENGS:
            self.sem[e] = self.es.enter_context(nc.semaphore("s_" + e))
            self.cnt[e] = 0
        self.dma_sems = {}
        self.dma_key = {}
        self.dma_rr = {}
        self.NDMASEM = {"sp": 32, "pool": 24, "act": 8, "pe": 4, "dve": 4}
        self.seen = {e: {} for e in self.ENGS}
        self.ops = {e: [] for e in self.ENGS}
        self.phase_es = None
        self.nres = 0
        self.ndma = 0

    def sbuf(self, shape, dt, name=None, persist=False):
        self.nres += 1
        name = (name or "t") + "_%d" % self.nres
        es = self.es if persist else self.phase_es
        t = es.enter_context(self.nc.sbuf_tensor(name, list(shape), dt))
        return Tl(t, Res(name))

    def psum(self, shape, dt, name=None):
        self.nres += 1
        name = (name or "p") + "_%d" % self.nres
        t = self.phase_es.enter_context(self.nc.psum_tensor(name, list(shape), dt))
        return Tl(t, Res(name, excl=True))

    def _need(self, reads, writes, eng=None):
        evs = []
        for r in reads:
            if r.last_w is not None: evs.append(r.last_w)
            if r.excl:
                evs.extend((kk[0], kk[1], v) for kk, v in r.reads.items() if not (kk[0] == "eng" and kk[1] == eng))
        for w in writes:
            if w.last_w is not None: evs.append(w.last_w)
            evs.extend((kk[0], kk[1], v) for kk, v in w.reads.items())
        return evs

    def _emit_waits(self, eng, evs, force_self=False):
        need = {}
        for kind, key, val in evs:
            if kind == "eng":
                if key == eng and eng not in SAME_ENGINE_SYNC and not force_self: continue
                v = val
            else:
                v = val
            if self.seen[eng].get((kind, key), 0) >= v: continue
            if need.get((kind, key), 0) < v: need[(kind, key)] = v
        for (kind, key), v in need.items():
            self.seen[eng][(kind, key)] = v
            sem = self.sem[key] if kind == "eng" else self.dma_sems[key][0]
            self.ops[eng].append(lambda e, sem=sem, v=v: e.wait_ge(sem, v))

    def _commit(self, ev, reads, writes):
        for r in reads:
            kk = (ev[0], ev[1])
            if r.reads.get(kk, 0) < ev[2]: r.reads[kk] = ev[2]
        for w in writes:
            w.last_w = ev; w.reads = {}

    def op(self, eng, fn, reads=(), writes=(), force_self=False):
        reads = [x.r if isinstance(x, Tl) else x for x in reads]
        writes = [x.r if isinstance(x, Tl) else x for x in writes]
        self._emit_waits(eng, self._need(reads, writes, eng), force_self)
        self.cnt[eng] += 1
        sem = self.sem[eng]
        self.ops[eng].append(lambda e, fn=fn, sem=sem: fn(e).then_inc(sem, 1))
        self._commit(("eng", eng, self.cnt[eng]), reads, writes)

    def dma(self, q, out, in_, reads=(), writes=(), **kw):
        reads = [x.r if isinstance(x, Tl) else x for x in reads]
        writes = [x.r if isinstance(x, Tl) else x for x in writes]
        npool = self.NDMASEM[q]
        idx = (q, self.dma_rr.get(q, 0) % npool)
        self.dma_rr[q] = self.dma_rr.get(q, 0) + 1
        if idx not in self.dma_sems:
            s_ = self.es.enter_context(self.nc.semaphore("d_%s_%d" % idx))
            self.dma_sems[idx] = [s_, 0]
        ent = self.dma_sems[idx]
        evs = self._need(reads, writes, q)
        if ent[1] > 0:
            evs.append(("dma", idx, ent[1] * 16))
        self._emit_waits(q, evs)
        ent[1] += 1
        sem = ent[0]
        self.ndma += 1
        self.ops[q].append(lambda e, out=out, in_=in_, sem=sem, kw=kw: e.dma_start(out=out, in_=in_, **kw).then_inc(sem, 16))
        self._commit(("dma", idx, ent[1] * 16), reads, writes)

    def barrier(self):
        evs = [("eng", e, self.cnt[e]) for e in self.ENGS if self.cnt[e] > 0]
        evs += [("dma", kk, v[1] * 16) for kk, v in self.dma_sems.items() if v[1] > 0]
        for e in self.ENGS:
            self._emit_waits(e, [ev for ev in evs if not (ev[0] == "eng" and ev[1] == e)])

    def phase(self, body):
        with ExitStack() as pes:
            self.phase_es = pes
            body()
            self.barrier()
            ops = self.ops
            self.ops = {e: [] for e in self.ENGS}
            with self.nc.Block() as block:
                @block.tensor
                def _(e):
                    for f in ops["pe"]: f(e)
                @block.scalar
                def _(e):
                    for f in ops["act"]: f(e)
                @block.vector
                def _(e):
                    for f in ops["dve"]: f(e)
                @block.gpsimd
                def _(e):
                    for f in ops["pool"]: f(e)
                @block.sync
                def _(e):
                    for f in ops["sp"]: f(e)
        self.phase_es = None

    def close(self):
        self.es.close()

    def ts(self, eng, out, in0, s1, s2, op0, op1, r, w, force_self=False):
        if s2 is None:
            s2 = 0.0; op1 = ALU.add
        self.op(eng, lambda e: e.tensor_scalar(out=out, in0=in0, scalar1=s1, scalar2=s2, op0=op0, op1=op1), r, w, force_self)
    def stt(self, eng, out, in0, sc, in1, op0, op1, r, w):
        self.op(eng, lambda e: e.scalar_tensor_tensor(out=out, in0=in0, scalar=sc, in1=in1, op0=op0, op1=op1), r, w)
    def tt(self, eng, out, in0, in1, op, r, w):
        self.op(eng, lambda e: e.tensor_tensor(out=out, in0=in0, in1=in1, op=op), r, w)
    def act(self, out, in_, func, r, w, bias=None, scale=None, accum=None):
        kw = {}
        if bias is not None: kw["bias"] = bias
        if scale is not None: kw["scale"] = scale
        if accum is not None: kw["accum_out"] = accum
        self.op("act", lambda e: e.activation(out=out, in_=in_, func=func, **kw), r, w)
    def mm(self, out, lhsT, rhs, start, stop, r, w):
        self.op("pe", lambda e: e.matmul(out, lhsT=lhsT, rhs=rhs, start=start, stop=stop), r, w)
    def copy(self, eng, out, in_, r, w):
        if eng == "act":
            self.op("act", lambda e: e.copy(out=out, in_=in_), r, w)
        else:
            self.op(eng, lambda e: e.tensor_copy(out=out, in_=in_), r, w)
    def memset(self, eng, ap, val, w):
        self.op(eng, lambda e: e.memset(ap, val), [], w)
    def recip(self, out, in_, r, w, force_self=False):
        self.op("dve", lambda e: e.reciprocal(out=out, in_=in_), r, w, force_self)

def _consts():
    c = {}
    nf = 16
    inv = 10000.0 ** (-np.arange(nf, dtype=np.float64) / nf)
    t = np.arange(SEQ)
    row = (t // 64).astype(np.float64); col = (t % 64).astype(np.float64)
    ar = row[None, :] * inv[:, None]; ac = col[None, :] * inv[:, None]
    cos64 = np.concatenate([np.cos(ar), np.cos(ar), np.cos(ac), np.cos(ac)], 0)
    sin64 = np.concatenate([-np.sin(ar), np.sin(ar), -np.sin(ac), np.sin(ac)], 0)
    c["cos64"] = cos64.astype(np.float32); c["sin64"] = sin64.astype(np.float32)
    perm = np.concatenate([np.arange(16) + 16, np.arange(16), np.arange(16) + 48, np.arange(16) + 32])
    c["perm64"] = perm
    p = np.arange(128)[:, None]; f = np.arange(512)[None, :]
    c["bmask"] = np.stack([(np.abs(128 * r + p - f) <= 128) for r in range(-1, 5)], 0).astype(NPBF)
    return c


def _fft_consts(NA):
    N = 128 * NA
    c = {}
    a = np.arange(NA)[:, None]; f1 = np.arange(NA)[None, :]
    th = 2 * np.pi * a * f1 / NA
    c["f1cs"] = np.concatenate([np.cos(th), -np.sin(th)], 1).astype(NPBF)
    p = np.arange(128)[:, None]; f2 = np.arange(128)[None, :]
    th2 = 2 * np.pi * p * f2 / 128
    C = np.cos(th2); S = np.sin(th2)
    c["fC"] = C.astype(NPBF); c["fS"] = S.astype(NPBF); c["fnS"] = (-S).astype(NPBF)
    c["fCS"] = np.concatenate([C, S], 1).astype(NPBF)
    c["fnSC"] = np.concatenate([-S, C], 1).astype(NPBF)
    tw = 2 * np.pi * np.arange(128)[:, None] * np.arange(NA)[None, :] / N
    G = 512 // (2 * NA)
    tc_ = np.cos(tw); ts_ = np.sin(tw)
    A = np.concatenate([tc_, tc_], 1)
    B = np.concatenate([ts_, -ts_], 1)
    c["twA"] = np.tile(A[:, None, :], (1, G, 1)).reshape(128, 512).astype(np.float32)
    c["twB"] = np.tile(B[:, None, :], (1, G, 1)).reshape(128, 512).astype(np.float32)
    tcT = np.cos(tw).T; tsT = np.sin(tw).T
    A2 = np.stack([tcT, tcT], 1)
    B2 = np.stack([tsT, -tsT], 1)
    c["twA2"] = np.tile(A2[:, None], (1, 4, 1, 1)).reshape(NA, 1024).astype(np.float32)
    c["twB2"] = np.tile(B2[:, None], (1, 4, 1, 1)).reshape(NA, 1024).astype(np.float32)
    th = 2 * np.pi * np.arange(NA)[:, None] * np.arange(NA)[None, :] / NA
    c["g3C"] = (np.cos(th) / N).astype(NPBF); c["g3nS"] = (-np.sin(th) / N).astype(NPBF)
    if NA == 4:
        f1i = np.arange(128) % 4
        twp = 2 * np.pi * f1i[:, None] * np.arange(128)[None, :] / N
        c["twA2c"] = np.concatenate([np.cos(twp), np.cos(twp)], 1).astype(np.float32)
        c["twB2c"] = np.concatenate([np.sin(twp), -np.sin(twp)], 1).astype(np.float32)
        gC = np.zeros((128, 64), np.float64); gS = np.zeros((128, 64), np.float64)
        for ch in range(32):
            for f1_ in range(4):
                for a_ in range(2):
                    gC[ch * 4 + f1_, ch * 2 + a_] = np.cos(2 * np.pi * f1_ * a_ / 4) / N
                    gS[ch * 4 + f1_, ch * 2 + a_] = -np.sin(2 * np.pi * f1_ * a_ / 4) / N
        c["gbC"] = gC.astype(NPBF); c["gbnS"] = gS.astype(NPBF)
    return c


def _filter_consts(n):
    q = np.arange(2 * n)
    tap = np.where(q < n, q, 2 * n - q).astype(np.int64)
    tap = np.minimum(tap, n - 1)
    t01 = np.linspace(0.0, 1.0, n, dtype=np.float32)
    bands = 16
    w = (2.0 * np.pi * np.arange(n, dtype=np.float32) / n).astype(np.float32)
    f = np.linspace(1e-4, bands - 1, bands, dtype=np.float32)[None, :]
    feats = np.concatenate([t01[:, None], np.cos(f * w[:, None]), -np.sin(f * w[:, None])], -1).astype(np.float32)
    featsT = np.ascontiguousarray(feats[tap].T)
    t01b = np.ascontiguousarray(np.tile(t01[tap][None, :], (128, 1)))
    deltas = np.abs(np.linspace(np.log(1e-2) / 1.5, np.log(1e-2) / 0.3, 512, dtype=np.float32))
    ndel = np.ascontiguousarray((-deltas).reshape(4, 128).T)
    return featsT.astype(np.float32), t01b.astype(np.float32), ndel.astype(np.float32)


def _pm(v, n):
    return np.ascontiguousarray(np.asarray(v, np.float32).reshape(n, 128).T)


def _host_prep(inp, b, hh, C):
    fl = (hh == 1)
    m = {}
    x = inp["x"][b]; cx = inp["ctx"][b]
    if fl: x = x[::-1]; cx = cx[::-1]
    m["xt0"] = np.ascontiguousarray(np.concatenate([x, cx], 0).T)
    cv = np.stack([inp["c"][b], inp["c_ctx"]], 1)
    m["cvec"] = np.ascontiguousarray(cv.reshape(8, 128, 2).transpose(1, 0, 2))
    perm = C["perm64"]
    for i in range(2):
        m[f"ada_w{i}"] = inp["ada_w"][i]
        m[f"ada_b{i}"] = _pm(inp["ada_b"][i], 48)
        m[f"nmix{i}"] = _pm(inp["norm_mix"][i], 8)
        m[f"nffn{i}"] = _pm(inp["norm_ffn"][i], 8)
        w = inp["mix_w_in"][i]
        qk = w[:, :768].reshape(D, 12, 64)[:, :, perm].reshape(D, 768)
        m[f"w_in{i}"] = np.ascontiguousarray(np.concatenate([w, qk], 1))
        m[f"w_out{i}"] = inp["mix_w_out"][i]
        gq = inp["attn_q_norm"][i]; gk = inp["attn_k_norm"][i]
        m[f"qkg{i}"] = np.ascontiguousarray(np.stack([np.tile(gq, 2), np.tile(gq[perm], 2), np.tile(gk, 2), np.tile(gk[perm], 2)], 1).astype(np.float32))
        m[f"w_up{i}"] = inp["ffn_w_up"][i]
        m[f"w_dn{i}"] = inp["ffn_w_down"][i]
        fw = inp["ffn_conv_w"][i]
        if fl: fw = fw[::-1]
        m[f"fcw{i}"] = np.ascontiguousarray(fw.reshape(3, NM, 128).transpose(2, 1, 0))
        m[f"fcb{i}"] = _pm(inp["ffn_conv_b"][i], NM)
    hw = inp["hy_conv_w"][0]
    if fl: hw = hw[::-1]
    m["hcw"] = np.ascontiguousarray(hw.reshape(3, 12, 128).transpose(2, 1, 0))
    m["hcb"] = _pm(inp["hy_conv_b"][0], 12)
    sw = inp["sc_conv_w"][0]
    if fl: sw = sw[::-1]
    m["scw"] = np.ascontiguousarray(sw.reshape(3, 4, 128).transpose(2, 1, 0))
    m["hw1"] = inp["hy_w1"][0]; m["hw2"] = inp["hy_w2"][0]; m["hw3"] = inp["hy_w3"][0]
    w4 = inp["hy_w4"][0]
    if fl: w4 = np.concatenate([w4[:, 512:], w4[:, :512]], 1)
    m["hw4"] = np.ascontiguousarray(w4)
    m["hb"] = np.ascontiguousarray(np.stack([inp["hy_b1"][0], inp["hy_b2"][0], inp["hy_b3"][0], inp["hy_freq"][0]], 1).astype(np.float32))
    m["hbd"] = _pm(inp["hy_bias_d"][0], 4)
    m["sink"] = np.ascontiguousarray(np.tile(inp["swa_sink"][0][None, :], (128, 1)).astype(np.float32))
    cos = C["cos64"]; sin = C["sin64"]
    if fl: cos = cos[:, ::-1]; sin = sin[:, ::-1]
    cosx = np.concatenate([cos, np.ones((64, CTX), np.float32)], 1)
    sinx = np.concatenate([sin, np.zeros((64, CTX), np.float32)], 1)
    m["cosT"] = np.ascontiguousarray(np.concatenate([cosx, cosx], 0))
    m["sinT"] = np.ascontiguousarray(np.concatenate([sinx, sinx], 0))
    m["bmask"] = C["bmask"]
    for tag, NA in (("L", 128), ("C", 4)):
        for kk, v in C["fft" + tag].items():
            m[kk + tag] = v
    for tag in ("L", "C"):
        ft, t01b, ndel = C["filt" + tag]
        m["featsT" + tag] = ft; m["t01b" + tag] = t01b
    m["ndel"] = C["filtL"][2]
    return m

def _build(stop_after=None, dbg=()):
    nc = bass.Bass("TRN2", target_bir_lowering=False)
    k = K(nc)
    def din(name, shape, dt=F32):
        return nc.dram_tensor(name, list(shape), dt, kind="ExternalInput").ap()
    def dscr(name, shape, dt):
        kind = "ExternalOutput" if name in dbg else "Internal"
        return nc.dram_tensor(name, list(shape), dt, kind=kind).ap()
    I = {}
    I["xt0"] = din("xt0", [D, TT]); I["cvec"] = din("cvec", [128, 8, 2])
    for i in range(2):
        I[f"ada_w{i}"] = din(f"ada_w{i}", [D, 6 * D]); I[f"ada_b{i}"] = din(f"ada_b{i}", [128, 48])
        I[f"nmix{i}"] = din(f"nmix{i}", [128, 8]); I[f"nffn{i}"] = din(f"nffn{i}", [128, 8])
        I[f"w_in{i}"] = din(f"w_in{i}", [D, 3328]); I[f"w_out{i}"] = din(f"w_out{i}", [D, D])
        I[f"qkg{i}"] = din(f"qkg{i}", [128, 4])
        I[f"w_up{i}"] = din(f"w_up{i}", [D, 2 * DFF]); I[f"w_dn{i}"] = din(f"w_dn{i}", [DFF, D])
        I[f"fcw{i}"] = din(f"fcw{i}", [128, NM, 3]); I[f"fcb{i}"] = din(f"fcb{i}", [128, NM])
    I["hcw"] = din("hcw", [128, 12, 3]); I["hcb"] = din("hcb", [128, 12]); I["scw"] = din("scw", [128, 4, 3])
    I["hw1"] = din("hw1", [33, 64]); I["hw2"] = din("hw2", [64, 64]); I["hw3"] = din("hw3", [64, 64])
    I["hw4"] = din("hw4", [64, 1024]); I["hb"] = din("hb", [64, 4]); I["hbd"] = din("hbd", [128, 4])
    I["sink"] = din("sink", [128, 8])
    I["cosT"] = din("cosT", [128, TT]); I["sinT"] = din("sinT", [128, TT])
    I["bmask"] = din("bmask", [6, 128, 512], BF16)
    for tag, NA in (("L", 128), ("C", 4)):
        I["f1cs" + tag] = din("f1cs" + tag, [NA, 2 * NA], BF16)
        for nm in ("fC", "fS", "fnS"): I[nm + tag] = din(nm + tag, [128, 128], BF16)
        for nm in ("fCS", "fnSC"): I[nm + tag] = din(nm + tag, [128, 256], BF16)
        for nm in ("twA", "twB"): I[nm + tag] = din(nm + tag, [128, 512])
        for nm in ("twA2", "twB2"): I[nm + tag] = din(nm + tag, [NA, 1024])
        for nm in ("g3C", "g3nS"): I[nm + tag] = din(nm + tag, [NA, NA], BF16)
        if NA == 4:
            for nm in ("twA2c", "twB2c"): I[nm + tag] = din(nm + tag, [128, 256])
            for nm in ("gbC", "gbnS"): I[nm + tag] = din(nm + tag, [128, 64], BF16)
        n = 64 * NA
        I["featsT" + tag] = din("featsT" + tag, [33, 2 * n]); I["t01b" + tag] = din("t01b" + tag, [128, 2 * n])
    I["ndel"] = din("ndel", [128, 4])
    OUT = nc.dram_tensor("out", [D, OWN], F32, kind="ExternalOutput").ap()

    S = {}
    for i in range(2):
        S[f"bw_in{i}"] = dscr(f"bw_in{i}", [D, 3328], BF16); S[f"bw_out{i}"] = dscr(f"bw_out{i}", [D, D], BF16)
        S[f"bw_up{i}"] = dscr(f"bw_up{i}", [D, 2 * DFF], BF16); S[f"bw_dn{i}"] = dscr(f"bw_dn{i}", [DFF, D], BF16)
    S["QT"] = dscr("QT", [8, 64, TT], BF16); S["KT"] = dscr("KT", [4, 64, TT], BF16); S["V"] = dscr("V", [TT, 256], BF16)
    S["UT"] = dscr("UT", [512, TT], BF16); S["X0T"] = dscr("X0T", [512, TT], F32)
    S["UF"] = dscr("UF", [512, TT], F32)
    S["YT"] = dscr("YT", [512, TT], F32)
    S["OAT"] = dscr("OAT", [512, TT], BF16); S["OCT"] = dscr("OCT", [512, TT], BF16)
    S["XM"] = dscr("XM", [D, TT], F32); S["X1"] = dscr("X1", [D, TT], F32); S["XM1"] = dscr("XM1", [D, TT], F32)
    S["KRAWL"] = dscr("KRAWL", [512, 2 * SEQ], F32); S["KNL"] = dscr("KNL", [512, 2 * SEQ], BF16)
    S["KRAWC"] = dscr("KRAWC", [512, 2 * CTX], F32); S["KNC"] = dscr("KNC", [512, 2 * CTX], BF16)

    MOD = k.sbuf([128, 2, 48, 2], F32, "mod", persist=True)
    AB = k.sbuf([128, 2, 2, 2, 8, 2], F32, "ab", persist=True)
    ONES = k.sbuf([128, 128], BF16, "ones", persist=True)
    BONES = k.sbuf([128, 128], BF16, "bones", persist=True)
    SEL = k.sbuf([128, 64], F32, "sel", persist=True)
    ESINK = k.sbuf([128, 8], F32, "esink", persist=True)

    phases = []

    conv_items = []
    for i in range(2):
        for src, dst, rows, cols in ((f"w_in{i}", f"bw_in{i}", D, 3328), (f"w_out{i}", f"bw_out{i}", D, D),
                                     (f"w_up{i}", f"bw_up{i}", D, 2 * DFF), (f"w_dn{i}", f"bw_dn{i}", DFF, D)):
            for r0 in range(0, rows, 128):
                for c0 in range(0, cols, 2048):
                    conv_items.append((src, dst, r0, c0, min(2048, cols - c0)))
    CONV_EARLY = sum(1 for it in conv_items if it[0] in ("w_in0", "w_out0"))
    conv_pos = [0]

    class make_converter:
        def __init__(self):
            self.stg = [k.sbuf([128, 2048], F32, "stg") for _ in range(3)]
            self.stb = [k.sbuf([128, 2048], BF16, "stb") for _ in range(3)]
        def step(self, eng=None, q="sp"):
            n = conv_pos[0]
            if n >= len(conv_items): return False
            conv_pos[0] += 1
            src, dst, r0, c0, cw = conv_items[n]
            a = self.stg[n % 3]; bt = self.stb[n % 3]
            k.dma(q, a[:, 0:cw], I[src][r0:r0 + 128, c0:c0 + cw], writes=[a])
            k.copy(eng or ("dve" if n % 2 == 0 else "act"), bt[:, 0:cw], a[:, 0:cw], [a], [bt])
            k.dma(q, S[dst][r0:r0 + 128, c0:c0 + cw], bt[:, 0:cw], reads=[bt])
            return True

    def ph_setup():
        k.memset("dve", ONES[:], 1.0, [ONES])
        k.memset("dve", BONES[:], 0.0, [BONES])
        k.memset("dve", BONES[0:64, 0:64], 1.0, [BONES])
        k.memset("dve", BONES[64:128, 64:128], 1.0, [BONES])
        k.memset("dve", SEL[:], 0.0, [SEL])
        k.memset("dve", SEL[64:65, :], 1.0, [SEL])
        snk = k.sbuf([128, 8], F32)
        k.dma("sp", snk[:], I["sink"][:, :], writes=[snk])
        k.act(ESINK[:], snk[:], AF.Exp, [snk], [ESINK])
        cv_ = make_converter()
        for _ in range(CONV_EARLY):
            cv_.step()
        cv = k.sbuf([128, 8, 2], F32)
        k.dma("sp", cv[:], I["cvec"][:, :, :], writes=[cv])
        sc = k.sbuf([128, 8, 2], F32)
        k.act(sc[:], cv[:], AF.Silu, [cv], [sc])
        wst = [k.sbuf([128, 8, 512], F32, "wst") for _ in range(2)]
        ps = [k.psum([128, 512], F32) for _ in range(2)]
        n = 0
        for i in range(2):
            adb = k.sbuf([128, 48], F32)
            k.dma("sp", adb[:], I[f"ada_b{i}"][:, :], writes=[adb])
            for cb in range(12):
                wt = wst[n % 2]; n += 1
                k.dma("sp", wt[:], I[f"ada_w{i}"][:, cb * 512:(cb + 1) * 512].rearrange("(k p) f -> p k f", p=128), writes=[wt])
                for mi in range(4):
                    m = cb * 4 + mi
                    pt = ps[m % 2]
                    for kk in range(8):
                        k.mm(pt[:, 0:2], wt[:, kk, mi * 128:(mi + 1) * 128], sc[:, kk, :], kk == 0, kk == 7, [wt, sc], [pt])
                    k.ts("dve", MOD[:, i, m, :], pt[:, 0:2], adb[:, m:m + 1], None, ALU.add, None, [pt, adb], [MOD])
            for wh, (nm, sh0, sc0) in enumerate(((f"nmix{i}", 0, 8), (f"nffn{i}", 24, 32))):
                g = k.sbuf([128, 8], F32)
                k.dma("sp", g[:], I[nm][:, :], writes=[g])
                for j in range(2):
                    k.stt("dve", AB[:, i, wh, 0, :, j], MOD[:, i, sc0:sc0 + 8, j], 1.0, g[:], ALU.add, ALU.mult, [MOD, g], [AB])
                    k.copy("dve", AB[:, i, wh, 1, :, j], MOD[:, i, sh0:sh0 + 8, j], [MOD], [AB])
    phases.append(("setup", ph_setup))

    def norm_mod(xh, Wc, i, wh, j, sq, ssps, rstd, tmp, h):
        for kk in range(8):
            k.act(sq[:, kk, 0:Wc], xh[:, kk, 0:Wc], AF.Square, [xh], [sq])
        for (c0, c1) in ((0, min(512, Wc)), (512, Wc)):
            if c1 <= c0: continue
            for kk in range(8):
                k.mm(ssps[:, c0:c1], ONES[:, :], sq[:, kk, c0:c1], kk == 0, kk == 7, [ONES, sq], [ssps])
        k.act(rstd[:, 0:Wc], ssps[:, 0:Wc], AF.Sqrt, [ssps], [rstd], bias=1e-6, scale=1.0 / D)
        k.recip(rstd[:, 0:Wc], rstd[:, 0:Wc], [rstd], [rstd])
        for kk in range(8):
            t = tmp[kk % 2]
            k.stt("dve", t[:, 0:Wc], xh[:, kk, 0:Wc], AB[:, i, wh, 0, kk, j:j + 1], rstd[:, 0:Wc], ALU.mult, ALU.mult, [xh, AB, rstd], [t])
            k.act(h[:, kk, 0:Wc], t[:, 0:Wc], AF.Identity, [t, AB], [h], bias=AB[:, i, wh, 1, kk, j:j + 1], scale=1.0)

    CHUNKS_ALL = [(c * 512, 512, c == 0, c == 15, 0) for c in range(16)] + [(SEQ, CTX, True, True, 1)]
    CHUNKS_E = [(c * 512, 512, c == 0, False, 0) for c in range(9)]
    CTXCH = (SEQ, CTX, True, True, 1)

    def load_xh(xh, src, t0, W, ledge, redge, q="sp"):
        lo = 2 if ledge else 1
        hi = W + 2 if redge else W + 3
        if ledge: k.memset("pool", xh[:, :, 1:2], 0.0, [xh])
        if redge: k.memset("pool", xh[:, :, W + 2:W + 3], 0.0, [xh])
        k.dma(q, xh[:, :, lo:hi], src[:, t0 - 2 + lo:t0 - 2 + hi].rearrange("(k p) t -> p k t", p=128), writes=[xh])

    def make_inproj(i, XIN, chunks):
        def ph():
            w = k.sbuf([128, 8, 3328], BF16, "w_in")
            for kk in range(8):
                k.dma("sp", w[:, kk, :], S[f"bw_in{i}"][kk * 128:(kk + 1) * 128, :], writes=[w])
            qkg = k.sbuf([128, 4], F32); k.dma("sp", qkg[:], I[f"qkg{i}"][:, :], writes=[qkg])
            if i == 0:
                cw = k.sbuf([128, 12, 3], F32); k.dma("sp", cw[:], I["hcw"][:, :, :], writes=[cw])
                cb = k.sbuf([128, 12], F32); k.dma("sp", cb[:], I["hcb"][:, :], writes=[cb])
            else:
                cw = k.sbuf([128, 4, 3], F32); k.dma("sp", cw[:], I["scw"][:, :, :], writes=[cw])
            xhs = [k.sbuf([128, 8, 516], F32, "xh") for _ in range(2)]
            for t_ in xhs: k.memset("pool", t_[:], 0.0, [t_])
            sq = k.sbuf([128, 8, 516], BF16, "sq")
            h = k.sbuf([128, 8, 516], BF16, "h")
            rstd = k.sbuf([128, 516], F32, "rstd")
            tmp = [k.sbuf([128, 516], F32, "tmp") for _ in range(2)]
            ssps = k.psum([128, 1024], F32, "ssps")
            zps = [k.psum([128, 512], F32, "zps") for _ in range(4)]
            cps = k.psum([128, 1024], F32, "cps")
            cosb = k.sbuf([128, 512], F32, "cos"); sinb = k.sbuf([128, 512], F32, "sin")
            sq2 = [k.sbuf([128, 512], BF16, "sq2") for _ in range(2)]
            rs = [k.sbuf([128, 512], F32, "rs") for _ in range(2)]
            ta = [k.sbuf([128, 512], F32, "ta") for _ in range(2)]
            tb = [k.sbuf([128, 512], F32, "tb") for _ in range(2)]
            qo = [k.sbuf([128, 512], BF16, "qo") for _ in range(2)]
            vo = [k.sbuf([128, 256], BF16, "vo") for _ in range(2)]
            asb = [k.sbuf([128, 516], F32, "asb") for _ in range(3)]
            c1 = [k.sbuf([128, 512], F32, "c1") for _ in range(3)]
            uo = [k.sbuf([128, 512], F32, "uo") for _ in range(2)]
            ub = [k.sbuf([128, 512], BF16, "ub") for _ in range(2)]
            pp = [k.sbuf([128, 516], F32, "pp") for _ in range(2)]
            nq = [0]
            import os
            PARTS = os.environ.get("INPROJ_PARTS", "nqvc")
            NCHK = int(os.environ.get("INPROJ_NCH", "99"))
            for ci, (t0, W, le, re, j) in enumerate(chunks[:NCHK]):
                Wc = W + 4
                do_q = (t0 < E) or j == 1
                if i == 1 and j == 1: do_q = False
                xh = xhs[ci % 2]
                load_xh(xh, XIN, t0, W, le, re)
                norm_mod(xh, Wc, i, 0, j, sq, ssps, rstd, tmp, h)
                k.dma("sp", cosb[:, 0:W], I["cosT"][:, t0:t0 + W], writes=[cosb])
                k.dma("sp", sinb[:, 0:W], I["sinT"][:, t0:t0 + W], writes=[sinb])
                for pr in range(6):
                    if "q" not in PARTS: continue
                    if pr < 4 and not do_q: continue
                    n = nq[0]; nq[0] += 1
                    zp = zps[(2 * n) % 4]; zsp = zps[(2 * n + 1) % 4]
                    for kk in range(8):
                        k.mm(zp[:, 0:W], w[:, kk, pr * 128:(pr + 1) * 128], h[:, kk, 2:W + 2], kk == 0, kk == 7, [w, h], [zp])
                    for kk in range(8):
                        k.mm(zsp[:, 0:W], w[:, kk, 2560 + pr * 128:2560 + (pr + 1) * 128], h[:, kk, 2:W + 2], kk == 0, kk == 7, [w, h], [zsp])
                    s2 = sq2[n % 2]; r_ = rs[n % 2]; a_ = ta[n % 2]; b_ = tb[n % 2]; q_ = qo[n % 2]
                    gi = 0 if pr < 4 else 2
                    k.act(s2[:, 0:W], zp[:, 0:W], AF.Square, [zp], [s2])
                    k.stt("dve", a_[:, 0:W], zp[:, 0:W], qkg[:, gi:gi + 1], cosb[:, 0:W], ALU.mult, ALU.mult, [zp, qkg, cosb], [a_])
                    k.stt("dve", b_[:, 0:W], zsp[:, 0:W], qkg[:, gi + 1:gi + 2], sinb[:, 0:W], ALU.mult, ALU.mult, [zsp, qkg, sinb], [b_])
                    k.mm(zp[:, 0:W], BONES[:, :], s2[:, 0:W], True, True, [BONES, s2], [zp])
                    k.act(r_[:, 0:W], zp[:, 0:W], AF.Sqrt, [zp], [r_], bias=1e-6, scale=1.0 / 64)
                    k.recip(r_[:, 0:W], r_[:, 0:W], [r_], [r_])
                    QS = os.environ.get("QSKIP", "")
                    pe_ = "dve" if "pool" in QS else "pool"
                    k.tt(pe_, a_[:, 0:W], a_[:, 0:W], b_[:, 0:W], ALU.add, [a_, b_], [a_])
                    k.tt(pe_, q_[:, 0:W], a_[:, 0:W], r_[:, 0:W], ALU.mult, [a_, r_], [q_])
                    for hf in range(2):
                        if "dma" in QS: continue
                        if pr < 4:
                            dst = S["QT"][2 * pr + hf, :, t0:t0 + W]
                        else:
                            dst = S["KT"][2 * (pr - 4) + hf, :, t0:t0 + W]
                        k.dma("pool", dst, q_[hf * 64:(hf + 1) * 64, 0:W], reads=[q_])
                for tj in range(W // 128):
                    if "v" not in PARTS: continue
                    n = nq[0]; nq[0] += 1
                    vp = zps[n % 4]
                    for kk in range(8):
                        k.mm(vp[:, 0:256], h[:, kk, 2 + tj * 128:2 + (tj + 1) * 128], w[:, kk, 768:1024], kk == 0, kk == 7, [w, h], [vp])
                    v_ = vo[n % 2]
                    k.copy("act", v_[:, :], vp[:, 0:256], [vp], [v_])
                    k.dma("pool", S["V"][t0 + tj * 128:t0 + (tj + 1) * 128, :], v_[:, :], reads=[v_])
                def convproj(m, dst):
                    for (c0, c1_) in ((0, min(512, Wc)), (512, Wc)):
                        if c1_ <= c0: continue
                        for kk in range(8):
                            k.mm(cps[:, c0:c1_], w[:, kk, 1024 + m * 128:1024 + (m + 1) * 128], h[:, kk, c0:c1_], kk == 0, kk == 7, [w, h], [cps])
                    k.copy("act", dst[:, 0:Wc], cps[:, 0:Wc], [cps], [dst])
                    if le: k.memset("pool", dst[:, 1:2], 0.0, [dst])
                    if re: k.memset("pool", dst[:, W + 2:W + 3], 0.0, [dst])
                def conv3(out, a, wts, m, bias, eng="dve"):
                    if bias is not None:
                        k.ts(eng, out[:, 0:W], a[:, 2:W + 2], wts[:, m, 1:2], bias, ALU.mult, ALU.add, [a, wts, cb], [out])
                    else:
                        k.ts(eng, out[:, 0:W], a[:, 2:W + 2], wts[:, m, 1:2], None, ALU.mult, None, [a, wts], [out])
                    k.stt(eng, out[:, 0:W], a[:, 1:W + 1], wts[:, m, 0:1], out[:, 0:W], ALU.mult, ALU.add, [a, wts, out], [out])
                    k.stt(eng, out[:, 0:W], a[:, 3:W + 3], wts[:, m, 2:3], out[:, 0:W], ALU.mult, ALU.add, [a, wts, out], [out])
                for jc in range(4):
                    if "c" not in PARTS: continue
                    if i == 0:
                        convproj(4 + jc, asb[0]); conv3(c1[0], asb[0], cw, 4 + jc, cb[:, 4 + jc:5 + jc])
                        convproj(8 + jc, asb[1]); conv3(c1[1], asb[1], cw, 8 + jc, cb[:, 8 + jc:9 + jc])
                        u_ = uo[jc % 2]; ub_ = ub[jc % 2]
                        k.tt("dve", u_[:, 0:W], c1[0][:, 0:W], c1[1][:, 0:W], ALU.mult, [c1[0], c1[1]], [u_])
                        k.copy("pool", ub_[:, 0:W], u_[:, 0:W], [u_], [ub_])
                        k.dma("pool", S["UF"][jc * 128:(jc + 1) * 128, t0:t0 + W], u_[:, 0:W], reads=[u_])
                        k.dma("pool", S["UT"][jc * 128:(jc + 1) * 128, t0:t0 + W], ub_[:, 0:W], reads=[ub_])
                        if do_q:
                            convproj(jc, asb[2]); conv3(c1[2], asb[2], cw, jc, cb[:, jc:jc + 1])
                            k.dma("pool", S["X0T"][jc * 128:(jc + 1) * 128, t0:t0 + W], c1[2][:, 0:W], reads=[c1[2]])
                    elif j == 0:
                        convproj(4 + jc, asb[0]); convproj(8 + jc, asb[1])
                        p_ = pp[jc % 2]
                        k.tt("dve", p_[:, 0:Wc], asb[0][:, 0:Wc], asb[1][:, 0:Wc], ALU.mult, [asb[0], asb[1]], [p_])
                        conv3(c1[0], p_, cw, jc, None)
                        convproj(jc, asb[2])
                        ub_ = ub[jc % 2]
                        k.tt("pool", ub_[:, 0:W], c1[0][:, 0:W], asb[2][:, 2:W + 2], ALU.mult, [c1[0], asb[2]], [ub_])
                        k.dma("pool", S["OCT"][jc * 128:(jc + 1) * 128, t0:t0 + W], ub_[:, 0:W], reads=[ub_])
        return ph

    def make_attn(i):
        def ph():
            NKT = TT // 128
            kT = k.sbuf([128, TT], BF16, "kT")
            k.memset("pool", kT[64:128, :], 0.0, [kT])
            va = k.sbuf([128, NKT, 65], BF16, "va")
            qT = [k.sbuf([128, 512], BF16, "qT") for _ in range(2)]
            for t_ in qT: k.memset("pool", t_[64:128, :], 0.0, [t_])
            sps = [k.psum([128, 512], F32, "sps") for _ in range(4)]
            ops_ = [k.psum([128, 512], F32, "ops") for _ in range(2)]
            bps = k.psum([128, 512], F32, "bps")
            pT = [k.sbuf([128, 512], BF16, "pT") for _ in range(4)]
            osb = [k.sbuf([65, 512], F32, "osb") for _ in range(2)]
            rb = [k.sbuf([64, 512], F32, "rb") for _ in range(2)]
            ob = [k.sbuf([64, 512], BF16, "ob") for _ in range(2)]
            if i == 1:
                bm = k.sbuf([128, 6, 512], BF16, "bm")
                for r in range(6):
                    k.dma("sp", bm[:, r, :], I["bmask"][r, :, :], writes=[bm])
            qch = []
            for c in range(9):
                t0 = c * 512
                if i == 0:
                    kts = [(kt, 0, 512, None) for kt in range(NKT)]
                else:
                    kts = []
                    for r in range(-1, 5):
                        kt = 4 * c + r
                        if kt < 0 or kt >= E // 128: continue
                        f0 = max(0, 128 * (r - 1)); f1 = min(512, 128 * (r + 2))
                        kts.append((kt, f0, f1, r + 1))
                    kts += [(64, 0, 512, None), (65, 0, 512, None)]
                qch.append((t0, 512, kts))
            if i == 0:
                qch.append((SEQ, CTX, [(64, 0, CTX, None), (65, 0, CTX, None)]))
            n = 0; nh = 0
            cvt = make_converter() if i == 0 else None
            for jkv in range(4):
                k.dma("sp", kT[0:64, :], S["KT"][jkv, :, :], writes=[kT])
                k.dma("sp", va[:, :, 0:64], S["V"][:, jkv * 64:(jkv + 1) * 64].rearrange("(n p) d -> p n d", p=128), writes=[va])
                k.memset("pool", va[:, :, 64:65], 1.0, [va])
                for g in range(2):
                    hq = 2 * jkv + g
                    for (t0, W, kts) in qch:
                        q_ = qT[nh % 2]; op_ = ops_[nh % 2]; o_ = osb[nh % 2]; r_ = rb[nh % 2]; b_ = ob[nh % 2]
                        nh += 1
                        k.dma("sp", q_[0:64, 0:W], S["QT"][hq, :, t0:t0 + W], writes=[q_])
                        if i == 1:
                            pass
                        nk = len(kts)
                        def smm(idx):
                            kt, f0, f1, mi = kts[idx]
                            sp_ = sps[(n + idx) % 4]
                            k.mm(sp_[:, f0:f1], kT[:, kt * 128:(kt + 1) * 128], q_[:, f0:f1], True, True, [kT, q_], [sp_])
                        smm(0)
                        if nk > 1: smm(1)
                        for idx in range(nk):
                            if idx + 2 < nk: smm(idx + 2)
                            kt, f0, f1, mi = kts[idx]
                            sp_ = sps[(n + idx) % 4]; p_ = pT[(n + idx) % 4]
                            if mi is not None and (f0 > 0 or f1 < W):
                                k.memset("pool", p_[:, 0:W], 0.0, [p_])
                            k.act(p_[:, f0:f1], sp_[:, f0:f1], AF.Exp, [sp_], [p_], scale=0.125)
                            if mi is not None:
                                k.tt("pool", p_[:, f0:f1], p_[:, f0:f1], bm[:, mi, f0:f1], ALU.mult, [p_, bm], [p_])
                            k.mm(op_[0:65, 0:W], va[:, kt, :], p_[:, 0:W], idx == 0, idx == nk - 1, [va, p_], [op_])
                        n += nk
                        k.copy("act", o_[:, 0:W], op_[0:65, 0:W], [op_], [o_])
                        k.mm(bps[0:64, 0:W], SEL[0:65, :], o_[0:65, 0:W], True, True, [SEL, o_], [bps])
                        if i == 1:
                            k.ts("dve", r_[:, 0:W], bps[0:64, 0:W], ESINK[0:64, hq:hq + 1], None, ALU.add, None, [bps, ESINK], [r_])
                            k.recip(r_[:, 0:W], r_[:, 0:W], [r_], [r_])
                        else:
                            k.recip(r_[:, 0:W], bps[0:64, 0:W], [bps], [r_])
                        k.tt("dve", b_[:, 0:W], o_[0:64, 0:W], r_[:, 0:W], ALU.mult, [o_, r_], [b_])
                        k.dma("pool", S["OAT"][hq * 64:(hq + 1) * 64, t0:t0 + W], b_[:, 0:W], reads=[b_])
                        if cvt is not None:
                            cvt.step("dve", "pool"); cvt.step("dve", "pool")
            if cvt is not None:
                while cvt.step("dve", "pool"): pass
        return ph

    def make_outproj(i, XIN, XOUT, chunks):
        def ph():
            wa = k.sbuf([128, 4, D], BF16, "woa")
            wc = k.sbuf([128, 4, D], BF16, "woc")
            k.dma("sp", wa[:], S[f"bw_out{i}"][0:512, :].rearrange("(c p) f -> p c f", p=128), writes=[wa])
            k.dma("sp", wc[:], S[f"bw_out{i}"][512:1024, :].rearrange("(c p) f -> p c f", p=128), writes=[wc])
            oa = [k.sbuf([128, 4, 512], BF16, "oa") for _ in range(2)]
            oc = [k.sbuf([128, 4, 512], BF16, "oc") for _ in range(2)]
            xs = [k.sbuf([128, 8, 512], F32, "xs") for _ in range(2)]
            xo = [k.sbuf([128, 8, 512], F32, "xo") for _ in range(2)]
            ps = [k.psum([128, 512], F32, "ps") for _ in range(4)]
            n = 0
            for ci, (t0, W, le, re, j) in enumerate(chunks):
                a_ = oa[ci % 2]; c_ = oc[ci % 2]; x_ = xs[ci % 2]; o_ = xo[ci % 2]
                k.dma("sp", a_[:, :, 0:W], S["OAT"][:, t0:t0 + W].rearrange("(c p) t -> p c t", p=128), writes=[a_])
                k.dma("sp", c_[:, :, 0:W], S["OCT"][:, t0:t0 + W].rearrange("(c p) t -> p c t", p=128), writes=[c_])
                k.dma("sp", x_[:, :, 0:W], XIN[:, t0:t0 + W].rearrange("(k p) t -> p k t", p=128), writes=[x_])
                for m in range(8):
                    p_ = ps[n % 4]; n += 1
                    for hh_ in range(4):
                        k.mm(p_[:, 0:W], wa[:, hh_, m * 128:(m + 1) * 128], a_[:, hh_, 0:W], hh_ == 0, False, [wa, a_], [p_])
                    for cc in range(4):
                        k.mm(p_[:, 0:W], wc[:, cc, m * 128:(m + 1) * 128], c_[:, cc, 0:W], False, cc == 3, [wc, c_], [p_])
                    k.stt("dve", o_[:, m, 0:W], p_[:, 0:W], MOD[:, i, 16 + m, j:j + 1], x_[:, m, 0:W], ALU.mult, ALU.add, [p_, MOD, x_], [o_])
                k.dma("pool", XOUT[:, t0:t0 + W].rearrange("(k p) t -> p k t", p=128), o_[:, :, 0:W], reads=[o_])
        return ph

    def make_ffn(i, XIN, XOUT, chunks, final=False):
        def ph():
            wu = k.sbuf([128, 8, 2 * DFF], BF16, "wu")
            for kk in range(8):
                k.dma("sp", wu[:, kk, :], S[f"bw_up{i}"][kk * 128:(kk + 1) * 128, :], writes=[wu])
            wd = [k.sbuf([128, NM, 128], BF16, "wd") for _ in range(2)]
            cw = k.sbuf([128, NM, 3], F32); k.dma("sp", cw[:], I[f"fcw{i}"][:, :, :], writes=[cw])
            cb = k.sbuf([128, NM], F32); k.dma("sp", cb[:], I[f"fcb{i}"][:, :], writes=[cb])
            xh = k.sbuf([128, 8, 516], F32, "xh")
            k.memset("pool", xh[:], 0.0, [xh])
            sq = k.sbuf([128, 8, 516], BF16, "sq")
            h = k.sbuf([128, 8, 516], BF16, "h")
            rstd = k.sbuf([128, 516], F32, "rstd")
            tmp = [k.sbuf([128, 516], F32, "tmp") for _ in range(2)]
            gg = k.sbuf([128, NM, 512], BF16, "gg")
            asb = [k.sbuf([128, 516], F32, "asb") for _ in range(2)]
            c1 = [k.sbuf([128, 512], F32, "c1") for _ in range(2)]
            xo = [k.sbuf([128, 512], F32, "xo") for _ in range(2)]
            ssps = k.psum([128, 1024], F32, "ssps")
            aps = [k.psum([128, 1024], F32, "aps") for _ in range(2)]
            vps = [k.psum([128, 512], F32, "vps") for _ in range(2)]
            n = 0; nd = 0
            for ci, (t0, W, le, re, j) in enumerate(chunks):
                Wc = W + 4
                load_xh(xh, XIN, t0, W, le, re)
                norm_mod(xh, Wc, i, 1, j, sq, ssps, rstd, tmp, h)
                for m in range(NM):
                    ap_ = aps[n % 2]; vp_ = vps[n % 2]; a_ = asb[n % 2]; c_ = c1[n % 2]; n += 1
                    for (c0, c1_) in ((0, min(512, Wc)), (512, Wc)):
                        if c1_ <= c0: continue
                        for kk in range(8):
                            k.mm(ap_[:, c0:c1_], wu[:, kk, m * 128:(m + 1) * 128], h[:, kk, c0:c1_], kk == 0, kk == 7, [wu, h], [ap_])
                    for kk in range(8):
                        k.mm(vp_[:, 0:W], wu[:, kk, DFF + m * 128:DFF + (m + 1) * 128], h[:, kk, 2:W + 2], kk == 0, kk == 7, [wu, h], [vp_])
                    k.copy("act", a_[:, 0:Wc], ap_[:, 0:Wc], [ap_], [a_])
                    if le: k.memset("pool", a_[:, 1:2], 0.0, [a_])
                    if re: k.memset("pool", a_[:, W + 2:W + 3], 0.0, [a_])
                    eng = "dve"
                    k.ts(eng, c_[:, 0:W], a_[:, 2:W + 2], cw[:, m, 1:2], cb[:, m:m + 1], ALU.mult, ALU.add, [a_, cw, cb], [c_])
                    k.stt(eng, c_[:, 0:W], a_[:, 1:W + 1], cw[:, m, 0:1], c_[:, 0:W], ALU.mult, ALU.add, [a_, cw, c_], [c_])
                    k.stt(eng, c_[:, 0:W], a_[:, 3:W + 3], cw[:, m, 2:3], c_[:, 0:W], ALU.mult, ALU.add, [a_, cw, c_], [c_])
                    k.act(c_[:, 0:W], c_[:, 0:W], AF.Gelu_apprx_tanh, [c_], [c_])
                    k.tt("dve", gg[:, m, 0:W], c_[:, 0:W], vp_[:, 0:W], ALU.mult, [c_, vp_], [gg])
                for mo in range(8):
                    wd_ = wd[nd % 2]; o_ = xo[nd % 2]; p_ = vps[nd % 2]; nd += 1
                    k.dma("sp", wd_[:], S[f"bw_dn{i}"][:, mo * 128:(mo + 1) * 128].rearrange("(m p) f -> p m f", p=128), writes=[wd_])
                    for m in range(NM):
                        k.mm(p_[:, 0:W], wd_[:, m, :], gg[:, m, 0:W], m == 0, m == NM - 1, [wd_, gg], [p_])
                    k.stt("dve", o_[:, 0:W], p_[:, 0:W], MOD[:, i, 40 + mo, j:j + 1], xh[:, mo, 2:W + 2], ALU.mult, ALU.add, [p_, MOD, xh], [o_])
                    k.dma("pool", XOUT[mo * 128:(mo + 1) * 128, t0:t0 + W], o_[:, 0:W], reads=[o_])
        return ph

    def make_filter(tag, n):
        KRAW = S["KRAW" + tag]; KN = S["KN" + tag]
        def ph():
            w1 = k.sbuf([33, 64], F32); k.dma("sp", w1[:], I["hw1"][:, :], writes=[w1])
            w2 = k.sbuf([64, 64], F32); k.dma("sp", w2[:], I["hw2"][:, :], writes=[w2])
            w3 = k.sbuf([64, 64], F32); k.dma("sp", w3[:], I["hw3"][:, :], writes=[w3])
            w4 = k.sbuf([64, 1024], F32); k.dma("sp", w4[:], I["hw4"][:, :], writes=[w4])
            hb = k.sbuf([64, 4], F32); k.dma("sp", hb[:], I["hb"][:, :], writes=[hb])
            ndel = k.sbuf([128, 4], F32); k.dma("sp", ndel[:], I["ndel"][:, :], writes=[ndel])
            asum = k.sbuf([128, 4, 40], F32, "asum")
            k.memset("dve", asum[:], 0.0, [asum])
            ft = [k.sbuf([33, 512], F32, "ft") for _ in range(2)]
            t01 = [k.sbuf([128, 512], F32, "t01") for _ in range(2)]
            hid2 = [[k.sbuf([64, 512], F32, "hid") for _ in range(3)] for _ in range(2)]
            ki2 = [k.sbuf([64, 512], I32, "ki") for _ in range(2)]
            win = [k.sbuf([128, 512], F32, "win") for _ in range(2)]
            kr = [k.sbuf([128, 512], F32, "kr") for _ in range(2)]
            junk = k.sbuf([128, 512], F32, "junk")
            ps = [k.psum([128, 512], F32, "ps") for _ in range(4)]
            NCH = (2 * n) // 512
            n_ = 0
            for c in range(NCH):
                q0 = c * 512
                f_ = ft[c % 2]; t_ = t01[c % 2]
                k.dma("sp", f_[:, :], I["featsT" + tag][:, q0:q0 + 512], writes=[f_])
                k.dma("sp", t_[:, :], I["t01b" + tag][:, q0:q0 + 512], writes=[t_])
                src = f_; srcK = 33
                hid = hid2[c % 2]; ki = ki2[c % 2]
                for li, wl in enumerate((w1, w2, w3)):
                    p_ = ps[n_ % 4]; n_ += 1
                    k.mm(p_[0:64, :], wl[0:srcK, :], src[0:srcK, :], True, True, [wl, src], [p_])
                    hd = hid[li]
                    k.ts("dve", hd[:, :], p_[0:64, :], hb[:, li:li + 1], hb[:, 3:4], ALU.add, ALU.mult, [p_, hb], [hd])
                    k.ts("dve", ki[:, :], hd[:, :], float(1.0 / (2 * np.pi)), None, ALU.mult, None, [hd], [ki])
                    k.stt("dve", hd[:, :], ki[:, :], float(-2 * np.pi), hd[:, :], ALU.mult, ALU.add, [ki, hd], [hd])
                    k.act(hd[:, :], hd[:, :], AF.Sin, [hd], [hd])
                    src = hd; srcK = 64
                segs = []
                if q0 + 512 <= n: segs = [(0, 512, 0)]
                elif q0 >= n: segs = [(0, 512, 512)]
                else: segs = [(0, n - q0, 0), (n - q0, 512, 512)]
                for jc in range(4):
                    p_ = ps[n_ % 4]; n_ += 1
                    for (a0, a1, off) in segs:
                        k.mm(p_[:, a0:a1], w4[:, off + jc * 128:off + (jc + 1) * 128], hid[2][:, a0:a1], True, True, [w4, hid[2]], [p_])
                    wn = win[jc % 2]; kr_ = kr[jc % 2]
                    k.act(wn[:, :], t_[:, :], AF.Exp, [t_, ndel], [wn], scale=ndel[:, jc:jc + 1])
                    k.stt("dve", kr_[:, :], wn[:, :], 0.05, p_[:, :], ALU.add, ALU.mult, [wn, p_], [kr_])
                    if q0 <= n < q0 + 512:
                        k.memset("dve", kr_[:, n - q0:n - q0 + 1], 0.0, [kr_])
                    k.act(junk[:, :], kr_[:, :], AF.Abs, [kr_], [junk, asum], accum=asum[:, jc, c:c + 1])
                    k.dma("pool", KRAW[jc * 128:(jc + 1) * 128, q0:q0 + 512], kr_[:, :], reads=[kr_])
            rn = k.sbuf([128, 4], F32, "rn")
            for jc in range(4):
                k.op("dve", lambda e, jc=jc: e.reduce_sum(out=rn[:, jc:jc + 1], in_=asum[:, jc, 0:NCH], axis=mybir.AxisListType.X), [asum], [rn])
            k.recip(rn[:, :], rn[:, :], [rn], [rn], force_self=True)
            k.barrier()
            big = [k.sbuf([128, 2048], F32, "big") for _ in range(2)]
            bigb = [k.sbuf([128, 2048], BF16, "bigb") for _ in range(2)]
            n2 = 0
            for jc in range(4):
                for q0 in range(0, 2 * n, 2048):
                    qw = min(2048, 2 * n - q0)
                    b_ = big[n2 % 2]; bb_ = bigb[n2 % 2]; n2 += 1
                    k.dma("sp", b_[:, 0:qw], KRAW[jc * 128:(jc + 1) * 128, q0:q0 + qw], writes=[b_])
                    k.ts("dve", bb_[:, 0:qw], b_[:, 0:qw], rn[:, jc:jc + 1], None, ALU.mult, None, [b_, rn], [bb_], force_self=True)
                    k.dma("pool", KN[jc * 128:(jc + 1) * 128, q0:q0 + qw], bb_[:, 0:qw], reads=[bb_])
        return ph

    def make_fftconv(tag, NA, n, tok0, n_out_blocks):
        KN = S["KN" + tag]
        NR = NA // 2
        GF = 512 // NA
        GB = 512 // (2 * NA)
        def ph():
            cst = {}
            for nm, shp, dt in (("f1cs", [NA, 2 * NA], BF16), ("fC", [128, 128], BF16), ("fS", [128, 128], BF16), ("fnS", [128, 128], BF16),
                                ("fCS", [128, 256], BF16), ("fnSC", [128, 256], BF16), ("twA", [128, 512], F32), ("twB", [128, 512], F32),
                                ("twA2", [NA, 1024], F32), ("twB2", [NA, 1024], F32), ("g3C", [NA, NA], BF16), ("g3nS", [NA, NA], BF16)):
                cst[nm] = k.sbuf(shp, dt, nm)
                k.dma("sp", cst[nm][:], I[nm + tag][tuple(slice(None) for _ in shp)], writes=[cst[nm]])
            if NA == 4:
                for nm, shp, dt in (("twA2c", [128, 256], F32), ("twB2c", [128, 256], F32), ("gbC", [128, 64], BF16), ("gbnS", [128, 64], BF16)):
                    cst[nm] = k.sbuf(shp, dt, nm)
                    k.dma("sp", cst[nm][:], I[nm + tag][:, :], writes=[cst[nm]])
                c2a = [k.sbuf([128, 256], F32, "c2a") for _ in range(2)]; c2b = [k.sbuf([128, 256], F32, "c2b") for _ in range(2)]
                y3c = [k.sbuf([128, 2, 128], BF16, "y3c") for _ in range(2)]
                yoc = [k.sbuf([64, 128], F32, "yoc") for _ in range(2)]
            LG = 32 if NA == 128 else 128
            NXB = 2 if NA == 128 else 1
            xu = [k.sbuf([NA, LG, 128], BF16, "xu") for _ in range(NXB)]
            for t_ in xu: k.memset("pool", t_[:], 0.0, [t_])
            xk = [k.sbuf([NA, LG, 128], BF16, "xk") for _ in range(NXB)]
            s1 = [k.psum([128, 512], F32, "s1") for _ in range(2)]
            s2 = [k.psum([128, 512], F32, "s2") for _ in range(2)]
            s3 = [k.psum([128, 1024], F32, "s3") for _ in range(1)]
            s4 = [k.psum([128, 512], F32, "s4") for _ in range(2)]
            NB = 2
            ta = [k.sbuf([128, 512], F32, "ta") for _ in range(NB)]; tb = [k.sbuf([128, 512], F32, "tb") for _ in range(NB)]
            bu = [k.sbuf([128, 2, GF, NA], BF16, "bu") for _ in range(NB)]; bk = [k.sbuf([128, 2, GF, NA], BF16, "bk") for _ in range(NB)]
            kh = [k.sbuf([128, 2, 512], F32, "kh") for _ in range(NB)]
            m1 = [k.sbuf([128, 512], F32, "m1") for _ in range(NB)]; m2 = [k.sbuf([128, 512], F32, "m2") for _ in range(NB)]
            m3 = [k.sbuf([128, 512], F32, "m3") for _ in range(NB)]; m4 = [k.sbuf([128, 512], F32, "m4") for _ in range(NB)]
            yh = [k.sbuf([128, 2, GF, NA], BF16, "yh") for _ in range(NB)]
            t2a = [k.sbuf([NA, 1024], F32, "t2a") for _ in range(NB)]; t2b = [k.sbuf([NA, 1024], F32, "t2b") for _ in range(NB)]
            y3 = [k.sbuf([NA, 2, 4, 128], BF16, "y3") for _ in range(NB)]
            yo = [k.sbuf([n_out_blocks, 4, 128], F32, "yo") for _ in range(2)]
            MO = n_out_blocks
            cnt = {"tw": 0, "inv": 0, "g": 0}
            def fwd(x, cbase, bdst):
                for half in range(2):
                    bank = s1[half]
                    ta_ = ta[cnt["tw"] % NB]; tb_ = tb[cnt["tw"] % NB]; cnt["tw"] += 1
                    for cc in range(GB):
                        ch = cbase + half * GB + cc
                        k.mm(bank[:, cc * 2 * NA:(cc + 1) * 2 * NA], x[0:NA, ch, :], cst["f1cs"][0:NA, :], True, True, [x, cst["f1cs"]], [bank])
                    k.tt("dve", ta_[:, :], bank[:, :], cst["twA"][:, :], ALU.mult, [bank, cst["twA"]], [ta_])
                    k.tt("dve", tb_[:, :], bank[:, :], cst["twB"][:, :], ALU.mult, [bank, cst["twB"]], [tb_])
                    tav = ta_[:, :].rearrange("p (g r f) -> p g r f", g=GB, r=2)
                    tbv = tb_[:, :].rearrange("p (g r f) -> p g r f", g=GB, r=2)
                    k.tt("dve", bdst[:, 0, half * GB:(half + 1) * GB, :], tav[:, :, 0, :], tbv[:, :, 1, :], ALU.subtract, [ta_, tb_], [bdst])
                    k.tt("dve", bdst[:, 1, half * GB:(half + 1) * GB, :], tav[:, :, 1, :], tbv[:, :, 0, :], ALU.subtract, [ta_, tb_], [bdst])
                bre = bdst[:, 0, :, :].rearrange("p g f -> p (g f)"); bim = bdst[:, 1, :, :].rearrange("p g f -> p (g f)")
                k.mm(s2[0][:, :], cst["fC"][:, :], bre, True, False, [cst["fC"], bdst], [s2[0]])
                k.mm(s2[0][:, :], cst["fS"][:, :], bim, False, True, [cst["fS"], bdst], [s2[0]])
                k.mm(s2[1][:, :], cst["fC"][:, :], bim, True, False, [cst["fC"], bdst], [s2[1]])
                k.mm(s2[1][:, :], cst["fnS"][:, :], bre, False, True, [cst["fnS"], bdst], [s2[1]])
            for lg in range(512 // LG):
                xu_ = xu[lg % NXB]; xk_ = xk[lg % NXB]
                k.dma("sp", xu_[0:NR, :, :], S["UT"][lg * LG:(lg + 1) * LG, tok0:tok0 + n].rearrange("c (a p) -> a c p", p=128), writes=[xu_])
                k.dma("sp", xk_[:, :, :], KN[lg * LG:(lg + 1) * LG, :].rearrange("c (a p) -> a c p", p=128), writes=[xk_])
                for gf in range(LG // GF):
                    cbase = gf * GF
                    g_ = cnt["g"] % NB; cnt["g"] += 1
                    kh_ = kh[g_]; yh_ = yh[g_]
                    fwd(xk_, cbase, bk[g_])
                    k.copy("act", kh_[:, 0, :], s2[0][:, :], [s2[0]], [kh_])
                    k.copy("act", kh_[:, 1, :], s2[1][:, :], [s2[1]], [kh_])
                    fwd(xu_, cbase, bu[g_])
                    k.tt("dve", m1[g_][:, :], s2[0][:, :], kh_[:, 0, :], ALU.mult, [s2[0], kh_], [m1[g_]])
                    k.tt("dve", m3[g_][:, :], s2[0][:, :], kh_[:, 1, :], ALU.mult, [s2[0], kh_], [m3[g_]])
                    k.tt("dve", m2[g_][:, :], s2[1][:, :], kh_[:, 1, :], ALU.mult, [s2[1], kh_], [m2[g_]])
                    k.tt("dve", m4[g_][:, :], s2[1][:, :], kh_[:, 0, :], ALU.mult, [s2[1], kh_], [m4[g_]])
                    k.tt("pool", yh_[:, 0, :, :].rearrange("p g f -> p (g f)"), m1[g_][:, :], m2[g_][:, :], ALU.subtract, [m1[g_], m2[g_]], [yh_])
                    k.tt("pool", yh_[:, 1, :, :].rearrange("p g f -> p (g f)"), m3[g_][:, :], m4[g_][:, :], ALU.add, [m3[g_], m4[g_]], [yh_])
                    if NA == 4:
                        for sg in range(GF // 32):
                            b3 = s3[0]
                            iv = cnt["inv"] % 2; cnt["inv"] += 1
                            lre = yh_[:, 0, sg * 32:(sg + 1) * 32, :].rearrange("p g f -> p (g f)")
                            lim = yh_[:, 1, sg * 32:(sg + 1) * 32, :].rearrange("p g f -> p (g f)")
                            k.mm(b3[:, 0:256], lre, cst["fCS"][:, :], True, False, [yh_, cst["fCS"]], [b3])
                            k.mm(b3[:, 0:256], lim, cst["fnSC"][:, :], False, True, [yh_, cst["fnSC"]], [b3])
                            a_ = c2a[iv]; b_ = c2b[iv]; y_ = y3c[iv]; o_ = yoc[iv]
                            k.tt("dve", a_[:, :], b3[:, 0:256], cst["twA2c"][:, :], ALU.mult, [b3, cst["twA2c"]], [a_])
                            k.tt("dve", b_[:, :], b3[:, 0:256], cst["twB2c"][:, :], ALU.mult, [b3, cst["twB2c"]], [b_])
                            k.tt("pool", y_[:, 0, :], a_[:, 0:128], b_[:, 128:256], ALU.add, [a_, b_], [y_])
                            k.tt("pool", y_[:, 1, :], a_[:, 128:256], b_[:, 0:128], ALU.add, [a_, b_], [y_])
                            p4 = s4[iv]
                            k.mm(p4[0:64, 0:128], cst["gbC"][:, :], y_[:, 0, :], True, False, [cst["gbC"], y_], [p4])
                            k.mm(p4[0:64, 0:128], cst["gbnS"][:, :], y_[:, 1, :], False, True, [cst["gbnS"], y_], [p4])
                            k.copy("act", o_[:, :], p4[0:64, 0:128], [p4], [o_])
                            c0 = lg * LG + cbase + sg * 32
                            for a2 in range(2):
                                k.dma("sp", S["YT"][c0:c0 + 32, tok0 + a2 * 128:tok0 + (a2 + 1) * 128], o_[a2:64:2, :], reads=[o_])
                        continue
                    for sg in range(GF // 4):
                        b3 = s3[0]
                        iv = cnt["inv"] % NB; cnt["inv"] += 1
                        t2a_ = t2a[iv]; t2b_ = t2b[iv]; y3_ = y3[iv]
                        for cc in range(4):
                            ch = sg * 4 + cc
                            k.mm(b3[0:NA, cc * 256:(cc + 1) * 256], yh_[:, 0, ch, :], cst["fCS"][:, :], True, False, [yh_, cst["fCS"]], [b3])
                            k.mm(b3[0:NA, cc * 256:(cc + 1) * 256], yh_[:, 1, ch, :], cst["fnSC"][:, :], False, True, [yh_, cst["fnSC"]], [b3])
                        k.tt("dve", t2a_[:, :], b3[0:NA, :], cst["twA2"][:, :], ALU.mult, [b3, cst["twA2"]], [t2a_])
                        k.tt("dve", t2b_[:, :], b3[0:NA, :], cst["twB2"][:, :], ALU.mult, [b3, cst["twB2"]], [t2b_])
                        av = t2a_[:, :].rearrange("p (g r f) -> p g r f", g=4, r=2)
                        bv = t2b_[:, :].rearrange("p (g r f) -> p g r f", g=4, r=2)
                        k.tt("pool", y3_[:, 0, :, :], av[:, :, 0, :], bv[:, :, 1, :], ALU.add, [t2a_, t2b_], [y3_])
                        k.tt("pool", y3_[:, 1, :, :], av[:, :, 1, :], bv[:, :, 0, :], ALU.add, [t2a_, t2b_], [y3_])
                        p4 = s4[iv % 2]; yo_ = yo[iv % 2]
                        k.mm(p4[0:MO, :], cst["g3C"][:, 0:MO], y3_[:, 0, :, :].rearrange("p g f -> p (g f)"), True, False, [cst["g3C"], y3_], [p4])
                        k.mm(p4[0:MO, :], cst["g3nS"][:, 0:MO], y3_[:, 1, :, :].rearrange("p g f -> p (g f)"), False, True, [cst["g3nS"], y3_], [p4])
                        k.copy("act", yo_[:, :, :].rearrange("p g f -> p (g f)"), p4[0:MO, :], [p4], [yo_])
                        c0 = lg * LG + cbase + sg * 4
                        k.dma("sp", S["YT"][c0:c0 + 4, tok0:tok0 + MO * 128].rearrange("c (a p) -> a c p", p=128), yo_[:, :, :], reads=[yo_])
        return ph

    def make_hycombine(chunks):
        def ph():
            bd = k.sbuf([128, 4], F32); k.dma("sp", bd[:], I["hbd"][:, :], writes=[bd])
            yt = [k.sbuf([128, 512], F32, "yt") for _ in range(2)]
            ut = [k.sbuf([128, 512], F32, "ut") for _ in range(2)]
            x0 = [k.sbuf([128, 512], F32, "x0") for _ in range(2)]
            ob = [k.sbuf([128, 512], BF16, "ob") for _ in range(2)]
            n = 0
            for (t0, W, le, re, j) in chunks:
                for jc in range(4):
                    y_ = yt[n % 2]; u_ = ut[n % 2]; x_ = x0[n % 2]; o_ = ob[n % 2]; n += 1
                    rows = slice(jc * 128, (jc + 1) * 128)
                    k.dma("sp", y_[:, 0:W], S["YT"][rows, t0:t0 + W], writes=[y_])
                    k.dma("sp", u_[:, 0:W], S["UF"][rows, t0:t0 + W], writes=[u_])
                    k.dma("sp", x_[:, 0:W], S["X0T"][rows, t0:t0 + W], writes=[x_])
                    k.stt("dve", y_[:, 0:W], u_[:, 0:W], bd[:, jc:jc + 1], y_[:, 0:W], ALU.mult, ALU.add, [u_, bd, y_], [y_])
                    k.tt("dve", o_[:, 0:W], y_[:, 0:W], x_[:, 0:W], ALU.mult, [y_, x_], [o_])
                    k.dma("pool", S["OCT"][rows, t0:t0 + W], o_[:, 0:W], reads=[o_])
        return ph

    CH_E = [(c * 512, 512, c == 0, c == 8, 0) for c in range(9)]
    CH_OWN = [(c * 512, 512, c == 0, False, 0) for c in range(8)]
    phases.append(("inproj0", make_inproj(0, I["xt0"], CHUNKS_ALL)))
    phases.append(("filterL", make_filter("L", SEQ)))
    phases.append(("filterC", make_filter("C", CTX)))
    phases.append(("fftL", make_fftconv("L", 128, SEQ, 0, E // 128)))
    phases.append(("fftC", make_fftconv("C", 4, CTX, SEQ, 2)))
    phases.append(("hycomb", make_hycombine(CH_E + [CTXCH])))
    phases.append(("attn0", make_attn(0)))
    phases.append(("outproj0", make_outproj(0, I["xt0"], S["XM"], CH_E + [CTXCH])))
    phases.append(("ffn0", make_ffn(0, S["XM"], S["X1"], CH_E + [CTXCH])))
    phases.append(("inproj1", make_inproj(1, S["X1"], CH_E + [CTXCH])))
    phases.append(("attn1", make_attn(1)))
    phases.append(("outproj1", make_outproj(1, S["X1"], S["XM1"], CH_E)))
    phases.append(("ffn1", make_ffn(1, S["XM1"], OUT, CH_OWN)))
    for nm, ph in phases:
        k.phase(ph)
        if stop_after == nm:
            break
    k.close()
    return nc


_CACHE = {}


def kernel(**inputs):
    inp = {kk: np.asarray(v) for kk, v in inputs.items()}
    if "C" not in _CACHE:
        C = _consts()
        C["fftL"] = _fft_consts(128); C["fftC"] = _fft_consts(4)
        C["filtL"] = _filter_consts(SEQ); C["filtC"] = _filter_consts(CTX)
        _CACHE["C"] = C
    C = _CACHE["C"]
    nc = _build()
    in_maps = []
    for core in range(8):
        b, hh = core // 2, core % 2
        in_maps.append(_host_prep(inp, b, hh, C))
    res = run_bass_kernel_spmd(nc, in_maps, core_ids=list(range(8)))
    out = np.empty((4, SEQ, D), np.float32)
    for core in range(8):
        b, hh = core // 2, core % 2
        o = np.asarray(res.results[core]["out"]).T
        if hh == 0:
            out[b, :OWN] = o
        else:
            out[b, OWN:] = o[::-1]
    return out
```

```python
import numpy as np
import ml_dtypes
from contextlib import ExitStack
import concourse.bass as bass
import concourse.mybir as mybir
from concourse.bass_utils import run_bass_kernel_spmd

F32 = mybir.dt.float32
BF16 = mybir.dt.bfloat16
I32 = mybir.dt.int32
AF = mybir.ActivationFunctionType
ALU = mybir.AluOpType
NPBF = ml_dtypes.bfloat16

D = 1024; SEQ = 8192; CTX = 256; TT = SEQ + CTX; E = 4608; OWN = 4096
DFF = 2816; NM = 22
SAME_ENGINE_SYNC = {"act", "pool"}


class Res:
    __slots__ = ("name", "last_w", "reads", "excl")
    def __init__(self, name, excl=False):
        self.name = name; self.last_w = None; self.reads = {}; self.excl = excl


class Tl:
    def __init__(self, t, r):
        self.t = t; self.r = r
    def __getitem__(self, idx):
        return self.t[idx]


class K:
    ENGS = ("pe", "act", "dve", "pool", "sp")

    def __init__(self, nc):
        self.nc = nc
        self.es = ExitStack()
        self.sem = {}; self.cnt = {}
        for e in self.ENGS:
            self.sem[e] = self.es.enter_context(nc.semaphore("s_" + e))
            self.cnt[e] = 0
        self.dma_sems = {}
        self.dma_key = {}
        self.dma_rr = {}
        self.NDMASEM = {"sp": 32, "pool": 24, "act": 8, "pe": 4, "dve": 4}
        self.seen = {e: {} for e in self.ENGS}
        self.ops = {e: [] for e in self.ENGS}
        self.phase_es = None
        self.nres = 0
        self.ndma = 0

    def sbuf(self, shape, dt, name=None, persist=False):
        self.nres += 1
        name = (name or "t") + "_%d" % self.nres
        es = self.es if persist else self.phase_es
        t = es.enter_context(self.nc.sbuf_tensor(name, list(shape), dt))
        return Tl(t, Res(name))

    def psum(self, shape, dt, name=None):
        self.nres += 1
        name = (name or "p") + "_%d" % self.nres
        t = self.phase_es.enter_context(self.nc.psum_tensor(name, list(shape), dt))
        return Tl(t, Res(name, excl=True))

    def _need(self, reads, writes, eng=None):
        evs = []
        for r in reads:
            if r.last_w is not None: evs.append(r.last_w)
            if r.excl:
                evs.extend((kk[0], kk[1], v) for kk, v in r.reads.items() if not (kk[0] == "eng" and kk[1] == eng))
        for w in writes:
            if w.last_w is not None: evs.append(w.last_w)
            evs.extend((kk[0], kk[1], v) for kk, v in w.reads.items())
        return evs

    def _emit_waits(self, eng, evs, force_self=False):
        need = {}
        for kind, key, val in evs:
            if kind == "eng":
                if key == eng and eng not in SAME_ENGINE_SYNC and not force_self: continue
                v = val
            else:
                v = val
            if self.seen[eng].get((kind, key), 0) >= v: continue
            if need.get((kind, key), 0) < v: need[(kind, key)] = v
        for (kind, key), v in need.items():
            self.seen[eng][(kind, key)] = v
            sem = self.sem[key] if kind == "eng" else self.dma_sems[key][0]
            self.ops[eng].append(lambda e, sem=sem, v=v: e.wait_ge(sem, v))

    def _commit(self, ev, reads, writes):
        for r in reads:
            kk = (ev[0], ev[1])
            if r.reads.get(kk, 0) < ev[2]: r.reads[kk] = ev[2]
        for w in writes:
            w.last_w = ev; w.reads = {}

    def op(self, eng, fn, reads=(), writes=(), force_self=False):
        reads = [x.r if isinstance(x, Tl) else x for x in reads]
        writes = [x.r if isinstance(x, Tl) else x for x in writes]
        self._emit_waits(eng, self._need(reads, writes, eng), force_self)
        self.cnt[eng] += 1
        sem = self.sem[eng]
        self.ops[eng].append(lambda e, fn=fn, sem=sem: fn(e).then_inc(sem, 1))
        self._commit(("eng", eng, self.cnt[eng]), reads, writes)

    def dma(self, q, out, in_, reads=(), writes=(), **kw):
        reads = [x.r if isinstance(x, Tl) else x for x in reads]
        writes = [x.r if isinstance(x, Tl) else x for x in writes]
        npool = self.NDMASEM[q]
        idx = (q, self.dma_rr.get(q, 0) % npool)
        self.dma_rr[q] = self.dma_rr.get(q, 0) + 1
        if idx not in self.dma_sems:
            s_ = self.es.enter_context(self.nc.semaphore("d_%s_%d" % idx))
            self.dma_sems[idx] = [s_, 0]
        ent = self.dma_sems[idx]
        evs = self._need(reads, writes, q)
        if ent[1] > 0:
            evs.append(("dma", idx, ent[1] * 16))
        self._emit_waits(q, evs)
        ent[1] += 1
        sem = ent[0]
        self.ndma += 1
        self.ops[q].append(lambda e, out=out, in_=in_, sem=sem, kw=kw: e.dma_start(out=out, in_=in_, **kw).then_inc(sem, 16))
        self._commit(("dma", idx, ent[1] * 16), reads, writes)

    def barrier(self):
        evs = [("eng", e, self.cnt[e]) for e in self.ENGS if self.cnt[e] > 0]
        evs += [("dma", kk, v[1] * 16) for kk, v in self.dma_sems.items() if v[1] > 0]
        for e in self.ENGS:
            self._emit_waits(e, [ev for ev in evs if not (ev[0] == "eng" and ev[1] == e)])

    def phase(self, body):
        with ExitStack() as pes:
            self.phase_es = pes
            body()
            self.barrier()
            ops = self.ops
            self.ops = {e: [] for e in self.ENGS}
            with self.nc.Block() as block:
                @block.tensor
                def _(e):
                    for f in ops["pe"]: f(e)
                @block.scalar
                def _(e):
                    for f in ops["act"]: f(e)
                @block.vector
                def _(e):
                    for f in ops["dve"]: f(e)
                @block.gpsimd
                def _(e):
                    for f in ops["pool"]: f(e)
                @block.sync
                def _(e):
                    for f in ops["sp"]: f(e)
        self.phase_es = None

    def close(self):
        self.es.close()

    def ts(self, eng, out, in0, s1, s2, op0, op1, r, w, force_self=False):
        if s2 is None:
            s2 = 0.0; op1 = ALU.add
        self.op(eng, lambda e: e.tensor_scalar(out=out, in0=in0, scalar1=s1, scalar2=s2, op0=op0, op1=op1), r, w, force_self)
    def stt(self, eng, out, in0, sc, in1, op0, op1, r, w):
        self.op(eng, lambda e: e.scalar_tensor_tensor(out=out, in0=in0, scalar=sc, in1=in1, op0=op0, op1=op1), r, w)
    def tt(self, eng, out, in0, in1, op, r, w):
        self.op(eng, lambda e: e.tensor_tensor(out=out, in0=in0, in1=in1, op=op), r, w)
    def act(self, out, in_, func, r, w, bias=None, scale=None, accum=None):
        kw = {}
        if bias is not None: kw["bias"] = bias
        if scale is not None: kw["scale"] = scale
        if accum is not None: kw["accum_out"] = accum
        self.op("act", lambda e: e.activation(out=out, in_=in_, func=func, **kw), r, w)
    def mm(self, out, lhsT, rhs, start, stop, r, w):
        self.op("pe", lambda e: e.matmul(out, lhsT=lhsT, rhs=rhs, start=start, stop=stop), r, w)
    def copy(self, eng, out, in_, r, w):
        if eng == "act":
            self.op("act", lambda e: e.copy(out=out, in_=in_), r, w)
        else:
            self.op(eng, lambda e: e.tensor_copy(out=out, in_=in_), r, w)
    def memset(self, eng, ap, val, w):
        self.op(eng, lambda e: e.memset(ap, val), [], w)
    def recip(self, out, in_, r, w, force_self=False):
        self.op("dve", lambda e: e.reciprocal(out=out, in_=in_), r, w, force_self)

def _consts():
    c = {}
    nf = 16
    inv = 10000.0 ** (-np.arange(nf, dtype=np.float64) / nf)
    t = np.arange(SEQ)
    row = (t // 64).astype(np.float64); col = (t % 64).astype(np.float64)
    ar = row[None, :] * inv[:, None]; ac = col[None, :] * inv[:, None]
    cos64 = np.concatenate([np.cos(ar), np.cos(ar), np.cos(ac), np.cos(ac)], 0)
    sin64 = np.concatenate([-np.sin(ar), np.sin(ar), -np.sin(ac), np.sin(ac)], 0)
    c["cos64"] = cos64.astype(np.float32); c["sin64"] = sin64.astype(np.float32)
    perm = np.concatenate([np.arange(16) + 16, np.arange(16), np.arange(16) + 48, np.arange(16) + 32])
    c["perm64"] = perm
    p = np.arange(128)[:, None]; f = np.arange(512)[None, :]
    c["bmask"] = np.stack([(np.abs(128 * r + p - f) <= 128) for r in range(-1, 5)], 0).astype(NPBF)
    return c


def _fft_consts(NA):
    N = 128 * NA
    c = {}
    a = np.arange(NA)[:, None]; f1 = np.arange(NA)[None, :]
    th = 2 * np.pi * a * f1 / NA
    c["f1cs"] = np.concatenate([np.cos(th), -np.sin(th)], 1).astype(NPBF)
    p = np.arange(128)[:, None]; f2 = np.arange(128)[None, :]
    th2 = 2 * np.pi * p * f2 / 128
    C = np.cos(th2); S = np.sin(th2)
    c["fC"] = C.astype(NPBF); c["fS"] = S.astype(NPBF); c["fnS"] = (-S).astype(NPBF)
    c["fCS"] = np.concatenate([C, S], 1).astype(NPBF)
    c["fnSC"] = np.concatenate([-S, C], 1).astype(NPBF)
    tw = 2 * np.pi * np.arange(128)[:, None] * np.arange(NA)[None, :] / N
    G = 512 // (2 * NA)
    tc_ = np.cos(tw); ts_ = np.sin(tw)
    A = np.concatenate([tc_, tc_], 1)
    B = np.concatenate([ts_, -ts_], 1)
    c["twA"] = np.tile(A[:, None, :], (1, G, 1)).reshape(128, 512).astype(np.float32)
    c["twB"] = np.tile(B[:, None, :], (1, G, 1)).reshape(128, 512).astype(np.float32)
    tcT = np.cos(tw).T; tsT = np.sin(tw).T
    A2 = np.stack([tcT, tcT], 1)
    B2 = np.stack([tsT, -tsT], 1)
    c["twA2"] = np.tile(A2[:, None], (1, 4, 1, 1)).reshape(NA, 1024).astype(np.float32)
    c["twB2"] = np.tile(B2[:, None], (1, 4, 1, 1)).reshape(NA, 1024).astype(np.float32)
    th = 2 * np.pi * np.arange(NA)[:, None] * np.arange(NA)[None, :] / NA
    c["g3C"] = (np.cos(th) / N).astype(NPBF); c["g3nS"] = (-np.sin(th) / N).astype(NPBF)
    if NA == 4:
        f1i = np.arange(128) % 4
        twp = 2 * np.pi * f1i[:, None] * np.arange(128)[None, :] / N
        c["twA2c"] = np.concatenate([np.cos(twp), np.cos(twp)], 1).astype(np.float32)
        c["twB2c"] = np.concatenate([np.sin(twp), -np.sin(twp)], 1).astype(np.float32)
        gC = np.zeros((128, 64), np.float64); gS = np.zeros((128, 64), np.float64)
        for ch in range(32):
            for f1_ in range(4):
                for a_ in range(2):
                    gC[ch * 4 + f1_, ch * 2 + a_] = np.cos(2 * np.pi * f1_ * a_ / 4) / N
                    gS[ch * 4 + f1_, ch * 2 + a_] = -np.sin(2 * np.pi * f1_ * a_ / 4) / N
        c["gbC"] = gC.astype(NPBF); c["gbnS"] = gS.astype(NPBF)
    return c


def _filter_consts(n):
    q = np.arange(2 * n)
    tap = np.where(q < n, q, 2 * n - q).astype(np.int64)
    tap = np.minimum(tap, n - 1)
    t01 = np.linspace(0.0, 1.0, n, dtype=np.float32)
    bands = 16
    w = (2.0 * np.pi * np.arange(n, dtype=np.float32) / n).astype(np.float32)
    f = np.linspace(1e-4, bands - 1, bands, dtype=np.float32)[None, :]
    feats = np.concatenate([t01[:, None], np.cos(f * w[:, None]), -np.sin(f * w[:, None])], -1).astype(np.float32)
    featsT = np.ascontiguousarray(feats[tap].T)
    t01b = np.ascontiguousarray(np.tile(t01[tap][None, :], (128, 1)))
    deltas = np.abs(np.linspace(np.log(1e-2) / 1.5, np.log(1e-2) / 0.3, 512, dtype=np.float32))
    ndel = np.ascontiguousarray((-deltas).reshape(4, 128).T)
    return featsT.astype(np.float32), t01b.astype(np.float32), ndel.astype(np.float32)


def _pm(v, n):
    return np.ascontiguousarray(np.asarray(v, np.float32).reshape(n, 128).T)


def _host_prep(inp, b, hh, C):
    fl = (hh == 1)
    m = {}
    x = inp["x"][b]; cx = inp["ctx"][b]
    if fl: x = x[::-1]; cx = cx[::-1]
    m["xt0"] = np.ascontiguousarray(np.concatenate([x, cx], 0).T)
    cv = np.stack([inp["c"][b], inp["c_ctx"]], 1)
    m["cvec"] = np.ascontiguousarray(cv.reshape(8, 128, 2).transpose(1, 0, 2))
    perm = C["perm64"]
    for i in range(2):
        m[f"ada_w{i}"] = inp["ada_w"][i]
        m[f"ada_b{i}"] = _pm(inp["ada_b"][i], 48)
        m[f"nmix{i}"] = _pm(inp["norm_mix"][i], 8)
        m[f"nffn{i}"] = _pm(inp["norm_ffn"][i], 8)
        w = inp["mix_w_in"][i]
        qk = w[:, :768].reshape(D, 12, 64)[:, :, perm].reshape(D, 768)
        m[f"w_in{i}"] = np.ascontiguousarray(np.concatenate([w, qk], 1))
        m[f"w_out{i}"] = inp["mix_w_out"][i]
        gq = inp["attn_q_norm"][i]; gk = inp["attn_k_norm"][i]
        m[f"qkg{i}"] = np.ascontiguousarray(np.stack([np.tile(gq, 2), np.tile(gq[perm], 2), np.tile(gk, 2), np.tile(gk[perm], 2)], 1).astype(np.float32))
        m[f"w_up{i}"] = inp["ffn_w_up"][i]
        m[f"w_dn{i}"] = inp["ffn_w_down"][i]
        fw = inp["ffn_conv_w"][i]
        if fl: fw = fw[::-1]
        m[f"fcw{i}"] = np.ascontiguousarray(fw.reshape(3, NM, 128).transpose(2, 1, 0))
        m[f"fcb{i}"] = _pm(inp["ffn_conv_b"][i], NM)
    hw = inp["hy_conv_w"][0]
    if fl: hw = hw[::-1]
    m["hcw"] = np.ascontiguousarray(hw.reshape(3, 12, 128).transpose(2, 1, 0))
    m["hcb"] = _pm(inp["hy_conv_b"][0], 12)
    sw = inp["sc_conv_w"][0]
    if fl: sw = sw[::-1]
    m["scw"] = np.ascontiguousarray(sw.reshape(3, 4, 128).transpose(2, 1, 0))
    m["hw1"] = inp["hy_w1"][0]; m["hw2"] = inp["hy_w2"][0]; m["hw3"] = inp["hy_w3"][0]
    w4 = inp["hy_w4"][0]
    if fl: w4 = np.concatenate([w4[:, 512:], w4[:, :512]], 1)
    m["hw4"] = np.ascontiguousarray(w4)
    m["hb"] = np.ascontiguousarray(np.stack([inp["hy_b1"][0], inp["hy_b2"][0], inp["hy_b3"][0], inp["hy_freq"][0]], 1).astype(np.float32))
    m["hbd"] = _pm(inp["hy_bias_d"][0], 4)
    m["sink"] = np.ascontiguousarray(np.tile(inp["swa_sink"][0][None, :], (128, 1)).astype(np.float32))
    cos = C["cos64"]; sin = C["sin64"]
    if fl: cos = cos[:, ::-1]; sin = sin[:, ::-1]
    cosx = np.concatenate([cos, np.ones((64, CTX), np.float32)], 1)
    sinx = np.concatenate([sin, np.zeros((64, CTX), np.float32)], 1)
    m["cosT"] = np.ascontiguousarray(np.concatenate([cosx, cosx], 0))
    m["sinT"] = np.ascontiguousarray(np.concatenate([sinx, sinx], 0))
    m["bmask"] = C["bmask"]
    for tag, NA in (("L", 128), ("C", 4)):
        for kk, v in C["fft" + tag].items():
            m[kk + tag] = v
    for tag in ("L", "C"):
        ft, t01b, ndel = C["filt" + tag]
        m["featsT" + tag] = ft; m["t01b" + tag] = t01b
    m["ndel"] = C["filtL"][2]
    return m

def _build(stop_after=None, dbg=()):
    nc = bass.Bass("TRN2", target_bir_lowering=False)
    k = K(nc)
    def din(name, shape, dt=F32):
        return nc.dram_tensor(name, list(shape), dt, kind="ExternalInput").ap()
    def dscr(name, shape, dt):
        kind = "ExternalOutput" if name in dbg else "Internal"
        return nc.dram_tensor(name, list(shape), dt, kind=kind).ap()
    I = {}
    I["xt0"] = din("xt0", [D, TT]); I["cvec"] = din("cvec", [128, 8, 2])
    for i in range(2):
        I[f"ada_w{i}"] = din(f"ada_w{i}", [D, 6 * D]); I[f"ada_b{i}"] = din(f"ada_b{i}", [128, 48])
        I[f"nmix{i}"] = din(f"nmix{i}", [128, 8]); I[f"nffn{i}"] = din(f"nffn{i}", [128, 8])
        I[f"w_in{i}"] = din(f"w_in{i}", [D, 3328]); I[f"w_out{i}"] = din(f"w_out{i}", [D, D])
        I[f"qkg{i}"] = din(f"qkg{i}", [128, 4])
        I[f"w_up{i}"] = din(f"w_up{i}", [D, 2 * DFF]); I[f"w_dn{i}"] = din(f"w_dn{i}", [DFF, D])
        I[f"fcw{i}"] = din(f"fcw{i}", [128, NM, 3]); I[f"fcb{i}"] = din(f"fcb{i}", [128, NM])
    I["hcw"] = din("hcw", [128, 12, 3]); I["hcb"] = din("hcb", [128, 12]); I["scw"] = din("scw", [128, 4, 3])
    I["hw1"] = din("hw1", [33, 64]); I["hw2"] = din("hw2", [64, 64]); I["hw3"] = din("hw3", [64, 64])
    I["hw4"] = din("hw4", [64, 1024]); I["hb"] = din("hb", [64, 4]); I["hbd"] = din("hbd", [128, 4])
    I["sink"] = din("sink", [128, 8])
    I["cosT"] = din("cosT", [128, TT]); I["sinT"] = din("sinT", [128, TT])
    I["bmask"] = din("bmask", [6, 128, 512], BF16)
    for tag, NA in (("L", 128), ("C", 4)):
        I["f1cs" + tag] = din("f1cs" + tag, [NA, 2 * NA], BF16)
        for nm in ("fC", "fS", "fnS"): I[nm + tag] = din(nm + tag, [128, 128], BF16)
        for nm in ("fCS", "fnSC"): I[nm + tag] = din(nm + tag, [128, 256], BF16)
        for nm in ("twA", "twB"): I[nm + tag] = din(nm + tag, [128, 512])
        for nm in ("twA2", "twB2"): I[nm + tag] = din(nm + tag, [NA, 1024])
        for nm in ("g3C", "g3nS"): I[nm + tag] = din(nm + tag, [NA, NA], BF16)
        if NA == 4:
            for nm in ("twA2c", "twB2c"): I[nm + tag] = din(nm + tag, [128, 256])
            for nm in ("gbC", "gbnS"): I[nm + tag] = din(nm + tag, [128, 64], BF16)
        n = 64 * NA
        I["featsT" + tag] = din("featsT" + tag, [33, 2 * n]); I["t01b" + tag] = din("t01b" + tag, [128, 2 * n])
    I["ndel"] = din("ndel", [128, 4])
    OUT = nc.dram_tensor("out", [D, OWN], F32, kind="ExternalOutput").ap()

    S = {}
    for i in range(2):
        S[f"bw_in{i}"] = dscr(f"bw_in{i}", [D, 3328], BF16); S[f"bw_out{i}"] = dscr(f"bw_out{i}", [D, D], BF16)
        S[f"bw_up{i}"] = dscr(f"bw_up{i}", [D, 2 * DFF], BF16); S[f"bw_dn{i}"] = dscr(f"bw_dn{i}", [DFF, D], BF16)
    S["QT"] = dscr("QT", [8, 64, TT], BF16); S["KT"] = dscr("KT", [4, 64, TT], BF16); S["V"] = dscr("V", [TT, 256], BF16)
    S["UT"] = dscr("UT", [512, TT], BF16); S["X0T"] = dscr("X0T", [512, TT], F32)
    S["UF"] = dscr("UF", [512, TT], F32)
    S["YT"] = dscr("YT", [512, TT], F32)
    S["OAT"] = dscr("OAT", [512, TT], BF16); S["OCT"] = dscr("OCT", [512, TT], BF16)
    S["XM"] = dscr("XM", [D, TT], F32); S["X1"] = dscr("X1", [D, TT], F32); S["XM1"] = dscr("XM1", [D, TT], F32)
    S["KRAWL"] = dscr("KRAWL", [512, 2 * SEQ], F32); S["KNL"] = dscr("KNL", [512, 2 * SEQ], BF16)
    S["KRAWC"] = dscr("KRAWC", [512, 2 * CTX], F32); S["KNC"] = dscr("KNC", [512, 2 * CTX], BF16)

    MOD = k.sbuf([128, 2, 48, 2], F32, "mod", persist=True)
    AB = k.sbuf([128, 2, 2, 2, 8, 2], F32, "ab", persist=True)
    ONES = k.sbuf([128, 128], BF16, "ones", persist=True)
    BONES = k.sbuf([128, 128], BF16, "bones", persist=True)
    SEL = k.sbuf([128, 64], F32, "sel", persist=True)
    ESINK = k.sbuf([128, 8], F32, "esink", persist=True)
    RNORM = {"L": k.sbuf([128, 4], F32, "rnL", persist=True), "C": k.sbuf([128, 4], F32, "rnC", persist=True)}

    phases = []

    conv_items = []
    for i in range(2):
        for src, dst, rows, cols in ((f"w_in{i}", f"bw_in{i}", D, 3328), (f"w_out{i}", f"bw_out{i}", D, D),
                                     (f"w_up{i}", f"bw_up{i}", D, 2 * DFF), (f"w_dn{i}", f"bw_dn{i}", DFF, D)):
            for r0 in range(0, rows, 128):
                for c0 in range(0, cols, 2048):
                    conv_items.append((src, dst, r0, c0, min(2048, cols - c0)))
    CONV_EARLY = sum(1 for it in conv_items if it[0] in ("w_in0", "w_out0"))
    conv_pos = [0]

    class make_converter:
        def __init__(self):
            self.stg = [k.sbuf([128, 2048], F32, "stg") for _ in range(3)]
            self.stb = [k.sbuf([128, 2048], BF16, "stb") for _ in range(3)]
        def step(self, eng=None, q="sp"):
            n = conv_pos[0]
            if n >= len(conv_items): return False
            conv_pos[0] += 1
            src, dst, r0, c0, cw = conv_items[n]
            a = self.stg[n % 3]; bt = self.stb[n % 3]
            k.dma(q, a[:, 0:cw], I[src][r0:r0 + 128, c0:c0 + cw], writes=[a])
            k.copy(eng or ("dve" if n % 2 == 0 else "act"), bt[:, 0:cw], a[:, 0:cw], [a], [bt])
            k.dma(q, S[dst][r0:r0 + 128, c0:c0 + cw], bt[:, 0:cw], reads=[bt])
            return True

    def ph_setup():
        k.memset("dve", ONES[:], 1.0, [ONES])
        k.memset("dve", BONES[:], 0.0, [BONES])
        k.memset("dve", BONES[0:64, 0:64], 1.0, [BONES])
        k.memset("dve", BONES[64:128, 64:128], 1.0, [BONES])
        k.memset("dve", SEL[:], 0.0, [SEL])
        k.memset("dve", SEL[64:65, :], 1.0, [SEL])
        snk = k.sbuf([128, 8], F32)
        k.dma("sp", snk[:], I["sink"][:, :], writes=[snk])
        k.act(ESINK[:], snk[:], AF.Exp, [snk], [ESINK])
        cv_ = make_converter()
        for _ in range(CONV_EARLY):
            cv_.step()
        cv = k.sbuf([128, 8, 2], F32)
        k.dma("sp", cv[:], I["cvec"][:, :, :], writes=[cv])
        sc = k.sbuf([128, 8, 2], F32)
        k.act(sc[:], cv[:], AF.Silu, [cv], [sc])
        wst = [k.sbuf([128, 8, 512], F32, "wst") for _ in range(2)]
        ps = [k.psum([128, 512], F32) for _ in range(2)]
        n = 0
        for i in range(2):
            adb = k.sbuf([128, 48], F32)
            k.dma("sp", adb[:], I[f"ada_b{i}"][:, :], writes=[adb])
            for cb in range(12):
                wt = wst[n % 2]; n += 1
                k.dma("sp", wt[:], I[f"ada_w{i}"][:, cb * 512:(cb + 1) * 512].rearrange("(k p) f -> p k f", p=128), writes=[wt])
                for mi in range(4):
                    m = cb * 4 + mi
                    pt = ps[m % 2]
                    for kk in range(8):
                        k.mm(pt[:, 0:2], wt[:, kk, mi * 128:(mi + 1) * 128], sc[:, kk, :], kk == 0, kk == 7, [wt, sc], [pt])
                    k.ts("dve", MOD[:, i, m, :], pt[:, 0:2], adb[:, m:m + 1], None, ALU.add, None, [pt, adb], [MOD])
            for wh, (nm, sh0, sc0) in enumerate(((f"nmix{i}", 0, 8), (f"nffn{i}", 24, 32))):
                g = k.sbuf([128, 8], F32)
                k.dma("sp", g[:], I[nm][:, :], writes=[g])
                for j in range(2):
                    k.stt("dve", AB[:, i, wh, 0, :, j], MOD[:, i, sc0:sc0 + 8, j], 1.0, g[:], ALU.add, ALU.mult, [MOD, g], [AB])
                    k.copy("dve", AB[:, i, wh, 1, :, j], MOD[:, i, sh0:sh0 + 8, j], [MOD], [AB])
    phases.append(("setup", ph_setup))

    def norm_mod(xh, Wc, i, wh, j, sq, ssps, rstd, tmp, h):
        for kk in range(8):
            k.act(sq[:, kk, 0:Wc], xh[:, kk, 0:Wc], AF.Square, [xh], [sq])
        for (c0, c1) in ((0, min(512, Wc)), (512, Wc)):
            if c1 <= c0: continue
            for kk in range(8):
                k.mm(ssps[:, c0:c1], ONES[:, :], sq[:, kk, c0:c1], kk == 0, kk == 7, [ONES, sq], [ssps])
        k.act(rstd[:, 0:Wc], ssps[:, 0:Wc], AF.Sqrt, [ssps], [rstd], bias=1e-6, scale=1.0 / D)
        k.recip(rstd[:, 0:Wc], rstd[:, 0:Wc], [rstd], [rstd])
        for kk in range(8):
            t = tmp[kk % 2]
            k.stt("dve", t[:, 0:Wc], xh[:, kk, 0:Wc], AB[:, i, wh, 0, kk, j:j + 1], rstd[:, 0:Wc], ALU.mult, ALU.mult, [xh, AB, rstd], [t])
            k.act(h[:, kk, 0:Wc], t[:, 0:Wc], AF.Identity, [t, AB], [h], bias=AB[:, i, wh, 1, kk, j:j + 1], scale=1.0)

    CHUNKS_ALL = [(c * 512, 512, c == 0, c == 15, 0) for c in range(16)] + [(SEQ, CTX, True, True, 1)]
    CHUNKS_E = [(c * 512, 512, c == 0, False, 0) for c in range(9)]
    CTXCH = (SEQ, CTX, True, True, 1)

    def load_xh(xh, src, t0, W, ledge, redge, q="sp"):
        lo = 2 if ledge else 1
        hi = W + 2 if redge else W + 3
        if ledge: k.memset("pool", xh[:, :, 1:2], 0.0, [xh])
        if redge: k.memset("pool", xh[:, :, W + 2:W + 3], 0.0, [xh])
        k.dma(q, xh[:, :, lo:hi], src[:, t0 - 2 + lo:t0 - 2 + hi].rearrange("(k p) t -> p k t", p=128), writes=[xh])

    def make_inproj(i, XIN, chunks):
        def ph():
            w = k.sbuf([128, 8, 3328], BF16, "w_in")
            for kk in range(8):
                k.dma("sp", w[:, kk, :], S[f"bw_in{i}"][kk * 128:(kk + 1) * 128, :], writes=[w])
            qkg = k.sbuf([128, 4], F32); k.dma("sp", qkg[:], I[f"qkg{i}"][:, :], writes=[qkg])
            if i == 0:
                cw = k.sbuf([128, 12, 3], F32); k.dma("sp", cw[:], I["hcw"][:, :, :], writes=[cw])
                cb = k.sbuf([128, 12], F32); k.dma("sp", cb[:], I["hcb"][:, :], writes=[cb])
            else:
                cw = k.sbuf([128, 4, 3], F32); k.dma("sp", cw[:], I["scw"][:, :, :], writes=[cw])
            xhs = [k.sbuf([128, 8, 516], F32, "xh") for _ in range(2)]
            for t_ in xhs: k.memset("pool", t_[:], 0.0, [t_])
            sq = k.sbuf([128, 8, 516], BF16, "sq")
            h = k.sbuf([128, 8, 516], BF16, "h")
            rstd = k.sbuf([128, 516], F32, "rstd")
            tmp = [k.sbuf([128, 516], F32, "tmp") for _ in range(2)]
            ssps = k.psum([128, 1024], F32, "ssps")
            zps = [k.psum([128, 512], F32, "zps") for _ in range(4)]
            cps = k.psum([128, 1024], F32, "cps")
            cosb = k.sbuf([128, 512], F32, "cos"); sinb = k.sbuf([128, 512], F32, "sin")
            sq2 = [k.sbuf([128, 512], BF16, "sq2") for _ in range(2)]
            rs = [k.sbuf([128, 512], F32, "rs") for _ in range(2)]
            ta = [k.sbuf([128, 512], F32, "ta") for _ in range(2)]
            tb = [k.sbuf([128, 512], F32, "tb") for _ in range(2)]
            qo = [k.sbuf([128, 512], BF16, "qo") for _ in range(2)]
            vo = [k.sbuf([128, 256], BF16, "vo") for _ in range(2)]
            asb = [k.sbuf([128, 516], F32, "asb") for _ in range(3)]
            c1 = [k.sbuf([128, 512], F32, "c1") for _ in range(3)]
            uo = [k.sbuf([128, 512], F32, "uo") for _ in range(2)]
            ub = [k.sbuf([128, 512], BF16, "ub") for _ in range(2)]
            pp = [k.sbuf([128, 516], F32, "pp") for _ in range(2)]
            nq = [0]
            import os
            PARTS = os.environ.get("INPROJ_PARTS", "nqvc")
            NCHK = int(os.environ.get("INPROJ_NCH", "99"))
            for ci, (t0, W, le, re, j) in enumerate(chunks[:NCHK]):
                Wc = W + 4
                do_q = (t0 < E) or j == 1
                if i == 1 and j == 1: do_q = False
                xh = xhs[ci % 2]
                load_xh(xh, XIN, t0, W, le, re)
                norm_mod(xh, Wc, i, 0, j, sq, ssps, rstd, tmp, h)
                k.dma("sp", cosb[:, 0:W], I["cosT"][:, t0:t0 + W], writes=[cosb])
                k.dma("sp", sinb[:, 0:W], I["sinT"][:, t0:t0 + W], writes=[sinb])
                for pr in range(6):
                    if "q" not in PARTS: continue
                    if pr < 4 and not do_q: continue
                    n = nq[0]; nq[0] += 1
                    zp = zps[(2 * n) % 4]; zsp = zps[(2 * n + 1) % 4]
                    for kk in range(8):
                        k.mm(zp[:, 0:W], w[:, kk, pr * 128:(pr + 1) * 128], h[:, kk, 2:W + 2], kk == 0, kk == 7, [w, h], [zp])
                    for kk in range(8):
                        k.mm(zsp[:, 0:W], w[:, kk, 2560 + pr * 128:2560 + (pr + 1) * 128], h[:, kk, 2:W + 2], kk == 0, kk == 7, [w, h], [zsp])
                    s2 = sq2[n % 2]; r_ = rs[n % 2]; a_ = ta[n % 2]; b_ = tb[n % 2]; q_ = qo[n % 2]
                    gi = 0 if pr < 4 else 2
                    k.act(s2[:, 0:W], zp[:, 0:W], AF.Square, [zp], [s2])
                    k.stt("dve", a_[:, 0:W], zp[:, 0:W], qkg[:, gi:gi + 1], cosb[:, 0:W], ALU.mult, ALU.mult, [zp, qkg, cosb], [a_])
                    k.stt("dve", b_[:, 0:W], zsp[:, 0:W], qkg[:, gi + 1:gi + 2], sinb[:, 0:W], ALU.mult, ALU.mult, [zsp, qkg, sinb], [b_])
                    k.mm(zp[:, 0:W], BONES[:, :], s2[:, 0:W], True, True, [BONES, s2], [zp])
                    k.act(r_[:, 0:W], zp[:, 0:W], AF.Sqrt, [zp], [r_], bias=1e-6, scale=1.0 / 64)
                    k.recip(r_[:, 0:W], r_[:, 0:W], [r_], [r_])
                    QS = os.environ.get("QSKIP", "")
                    pe_ = "dve" if "pool" in QS else "pool"
                    k.tt(pe_, a_[:, 0:W], a_[:, 0:W], b_[:, 0:W], ALU.add, [a_, b_], [a_])
                    k.tt(pe_, q_[:, 0:W], a_[:, 0:W], r_[:, 0:W], ALU.mult, [a_, r_], [q_])
                    for hf in range(2):
                        if "dma" in QS: continue
                        if pr < 4:
                            dst = S["QT"][2 * pr + hf, :, t0:t0 + W]
                        else:
                            dst = S["KT"][2 * (pr - 4) + hf, :, t0:t0 + W]
                        k.dma("pool", dst, q_[hf * 64:(hf + 1) * 64, 0:W], reads=[q_])
                for tj in range(W // 128):
                    if "v" not in PARTS: continue
                    n = nq[0]; nq[0] += 1
                    vp = zps[n % 4]
                    for kk in range(8):
                        k.mm(vp[:, 0:256], h[:, kk, 2 + tj * 128:2 + (tj + 1) * 128], w[:, kk, 768:1024], kk == 0, kk == 7, [w, h], [vp])
                    v_ = vo[n % 2]
                    k.copy("act", v_[:, :], vp[:, 0:256], [vp], [v_])
                    k.dma("pool", S["V"][t0 + tj * 128:t0 + (tj + 1) * 128, :], v_[:, :], reads=[v_])
                def convproj(m, dst):
                    for (c0, c1_) in ((0, min(512, Wc)), (512, Wc)):
                        if c1_ <= c0: continue
                        for kk in range(8):
                            k.mm(cps[:, c0:c1_], w[:, kk, 1024 + m * 128:1024 + (m + 1) * 128], h[:, kk, c0:c1_], kk == 0, kk == 7, [w, h], [cps])
                    k.copy("act", dst[:, 0:Wc], cps[:, 0:Wc], [cps], [dst])
                    if le: k.memset("pool", dst[:, 1:2], 0.0, [dst])
                    if re: k.memset("pool", dst[:, W + 2:W + 3], 0.0, [dst])
                def conv3(out, a, wts, m, bias, eng="dve"):
                    if bias is not None:
                        k.ts(eng, out[:, 0:W], a[:, 2:W + 2], wts[:, m, 1:2], bias, ALU.mult, ALU.add, [a, wts, cb], [out])
                    else:
                        k.ts(eng, out[:, 0:W], a[:, 2:W + 2], wts[:, m, 1:2], None, ALU.mult, None, [a, wts], [out])
                    k.stt(eng, out[:, 0:W], a[:, 1:W + 1], wts[:, m, 0:1], out[:, 0:W], ALU.mult, ALU.add, [a, wts, out], [out])
                    k.stt(eng, out[:, 0:W], a[:, 3:W + 3], wts[:, m, 2:3], out[:, 0:W], ALU.mult, ALU.add, [a, wts, out], [out])
                for jc in range(4):
                    if "c" not in PARTS: continue
                    if i == 0:
                        convproj(4 + jc, asb[0]); conv3(c1[0], asb[0], cw, 4 + jc, cb[:, 4 + jc:5 + jc])
                        convproj(8 + jc, asb[1]); conv3(c1[1], asb[1], cw, 8 + jc, cb[:, 8 + jc:9 + jc])
                        u_ = uo[jc % 2]; ub_ = ub[jc % 2]
                        k.tt("dve", u_[:, 0:W], c1[0][:, 0:W], c1[1][:, 0:W], ALU.mult, [c1[0], c1[1]], [u_])
                        k.copy("pool", ub_[:, 0:W], u_[:, 0:W], [u_], [ub_])
                        k.dma("pool", S["UF"][jc * 128:(jc + 1) * 128, t0:t0 + W], u_[:, 0:W], reads=[u_])
                        k.dma("pool", S["UT"][jc * 128:(jc + 1) * 128, t0:t0 + W], ub_[:, 0:W], reads=[ub_])
                        if do_q:
                            convproj(jc, asb[2]); conv3(c1[2], asb[2], cw, jc, cb[:, jc:jc + 1])
                            k.dma("pool", S["X0T"][jc * 128:(jc + 1) * 128, t0:t0 + W], c1[2][:, 0:W], reads=[c1[2]])
                    elif j == 0:
                        convproj(4 + jc, asb[0]); convproj(8 + jc, asb[1])
                        p_ = pp[jc % 2]
                        k.tt("dve", p_[:, 0:Wc], asb[0][:, 0:Wc], asb[1][:, 0:Wc], ALU.mult, [asb[0], asb[1]], [p_])
                        conv3(c1[0], p_, cw, jc, None)
                        convproj(jc, asb[2])
                        ub_ = ub[jc % 2]
                        k.tt("pool", ub_[:, 0:W], c1[0][:, 0:W], asb[2][:, 2:W + 2], ALU.mult, [c1[0], asb[2]], [ub_])
                        k.dma("pool", S["OCT"][jc * 128:(jc + 1) * 128, t0:t0 + W], ub_[:, 0:W], reads=[ub_])
        return ph

    def make_attn(i):
        def ph():
            NKT = TT // 128
            kT = k.sbuf([128, TT], BF16, "kT")
            k.memset("pool", kT[64:128, :], 0.0, [kT])
            va = k.sbuf([128, NKT, 65], BF16, "va")
            qT = [k.sbuf([128, 512], BF16, "qT") for _ in range(2)]
            for t_ in qT: k.memset("pool", t_[64:128, :], 0.0, [t_])
            sps = [k.psum([128, 512], F32, "sps") for _ in range(4)]
            ops_ = [k.psum([128, 512], F32, "ops") for _ in range(2)]
            bps = k.psum([128, 512], F32, "bps")
            pT = [k.sbuf([128, 512], BF16, "pT") for _ in range(4)]
            osb = [k.sbuf([65, 512], F32, "osb") for _ in range(2)]
            rb = [k.sbuf([64, 512], F32, "rb") for _ in range(2)]
            ob = [k.sbuf([64, 512], BF16, "ob") for _ in range(2)]
            if i == 1:
                bm = k.sbuf([128, 6, 512], BF16, "bm")
                for r in range(6):
                    k.dma("sp", bm[:, r, :], I["bmask"][r, :, :], writes=[bm])
            qch = []
            for c in range(9):
                t0 = c * 512
                if i == 0:
                    kts = [(kt, 0, 512, None) for kt in range(NKT)]
                else:
                    kts = []
                    for r in range(-1, 5):
                        kt = 4 * c + r
                        if kt < 0 or kt >= E // 128: continue
                        f0 = max(0, 128 * (r - 1)); f1 = min(512, 128 * (r + 2))
                        kts.append((kt, f0, f1, r + 1))
                    kts += [(64, 0, 512, None), (65, 0, 512, None)]
                qch.append((t0, 512, kts))
            if i == 0:
                qch.append((SEQ, CTX, [(64, 0, CTX, None), (65, 0, CTX, None)]))
            n = 0; nh = 0
            cvt = make_converter() if i == 0 else None
            for jkv in range(4):
                k.dma("sp", kT[0:64, :], S["KT"][jkv, :, :], writes=[kT])
                k.dma("sp", va[:, :, 0:64], S["V"][:, jkv * 64:(jkv + 1) * 64].rearrange("(n p) d -> p n d", p=128), writes=[va])
                k.memset("pool", va[:, :, 64:65], 1.0, [va])
                for g in range(2):
                    hq = 2 * jkv + g
                    for (t0, W, kts) in qch:
                        q_ = qT[nh % 2]; op_ = ops_[nh % 2]; o_ = osb[nh % 2]; r_ = rb[nh % 2]; b_ = ob[nh % 2]
                        nh += 1
                        k.dma("sp", q_[0:64, 0:W], S["QT"][hq, :, t0:t0 + W], writes=[q_])
                        if i == 1:
                            pass
                        nk = len(kts)
                        def smm(idx):
                            kt, f0, f1, mi = kts[idx]
                            sp_ = sps[(n + idx) % 4]
                            k.mm(sp_[:, f0:f1], kT[:, kt * 128:(kt + 1) * 128], q_[:, f0:f1], True, True, [kT, q_], [sp_])
                        smm(0)
                        if nk > 1: smm(1)
                        for idx in range(nk):
                            if idx + 2 < nk: smm(idx + 2)
                            kt, f0, f1, mi = kts[idx]
                            sp_ = sps[(n + idx) % 4]; p_ = pT[(n + idx) % 4]
                            if mi is not None and (f0 > 0 or f1 < W):
                                k.memset("dve", p_[:, 0:W], 0.0, [p_])
                            k.act(p_[:, f0:f1], sp_[:, f0:f1], AF.Exp, [sp_], [p_], scale=0.125)
                            if mi is not None:
                                k.tt("dve", p_[:, f0:f1], p_[:, f0:f1], bm[:, mi, f0:f1], ALU.mult, [p_, bm], [p_])
                            k.mm(op_[0:65, 0:W], va[:, kt, :], p_[:, 0:W], idx == 0, idx == nk - 1, [va, p_], [op_])
                        n += nk
                        k.copy("act", o_[:, 0:W], op_[0:65, 0:W], [op_], [o_])
                        k.mm(bps[0:64, 0:W], SEL[0:65, :], o_[0:65, 0:W], True, True, [SEL, o_], [bps])
                        if i == 1:
                            k.ts("dve", r_[:, 0:W], bps[0:64, 0:W], ESINK[0:64, hq:hq + 1], None, ALU.add, None, [bps, ESINK], [r_])
                            k.recip(r_[:, 0:W], r_[:, 0:W], [r_], [r_])
                        else:
                            k.recip(r_[:, 0:W], bps[0:64, 0:W], [bps], [r_])
                        k.tt("dve", b_[:, 0:W], o_[0:64, 0:W], r_[:, 0:W], ALU.mult, [o_, r_], [b_])
                        k.dma("pool", S["OAT"][hq * 64:(hq + 1) * 64, t0:t0 + W], b_[:, 0:W], reads=[b_])
                        if cvt is not None:
                            cvt.step("dve", "pool"); cvt.step("dve", "pool")
            if cvt is not None:
                while cvt.step("dve", "pool"): pass
        return ph

    def make_outproj(i, XIN, XOUT, chunks):
        def ph():
            wa = k.sbuf([128, 4, D], BF16, "woa")
            wc = k.sbuf([128, 4, D], BF16, "woc")
            k.dma("sp", wa[:], S[f"bw_out{i}"][0:512, :].rearrange("(c p) f -> p c f", p=128), writes=[wa])
            k.dma("sp", wc[:], S[f"bw_out{i}"][512:1024, :].rearrange("(c p) f -> p c f", p=128), writes=[wc])
            oa = [k.sbuf([128, 4, 512], BF16, "oa") for _ in range(2)]
            oc = [k.sbuf([128, 4, 512], BF16, "oc") for _ in range(2)]
            xs = [k.sbuf([128, 8, 512], F32, "xs") for _ in range(2)]
            xo = [k.sbuf([128, 8, 512], F32, "xo") for _ in range(2)]
            ps = [k.psum([128, 512], F32, "ps") for _ in range(4)]
            n = 0
            for ci, (t0, W, le, re, j) in enumerate(chunks):
                a_ = oa[ci % 2]; c_ = oc[ci % 2]; x_ = xs[ci % 2]; o_ = xo[ci % 2]
                k.dma("sp", a_[:, :, 0:W], S["OAT"][:, t0:t0 + W].rearrange("(c p) t -> p c t", p=128), writes=[a_])
                k.dma("sp", c_[:, :, 0:W], S["OCT"][:, t0:t0 + W].rearrange("(c p) t -> p c t", p=128), writes=[c_])
                k.dma("sp", x_[:, :, 0:W], XIN[:, t0:t0 + W].rearrange("(k p) t -> p k t", p=128), writes=[x_])
                for m in range(8):
                    p_ = ps[n % 4]; n += 1
                    for hh_ in range(4):
                        k.mm(p_[:, 0:W], wa[:, hh_, m * 128:(m + 1) * 128], a_[:, hh_, 0:W], hh_ == 0, False, [wa, a_], [p_])
                    for cc in range(4):
                        k.mm(p_[:, 0:W], wc[:, cc, m * 128:(m + 1) * 128], c_[:, cc, 0:W], False, cc == 3, [wc, c_], [p_])
                    k.stt("dve", o_[:, m, 0:W], p_[:, 0:W], MOD[:, i, 16 + m, j:j + 1], x_[:, m, 0:W], ALU.mult, ALU.add, [p_, MOD, x_], [o_])
                k.dma("pool", XOUT[:, t0:t0 + W].rearrange("(k p) t -> p k t", p=128), o_[:, :, 0:W], reads=[o_])
        return ph

    def make_ffn(i, XIN, XOUT, chunks, final=False):
        def ph():
            wu = k.sbuf([128, 8, 2 * DFF], BF16, "wu")
            for kk in range(8):
                k.dma("sp", wu[:, kk, :], S[f"bw_up{i}"][kk * 128:(kk + 1) * 128, :], writes=[wu])
            wd = [k.sbuf([128, NM, 128], BF16, "wd") for _ in range(2)]
            cw = k.sbuf([128, NM, 3], F32); k.dma("sp", cw[:], I[f"fcw{i}"][:, :, :], writes=[cw])
            cb = k.sbuf([128, NM], F32); k.dma("sp", cb[:], I[f"fcb{i}"][:, :], writes=[cb])
            xh = k.sbuf([128, 8, 516], F32, "xh")
            k.memset("pool", xh[:], 0.0, [xh])
            sq = k.sbuf([128, 8, 516], BF16, "sq")
            h = k.sbuf([128, 8, 516], BF16, "h")
            rstd = k.sbuf([128, 516], F32, "rstd")
            tmp = [k.sbuf([128, 516], F32, "tmp") for _ in range(2)]
            gg = k.sbuf([128, NM, 512], BF16, "gg")
            asb = [k.sbuf([128, 516], F32, "asb") for _ in range(2)]
            c1 = [k.sbuf([128, 512], F32, "c1") for _ in range(2)]
            xo = [k.sbuf([128, 512], F32, "xo") for _ in range(2)]
            ssps = k.psum([128, 1024], F32, "ssps")
            aps = [k.psum([128, 1024], F32, "aps") for _ in range(2)]
            vps = [k.psum([128, 512], F32, "vps") for _ in range(2)]
            n = 0; nd = 0
            for ci, (t0, W, le, re, j) in enumerate(chunks):
                Wc = W + 4
                load_xh(xh, XIN, t0, W, le, re)
                norm_mod(xh, Wc, i, 1, j, sq, ssps, rstd, tmp, h)
                for m in range(NM):
                    ap_ = aps[n % 2]; vp_ = vps[n % 2]; a_ = asb[n % 2]; c_ = c1[n % 2]; n += 1
                    for (c0, c1_) in ((0, min(512, Wc)), (512, Wc)):
                        if c1_ <= c0: continue
                        for kk in range(8):
                            k.mm(ap_[:, c0:c1_], wu[:, kk, m * 128:(m + 1) * 128], h[:, kk, c0:c1_], kk == 0, kk == 7, [wu, h], [ap_])
                    for kk in range(8):
                        k.mm(vp_[:, 0:W], wu[:, kk, DFF + m * 128:DFF + (m + 1) * 128], h[:, kk, 2:W + 2], kk == 0, kk == 7, [wu, h], [vp_])
                    k.copy("act", a_[:, 0:Wc], ap_[:, 0:Wc], [ap_], [a_])
                    if le: k.memset("pool", a_[:, 1:2], 0.0, [a_])
                    if re: k.memset("pool", a_[:, W + 2:W + 3], 0.0, [a_])
                    eng = "dve"
                    k.ts(eng, c_[:, 0:W], a_[:, 2:W + 2], cw[:, m, 1:2], cb[:, m:m + 1], ALU.mult, ALU.add, [a_, cw, cb], [c_])
                    k.stt(eng, c_[:, 0:W], a_[:, 1:W + 1], cw[:, m, 0:1], c_[:, 0:W], ALU.mult, ALU.add, [a_, cw, c_], [c_])
                    k.stt(eng, c_[:, 0:W], a_[:, 3:W + 3], cw[:, m, 2:3], c_[:, 0:W], ALU.mult, ALU.add, [a_, cw, c_], [c_])
                    k.act(c_[:, 0:W], c_[:, 0:W], AF.Gelu_apprx_tanh, [c_], [c_])
                    k.tt("dve", gg[:, m, 0:W], c_[:, 0:W], vp_[:, 0:W], ALU.mult, [c_, vp_], [gg])
                for mo in range(8):
                    wd_ = wd[nd % 2]; o_ = xo[nd % 2]; p_ = vps[nd % 2]; nd += 1
                    k.dma("sp", wd_[:], S[f"bw_dn{i}"][:, mo * 128:(mo + 1) * 128].rearrange("(m p) f -> p m f", p=128), writes=[wd_])
                    for m in range(NM):
                        k.mm(p_[:, 0:W], wd_[:, m, :], gg[:, m, 0:W], m == 0, m == NM - 1, [wd_, gg], [p_])
                    k.stt("dve", o_[:, 0:W], p_[:, 0:W], MOD[:, i, 40 + mo, j:j + 1], xh[:, mo, 2:W + 2], ALU.mult, ALU.add, [p_, MOD, xh], [o_])
                    k.dma("pool", XOUT[mo * 128:(mo + 1) * 128, t0:t0 + W], o_[:, 0:W], reads=[o_])
        return ph

    def make_filter(tag, n):
        KRAW = S["KRAW" + tag]; KN = S["KN" + tag]
        def ph():
            w1 = k.sbuf([33, 64], F32); k.dma("sp", w1[:], I["hw1"][:, :], writes=[w1])
            w2 = k.sbuf([64, 64], F32); k.dma("sp", w2[:], I["hw2"][:, :], writes=[w2])
            w3 = k.sbuf([64, 64], F32); k.dma("sp", w3[:], I["hw3"][:, :], writes=[w3])
            w4 = k.sbuf([64, 1024], F32); k.dma("sp", w4[:], I["hw4"][:, :], writes=[w4])
            hb = k.sbuf([64, 4], F32); k.dma("sp", hb[:], I["hb"][:, :], writes=[hb])
            ndel = k.sbuf([128, 4], F32); k.dma("sp", ndel[:], I["ndel"][:, :], writes=[ndel])
            asum = k.sbuf([128, 4, 40], F32, "asum")
            k.memset("dve", asum[:], 0.0, [asum])
            ft = [k.sbuf([33, 512], F32, "ft") for _ in range(2)]
            t01 = [k.sbuf([128, 512], F32, "t01") for _ in range(2)]
            hid2 = [[k.sbuf([64, 512], F32, "hid") for _ in range(3)] for _ in range(2)]
            ki2 = [k.sbuf([64, 512], I32, "ki") for _ in range(2)]
            win = [k.sbuf([128, 512], F32, "win") for _ in range(2)]
            kr = [k.sbuf([128, 512], F32, "kr") for _ in range(2)]
            junk = k.sbuf([128, 512], F32, "junk")
            krb = [k.sbuf([128, 512], BF16, "krb") for _ in range(2)]
            ps = [k.psum([128, 512], F32, "ps") for _ in range(4)]
            NCH = (2 * n) // 512
            n_ = 0
            for c in range(NCH):
                q0 = c * 512
                f_ = ft[c % 2]; t_ = t01[c % 2]
                k.dma("sp", f_[:, :], I["featsT" + tag][:, q0:q0 + 512], writes=[f_])
                k.dma("sp", t_[:, :], I["t01b" + tag][:, q0:q0 + 512], writes=[t_])
                src = f_; srcK = 33
                hid = hid2[c % 2]; ki = ki2[c % 2]
                for li, wl in enumerate((w1, w2, w3)):
                    p_ = ps[n_ % 4]; n_ += 1
                    k.mm(p_[0:64, :], wl[0:srcK, :], src[0:srcK, :], True, True, [wl, src], [p_])
                    hd = hid[li]
                    k.ts("dve", hd[:, :], p_[0:64, :], hb[:, li:li + 1], hb[:, 3:4], ALU.add, ALU.mult, [p_, hb], [hd])
                    k.ts("dve", ki[:, :], hd[:, :], float(1.0 / (2 * np.pi)), None, ALU.mult, None, [hd], [ki])
                    k.stt("dve", hd[:, :], ki[:, :], float(-2 * np.pi), hd[:, :], ALU.mult, ALU.add, [ki, hd], [hd])
                    k.act(hd[:, :], hd[:, :], AF.Sin, [hd], [hd])
                    src = hd; srcK = 64
                segs = []
                if q0 + 512 <= n: segs = [(0, 512, 0)]
                elif q0 >= n: segs = [(0, 512, 512)]
                else: segs = [(0, n - q0, 0), (n - q0, 512, 512)]
                for jc in range(4):
                    p_ = ps[n_ % 4]; n_ += 1
                    for (a0, a1, off) in segs:
                        k.mm(p_[:, a0:a1], w4[:, off + jc * 128:off + (jc + 1) * 128], hid[2][:, a0:a1], True, True, [w4, hid[2]], [p_])
                    wn = win[jc % 2]; kr_ = kr[jc % 2]
                    k.act(wn[:, :], t_[:, :], AF.Exp, [t_, ndel], [wn], scale=ndel[:, jc:jc + 1])
                    k.stt("dve", kr_[:, :], wn[:, :], 0.05, p_[:, :], ALU.add, ALU.mult, [wn, p_], [kr_])
                    if q0 <= n < q0 + 512:
                        k.memset("dve", kr_[:, n - q0:n - q0 + 1], 0.0, [kr_])
                    k.act(junk[:, :], kr_[:, :], AF.Abs, [kr_], [junk, asum], accum=asum[:, jc, c:c + 1])
                    kb_ = krb[jc % 2]
                    k.copy("pool", kb_[:, :], kr_[:, :], [kr_], [kb_])
                    k.dma("pool", KN[jc * 128:(jc + 1) * 128, q0:q0 + 512], kb_[:, :], reads=[kb_])
            rn = RNORM[tag]
            for jc in range(4):
                k.op("dve", lambda e, jc=jc: e.reduce_sum(out=rn[:, jc:jc + 1], in_=asum[:, jc, 0:NCH], axis=mybir.AxisListType.X), [asum], [rn])
            k.recip(rn[:, :], rn[:, :], [rn], [rn], force_self=True)
        return ph

    def make_fftconv(tag, NA, n, tok0, n_out_blocks):
        KN = S["KN" + tag]
        NR = NA // 2
        GF = 512 // NA
        GB = 512 // (2 * NA)
        def ph():
            cst = {}
            for nm, shp, dt in (("f1cs", [NA, 2 * NA], BF16), ("fC", [128, 128], BF16), ("fS", [128, 128], BF16), ("fnS", [128, 128], BF16),
                                ("fCS", [128, 256], BF16), ("fnSC", [128, 256], BF16), ("twA", [128, 512], F32), ("twB", [128, 512], F32),
                                ("twA2", [NA, 1024], F32), ("twB2", [NA, 1024], F32), ("g3C", [NA, NA], BF16), ("g3nS", [NA, NA], BF16)):
                cst[nm] = k.sbuf(shp, dt, nm)
                k.dma("sp", cst[nm][:], I[nm + tag][tuple(slice(None) for _ in shp)], writes=[cst[nm]])
            if NA == 4:
                for nm, shp, dt in (("twA2c", [128, 256], F32), ("twB2c", [128, 256], F32), ("gbC", [128, 64], BF16), ("gbnS", [128, 64], BF16)):
                    cst[nm] = k.sbuf(shp, dt, nm)
                    k.dma("sp", cst[nm][:], I[nm + tag][:, :], writes=[cst[nm]])
                c2a = [k.sbuf([128, 256], F32, "c2a") for _ in range(2)]; c2b = [k.sbuf([128, 256], F32, "c2b") for _ in range(2)]
                y3c = [k.sbuf([128, 2, 128], BF16, "y3c") for _ in range(2)]
                yoc = [k.sbuf([64, 128], F32, "yoc") for _ in range(2)]
            LG = 32 if NA == 128 else 128
            NXB = 2 if NA == 128 else 1
            xu = [k.sbuf([NA, LG, 128], BF16, "xu") for _ in range(NXB)]
            for t_ in xu: k.memset("pool", t_[:], 0.0, [t_])
            xk = [k.sbuf([NA, LG, 128], BF16, "xk") for _ in range(NXB)]
            s1 = [k.psum([128, 512], F32, "s1") for _ in range(2)]
            s2 = [k.psum([128, 512], F32, "s2") for _ in range(2)]
            s3 = [k.psum([128, 1024], F32, "s3") for _ in range(1)]
            s4 = [k.psum([128, 512], F32, "s4") for _ in range(2)]
            NB = 2
            ta = [k.sbuf([128, 512], F32, "ta") for _ in range(NB)]; tb = [k.sbuf([128, 512], F32, "tb") for _ in range(NB)]
            bu = [k.sbuf([128, 2, GF, NA], BF16, "bu") for _ in range(NB)]; bk = [k.sbuf([128, 2, GF, NA], BF16, "bk") for _ in range(NB)]
            kh = [k.sbuf([128, 2, 512], F32, "kh") for _ in range(NB)]
            m1 = [k.sbuf([128, 512], F32, "m1") for _ in range(NB)]; m2 = [k.sbuf([128, 512], F32, "m2") for _ in range(NB)]
            m3 = [k.sbuf([128, 512], F32, "m3") for _ in range(NB)]; m4 = [k.sbuf([128, 512], F32, "m4") for _ in range(NB)]
            yh = [k.sbuf([128, 2, GF, NA], BF16, "yh") for _ in range(NB)]
            t2a = [k.sbuf([NA, 1024], F32, "t2a") for _ in range(NB)]; t2b = [k.sbuf([NA, 1024], F32, "t2b") for _ in range(NB)]
            y3 = [k.sbuf([NA, 2, 4, 128], BF16, "y3") for _ in range(NB)]
            yo = [k.sbuf([n_out_blocks, 4, 128], F32, "yo") for _ in range(2)]
            MO = n_out_blocks
            cnt = {"tw": 0, "inv": 0, "g": 0}
            def fwd(x, cbase, bdst):
                for half in range(2):
                    bank = s1[half]
                    ta_ = ta[cnt["tw"] % NB]; tb_ = tb[cnt["tw"] % NB]; cnt["tw"] += 1
                    for cc in range(GB):
                        ch = cbase + half * GB + cc
                        k.mm(bank[:, cc * 2 * NA:(cc + 1) * 2 * NA], x[0:NA, ch, :], cst["f1cs"][0:NA, :], True, True, [x, cst["f1cs"]], [bank])
                    k.tt("dve", ta_[:, :], bank[:, :], cst["twA"][:, :], ALU.mult, [bank, cst["twA"]], [ta_])
                    k.tt("dve", tb_[:, :], bank[:, :], cst["twB"][:, :], ALU.mult, [bank, cst["twB"]], [tb_])
                    tav = ta_[:, :].rearrange("p (g r f) -> p g r f", g=GB, r=2)
                    tbv = tb_[:, :].rearrange("p (g r f) -> p g r f", g=GB, r=2)
                    k.tt("dve", bdst[:, 0, half * GB:(half + 1) * GB, :], tav[:, :, 0, :], tbv[:, :, 1, :], ALU.subtract, [ta_, tb_], [bdst])
                    k.tt("dve", bdst[:, 1, half * GB:(half + 1) * GB, :], tav[:, :, 1, :], tbv[:, :, 0, :], ALU.subtract, [ta_, tb_], [bdst])
                bre = bdst[:, 0, :, :].rearrange("p g f -> p (g f)"); bim = bdst[:, 1, :, :].rearrange("p g f -> p (g f)")
                k.mm(s2[0][:, :], cst["fC"][:, :], bre, True, False, [cst["fC"], bdst], [s2[0]])
                k.mm(s2[0][:, :], cst["fS"][:, :], bim, False, True, [cst["fS"], bdst], [s2[0]])
                k.mm(s2[1][:, :], cst["fC"][:, :], bim, True, False, [cst["fC"], bdst], [s2[1]])
                k.mm(s2[1][:, :], cst["fnS"][:, :], bre, False, True, [cst["fnS"], bdst], [s2[1]])
            for lg in range(512 // LG):
                xu_ = xu[lg % NXB]; xk_ = xk[lg % NXB]
                k.dma("sp", xu_[0:NR, :, :], S["UT"][lg * LG:(lg + 1) * LG, tok0:tok0 + n].rearrange("c (a p) -> a c p", p=128), writes=[xu_])
                k.dma("sp", xk_[:, :, :], KN[lg * LG:(lg + 1) * LG, :].rearrange("c (a p) -> a c p", p=128), writes=[xk_])
                for gf in range(LG // GF):
                    cbase = gf * GF
                    g_ = cnt["g"] % NB; cnt["g"] += 1
                    kh_ = kh[g_]; yh_ = yh[g_]
                    fwd(xk_, cbase, bk[g_])
                    k.copy("act", kh_[:, 0, :], s2[0][:, :], [s2[0]], [kh_])
                    k.copy("act", kh_[:, 1, :], s2[1][:, :], [s2[1]], [kh_])
                    fwd(xu_, cbase, bu[g_])
                    k.tt("dve", m1[g_][:, :], s2[0][:, :], kh_[:, 0, :], ALU.mult, [s2[0], kh_], [m1[g_]])
                    k.tt("dve", m3[g_][:, :], s2[0][:, :], kh_[:, 1, :], ALU.mult, [s2[0], kh_], [m3[g_]])
                    k.tt("dve", m2[g_][:, :], s2[1][:, :], kh_[:, 1, :], ALU.mult, [s2[1], kh_], [m2[g_]])
                    k.tt("dve", m4[g_][:, :], s2[1][:, :], kh_[:, 0, :], ALU.mult, [s2[1], kh_], [m4[g_]])
                    k.tt("pool", yh_[:, 0, :, :].rearrange("p g f -> p (g f)"), m1[g_][:, :], m2[g_][:, :], ALU.subtract, [m1[g_], m2[g_]], [yh_])
                    k.tt("pool", yh_[:, 1, :, :].rearrange("p g f -> p (g f)"), m3[g_][:, :], m4[g_][:, :], ALU.add, [m3[g_], m4[g_]], [yh_])
                    if NA == 4:
                        for sg in range(GF // 32):
                            b3 = s3[0]
                            iv = cnt["inv"] % 2; cnt["inv"] += 1
                            lre = yh_[:, 0, sg * 32:(sg + 1) * 32, :].rearrange("p g f -> p (g f)")
                            lim = yh_[:, 1, sg * 32:(sg + 1) * 32, :].rearrange("p g f -> p (g f)")
                            k.mm(b3[:, 0:256], lre, cst["fCS"][:, :], True, False, [yh_, cst["fCS"]], [b3])
                            k.mm(b3[:, 0:256], lim, cst["fnSC"][:, :], False, True, [yh_, cst["fnSC"]], [b3])
                            a_ = c2a[iv]; b_ = c2b[iv]; y_ = y3c[iv]; o_ = yoc[iv]
                            k.tt("dve", a_[:, :], b3[:, 0:256], cst["twA2c"][:, :], ALU.mult, [b3, cst["twA2c"]], [a_])
                            k.tt("dve", b_[:, :], b3[:, 0:256], cst["twB2c"][:, :], ALU.mult, [b3, cst["twB2c"]], [b_])
                            k.tt("pool", y_[:, 0, :], a_[:, 0:128], b_[:, 128:256], ALU.add, [a_, b_], [y_])
                            k.tt("pool", y_[:, 1, :], a_[:, 128:256], b_[:, 0:128], ALU.add, [a_, b_], [y_])
                            p4 = s4[iv]
                            k.mm(p4[0:64, 0:128], cst["gbC"][:, :], y_[:, 0, :], True, False, [cst["gbC"], y_], [p4])
                            k.mm(p4[0:64, 0:128], cst["gbnS"][:, :], y_[:, 1, :], False, True, [cst["gbnS"], y_], [p4])
                            k.copy("act", o_[:, :], p4[0:64, 0:128], [p4], [o_])
                            c0 = lg * LG + cbase + sg * 32
                            for a2 in range(2):
                                k.dma("sp", S["YT"][c0:c0 + 32, tok0 + a2 * 128:tok0 + (a2 + 1) * 128], o_[a2:64:2, :], reads=[o_])
                        continue
                    for sg in range(GF // 4):
                        b3 = s3[0]
                        iv = cnt["inv"] % NB; cnt["inv"] += 1
                        t2a_ = t2a[iv]; t2b_ = t2b[iv]; y3_ = y3[iv]
                        for cc in range(4):
                            ch = sg * 4 + cc
                            k.mm(b3[0:NA, cc * 256:(cc + 1) * 256], yh_[:, 0, ch, :], cst["fCS"][:, :], True, False, [yh_, cst["fCS"]], [b3])
                            k.mm(b3[0:NA, cc * 256:(cc + 1) * 256], yh_[:, 1, ch, :], cst["fnSC"][:, :], False, True, [yh_, cst["fnSC"]], [b3])
                        k.tt("dve", t2a_[:, :], b3[0:NA, :], cst["twA2"][:, :], ALU.mult, [b3, cst["twA2"]], [t2a_])
                        k.tt("dve", t2b_[:, :], b3[0:NA, :], cst["twB2"][:, :], ALU.mult, [b3, cst["twB2"]], [t2b_])
                        av = t2a_[:, :].rearrange("p (g r f) -> p g r f", g=4, r=2)
                        bv = t2b_[:, :].rearrange("p (g r f) -> p g r f", g=4, r=2)
                        k.tt("pool", y3_[:, 0, :, :], av[:, :, 0, :], bv[:, :, 1, :], ALU.add, [t2a_, t2b_], [y3_])
                        k.tt("pool", y3_[:, 1, :, :], av[:, :, 1, :], bv[:, :, 0, :], ALU.add, [t2a_, t2b_], [y3_])
                        p4 = s4[iv % 2]; yo_ = yo[iv % 2]
                        k.mm(p4[0:MO, :], cst["g3C"][:, 0:MO], y3_[:, 0, :, :].rearrange("p g f -> p (g f)"), True, False, [cst["g3C"], y3_], [p4])
                        k.mm(p4[0:MO, :], cst["g3nS"][:, 0:MO], y3_[:, 1, :, :].rearrange("p g f -> p (g f)"), False, True, [cst["g3nS"], y3_], [p4])
                        k.copy("act", yo_[:, :, :].rearrange("p g f -> p (g f)"), p4[0:MO, :], [p4], [yo_])
                        c0 = lg * LG + cbase + sg * 4
                        k.dma("sp", S["YT"][c0:c0 + 4, tok0:tok0 + MO * 128].rearrange("c (a p) -> a c p", p=128), yo_[:, :, :], reads=[yo_])
        return ph

    def make_hycombine(chunks):
        def ph():
            bd = k.sbuf([128, 4], F32); k.dma("sp", bd[:], I["hbd"][:, :], writes=[bd])
            yt = [k.sbuf([128, 512], F32, "yt") for _ in range(2)]
            ut = [k.sbuf([128, 512], F32, "ut") for _ in range(2)]
            x0 = [k.sbuf([128, 512], F32, "x0") for _ in range(2)]
            ob = [k.sbuf([128, 512], BF16, "ob") for _ in range(2)]
            n = 0
            for (t0, W, le, re, j) in chunks:
                rn = RNORM["C" if j == 1 else "L"]
                for jc in range(4):
                    y_ = yt[n % 2]; u_ = ut[n % 2]; x_ = x0[n % 2]; o_ = ob[n % 2]; n += 1
                    rows = slice(jc * 128, (jc + 1) * 128)
                    k.dma("sp", y_[:, 0:W], S["YT"][rows, t0:t0 + W], writes=[y_])
                    k.dma("sp", u_[:, 0:W], S["UF"][rows, t0:t0 + W], writes=[u_])
                    k.dma("sp", x_[:, 0:W], S["X0T"][rows, t0:t0 + W], writes=[x_])
                    k.ts("dve", y_[:, 0:W], y_[:, 0:W], rn[:, jc:jc + 1], None, ALU.mult, None, [y_, rn], [y_])
                    k.stt("dve", y_[:, 0:W], u_[:, 0:W], bd[:, jc:jc + 1], y_[:, 0:W], ALU.mult, ALU.add, [u_, bd, y_], [y_])
                    k.tt("dve", o_[:, 0:W], y_[:, 0:W], x_[:, 0:W], ALU.mult, [y_, x_], [o_])
                    k.dma("pool", S["OCT"][rows, t0:t0 + W], o_[:, 0:W], reads=[o_])
        return ph

    CH_E = [(c * 512, 512, c == 0, c == 8, 0) for c in range(9)]
    CH_OWN = [(c * 512, 512, c == 0, False, 0) for c in range(8)]
    phases.append(("inproj0", make_inproj(0, I["xt0"], CHUNKS_ALL)))
    phases.append(("filterL", make_filter("L", SEQ)))
    phases.append(("filterC", make_filter("C", CTX)))
    phases.append(("fftL", make_fftconv("L", 128, SEQ, 0, E // 128)))
    phases.append(("fftC", make_fftconv("C", 4, CTX, SEQ, 2)))
    phases.append(("hycomb", make_hycombine(CH_E + [CTXCH])))
    phases.append(("attn0", make_attn(0)))
    phases.append(("outproj0", make_outproj(0, I["xt0"], S["XM"], CH_E + [CTXCH])))
    phases.append(("ffn0", make_ffn(0, S["XM"], S["X1"], CH_E + [CTXCH])))
    phases.append(("inproj1", make_inproj(1, S["X1"], CH_E + [CTXCH])))
    phases.append(("attn1", make_attn(1)))
    phases.append(("outproj1", make_outproj(1, S["X1"], S["XM1"], CH_E)))
    phases.append(("ffn1", make_ffn(1, S["XM1"], OUT, CH_OWN)))
    for nm, ph in phases:
        k.phase(ph)
        if stop_after == nm:
            break
    k.close()
    return nc


_CACHE = {}


def kernel(**inputs):
    inp = {kk: np.asarray(v) for kk, v in inputs.items()}
    if "C" not in _CACHE:
        C = _consts()
        C["fftL"] = _fft_consts(128); C["fftC"] = _fft_consts(4)
        C["filtL"] = _filter_consts(SEQ); C["filtC"] = _filter_consts(CTX)
        _CACHE["C"] = C
    C = _CACHE["C"]
    nc = _build()
    in_maps = []
    for core in range(8):
        b, hh = core // 2, core % 2
        in_maps.append(_host_prep(inp, b, hh, C))
    res = run_bass_kernel_spmd(nc, in_maps, core_ids=list(range(8)))
    out = np.empty((4, SEQ, D), np.float32)
    for core in range(8):
        b, hh = core // 2, core % 2
        o = np.asarray(res.results[core]["out"]).T
        if hh == 0:
            out[b, :OWN] = o
        else:
            out[b, OWN:] = o[::-1]
    return out
```

```python
import numpy as np
import ml_dtypes
from contextlib import ExitStack
import concourse.bass as bass
import concourse.mybir as mybir
from concourse.bass_utils import run_bass_kernel_spmd

F32 = mybir.dt.float32
BF16 = mybir.dt.bfloat16
I32 = mybir.dt.int32
AF = mybir.ActivationFunctionType
ALU = mybir.AluOpType
NPBF = ml_dtypes.bfloat16

D = 1024; SEQ = 8192; CTX = 256; TT = SEQ + CTX; E = 4352; OWN = 4096
DFF = 2816; NM = 22
SAME_ENGINE_SYNC = {"act", "pool"}


class Res:
    __slots__ = ("name", "last_w", "reads", "excl")
    def __init__(self, name, excl=False):
        self.name = name; self.last_w = None; self.reads = {}; self.excl = excl


class Tl:
    def __init__(self, t, r):
        self.t = t; self.r = r
    def __getitem__(self, idx):
        return self.t[idx]


class K:
    ENGS = ("pe", "act", "dve", "pool", "sp")

    def __init__(self, nc):
        self.nc = nc
        self.es = ExitStack()
        self.sem = {}; self.cnt = {}
        for e in self.ENGS:
            self.sem[e] = self.es.enter_context(nc.semaphore("s_" + e))
            self.cnt[e] = 0
        self.dma_sems = {}
        self.dma_key = {}
        self.dma_rr = {}
        self.NDMASEM = {"sp": 32, "pool": 24, "act": 8, "pe": 4, "dve": 4}
        self.seen = {e: {} for e in self.ENGS}
        self.ops = {e: [] for e in self.ENGS}
        self.phase_es = None
        self.nres = 0
        self.ndma = 0

    def sbuf(self, shape, dt, name=None, persist=False):
        self.nres += 1
        name = (name or "t") + "_%d" % self.nres
        es = self.es if persist else self.phase_es
        t = es.enter_context(self.nc.sbuf_tensor(name, list(shape), dt))
        return Tl(t, Res(name))

    def psum(self, shape, dt, name=None):
        self.nres += 1
        name = (name or "p") + "_%d" % self.nres
        t = self.phase_es.enter_context(self.nc.psum_tensor(name, list(shape), dt))
        return Tl(t, Res(name, excl=True))

    def _need(self, reads, writes, eng=None):
        evs = []
        for r in reads:
            if r.last_w is not None: evs.append(r.last_w)
            if r.excl:
                evs.extend((kk[0], kk[1], v) for kk, v in r.reads.items() if not (kk[0] == "eng" and kk[1] == eng))
        for w in writes:
            if w.last_w is not None: evs.append(w.last_w)
            evs.extend((kk[0], kk[1], v) for kk, v in w.reads.items())
        return evs

    def _emit_waits(self, eng, evs, force_self=False):
        need = {}
        for kind, key, val in evs:
            if kind == "eng":
                if key == eng and eng not in SAME_ENGINE_SYNC and not force_self: continue
                v = val
            else:
                v = val
            if self.seen[eng].get((kind, key), 0) >= v: continue
            if need.get((kind, key), 0) < v: need[(kind, key)] = v
        for (kind, key), v in need.items():
            self.seen[eng][(kind, key)] = v
            sem = self.sem[key] if kind == "eng" else self.dma_sems[key][0]
            self.ops[eng].append(lambda e, sem=sem, v=v: e.wait_ge(sem, v))

    def _commit(self, ev, reads, writes):
        for r in reads:
            kk = (ev[0], ev[1])
            if r.reads.get(kk, 0) < ev[2]: r.reads[kk] = ev[2]
        for w in writes:
            w.last_w = ev; w.reads = {}

    def op(self, eng, fn, reads=(), writes=(), force_self=False):
        reads = [x.r if isinstance(x, Tl) else x for x in reads]
        writes = [x.r if isinstance(x, Tl) else x for x in writes]
        self._emit_waits(eng, self._need(reads, writes, eng), force_self)
        self.cnt[eng] += 1
        sem = self.sem[eng]
        self.ops[eng].append(lambda e, fn=fn, sem=sem: fn(e).then_inc(sem, 1))
        self._commit(("eng", eng, self.cnt[eng]), reads, writes)

    def dma(self, q, out, in_, reads=(), writes=(), **kw):
        reads = [x.r if isinstance(x, Tl) else x for x in reads]
        writes = [x.r if isinstance(x, Tl) else x for x in writes]
        npool = self.NDMASEM[q]
        idx = (q, self.dma_rr.get(q, 0) % npool)
        self.dma_rr[q] = self.dma_rr.get(q, 0) + 1
        if idx not in self.dma_sems:
            s_ = self.es.enter_context(self.nc.semaphore("d_%s_%d" % idx))
            self.dma_sems[idx] = [s_, 0]
        ent = self.dma_sems[idx]
        evs = self._need(reads, writes, q)
        if ent[1] > 0:
            evs.append(("dma", idx, ent[1] * 16))
        self._emit_waits(q, evs)
        ent[1] += 1
        sem = ent[0]
        self.ndma += 1
        self.ops[q].append(lambda e, out=out, in_=in_, sem=sem, kw=kw: e.dma_start(out=out, in_=in_, **kw).then_inc(sem, 16))
        self._commit(("dma", idx, ent[1] * 16), reads, writes)

    def barrier(self):
        evs = [("eng", e, self.cnt[e]) for e in self.ENGS if self.cnt[e] > 0]
        evs += [("dma", kk, v[1] * 16) for kk, v in self.dma_sems.items() if v[1] > 0]
        for e in self.ENGS:
            self._emit_waits(e, [ev for ev in evs if not (ev[0] == "eng" and ev[1] == e)])

    def phase(self, body):
        with ExitStack() as pes:
            self.phase_es = pes
            body()
            self.barrier()
            ops = self.ops
            self.ops = {e: [] for e in self.ENGS}
            with self.nc.Block() as block:
                @block.tensor
                def _(e):
                    for f in ops["pe"]: f(e)
                @block.scalar
                def _(e):
                    for f in ops["act"]: f(e)
                @block.vector
                def _(e):
                    for f in ops["dve"]: f(e)
                @block.gpsimd
                def _(e):
                    for f in ops["pool"]: f(e)
                @block.sync
                def _(e):
                    for f in ops["sp"]: f(e)
        self.phase_es = None

    def close(self):
        self.es.close()

    def ts(self, eng, out, in0, s1, s2, op0, op1, r, w, force_self=False):
        if s2 is None:
            s2 = 0.0; op1 = ALU.add
        self.op(eng, lambda e: e.tensor_scalar(out=out, in0=in0, scalar1=s1, scalar2=s2, op0=op0, op1=op1), r, w, force_self)
    def stt(self, eng, out, in0, sc, in1, op0, op1, r, w):
        self.op(eng, lambda e: e.scalar_tensor_tensor(out=out, in0=in0, scalar=sc, in1=in1, op0=op0, op1=op1), r, w)
    def tt(self, eng, out, in0, in1, op, r, w):
        self.op(eng, lambda e: e.tensor_tensor(out=out, in0=in0, in1=in1, op=op), r, w)
    def act(self, out, in_, func, r, w, bias=None, scale=None, accum=None):
        kw = {}
        if bias is not None: kw["bias"] = bias
        if scale is not None: kw["scale"] = scale
        if accum is not None: kw["accum_out"] = accum
        self.op("act", lambda e: e.activation(out=out, in_=in_, func=func, **kw), r, w)
    def mm(self, out, lhsT, rhs, start, stop, r, w):
        self.op("pe", lambda e: e.matmul(out, lhsT=lhsT, rhs=rhs, start=start, stop=stop), r, w)
    def copy(self, eng, out, in_, r, w):
        if eng == "act":
            self.op("act", lambda e: e.copy(out=out, in_=in_), r, w)
        else:
            self.op(eng, lambda e: e.tensor_copy(out=out, in_=in_), r, w)
    def memset(self, eng, ap, val, w):
        self.op(eng, lambda e: e.memset(ap, val), [], w)
    def recip(self, out, in_, r, w, force_self=False):
        self.op("dve", lambda e: e.reciprocal(out=out, in_=in_), r, w, force_self)

def _consts():
    c = {}
    nf = 16
    inv = 10000.0 ** (-np.arange(nf, dtype=np.float64) / nf)
    t = np.arange(SEQ)
    row = (t // 64).astype(np.float64); col = (t % 64).astype(np.float64)
    ar = row[None, :] * inv[:, None]; ac = col[None, :] * inv[:, None]
    cos64 = np.concatenate([np.cos(ar), np.cos(ar), np.cos(ac), np.cos(ac)], 0)
    sin64 = np.concatenate([-np.sin(ar), np.sin(ar), -np.sin(ac), np.sin(ac)], 0)
    c["cos64"] = cos64.astype(np.float32); c["sin64"] = sin64.astype(np.float32)
    perm = np.concatenate([np.arange(16) + 16, np.arange(16), np.arange(16) + 48, np.arange(16) + 32])
    c["perm64"] = perm
    p = np.arange(128)[:, None]; f = np.arange(512)[None, :]
    c["bmask"] = np.stack([(np.abs(128 * r + p - f) <= 128) for r in range(-1, 5)], 0).astype(NPBF)
    return c


def _fft_consts(NA):
    N = 128 * NA
    c = {}
    a = np.arange(NA)[:, None]; f1 = np.arange(NA)[None, :]
    th = 2 * np.pi * a * f1 / NA
    c["f1cs"] = np.concatenate([np.cos(th), -np.sin(th)], 1).astype(NPBF)
    p = np.arange(128)[:, None]; f2 = np.arange(128)[None, :]
    th2 = 2 * np.pi * p * f2 / 128
    C = np.cos(th2); S = np.sin(th2)
    c["fC"] = C.astype(NPBF); c["fS"] = S.astype(NPBF); c["fnS"] = (-S).astype(NPBF)
    c["fCS"] = np.concatenate([C, S], 1).astype(NPBF)
    c["fnSC"] = np.concatenate([-S, C], 1).astype(NPBF)
    tw = 2 * np.pi * np.arange(128)[:, None] * np.arange(NA)[None, :] / N
    G = 512 // (2 * NA)
    tc_ = np.cos(tw); ts_ = np.sin(tw)
    A = np.concatenate([tc_, tc_], 1)
    B = np.concatenate([ts_, -ts_], 1)
    c["twA"] = np.tile(A[:, None, :], (1, G, 1)).reshape(128, 512).astype(np.float32)
    c["twB"] = np.tile(B[:, None, :], (1, G, 1)).reshape(128, 512).astype(np.float32)
    tcT = np.cos(tw).T; tsT = np.sin(tw).T
    A2 = np.stack([tcT, tcT], 1)
    B2 = np.stack([tsT, -tsT], 1)
    c["twA2"] = np.tile(A2[:, None], (1, 4, 1, 1)).reshape(NA, 1024).astype(np.float32)
    c["twB2"] = np.tile(B2[:, None], (1, 4, 1, 1)).reshape(NA, 1024).astype(np.float32)
    th = 2 * np.pi * np.arange(NA)[:, None] * np.arange(NA)[None, :] / NA
    c["g3C"] = (np.cos(th) / N).astype(NPBF); c["g3nS"] = (-np.sin(th) / N).astype(NPBF)
    if NA == 4:
        f1i = np.arange(128) % 4
        twp = 2 * np.pi * f1i[:, None] * np.arange(128)[None, :] / N
        c["twA2c"] = np.concatenate([np.cos(twp), np.cos(twp)], 1).astype(np.float32)
        c["twB2c"] = np.concatenate([np.sin(twp), -np.sin(twp)], 1).astype(np.float32)
        gC = np.zeros((128, 64), np.float64); gS = np.zeros((128, 64), np.float64)
        for ch in range(32):
            for f1_ in range(4):
                for a_ in range(2):
                    gC[ch * 4 + f1_, ch * 2 + a_] = np.cos(2 * np.pi * f1_ * a_ / 4) / N
                    gS[ch * 4 + f1_, ch * 2 + a_] = -np.sin(2 * np.pi * f1_ * a_ / 4) / N
        c["gbC"] = gC.astype(NPBF); c["gbnS"] = gS.astype(NPBF)
    return c


def _filter_consts(n):
    q = np.arange(2 * n)
    tap = np.where(q < n, q, 2 * n - q).astype(np.int64)
    tap = np.minimum(tap, n - 1)
    t01 = np.linspace(0.0, 1.0, n, dtype=np.float32)
    bands = 16
    w = (2.0 * np.pi * np.arange(n, dtype=np.float32) / n).astype(np.float32)
    f = np.linspace(1e-4, bands - 1, bands, dtype=np.float32)[None, :]
    feats = np.concatenate([t01[:, None], np.cos(f * w[:, None]), -np.sin(f * w[:, None])], -1).astype(np.float32)
    featsT = np.ascontiguousarray(feats[tap].T)
    t01b = np.ascontiguousarray(np.tile(t01[tap][None, :], (128, 1)))
    deltas = np.abs(np.linspace(np.log(1e-2) / 1.5, np.log(1e-2) / 0.3, 512, dtype=np.float32))
    ndel = np.ascontiguousarray((-deltas).reshape(4, 128).T)
    return featsT.astype(np.float32), t01b.astype(np.float32), ndel.astype(np.float32)


def _pm(v, n):
    return np.ascontiguousarray(np.asarray(v, np.float32).reshape(n, 128).T)


def _host_prep(inp, b, hh, C):
    fl = (hh == 1)
    m = {}
    x = inp["x"][b]; cx = inp["ctx"][b]
    if fl: x = x[::-1]; cx = cx[::-1]
    m["xt0"] = np.ascontiguousarray(np.concatenate([x, cx], 0).T)
    cv = np.stack([inp["c"][b], inp["c_ctx"]], 1)
    m["cvec"] = np.ascontiguousarray(cv.reshape(8, 128, 2).transpose(1, 0, 2))
    perm = C["perm64"]
    for i in range(2):
        m[f"ada_w{i}"] = inp["ada_w"][i]
        m[f"ada_b{i}"] = _pm(inp["ada_b"][i], 48)
        m[f"nmix{i}"] = _pm(inp["norm_mix"][i], 8)
        m[f"nffn{i}"] = _pm(inp["norm_ffn"][i], 8)
        w = inp["mix_w_in"][i]
        qk = w[:, :768].reshape(D, 12, 64)[:, :, perm].reshape(D, 768)
        m[f"w_in{i}"] = np.ascontiguousarray(np.concatenate([w, qk], 1))
        m[f"w_out{i}"] = inp["mix_w_out"][i]
        gq = inp["attn_q_norm"][i]; gk = inp["attn_k_norm"][i]
        m[f"qkg{i}"] = np.ascontiguousarray(np.stack([np.tile(gq, 2), np.tile(gq[perm], 2), np.tile(gk, 2), np.tile(gk[perm], 2)], 1).astype(np.float32))
        m[f"w_up{i}"] = inp["ffn_w_up"][i]
        m[f"w_dn{i}"] = inp["ffn_w_down"][i]
        fw = inp["ffn_conv_w"][i]
        if fl: fw = fw[::-1]
        m[f"fcw{i}"] = np.ascontiguousarray(fw.reshape(3, NM, 128).transpose(2, 1, 0))
        m[f"fcb{i}"] = _pm(inp["ffn_conv_b"][i], NM)
    hw = inp["hy_conv_w"][0]
    if fl: hw = hw[::-1]
    m["hcw"] = np.ascontiguousarray(hw.reshape(3, 12, 128).transpose(2, 1, 0))
    m["hcb"] = _pm(inp["hy_conv_b"][0], 12)
    sw = inp["sc_conv_w"][0]
    if fl: sw = sw[::-1]
    m["scw"] = np.ascontiguousarray(sw.reshape(3, 4, 128).transpose(2, 1, 0))
    m["hw1"] = inp["hy_w1"][0]; m["hw2"] = inp["hy_w2"][0]; m["hw3"] = inp["hy_w3"][0]
    w4 = inp["hy_w4"][0]
    if fl: w4 = np.concatenate([w4[:, 512:], w4[:, :512]], 1)
    m["hw4"] = np.ascontiguousarray(w4)
    m["hb"] = np.ascontiguousarray(np.stack([inp["hy_b1"][0], inp["hy_b2"][0], inp["hy_b3"][0], inp["hy_freq"][0]], 1).astype(np.float32))
    m["hbd"] = _pm(inp["hy_bias_d"][0], 4)
    m["sink"] = np.ascontiguousarray(np.tile(inp["swa_sink"][0][None, :], (128, 1)).astype(np.float32))
    cos = C["cos64"]; sin = C["sin64"]
    if fl: cos = cos[:, ::-1]; sin = sin[:, ::-1]
    cosx = np.concatenate([cos, np.ones((64, CTX), np.float32)], 1)
    sinx = np.concatenate([sin, np.zeros((64, CTX), np.float32)], 1)
    m["cosT"] = np.ascontiguousarray(np.concatenate([cosx, cosx], 0))
    m["sinT"] = np.ascontiguousarray(np.concatenate([sinx, sinx], 0))
    m["bmask"] = C["bmask"]
    for tag, NA in (("L", 128), ("C", 4)):
        for kk, v in C["fft" + tag].items():
            m[kk + tag] = v
    for tag in ("L", "C"):
        ft, t01b, ndel = C["filt" + tag]
        m["featsT" + tag] = ft; m["t01b" + tag] = t01b
    m["ndel"] = C["filtL"][2]
    return m

def _build(stop_after=None, dbg=()):
    nc = bass.Bass("TRN2", target_bir_lowering=False)
    k = K(nc)
    def din(name, shape, dt=F32):
        return nc.dram_tensor(name, list(shape), dt, kind="ExternalInput").ap()
    def dscr(name, shape, dt):
        kind = "ExternalOutput" if name in dbg else "Internal"
        return nc.dram_tensor(name, list(shape), dt, kind=kind).ap()
    I = {}
    I["xt0"] = din("xt0", [D, TT]); I["cvec"] = din("cvec", [128, 8, 2])
    for i in range(2):
        I[f"ada_w{i}"] = din(f"ada_w{i}", [D, 6 * D]); I[f"ada_b{i}"] = din(f"ada_b{i}", [128, 48])
        I[f"nmix{i}"] = din(f"nmix{i}", [128, 8]); I[f"nffn{i}"] = din(f"nffn{i}", [128, 8])
        I[f"w_in{i}"] = din(f"w_in{i}", [D, 3328]); I[f"w_out{i}"] = din(f"w_out{i}", [D, D])
        I[f"qkg{i}"] = din(f"qkg{i}", [128, 4])
        I[f"w_up{i}"] = din(f"w_up{i}", [D, 2 * DFF]); I[f"w_dn{i}"] = din(f"w_dn{i}", [DFF, D])
        I[f"fcw{i}"] = din(f"fcw{i}", [128, NM, 3]); I[f"fcb{i}"] = din(f"fcb{i}", [128, NM])
    I["hcw"] = din("hcw", [128, 12, 3]); I["hcb"] = din("hcb", [128, 12]); I["scw"] = din("scw", [128, 4, 3])
    I["hw1"] = din("hw1", [33, 64]); I["hw2"] = din("hw2", [64, 64]); I["hw3"] = din("hw3", [64, 64])
    I["hw4"] = din("hw4", [64, 1024]); I["hb"] = din("hb", [64, 4]); I["hbd"] = din("hbd", [128, 4])
    I["sink"] = din("sink", [128, 8])
    I["cosT"] = din("cosT", [128, TT]); I["sinT"] = din("sinT", [128, TT])
    I["bmask"] = din("bmask", [6, 128, 512], BF16)
    for tag, NA in (("L", 128), ("C", 4)):
        I["f1cs" + tag] = din("f1cs" + tag, [NA, 2 * NA], BF16)
        for nm in ("fC", "fS", "fnS"): I[nm + tag] = din(nm + tag, [128, 128], BF16)
        for nm in ("fCS", "fnSC"): I[nm + tag] = din(nm + tag, [128, 256], BF16)
        for nm in ("twA", "twB"): I[nm + tag] = din(nm + tag, [128, 512])
        for nm in ("twA2", "twB2"): I[nm + tag] = din(nm + tag, [NA, 1024])
        for nm in ("g3C", "g3nS"): I[nm + tag] = din(nm + tag, [NA, NA], BF16)
        if NA == 4:
            for nm in ("twA2c", "twB2c"): I[nm + tag] = din(nm + tag, [128, 256])
            for nm in ("gbC", "gbnS"): I[nm + tag] = din(nm + tag, [128, 64], BF16)
        n = 64 * NA
        I["featsT" + tag] = din("featsT" + tag, [33, 2 * n]); I["t01b" + tag] = din("t01b" + tag, [128, 2 * n])
    I["ndel"] = din("ndel", [128, 4])
    OUT = nc.dram_tensor("out", [D, OWN], F32, kind="ExternalOutput").ap()

    S = {}
    for i in range(2):
        S[f"bw_in{i}"] = dscr(f"bw_in{i}", [D, 3328], BF16); S[f"bw_out{i}"] = dscr(f"bw_out{i}", [D, D], BF16)
        S[f"bw_up{i}"] = dscr(f"bw_up{i}", [D, 2 * DFF], BF16); S[f"bw_dn{i}"] = dscr(f"bw_dn{i}", [DFF, D], BF16)
    S["QT"] = dscr("QT", [8, 64, TT], BF16); S["KT"] = dscr("KT", [4, 64, TT], BF16); S["V"] = dscr("V", [TT, 256], BF16)
    S["UT"] = dscr("UT", [512, TT], BF16); S["X0T"] = dscr("X0T", [512, TT], F32)
    S["UF"] = dscr("UF", [512, TT], F32)
    S["YT"] = dscr("YT", [512, TT], F32)
    S["OAT"] = dscr("OAT", [512, TT], BF16); S["OCT"] = dscr("OCT", [512, TT], BF16)
    S["XM"] = dscr("XM", [D, TT], F32); S["X1"] = dscr("X1", [D, TT], F32); S["XM1"] = dscr("XM1", [D, TT], F32)
    S["KRAWL"] = dscr("KRAWL", [512, 2 * SEQ], F32); S["KNL"] = dscr("KNL", [512, 2 * SEQ], BF16)
    S["KRAWC"] = dscr("KRAWC", [512, 2 * CTX], F32); S["KNC"] = dscr("KNC", [512, 2 * CTX], BF16)

    MOD = k.sbuf([128, 2, 48, 2], F32, "mod", persist=True)
    AB = k.sbuf([128, 2, 2, 2, 8, 2], F32, "ab", persist=True)
    ONES = k.sbuf([128, 128], BF16, "ones", persist=True)
    BONES = k.sbuf([128, 128], BF16, "bones", persist=True)
    SEL = k.sbuf([128, 64], F32, "sel", persist=True)
    ESINK = k.sbuf([128, 8], F32, "esink", persist=True)
    RNORM = {"L": k.sbuf([128, 4], F32, "rnL", persist=True), "C": k.sbuf([128, 4], F32, "rnC", persist=True)}

    phases = []

    conv_items = []
    for i in range(2):
        for src, dst, rows, cols in ((f"w_in{i}", f"bw_in{i}", D, 3328), (f"w_out{i}", f"bw_out{i}", D, D),
                                     (f"w_up{i}", f"bw_up{i}", D, 2 * DFF), (f"w_dn{i}", f"bw_dn{i}", DFF, D)):
            for r0 in range(0, rows, 128):
                for c0 in range(0, cols, 2048):
                    conv_items.append((src, dst, r0, c0, min(2048, cols - c0)))
    CONV_EARLY = sum(1 for it in conv_items if it[0] in ("w_in0", "w_out0"))
    conv_pos = [0]

    class make_converter:
        def __init__(self):
            self.stg = [k.sbuf([128, 2048], F32, "stg") for _ in range(3)]
            self.stb = [k.sbuf([128, 2048], BF16, "stb") for _ in range(3)]
        def step(self, eng=None, q="sp"):
            n = conv_pos[0]
            if n >= len(conv_items): return False
            conv_pos[0] += 1
            src, dst, r0, c0, cw = conv_items[n]
            a = self.stg[n % 3]; bt = self.stb[n % 3]
            k.dma(q, a[:, 0:cw], I[src][r0:r0 + 128, c0:c0 + cw], writes=[a])
            k.copy(eng or ("dve" if n % 2 == 0 else "act"), bt[:, 0:cw], a[:, 0:cw], [a], [bt])
            k.dma(q, S[dst][r0:r0 + 128, c0:c0 + cw], bt[:, 0:cw], reads=[bt])
            return True

    def ph_setup():
        k.memset("dve", ONES[:], 1.0, [ONES])
        k.memset("dve", BONES[:], 0.0, [BONES])
        k.memset("dve", BONES[0:64, 0:64], 1.0, [BONES])
        k.memset("dve", BONES[64:128, 64:128], 1.0, [BONES])
        k.memset("dve", SEL[:], 0.0, [SEL])
        k.memset("dve", SEL[64:65, :], 1.0, [SEL])
        snk = k.sbuf([128, 8], F32)
        k.dma("sp", snk[:], I["sink"][:, :], writes=[snk])
        k.act(ESINK[:], snk[:], AF.Exp, [snk], [ESINK])
        cv_ = make_converter()
        for _ in range(CONV_EARLY):
            cv_.step()
        cv = k.sbuf([128, 8, 2], F32)
        k.dma("sp", cv[:], I["cvec"][:, :, :], writes=[cv])
        sc = k.sbuf([128, 8, 2], F32)
        k.act(sc[:], cv[:], AF.Silu, [cv], [sc])
        wst = [k.sbuf([128, 8, 512], F32, "wst") for _ in range(2)]
        ps = [k.psum([128, 512], F32) for _ in range(2)]
        n = 0
        for i in range(2):
            adb = k.sbuf([128, 48], F32)
            k.dma("sp", adb[:], I[f"ada_b{i}"][:, :], writes=[adb])
            for cb in range(12):
                wt = wst[n % 2]; n += 1
                k.dma("sp", wt[:], I[f"ada_w{i}"][:, cb * 512:(cb + 1) * 512].rearrange("(k p) f -> p k f", p=128), writes=[wt])
                for mi in range(4):
                    m = cb * 4 + mi
                    pt = ps[m % 2]
                    for kk in range(8):
                        k.mm(pt[:, 0:2], wt[:, kk, mi * 128:(mi + 1) * 128], sc[:, kk, :], kk == 0, kk == 7, [wt, sc], [pt])
                    k.ts("dve", MOD[:, i, m, :], pt[:, 0:2], adb[:, m:m + 1], None, ALU.add, None, [pt, adb], [MOD])
            for wh, (nm, sh0, sc0) in enumerate(((f"nmix{i}", 0, 8), (f"nffn{i}", 24, 32))):
                g = k.sbuf([128, 8], F32)
                k.dma("sp", g[:], I[nm][:, :], writes=[g])
                for j in range(2):
                    k.stt("dve", AB[:, i, wh, 0, :, j], MOD[:, i, sc0:sc0 + 8, j], 1.0, g[:], ALU.add, ALU.mult, [MOD, g], [AB])
                    k.copy("dve", AB[:, i, wh, 1, :, j], MOD[:, i, sh0:sh0 + 8, j], [MOD], [AB])
    phases.append(("setup", ph_setup))

    def norm_mod(xh, Wc, i, wh, j, sq, ssps, rstd, tmp, h):
        for kk in range(8):
            k.act(sq[:, kk, 0:Wc], xh[:, kk, 0:Wc], AF.Square, [xh], [sq])
        for (c0, c1) in ((0, min(512, Wc)), (512, Wc)):
            if c1 <= c0: continue
            for kk in range(8):
                k.mm(ssps[:, c0:c1], ONES[:, :], sq[:, kk, c0:c1], kk == 0, kk == 7, [ONES, sq], [ssps])
        k.act(rstd[:, 0:Wc], ssps[:, 0:Wc], AF.Sqrt, [ssps], [rstd], bias=1e-6, scale=1.0 / D)
        k.recip(rstd[:, 0:Wc], rstd[:, 0:Wc], [rstd], [rstd])
        for kk in range(8):
            t = tmp[kk % 2]
            k.stt("dve", t[:, 0:Wc], xh[:, kk, 0:Wc], AB[:, i, wh, 0, kk, j:j + 1], rstd[:, 0:Wc], ALU.mult, ALU.mult, [xh, AB, rstd], [t])
            k.act(h[:, kk, 0:Wc], t[:, 0:Wc], AF.Identity, [t, AB], [h], bias=AB[:, i, wh, 1, kk, j:j + 1], scale=1.0)

    CHUNKS_ALL = [(c * 512, 512, c == 0, c == 15, 0) for c in range(16)] + [(SEQ, CTX, True, True, 1)]
    CHUNKS_E = [(c * 512, 512, c == 0, False, 0) for c in range(9)]
    CTXCH = (SEQ, CTX, True, True, 1)

    def load_xh(xh, src, t0, W, ledge, redge, q="sp"):
        lo = 2 if ledge else 1
        hi = W + 2 if redge else W + 3
        if ledge: k.memset("pool", xh[:, :, 1:2], 0.0, [xh])
        if redge: k.memset("pool", xh[:, :, W + 2:W + 3], 0.0, [xh])
        k.dma(q, xh[:, :, lo:hi], src[:, t0 - 2 + lo:t0 - 2 + hi].rearrange("(k p) t -> p k t", p=128), writes=[xh])

    def make_inproj(i, XIN, chunks):
        def ph():
            w = k.sbuf([128, 8, 3328], BF16, "w_in")
            for kk in range(8):
                k.dma("sp", w[:, kk, :], S[f"bw_in{i}"][kk * 128:(kk + 1) * 128, :], writes=[w])
            qkg = k.sbuf([128, 4], F32); k.dma("sp", qkg[:], I[f"qkg{i}"][:, :], writes=[qkg])
            if i == 0:
                cw = k.sbuf([128, 12, 3], F32); k.dma("sp", cw[:], I["hcw"][:, :, :], writes=[cw])
                cb = k.sbuf([128, 12], F32); k.dma("sp", cb[:], I["hcb"][:, :], writes=[cb])
            else:
                cw = k.sbuf([128, 4, 3], F32); k.dma("sp", cw[:], I["scw"][:, :, :], writes=[cw])
            xhs = [k.sbuf([128, 8, 516], F32, "xh") for _ in range(2)]
            for t_ in xhs: k.memset("pool", t_[:], 0.0, [t_])
            sq = k.sbuf([128, 8, 516], BF16, "sq")
            h = k.sbuf([128, 8, 516], BF16, "h")
            rstd = k.sbuf([128, 516], F32, "rstd")
            tmp = [k.sbuf([128, 516], F32, "tmp") for _ in range(2)]
            ssps = k.psum([128, 1024], F32, "ssps")
            zps = [k.psum([128, 512], F32, "zps") for _ in range(4)]
            cps = k.psum([128, 1024], F32, "cps")
            cosb = k.sbuf([128, 512], F32, "cos"); sinb = k.sbuf([128, 512], F32, "sin")
            sq2 = [k.sbuf([128, 512], BF16, "sq2") for _ in range(2)]
            rs = [k.sbuf([128, 512], F32, "rs") for _ in range(2)]
            ta = [k.sbuf([128, 512], F32, "ta") for _ in range(2)]
            tb = [k.sbuf([128, 512], F32, "tb") for _ in range(2)]
            qo = [k.sbuf([128, 512], BF16, "qo") for _ in range(2)]
            vo = [k.sbuf([128, 256], BF16, "vo") for _ in range(2)]
            asb = [k.sbuf([128, 516], F32, "asb") for _ in range(3)]
            c1 = [k.sbuf([128, 512], F32, "c1") for _ in range(3)]
            uo = [k.sbuf([128, 512], F32, "uo") for _ in range(2)]
            ub = [k.sbuf([128, 512], BF16, "ub") for _ in range(2)]
            pp = [k.sbuf([128, 516], F32, "pp") for _ in range(2)]
            nq = [0]
            import os
            PARTS = os.environ.get("INPROJ_PARTS", "nqvc")
            NCHK = int(os.environ.get("INPROJ_NCH", "99"))
            for ci, (t0, W, le, re, j) in enumerate(chunks[:NCHK]):
                Wc = W + 4
                do_q = (t0 < E) or j == 1
                if i == 1 and j == 1: do_q = False
                xh = xhs[ci % 2]
                load_xh(xh, XIN, t0, W, le, re)
                norm_mod(xh, Wc, i, 0, j, sq, ssps, rstd, tmp, h)
                k.dma("sp", cosb[:, 0:W], I["cosT"][:, t0:t0 + W], writes=[cosb])
                k.dma("sp", sinb[:, 0:W], I["sinT"][:, t0:t0 + W], writes=[sinb])
                for pr in range(6):
                    if "q" not in PARTS: continue
                    if pr < 4 and not do_q: continue
                    n = nq[0]; nq[0] += 1
                    zp = zps[(2 * n) % 4]; zsp = zps[(2 * n + 1) % 4]
                    for kk in range(8):
                        k.mm(zp[:, 0:W], w[:, kk, pr * 128:(pr + 1) * 128], h[:, kk, 2:W + 2], kk == 0, kk == 7, [w, h], [zp])
                    for kk in range(8):
                        k.mm(zsp[:, 0:W], w[:, kk, 2560 + pr * 128:2560 + (pr + 1) * 128], h[:, kk, 2:W + 2], kk == 0, kk == 7, [w, h], [zsp])
                    s2 = sq2[n % 2]; r_ = rs[n % 2]; a_ = ta[n % 2]; b_ = tb[n % 2]; q_ = qo[n % 2]
                    gi = 0 if pr < 4 else 2
                    k.act(s2[:, 0:W], zp[:, 0:W], AF.Square, [zp], [s2])
                    k.stt("dve", a_[:, 0:W], zp[:, 0:W], qkg[:, gi:gi + 1], cosb[:, 0:W], ALU.mult, ALU.mult, [zp, qkg, cosb], [a_])
                    k.stt("dve", b_[:, 0:W], zsp[:, 0:W], qkg[:, gi + 1:gi + 2], sinb[:, 0:W], ALU.mult, ALU.mult, [zsp, qkg, sinb], [b_])
                    k.mm(zp[:, 0:W], BONES[:, :], s2[:, 0:W], True, True, [BONES, s2], [zp])
                    k.act(r_[:, 0:W], zp[:, 0:W], AF.Sqrt, [zp], [r_], bias=1e-6, scale=1.0 / 64)
                    k.recip(r_[:, 0:W], r_[:, 0:W], [r_], [r_])
                    QS = os.environ.get("QSKIP", "")
                    pe_ = "dve" if "pool" in QS else "pool"
                    k.tt(pe_, a_[:, 0:W], a_[:, 0:W], b_[:, 0:W], ALU.add, [a_, b_], [a_])
                    k.tt(pe_, q_[:, 0:W], a_[:, 0:W], r_[:, 0:W], ALU.mult, [a_, r_], [q_])
                    for hf in range(2):
                        if "dma" in QS: continue
                        if pr < 4:
                            dst = S["QT"][2 * pr + hf, :, t0:t0 + W]
                        else:
                            dst = S["KT"][2 * (pr - 4) + hf, :, t0:t0 + W]
                        k.dma("pool", dst, q_[hf * 64:(hf + 1) * 64, 0:W], reads=[q_])
                for tj in range(W // 128):
                    if "v" not in PARTS: continue
                    n = nq[0]; nq[0] += 1
                    vp = zps[n % 4]
                    for kk in range(8):
                        k.mm(vp[:, 0:256], h[:, kk, 2 + tj * 128:2 + (tj + 1) * 128], w[:, kk, 768:1024], kk == 0, kk == 7, [w, h], [vp])
                    v_ = vo[n % 2]
                    k.copy("act", v_[:, :], vp[:, 0:256], [vp], [v_])
                    k.dma("pool", S["V"][t0 + tj * 128:t0 + (tj + 1) * 128, :], v_[:, :], reads=[v_])
                def convproj(m, dst):
                    for (c0, c1_) in ((0, min(512, Wc)), (512, Wc)):
                        if c1_ <= c0: continue
                        for kk in range(8):
                            k.mm(cps[:, c0:c1_], w[:, kk, 1024 + m * 128:1024 + (m + 1) * 128], h[:, kk, c0:c1_], kk == 0, kk == 7, [w, h], [cps])
                    k.copy("act", dst[:, 0:Wc], cps[:, 0:Wc], [cps], [dst])
                    if le: k.memset("pool", dst[:, 1:2], 0.0, [dst])
                    if re: k.memset("pool", dst[:, W + 2:W + 3], 0.0, [dst])
                def conv3(out, a, wts, m, bias, eng="dve"):
                    if bias is not None:
                        k.ts(eng, out[:, 0:W], a[:, 2:W + 2], wts[:, m, 1:2], bias, ALU.mult, ALU.add, [a, wts, cb], [out])
                    else:
                        k.ts(eng, out[:, 0:W], a[:, 2:W + 2], wts[:, m, 1:2], None, ALU.mult, None, [a, wts], [out])
                    k.stt(eng, out[:, 0:W], a[:, 1:W + 1], wts[:, m, 0:1], out[:, 0:W], ALU.mult, ALU.add, [a, wts, out], [out])
                    k.stt(eng, out[:, 0:W], a[:, 3:W + 3], wts[:, m, 2:3], out[:, 0:W], ALU.mult, ALU.add, [a, wts, out], [out])
                for jc in range(4):
                    if "c" not in PARTS: continue
                    if i == 0:
                        convproj(4 + jc, asb[0]); conv3(c1[0], asb[0], cw, 4 + jc, cb[:, 4 + jc:5 + jc])
                        convproj(8 + jc, asb[1]); conv3(c1[1], asb[1], cw, 8 + jc, cb[:, 8 + jc:9 + jc])
                        u_ = uo[jc % 2]; ub_ = ub[jc % 2]
                        k.tt("dve", u_[:, 0:W], c1[0][:, 0:W], c1[1][:, 0:W], ALU.mult, [c1[0], c1[1]], [u_])
                        k.copy("pool", ub_[:, 0:W], u_[:, 0:W], [u_], [ub_])
                        k.dma("pool", S["UF"][jc * 128:(jc + 1) * 128, t0:t0 + W], u_[:, 0:W], reads=[u_])
                        k.dma("pool", S["UT"][jc * 128:(jc + 1) * 128, t0:t0 + W], ub_[:, 0:W], reads=[ub_])
                        if do_q:
                            convproj(jc, asb[2]); conv3(c1[2], asb[2], cw, jc, cb[:, jc:jc + 1])
                            k.dma("pool", S["X0T"][jc * 128:(jc + 1) * 128, t0:t0 + W], c1[2][:, 0:W], reads=[c1[2]])
                    elif j == 0:
                        convproj(4 + jc, asb[0]); convproj(8 + jc, asb[1])
                        p_ = pp[jc % 2]
                        k.tt("dve", p_[:, 0:Wc], asb[0][:, 0:Wc], asb[1][:, 0:Wc], ALU.mult, [asb[0], asb[1]], [p_])
                        conv3(c1[0], p_, cw, jc, None)
                        convproj(jc, asb[2])
                        ub_ = ub[jc % 2]
                        k.tt("pool", ub_[:, 0:W], c1[0][:, 0:W], asb[2][:, 2:W + 2], ALU.mult, [c1[0], asb[2]], [ub_])
                        k.dma("pool", S["OCT"][jc * 128:(jc + 1) * 128, t0:t0 + W], ub_[:, 0:W], reads=[ub_])
        return ph

    def make_attn(i):
        def ph():
            NKT = TT // 128
            kT = k.sbuf([128, TT], BF16, "kT")
            k.memset("pool", kT[64:128, :], 0.0, [kT])
            va = k.sbuf([128, NKT, 65], BF16, "va")
            qT = [k.sbuf([128, 512], BF16, "qT") for _ in range(2)]
            for t_ in qT: k.memset("pool", t_[64:128, :], 0.0, [t_])
            sps = [k.psum([128, 512], F32, "sps") for _ in range(4)]
            ops_ = [k.psum([128, 512], F32, "ops") for _ in range(2)]
            bps = k.psum([128, 512], F32, "bps")
            pT = [k.sbuf([128, 512], BF16, "pT") for _ in range(4)]
            osb = [k.sbuf([65, 512], F32, "osb") for _ in range(2)]
            rb = [k.sbuf([64, 512], F32, "rb") for _ in range(2)]
            ob = [k.sbuf([64, 512], BF16, "ob") for _ in range(2)]
            if i == 1:
                bm = k.sbuf([128, 6, 512], BF16, "bm")
                for r in range(6):
                    k.dma("sp", bm[:, r, :], I["bmask"][r, :, :], writes=[bm])
            qch = []
            for c in range(9):
                t0 = c * 512
                Wq = 512 if c < 8 else 256
                if i == 0:
                    kts = [(kt, 0, Wq, None) for kt in range(NKT)]
                else:
                    kts = []
                    for r in range(-1, 5):
                        kt = 4 * c + r
                        if kt < 0 or kt >= E // 128: continue
                        f0 = max(0, 128 * (r - 1)); f1 = min(Wq, 128 * (r + 2))
                        if f1 <= f0: continue
                        kts.append((kt, f0, f1, r + 1))
                    kts += [(64, 0, Wq, None), (65, 0, Wq, None)]
                qch.append((t0, Wq, kts))
            if i == 0:
                qch.append((SEQ, CTX, [(64, 0, CTX, None), (65, 0, CTX, None)]))
            n = 0; nh = 0
            cvt = make_converter() if i == 0 else None
            for jkv in range(4):
                k.dma("sp", kT[0:64, :], S["KT"][jkv, :, :], writes=[kT])
                k.dma("sp", va[:, :, 0:64], S["V"][:, jkv * 64:(jkv + 1) * 64].rearrange("(n p) d -> p n d", p=128), writes=[va])
                k.memset("pool", va[:, :, 64:65], 1.0, [va])
                for g in range(2):
                    hq = 2 * jkv + g
                    for (t0, W, kts) in qch:
                        q_ = qT[nh % 2]; op_ = ops_[nh % 2]; o_ = osb[nh % 2]; r_ = rb[nh % 2]; b_ = ob[nh % 2]
                        nh += 1
                        k.dma("sp", q_[0:64, 0:W], S["QT"][hq, :, t0:t0 + W], writes=[q_])
                        if i == 1:
                            pass
                        nk = len(kts)
                        def smm(idx):
                            kt, f0, f1, mi = kts[idx]
                            sp_ = sps[(n + idx) % 4]
                            k.mm(sp_[:, f0:f1], kT[:, kt * 128:(kt + 1) * 128], q_[:, f0:f1], True, True, [kT, q_], [sp_])
                        smm(0)
                        if nk > 1: smm(1)
                        for idx in range(nk):
                            if idx + 2 < nk: smm(idx + 2)
                            kt, f0, f1, mi = kts[idx]
                            sp_ = sps[(n + idx) % 4]; p_ = pT[(n + idx) % 4]
                            if mi is not None and (f0 > 0 or f1 < W):
                                k.memset("dve", p_[:, 0:W], 0.0, [p_])
                            k.act(p_[:, f0:f1], sp_[:, f0:f1], AF.Exp, [sp_], [p_], scale=0.125)
                            if mi is not None:
                                k.tt("dve", p_[:, f0:f1], p_[:, f0:f1], bm[:, mi, f0:f1], ALU.mult, [p_, bm], [p_])
                            k.mm(op_[0:65, 0:W], va[:, kt, :], p_[:, 0:W], idx == 0, idx == nk - 1, [va, p_], [op_])
                        n += nk
                        k.copy("act", o_[:, 0:W], op_[0:65, 0:W], [op_], [o_])
                        k.mm(bps[0:64, 0:W], SEL[0:65, :], o_[0:65, 0:W], True, True, [SEL, o_], [bps])
                        if i == 1:
                            k.ts("dve", r_[:, 0:W], bps[0:64, 0:W], ESINK[0:64, hq:hq + 1], None, ALU.add, None, [bps, ESINK], [r_])
                            k.recip(r_[:, 0:W], r_[:, 0:W], [r_], [r_])
                        else:
                            k.recip(r_[:, 0:W], bps[0:64, 0:W], [bps], [r_])
                        k.tt("dve", b_[:, 0:W], o_[0:64, 0:W], r_[:, 0:W], ALU.mult, [o_, r_], [b_])
                        k.dma("pool", S["OAT"][hq * 64:(hq + 1) * 64, t0:t0 + W], b_[:, 0:W], reads=[b_])
                        if cvt is not None:
                            cvt.step("dve", "pool"); cvt.step("dve", "pool")
            if cvt is not None:
                while cvt.step("dve", "pool"): pass
        return ph

    def make_outproj(i, XIN, XOUT, chunks):
        def ph():
            wa = k.sbuf([128, 4, D], BF16, "woa")
            wc = k.sbuf([128, 4, D], BF16, "woc")
            k.dma("sp", wa[:], S[f"bw_out{i}"][0:512, :].rearrange("(c p) f -> p c f", p=128), writes=[wa])
            k.dma("sp", wc[:], S[f"bw_out{i}"][512:1024, :].rearrange("(c p) f -> p c f", p=128), writes=[wc])
            oa = [k.sbuf([128, 4, 512], BF16, "oa") for _ in range(2)]
            oc = [k.sbuf([128, 4, 512], BF16, "oc") for _ in range(2)]
            xs = [k.sbuf([128, 8, 512], F32, "xs") for _ in range(2)]
            xo = [k.sbuf([128, 8, 512], F32, "xo") for _ in range(2)]
            ps = [k.psum([128, 512], F32, "ps") for _ in range(4)]
            n = 0
            for ci, (t0, W, le, re, j) in enumerate(chunks):
                a_ = oa[ci % 2]; c_ = oc[ci % 2]; x_ = xs[ci % 2]; o_ = xo[ci % 2]
                k.dma("sp", a_[:, :, 0:W], S["OAT"][:, t0:t0 + W].rearrange("(c p) t -> p c t", p=128), writes=[a_])
                k.dma("sp", c_[:, :, 0:W], S["OCT"][:, t0:t0 + W].rearrange("(c p) t -> p c t", p=128), writes=[c_])
                k.dma("sp", x_[:, :, 0:W], XIN[:, t0:t0 + W].rearrange("(k p) t -> p k t", p=128), writes=[x_])
                for m in range(8):
                    p_ = ps[n % 4]; n += 1
                    for hh_ in range(4):
                        k.mm(p_[:, 0:W], wa[:, hh_, m * 128:(m + 1) * 128], a_[:, hh_, 0:W], hh_ == 0, False, [wa, a_], [p_])
                    for cc in range(4):
                        k.mm(p_[:, 0:W], wc[:, cc, m * 128:(m + 1) * 128], c_[:, cc, 0:W], False, cc == 3, [wc, c_], [p_])
                    k.stt("dve", o_[:, m, 0:W], p_[:, 0:W], MOD[:, i, 16 + m, j:j + 1], x_[:, m, 0:W], ALU.mult, ALU.add, [p_, MOD, x_], [o_])
                k.dma("pool", XOUT[:, t0:t0 + W].rearrange("(k p) t -> p k t", p=128), o_[:, :, 0:W], reads=[o_])
        return ph

    def make_ffn(i, XIN, XOUT, chunks, final=False):
        def ph():
            wu = k.sbuf([128, 8, 2 * DFF], BF16, "wu")
            for kk in range(8):
                k.dma("sp", wu[:, kk, :], S[f"bw_up{i}"][kk * 128:(kk + 1) * 128, :], writes=[wu])
            wd = [k.sbuf([128, NM, 128], BF16, "wd") for _ in range(2)]
            cw = k.sbuf([128, NM, 3], F32); k.dma("sp", cw[:], I[f"fcw{i}"][:, :, :], writes=[cw])
            cb = k.sbuf([128, NM], F32); k.dma("sp", cb[:], I[f"fcb{i}"][:, :], writes=[cb])
            xh = k.sbuf([128, 8, 516], F32, "xh")
            k.memset("pool", xh[:], 0.0, [xh])
            sq = k.sbuf([128, 8, 516], BF16, "sq")
            h = k.sbuf([128, 8, 516], BF16, "h")
            rstd = k.sbuf([128, 516], F32, "rstd")
            tmp = [k.sbuf([128, 516], F32, "tmp") for _ in range(2)]
            gg = k.sbuf([128, NM, 512], BF16, "gg")
            asb = [k.sbuf([128, 516], F32, "asb") for _ in range(2)]
            c1 = [k.sbuf([128, 512], F32, "c1") for _ in range(2)]
            xo = [k.sbuf([128, 512], F32, "xo") for _ in range(2)]
            ssps = k.psum([128, 1024], F32, "ssps")
            aps = [k.psum([128, 1024], F32, "aps") for _ in range(2)]
            vps = [k.psum([128, 512], F32, "vps") for _ in range(2)]
            n = 0; nd = 0
            for ci, (t0, W, le, re, j) in enumerate(chunks):
                Wc = W + 4
                load_xh(xh, XIN, t0, W, le, re)
                norm_mod(xh, Wc, i, 1, j, sq, ssps, rstd, tmp, h)
                for m in range(NM):
                    ap_ = aps[n % 2]; vp_ = vps[n % 2]; a_ = asb[n % 2]; c_ = c1[n % 2]; n += 1
                    for (c0, c1_) in ((0, min(512, Wc)), (512, Wc)):
                        if c1_ <= c0: continue
                        for kk in range(8):
                            k.mm(ap_[:, c0:c1_], wu[:, kk, m * 128:(m + 1) * 128], h[:, kk, c0:c1_], kk == 0, kk == 7, [wu, h], [ap_])
                    for kk in range(8):
                        k.mm(vp_[:, 0:W], wu[:, kk, DFF + m * 128:DFF + (m + 1) * 128], h[:, kk, 2:W + 2], kk == 0, kk == 7, [wu, h], [vp_])
                    k.copy("act", a_[:, 0:Wc], ap_[:, 0:Wc], [ap_], [a_])
                    if le: k.memset("pool", a_[:, 1:2], 0.0, [a_])
                    if re: k.memset("pool", a_[:, W + 2:W + 3], 0.0, [a_])
                    eng = "dve"
                    k.ts(eng, c_[:, 0:W], a_[:, 2:W + 2], cw[:, m, 1:2], cb[:, m:m + 1], ALU.mult, ALU.add, [a_, cw, cb], [c_])
                    k.stt(eng, c_[:, 0:W], a_[:, 1:W + 1], cw[:, m, 0:1], c_[:, 0:W], ALU.mult, ALU.add, [a_, cw, c_], [c_])
                    k.stt(eng, c_[:, 0:W], a_[:, 3:W + 3], cw[:, m, 2:3], c_[:, 0:W], ALU.mult, ALU.add, [a_, cw, c_], [c_])
                    k.act(c_[:, 0:W], c_[:, 0:W], AF.Gelu_apprx_tanh, [c_], [c_])
                    k.tt("dve", gg[:, m, 0:W], c_[:, 0:W], vp_[:, 0:W], ALU.mult, [c_, vp_], [gg])
                for mo in range(8):
                    wd_ = wd[nd % 2]; o_ = xo[nd % 2]; p_ = vps[nd % 2]; nd += 1
                    k.dma("sp", wd_[:], S[f"bw_dn{i}"][:, mo * 128:(mo + 1) * 128].rearrange("(m p) f -> p m f", p=128), writes=[wd_])
                    for m in range(NM):
                        k.mm(p_[:, 0:W], wd_[:, m, :], gg[:, m, 0:W], m == 0, m == NM - 1, [wd_, gg], [p_])
                    k.stt("dve", o_[:, 0:W], p_[:, 0:W], MOD[:, i, 40 + mo, j:j + 1], xh[:, mo, 2:W + 2], ALU.mult, ALU.add, [p_, MOD, xh], [o_])
                    k.dma("pool", XOUT[mo * 128:(mo + 1) * 128, t0:t0 + W], o_[:, 0:W], reads=[o_])
        return ph

    def make_filter(tag, n):
        KRAW = S["KRAW" + tag]; KN = S["KN" + tag]
        def ph():
            w1 = k.sbuf([33, 64], F32); k.dma("sp", w1[:], I["hw1"][:, :], writes=[w1])
            w2 = k.sbuf([64, 64], F32); k.dma("sp", w2[:], I["hw2"][:, :], writes=[w2])
            w3 = k.sbuf([64, 64], F32); k.dma("sp", w3[:], I["hw3"][:, :], writes=[w3])
            w4 = k.sbuf([64, 1024], F32); k.dma("sp", w4[:], I["hw4"][:, :], writes=[w4])
            hb = k.sbuf([64, 4], F32); k.dma("sp", hb[:], I["hb"][:, :], writes=[hb])
            ndel = k.sbuf([128, 4], F32); k.dma("sp", ndel[:], I["ndel"][:, :], writes=[ndel])
            asum = k.sbuf([128, 4, 40], F32, "asum")
            k.memset("dve", asum[:], 0.0, [asum])
            ft = [k.sbuf([33, 512], F32, "ft") for _ in range(2)]
            t01 = [k.sbuf([128, 512], F32, "t01") for _ in range(2)]
            hid2 = [[k.sbuf([64, 512], F32, "hid") for _ in range(3)] for _ in range(2)]
            ki2 = [k.sbuf([64, 512], I32, "ki") for _ in range(2)]
            win = [k.sbuf([128, 512], F32, "win") for _ in range(2)]
            kr = [k.sbuf([128, 512], F32, "kr") for _ in range(2)]
            junk = k.sbuf([128, 512], F32, "junk")
            krb = [k.sbuf([128, 512], BF16, "krb") for _ in range(2)]
            ps = [k.psum([128, 512], F32, "ps") for _ in range(4)]
            NCH = (2 * n) // 512
            n_ = 0
            for c in range(NCH):
                q0 = c * 512
                f_ = ft[c % 2]; t_ = t01[c % 2]
                k.dma("sp", f_[:, :], I["featsT" + tag][:, q0:q0 + 512], writes=[f_])
                k.dma("sp", t_[:, :], I["t01b" + tag][:, q0:q0 + 512], writes=[t_])
                src = f_; srcK = 33
                hid = hid2[c % 2]; ki = ki2[c % 2]
                for li, wl in enumerate((w1, w2, w3)):
                    p_ = ps[n_ % 4]; n_ += 1
                    k.mm(p_[0:64, :], wl[0:srcK, :], src[0:srcK, :], True, True, [wl, src], [p_])
                    hd = hid[li]
                    k.ts("dve", hd[:, :], p_[0:64, :], hb[:, li:li + 1], hb[:, 3:4], ALU.add, ALU.mult, [p_, hb], [hd])
                    k.ts("dve", ki[:, :], hd[:, :], float(1.0 / (2 * np.pi)), None, ALU.mult, None, [hd], [ki])
                    k.stt("dve", hd[:, :], ki[:, :], float(-2 * np.pi), hd[:, :], ALU.mult, ALU.add, [ki, hd], [hd])
                    k.act(hd[:, :], hd[:, :], AF.Sin, [hd], [hd])
                    src = hd; srcK = 64
                segs = []
                if q0 + 512 <= n: segs = [(0, 512, 0)]
                elif q0 >= n: segs = [(0, 512, 512)]
                else: segs = [(0, n - q0, 0), (n - q0, 512, 512)]
                for jc in range(4):
                    p_ = ps[n_ % 4]; n_ += 1
                    for (a0, a1, off) in segs:
                        k.mm(p_[:, a0:a1], w4[:, off + jc * 128:off + (jc + 1) * 128], hid[2][:, a0:a1], True, True, [w4, hid[2]], [p_])
                    wn = win[jc % 2]; kr_ = kr[jc % 2]
                    k.act(wn[:, :], t_[:, :], AF.Exp, [t_, ndel], [wn], scale=ndel[:, jc:jc + 1])
                    k.stt("dve", kr_[:, :], wn[:, :], 0.05, p_[:, :], ALU.add, ALU.mult, [wn, p_], [kr_])
                    if q0 <= n < q0 + 512:
                        k.memset("dve", kr_[:, n - q0:n - q0 + 1], 0.0, [kr_])
                    k.act(junk[:, :], kr_[:, :], AF.Abs, [kr_], [junk, asum], accum=asum[:, jc, c:c + 1])
                    kb_ = krb[jc % 2]
                    k.copy("pool", kb_[:, :], kr_[:, :], [kr_], [kb_])
                    k.dma("pool", KN[jc * 128:(jc + 1) * 128, q0:q0 + 512], kb_[:, :], reads=[kb_])
            rn = RNORM[tag]
            for jc in range(4):
                k.op("dve", lambda e, jc=jc: e.reduce_sum(out=rn[:, jc:jc + 1], in_=asum[:, jc, 0:NCH], axis=mybir.AxisListType.X), [asum], [rn])
            k.recip(rn[:, :], rn[:, :], [rn], [rn], force_self=True)
        return ph

    def make_fftconv(tag, NA, n, tok0, n_out_blocks):
        KN = S["KN" + tag]
        NR = NA // 2
        GF = 512 // NA
        GB = 512 // (2 * NA)
        def ph():
            cst = {}
            for nm, shp, dt in (("f1cs", [NA, 2 * NA], BF16), ("fC", [128, 128], BF16), ("fS", [128, 128], BF16), ("fnS", [128, 128], BF16),
                                ("fCS", [128, 256], BF16), ("fnSC", [128, 256], BF16), ("twA", [128, 512], F32), ("twB", [128, 512], F32),
                                ("twA2", [NA, 1024], F32), ("twB2", [NA, 1024], F32), ("g3C", [NA, NA], BF16), ("g3nS", [NA, NA], BF16)):
                cst[nm] = k.sbuf(shp, dt, nm)
                k.dma("sp", cst[nm][:], I[nm + tag][tuple(slice(None) for _ in shp)], writes=[cst[nm]])
            if NA == 4:
                for nm, shp, dt in (("twA2c", [128, 256], F32), ("twB2c", [128, 256], F32), ("gbC", [128, 64], BF16), ("gbnS", [128, 64], BF16)):
                    cst[nm] = k.sbuf(shp, dt, nm)
                    k.dma("sp", cst[nm][:], I[nm + tag][:, :], writes=[cst[nm]])
                c2a = [k.sbuf([128, 256], F32, "c2a") for _ in range(2)]; c2b = [k.sbuf([128, 256], F32, "c2b") for _ in range(2)]
                y3c = [k.sbuf([128, 2, 128], BF16, "y3c") for _ in range(2)]
                yoc = [k.sbuf([64, 128], F32, "yoc") for _ in range(2)]
            LG = 32 if NA == 128 else 128
            NXB = 2 if NA == 128 else 1
            xu = [k.sbuf([NA, LG, 128], BF16, "xu") for _ in range(NXB)]
            for t_ in xu: k.memset("pool", t_[:], 0.0, [t_])
            xk = [k.sbuf([NA, LG, 128], BF16, "xk") for _ in range(NXB)]
            s1 = [k.psum([128, 512], F32, "s1") for _ in range(2)]
            s2 = [k.psum([128, 512], F32, "s2") for _ in range(2)]
            s3 = [k.psum([128, 1024], F32, "s3") for _ in range(1)]
            s4 = [k.psum([128, 512], F32, "s4") for _ in range(2)]
            NB = 2
            ta = [k.sbuf([128, 512], F32, "ta") for _ in range(NB)]; tb = [k.sbuf([128, 512], F32, "tb") for _ in range(NB)]
            bu = [k.sbuf([128, 2, GF, NA], BF16, "bu") for _ in range(NB)]; bk = [k.sbuf([128, 2, GF, NA], BF16, "bk") for _ in range(NB)]
            kh = [k.sbuf([128, 2, 512], F32, "kh") for _ in range(NB)]
            m1 = [k.sbuf([128, 512], F32, "m1") for _ in range(NB)]; m2 = [k.sbuf([128, 512], F32, "m2") for _ in range(NB)]
            m3 = [k.sbuf([128, 512], F32, "m3") for _ in range(NB)]; m4 = [k.sbuf([128, 512], F32, "m4") for _ in range(NB)]
            yh = [k.sbuf([128, 2, GF, NA], BF16, "yh") for _ in range(NB)]
            t2a = [k.sbuf([NA, 1024], F32, "t2a") for _ in range(NB)]; t2b = [k.sbuf([NA, 1024], F32, "t2b") for _ in range(NB)]
            y3 = [k.sbuf([NA, 2, 4, 128], BF16, "y3") for _ in range(NB)]
            yo = [k.sbuf([n_out_blocks, 4, 128], F32, "yo") for _ in range(2)]
            MO = n_out_blocks
            cnt = {"tw": 0, "inv": 0, "g": 0}
            def fwd(x, cbase, bdst):
                for half in range(2):
                    bank = s1[half]
                    ta_ = ta[cnt["tw"] % NB]; tb_ = tb[cnt["tw"] % NB]; cnt["tw"] += 1
                    for cc in range(GB):
                        ch = cbase + half * GB + cc
                        k.mm(bank[:, cc * 2 * NA:(cc + 1) * 2 * NA], x[0:NA, ch, :], cst["f1cs"][0:NA, :], True, True, [x, cst["f1cs"]], [bank])
                    k.tt("dve", ta_[:, :], bank[:, :], cst["twA"][:, :], ALU.mult, [bank, cst["twA"]], [ta_])
                    k.tt("dve", tb_[:, :], bank[:, :], cst["twB"][:, :], ALU.mult, [bank, cst["twB"]], [tb_])
                    tav = ta_[:, :].rearrange("p (g r f) -> p g r f", g=GB, r=2)
                    tbv = tb_[:, :].rearrange("p (g r f) -> p g r f", g=GB, r=2)
                    k.tt("dve", bdst[:, 0, half * GB:(half + 1) * GB, :], tav[:, :, 0, :], tbv[:, :, 1, :], ALU.subtract, [ta_, tb_], [bdst])
                    k.tt("dve", bdst[:, 1, half * GB:(half + 1) * GB, :], tav[:, :, 1, :], tbv[:, :, 0, :], ALU.subtract, [ta_, tb_], [bdst])
                bre = bdst[:, 0, :, :].rearrange("p g f -> p (g f)"); bim = bdst[:, 1, :, :].rearrange("p g f -> p (g f)")
                k.mm(s2[0][:, :], cst["fC"][:, :], bre, True, False, [cst["fC"], bdst], [s2[0]])
                k.mm(s2[0][:, :], cst["fS"][:, :], bim, False, True, [cst["fS"], bdst], [s2[0]])
                k.mm(s2[1][:, :], cst["fC"][:, :], bim, True, False, [cst["fC"], bdst], [s2[1]])
                k.mm(s2[1][:, :], cst["fnS"][:, :], bre, False, True, [cst["fnS"], bdst], [s2[1]])
            for lg in range(512 // LG):
                xu_ = xu[lg % NXB]; xk_ = xk[lg % NXB]
                k.dma("sp", xu_[0:NR, :, :], S["UT"][lg * LG:(lg + 1) * LG, tok0:tok0 + n].rearrange("c (a p) -> a c p", p=128), writes=[xu_])
                k.dma("sp", xk_[:, :, :], KN[lg * LG:(lg + 1) * LG, :].rearrange("c (a p) -> a c p", p=128), writes=[xk_])
                for gf in range(LG // GF):
                    cbase = gf * GF
                    g_ = cnt["g"] % NB; cnt["g"] += 1
                    kh_ = kh[g_]; yh_ = yh[g_]
                    fwd(xk_, cbase, bk[g_])
                    k.copy("act", kh_[:, 0, :], s2[0][:, :], [s2[0]], [kh_])
                    k.copy("act", kh_[:, 1, :], s2[1][:, :], [s2[1]], [kh_])
                    fwd(xu_, cbase, bu[g_])
                    k.tt("dve", m1[g_][:, :], s2[0][:, :], kh_[:, 0, :], ALU.mult, [s2[0], kh_], [m1[g_]])
                    k.tt("dve", m3[g_][:, :], s2[0][:, :], kh_[:, 1, :], ALU.mult, [s2[0], kh_], [m3[g_]])
                    k.tt("dve", m2[g_][:, :], s2[1][:, :], kh_[:, 1, :], ALU.mult, [s2[1], kh_], [m2[g_]])
                    k.tt("dve", m4[g_][:, :], s2[1][:, :], kh_[:, 0, :], ALU.mult, [s2[1], kh_], [m4[g_]])
                    k.tt("pool", yh_[:, 0, :, :].rearrange("p g f -> p (g f)"), m1[g_][:, :], m2[g_][:, :], ALU.subtract, [m1[g_], m2[g_]], [yh_])
                    k.tt("pool", yh_[:, 1, :, :].rearrange("p g f -> p (g f)"), m3[g_][:, :], m4[g_][:, :], ALU.add, [m3[g_], m4[g_]], [yh_])
                    if NA == 4:
                        for sg in range(GF // 32):
                            b3 = s3[0]
                            iv = cnt["inv"] % 2; cnt["inv"] += 1
                            lre = yh_[:, 0, sg * 32:(sg + 1) * 32, :].rearrange("p g f -> p (g f)")
                            lim = yh_[:, 1, sg * 32:(sg + 1) * 32, :].rearrange("p g f -> p (g f)")
                            k.mm(b3[:, 0:256], lre, cst["fCS"][:, :], True, False, [yh_, cst["fCS"]], [b3])
                            k.mm(b3[:, 0:256], lim, cst["fnSC"][:, :], False, True, [yh_, cst["fnSC"]], [b3])
                            a_ = c2a[iv]; b_ = c2b[iv]; y_ = y3c[iv]; o_ = yoc[iv]
                            k.tt("dve", a_[:, :], b3[:, 0:256], cst["twA2c"][:, :], ALU.mult, [b3, cst["twA2c"]], [a_])
                            k.tt("dve", b_[:, :], b3[:, 0:256], cst["twB2c"][:, :], ALU.mult, [b3, cst["twB2c"]], [b_])
                            k.tt("pool", y_[:, 0, :], a_[:, 0:128], b_[:, 128:256], ALU.add, [a_, b_], [y_])
                            k.tt("pool", y_[:, 1, :], a_[:, 128:256], b_[:, 0:128], ALU.add, [a_, b_], [y_])
                            p4 = s4[iv]
                            k.mm(p4[0:64, 0:128], cst["gbC"][:, :], y_[:, 0, :], True, False, [cst["gbC"], y_], [p4])
                            k.mm(p4[0:64, 0:128], cst["gbnS"][:, :], y_[:, 1, :], False, True, [cst["gbnS"], y_], [p4])
                            k.copy("act", o_[:, :], p4[0:64, 0:128], [p4], [o_])
                            c0 = lg * LG + cbase + sg * 32
                            for a2 in range(2):
                                k.dma("sp", S["YT"][c0:c0 + 32, tok0 + a2 * 128:tok0 + (a2 + 1) * 128], o_[a2:64:2, :], reads=[o_])
                        continue
                    for sg in range(GF // 4):
                        b3 = s3[0]
                        iv = cnt["inv"] % NB; cnt["inv"] += 1
                        t2a_ = t2a[iv]; t2b_ = t2b[iv]; y3_ = y3[iv]
                        for cc in range(4):
                            ch = sg * 4 + cc
                            k.mm(b3[0:NA, cc * 256:(cc + 1) * 256], yh_[:, 0, ch, :], cst["fCS"][:, :], True, False, [yh_, cst["fCS"]], [b3])
                            k.mm(b3[0:NA, cc * 256:(cc + 1) * 256], yh_[:, 1, ch, :], cst["fnSC"][:, :], False, True, [yh_, cst["fnSC"]], [b3])
                        k.tt("dve", t2a_[:, :], b3[0:NA, :], cst["twA2"][:, :], ALU.mult, [b3, cst["twA2"]], [t2a_])
                        k.tt("dve", t2b_[:, :], b3[0:NA, :], cst["twB2"][:, :], ALU.mult, [b3, cst["twB2"]], [t2b_])
                        av = t2a_[:, :].rearrange("p (g r f) -> p g r f", g=4, r=2)
                        bv = t2b_[:, :].rearrange("p (g r f) -> p g r f", g=4, r=2)
                        k.tt("pool", y3_[:, 0, :, :], av[:, :, 0, :], bv[:, :, 1, :], ALU.add, [t2a_, t2b_], [y3_])
                        k.tt("pool", y3_[:, 1, :, :], av[:, :, 1, :], bv[:, :, 0, :], ALU.add, [t2a_, t2b_], [y3_])
                        p4 = s4[iv % 2]; yo_ = yo[iv % 2]
                        k.mm(p4[0:MO, :], cst["g3C"][:, 0:MO], y3_[:, 0, :, :].rearrange("p g f -> p (g f)"), True, False, [cst["g3C"], y3_], [p4])
                        k.mm(p4[0:MO, :], cst["g3nS"][:, 0:MO], y3_[:, 1, :, :].rearrange("p g f -> p (g f)"), False, True, [cst["g3nS"], y3_], [p4])
                        k.copy("act", yo_[:, :, :].rearrange("p g f -> p (g f)"), p4[0:MO, :], [p4], [yo_])
                        c0 = lg * LG + cbase + sg * 4
                        k.dma("sp", S["YT"][c0:c0 + 4, tok0:tok0 + MO * 128].rearrange("c (a p) -> a c p", p=128), yo_[:, :, :], reads=[yo_])
        return ph

    def make_hycombine(chunks):
        def ph():
            bd = k.sbuf([128, 4], F32); k.dma("sp", bd[:], I["hbd"][:, :], writes=[bd])
            yt = [k.sbuf([128, 512], F32, "yt") for _ in range(2)]
            ut = [k.sbuf([128, 512], F32, "ut") for _ in range(2)]
            x0 = [k.sbuf([128, 512], F32, "x0") for _ in range(2)]
            ob = [k.sbuf([128, 512], BF16, "ob") for _ in range(2)]
            n = 0
            for (t0, W, le, re, j) in chunks:
                rn = RNORM["C" if j == 1 else "L"]
                for jc in range(4):
                    y_ = yt[n % 2]; u_ = ut[n % 2]; x_ = x0[n % 2]; o_ = ob[n % 2]; n += 1
                    rows = slice(jc * 128, (jc + 1) * 128)
                    k.dma("sp", y_[:, 0:W], S["YT"][rows, t0:t0 + W], writes=[y_])
                    k.dma("sp", u_[:, 0:W], S["UF"][rows, t0:t0 + W], writes=[u_])
                    k.dma("sp", x_[:, 0:W], S["X0T"][rows, t0:t0 + W], writes=[x_])
                    k.ts("dve", y_[:, 0:W], y_[:, 0:W], rn[:, jc:jc + 1], None, ALU.mult, None, [y_, rn], [y_])
                    k.stt("dve", y_[:, 0:W], u_[:, 0:W], bd[:, jc:jc + 1], y_[:, 0:W], ALU.mult, ALU.add, [u_, bd, y_], [y_])
                    k.tt("dve", o_[:, 0:W], y_[:, 0:W], x_[:, 0:W], ALU.mult, [y_, x_], [o_])
                    k.dma("pool", S["OCT"][rows, t0:t0 + W], o_[:, 0:W], reads=[o_])
        return ph

    CH_E = [(c * 512, 512, c == 0, False, 0) for c in range(8)] + [(4096, 256, False, True, 0)]
    CH_OWN = [(c * 512, 512, c == 0, False, 0) for c in range(8)]
    phases.append(("inproj0", make_inproj(0, I["xt0"], CHUNKS_ALL)))
    phases.append(("filterL", make_filter("L", SEQ)))
    phases.append(("filterC", make_filter("C", CTX)))
    phases.append(("fftL", make_fftconv("L", 128, SEQ, 0, E // 128)))
    phases.append(("fftC", make_fftconv("C", 4, CTX, SEQ, 2)))
    phases.append(("hycomb", make_hycombine(CH_E + [CTXCH])))
    phases.append(("attn0", make_attn(0)))
    phases.append(("outproj0", make_outproj(0, I["xt0"], S["XM"], CH_E + [CTXCH])))
    phases.append(("ffn0", make_ffn(0, S["XM"], S["X1"], CH_E + [CTXCH])))
    phases.append(("inproj1", make_inproj(1, S["X1"], CH_E + [CTXCH])))
    phases.append(("attn1", make_attn(1)))
    phases.append(("outproj1", make_outproj(1, S["X1"], S["XM1"], CH_E)))
    phases.append(("ffn1", make_ffn(1, S["XM1"], OUT, CH_OWN)))
    for nm, ph in phases:
        k.phase(ph)
        if stop_after == nm:
            break
    k.close()
    return nc


_CACHE = {}


def kernel(**inputs):
    inp = {kk: np.asarray(v) for kk, v in inputs.items()}
    if "C" not in _CACHE:
        C = _consts()
        C["fftL"] = _fft_consts(128); C["fftC"] = _fft_consts(4)
        C["filtL"] = _filter_consts(SEQ); C["filtC"] = _filter_consts(CTX)
        _CACHE["C"] = C
    C = _CACHE["C"]
    nc = _build()
    in_maps = []
    for core in range(8):
        b, hh = core // 2, core % 2
        in_maps.append(_host_prep(inp, b, hh, C))
    res = run_bass_kernel_spmd(nc, in_maps, core_ids=list(range(8)))
    out = np.empty((4, SEQ, D), np.float32)
    for core in range(8):
        b, hh = core // 2, core % 2
        o = np.asarray(res.results[core]["out"]).T
        if hh == 0:
            out[b, :OWN] = o
        else:
            out[b, OWN:] = o[::-1]
    return out
```

```python
import numpy as np
import ml_dtypes
from contextlib import ExitStack
import concourse.bass as bass
import concourse.mybir as mybir
from concourse.bass_utils import run_bass_kernel_spmd

F32 = mybir.dt.float32
BF16 = mybir.dt.bfloat16
I32 = mybir.dt.int32
AF = mybir.ActivationFunctionType
ALU = mybir.AluOpType
NPBF = ml_dtypes.bfloat16

D = 1024; SEQ = 8192; CTX = 256; TT = SEQ + CTX; E = 4352; OWN = 4096
DFF = 2816; NM = 22
SAME_ENGINE_SYNC = {"act", "pool"}


class Res:
    __slots__ = ("name", "last_w", "reads", "excl")
    def __init__(self, name, excl=False):
        self.name = name; self.last_w = None; self.reads = {}; self.excl = excl


class Tl:
    def __init__(self, t, r):
        self.t = t; self.r = r
    def __getitem__(self, idx):
        return self.t[idx]


class K:
    ENGS = ("pe", "act", "dve", "pool", "sp")

    def __init__(self, nc):
        self.nc = nc
        self.es = ExitStack()
        self.sem = {}; self.cnt = {}
        for e in self.ENGS:
            self.sem[e] = self.es.enter_context(nc.semaphore("s_" + e))
            self.cnt[e] = 0
        self.dma_sems = {}
        self.dma_key = {}
        self.dma_rr = {}
        self.NDMASEM = {"sp": 32, "pool": 24, "act": 8, "pe": 4, "dve": 4}
        self.seen = {e: {} for e in self.ENGS}
        self.ops = {e: [] for e in self.ENGS}
        self.phase_es = None
        self.nres = 0
        self.ndma = 0

    def sbuf(self, shape, dt, name=None, persist=False):
        self.nres += 1
        name = (name or "t") + "_%d" % self.nres
        es = self.es if persist else self.phase_es
        t = es.enter_context(self.nc.sbuf_tensor(name, list(shape), dt))
        return Tl(t, Res(name))

    def psum(self, shape, dt, name=None):
        self.nres += 1
        name = (name or "p") + "_%d" % self.nres
        t = self.phase_es.enter_context(self.nc.psum_tensor(name, list(shape), dt))
        return Tl(t, Res(name, excl=True))

    def _need(self, reads, writes, eng=None):
        evs = []
        for r in reads:
            if r.last_w is not None: evs.append(r.last_w)
            if r.excl:
                evs.extend((kk[0], kk[1], v) for kk, v in r.reads.items() if not (kk[0] == "eng" and kk[1] == eng))
        for w in writes:
            if w.last_w is not None: evs.append(w.last_w)
            evs.extend((kk[0], kk[1], v) for kk, v in w.reads.items())
        return evs

    def _emit_waits(self, eng, evs, force_self=False):
        need = {}
        for kind, key, val in evs:
            if kind == "eng":
                if key == eng and eng not in SAME_ENGINE_SYNC and not force_self: continue
                v = val
            else:
                v = val
            if self.seen[eng].get((kind, key), 0) >= v: continue
            if need.get((kind, key), 0) < v: need[(kind, key)] = v
        for (kind, key), v in need.items():
            self.seen[eng][(kind, key)] = v
            sem = self.sem[key] if kind == "eng" else self.dma_sems[key][0]
            self.ops[eng].append(lambda e, sem=sem, v=v: e.wait_ge(sem, v))

    def _commit(self, ev, reads, writes):
        for r in reads:
            kk = (ev[0], ev[1])
            if r.reads.get(kk, 0) < ev[2]: r.reads[kk] = ev[2]
        for w in writes:
            w.last_w = ev; w.reads = {}

    def op(self, eng, fn, reads=(), writes=(), force_self=False):
        reads = [x.r if isinstance(x, Tl) else x for x in reads]
        writes = [x.r if isinstance(x, Tl) else x for x in writes]
        self._emit_waits(eng, self._need(reads, writes, eng), force_self)
        self.cnt[eng] += 1
        sem = self.sem[eng]
        self.ops[eng].append(lambda e, fn=fn, sem=sem: fn(e).then_inc(sem, 1))
        self._commit(("eng", eng, self.cnt[eng]), reads, writes)

    def dma(self, q, out, in_, reads=(), writes=(), **kw):
        reads = [x.r if isinstance(x, Tl) else x for x in reads]
        writes = [x.r if isinstance(x, Tl) else x for x in writes]
        npool = self.NDMASEM[q]
        idx = (q, self.dma_rr.get(q, 0) % npool)
        self.dma_rr[q] = self.dma_rr.get(q, 0) + 1
        if idx not in self.dma_sems:
            s_ = self.es.enter_context(self.nc.semaphore("d_%s_%d" % idx))
            self.dma_sems[idx] = [s_, 0]
        ent = self.dma_sems[idx]
        evs = self._need(reads, writes, q)
        if ent[1] > 0:
            evs.append(("dma", idx, ent[1] * 16))
        self._emit_waits(q, evs)
        ent[1] += 1
        sem = ent[0]
        self.ndma += 1
        self.ops[q].append(lambda e, out=out, in_=in_, sem=sem, kw=kw: e.dma_start(out=out, in_=in_, **kw).then_inc(sem, 16))
        self._commit(("dma", idx, ent[1] * 16), reads, writes)

    def barrier(self):
        evs = [("eng", e, self.cnt[e]) for e in self.ENGS if self.cnt[e] > 0]
        evs += [("dma", kk, v[1] * 16) for kk, v in self.dma_sems.items() if v[1] > 0]
        for e in self.ENGS:
            self._emit_waits(e, [ev for ev in evs if not (ev[0] == "eng" and ev[1] == e)])

    def phase(self, body):
        with ExitStack() as pes:
            self.phase_es = pes
            body()
            self.barrier()
            ops = self.ops
            self.ops = {e: [] for e in self.ENGS}
            with self.nc.Block() as block:
                @block.tensor
                def _(e):
                    for f in ops["pe"]: f(e)
                @block.scalar
                def _(e):
                    for f in ops["act"]: f(e)
                @block.vector
                def _(e):
                    for f in ops["dve"]: f(e)
                @block.gpsimd
                def _(e):
                    for f in ops["pool"]: f(e)
                @block.sync
                def _(e):
                    for f in ops["sp"]: f(e)
        self.phase_es = None

    def close(self):
        self.es.close()

    def ts(self, eng, out, in0, s1, s2, op0, op1, r, w, force_self=False):
        if s2 is None:
            s2 = 0.0; op1 = ALU.add
        self.op(eng, lambda e: e.tensor_scalar(out=out, in0=in0, scalar1=s1, scalar2=s2, op0=op0, op1=op1), r, w, force_self)
    def stt(self, eng, out, in0, sc, in1, op0, op1, r, w):
        self.op(eng, lambda e: e.scalar_tensor_tensor(out=out, in0=in0, scalar=sc, in1=in1, op0=op0, op1=op1), r, w)
    def tt(self, eng, out, in0, in1, op, r, w):
        self.op(eng, lambda e: e.tensor_tensor(out=out, in0=in0, in1=in1, op=op), r, w)
    def act(self, out, in_, func, r, w, bias=None, scale=None, accum=None):
        kw = {}
        if bias is not None: kw["bias"] = bias
        if scale is not None: kw["scale"] = scale
        if accum is not None: kw["accum_out"] = accum
        self.op("act", lambda e: e.activation(out=out, in_=in_, func=func, **kw), r, w)
    def mm(self, out, lhsT, rhs, start, stop, r, w):
        self.op("pe", lambda e: e.matmul(out, lhsT=lhsT, rhs=rhs, start=start, stop=stop), r, w)
    def copy(self, eng, out, in_, r, w):
        if eng == "act":
            self.op("act", lambda e: e.copy(out=out, in_=in_), r, w)
        else:
            self.op(eng, lambda e: e.tensor_copy(out=out, in_=in_), r, w)
    def memset(self, eng, ap, val, w):
        self.op(eng, lambda e: e.memset(ap, val), [], w)
    def recip(self, out, in_, r, w, force_self=False):
        self.op("dve", lambda e: e.reciprocal(out=out, in_=in_), r, w, force_self)

def _consts():
    c = {}
    nf = 16
    inv = 10000.0 ** (-np.arange(nf, dtype=np.float64) / nf)
    t = np.arange(SEQ)
    row = (t // 64).astype(np.float64); col = (t % 64).astype(np.float64)
    ar = row[None, :] * inv[:, None]; ac = col[None, :] * inv[:, None]
    cos64 = np.concatenate([np.cos(ar), np.cos(ar), np.cos(ac), np.cos(ac)], 0)
    sin64 = np.concatenate([-np.sin(ar), np.sin(ar), -np.sin(ac), np.sin(ac)], 0)
    c["cos64"] = cos64.astype(np.float32); c["sin64"] = sin64.astype(np.float32)
    perm = np.concatenate([np.arange(16) + 16, np.arange(16), np.arange(16) + 48, np.arange(16) + 32])
    c["perm64"] = perm
    p = np.arange(128)[:, None]; f = np.arange(512)[None, :]
    c["bmask"] = np.stack([(np.abs(128 * r + p - f) <= 128) for r in range(-1, 5)], 0).astype(NPBF)
    return c


def _fft_consts(NA):
    N = 128 * NA
    c = {}
    a = np.arange(NA)[:, None]; f1 = np.arange(NA)[None, :]
    th = 2 * np.pi * a * f1 / NA
    c["f1cs"] = np.concatenate([np.cos(th), -np.sin(th)], 1).astype(NPBF)
    p = np.arange(128)[:, None]; f2 = np.arange(128)[None, :]
    th2 = 2 * np.pi * p * f2 / 128
    C = np.cos(th2); S = np.sin(th2)
    c["fC"] = C.astype(NPBF); c["fS"] = S.astype(NPBF); c["fnS"] = (-S).astype(NPBF)
    c["fCS"] = np.concatenate([C, S], 1).astype(NPBF)
    c["fnSC"] = np.concatenate([-S, C], 1).astype(NPBF)
    tw = 2 * np.pi * np.arange(128)[:, None] * np.arange(NA)[None, :] / N
    G = 512 // (2 * NA)
    tc_ = np.cos(tw); ts_ = np.sin(tw)
    A = np.concatenate([tc_, tc_], 1)
    B = np.concatenate([ts_, -ts_], 1)
    c["twA"] = np.tile(A[:, None, :], (1, G, 1)).reshape(128, 512).astype(np.float32)
    c["twB"] = np.tile(B[:, None, :], (1, G, 1)).reshape(128, 512).astype(np.float32)
    tcT = np.cos(tw).T; tsT = np.sin(tw).T
    A2 = np.stack([tcT, tcT], 1)
    B2 = np.stack([tsT, -tsT], 1)
    c["twA2"] = np.tile(A2[:, None], (1, 4, 1, 1)).reshape(NA, 1024).astype(np.float32)
    c["twB2"] = np.tile(B2[:, None], (1, 4, 1, 1)).reshape(NA, 1024).astype(np.float32)
    th = 2 * np.pi * np.arange(NA)[:, None] * np.arange(NA)[None, :] / NA
    c["g3C"] = (np.cos(th) / N).astype(NPBF); c["g3nS"] = (-np.sin(th) / N).astype(NPBF)
    if NA == 128:
        pp_ = np.arange(128, dtype=np.int64)[:, None, None]; f1_ = np.arange(128, dtype=np.int64)[None, :, None]; f2_ = np.arange(128, dtype=np.int64)[None, None, :]
        ang = 2 * np.pi * ((pp_ * (f1_ + 128 * f2_)) % N).astype(np.float64) / N
        c["MC"] = np.cos(ang).astype(NPBF); c["MS"] = np.sin(ang).astype(NPBF)
    if NA == 4:
        f1i = np.arange(128) % 4
        twp = 2 * np.pi * f1i[:, None] * np.arange(128)[None, :] / N
        c["twA2c"] = np.concatenate([np.cos(twp), np.cos(twp)], 1).astype(np.float32)
        c["twB2c"] = np.concatenate([np.sin(twp), -np.sin(twp)], 1).astype(np.float32)
        gC = np.zeros((128, 64), np.float64); gS = np.zeros((128, 64), np.float64)
        for ch in range(32):
            for f1_ in range(4):
                for a_ in range(2):
                    gC[ch * 4 + f1_, ch * 2 + a_] = np.cos(2 * np.pi * f1_ * a_ / 4) / N
                    gS[ch * 4 + f1_, ch * 2 + a_] = -np.sin(2 * np.pi * f1_ * a_ / 4) / N
        c["gbC"] = gC.astype(NPBF); c["gbnS"] = gS.astype(NPBF)
    return c


def _filter_consts(n):
    q = np.arange(2 * n)
    tap = np.where(q < n, q, 2 * n - q).astype(np.int64)
    tap = np.minimum(tap, n - 1)
    t01 = np.linspace(0.0, 1.0, n, dtype=np.float32)
    bands = 16
    w = (2.0 * np.pi * np.arange(n, dtype=np.float32) / n).astype(np.float32)
    f = np.linspace(1e-4, bands - 1, bands, dtype=np.float32)[None, :]
    feats = np.concatenate([t01[:, None], np.cos(f * w[:, None]), -np.sin(f * w[:, None])], -1).astype(np.float32)
    featsT = np.ascontiguousarray(feats[tap].T)
    t01b = np.ascontiguousarray(np.tile(t01[tap][None, :], (128, 1)))
    deltas = np.abs(np.linspace(np.log(1e-2) / 1.5, np.log(1e-2) / 0.3, 512, dtype=np.float32))
    ndel = np.ascontiguousarray((-deltas).reshape(4, 128).T)
    return featsT.astype(np.float32), t01b.astype(np.float32), ndel.astype(np.float32)


def _pm(v, n):
    return np.ascontiguousarray(np.asarray(v, np.float32).reshape(n, 128).T)


def _host_prep(inp, b, hh, C):
    fl = (hh == 1)
    m = {}
    x = inp["x"][b]; cx = inp["ctx"][b]
    if fl: x = x[::-1]; cx = cx[::-1]
    m["xt0"] = np.ascontiguousarray(np.concatenate([x, cx], 0).T)
    cv = np.stack([inp["c"][b], inp["c_ctx"]], 1)
    m["cvec"] = np.ascontiguousarray(cv.reshape(8, 128, 2).transpose(1, 0, 2))
    perm = C["perm64"]
    for i in range(2):
        m[f"ada_w{i}"] = inp["ada_w"][i]
        m[f"ada_b{i}"] = _pm(inp["ada_b"][i], 48)
        m[f"nmix{i}"] = _pm(inp["norm_mix"][i], 8)
        m[f"nffn{i}"] = _pm(inp["norm_ffn"][i], 8)
        w = inp["mix_w_in"][i]
        qk = w[:, :768].reshape(D, 12, 64)[:, :, perm].reshape(D, 768)
        m[f"w_in{i}"] = np.ascontiguousarray(np.concatenate([w, qk], 1))
        m[f"w_out{i}"] = inp["mix_w_out"][i]
        gq = inp["attn_q_norm"][i]; gk = inp["attn_k_norm"][i]
        m[f"qkg{i}"] = np.ascontiguousarray(np.stack([np.tile(gq, 2), np.tile(gq[perm], 2), np.tile(gk, 2), np.tile(gk[perm], 2)], 1).astype(np.float32))
        m[f"w_up{i}"] = inp["ffn_w_up"][i]
        m[f"w_dn{i}"] = inp["ffn_w_down"][i]
        fw = inp["ffn_conv_w"][i]
        if fl: fw = fw[::-1]
        m[f"fcw{i}"] = np.ascontiguousarray(fw.reshape(3, NM, 128).transpose(2, 1, 0))
        m[f"fcb{i}"] = _pm(inp["ffn_conv_b"][i], NM)
    hw = inp["hy_conv_w"][0]
    if fl: hw = hw[::-1]
    m["hcw"] = np.ascontiguousarray(hw.reshape(3, 12, 128).transpose(2, 1, 0))
    m["hcb"] = _pm(inp["hy_conv_b"][0], 12)
    sw = inp["sc_conv_w"][0]
    if fl: sw = sw[::-1]
    m["scw"] = np.ascontiguousarray(sw.reshape(3, 4, 128).transpose(2, 1, 0))
    m["hw1"] = inp["hy_w1"][0]; m["hw2"] = inp["hy_w2"][0]; m["hw3"] = inp["hy_w3"][0]
    w4 = inp["hy_w4"][0]
    if fl: w4 = np.concatenate([w4[:, 512:], w4[:, :512]], 1)
    m["hw4"] = np.ascontiguousarray(w4)
    m["hb"] = np.ascontiguousarray(np.stack([inp["hy_b1"][0], inp["hy_b2"][0], inp["hy_b3"][0], inp["hy_freq"][0]], 1).astype(np.float32))
    m["hbd"] = _pm(inp["hy_bias_d"][0], 4)
    m["sink"] = np.ascontiguousarray(np.tile(inp["swa_sink"][0][None, :], (128, 1)).astype(np.float32))
    cos = C["cos64"]; sin = C["sin64"]
    if fl: cos = cos[:, ::-1]; sin = sin[:, ::-1]
    cosx = np.concatenate([cos, np.ones((64, CTX), np.float32)], 1)
    sinx = np.concatenate([sin, np.zeros((64, CTX), np.float32)], 1)
    m["cosT"] = np.ascontiguousarray(np.concatenate([cosx, cosx], 0))
    m["sinT"] = np.ascontiguousarray(np.concatenate([sinx, sinx], 0))
    m["bmask"] = C["bmask"]
    for tag, NA in (("L", 128), ("C", 4)):
        for kk, v in C["fft" + tag].items():
            m[kk + tag] = v
    for tag in ("L", "C"):
        ft, t01b, ndel = C["filt" + tag]
        m["featsT" + tag] = ft; m["t01b" + tag] = t01b
    m["ndel"] = C["filtL"][2]
    return m

def _build(stop_after=None, dbg=()):
    nc = bass.Bass("TRN2", target_bir_lowering=False)
    k = K(nc)
    def din(name, shape, dt=F32):
        return nc.dram_tensor(name, list(shape), dt, kind="ExternalInput").ap()
    def dscr(name, shape, dt):
        kind = "ExternalOutput" if name in dbg else "Internal"
        return nc.dram_tensor(name, list(shape), dt, kind=kind).ap()
    I = {}
    I["xt0"] = din("xt0", [D, TT]); I["cvec"] = din("cvec", [128, 8, 2])
    for i in range(2):
        I[f"ada_w{i}"] = din(f"ada_w{i}", [D, 6 * D]); I[f"ada_b{i}"] = din(f"ada_b{i}", [128, 48])
        I[f"nmix{i}"] = din(f"nmix{i}", [128, 8]); I[f"nffn{i}"] = din(f"nffn{i}", [128, 8])
        I[f"w_in{i}"] = din(f"w_in{i}", [D, 3328]); I[f"w_out{i}"] = din(f"w_out{i}", [D, D])
        I[f"qkg{i}"] = din(f"qkg{i}", [128, 4])
        I[f"w_up{i}"] = din(f"w_up{i}", [D, 2 * DFF]); I[f"w_dn{i}"] = din(f"w_dn{i}", [DFF, D])
        I[f"fcw{i}"] = din(f"fcw{i}", [128, NM, 3]); I[f"fcb{i}"] = din(f"fcb{i}", [128, NM])
    I["hcw"] = din("hcw", [128, 12, 3]); I["hcb"] = din("hcb", [128, 12]); I["scw"] = din("scw", [128, 4, 3])
    I["hw1"] = din("hw1", [33, 64]); I["hw2"] = din("hw2", [64, 64]); I["hw3"] = din("hw3", [64, 64])
    I["hw4"] = din("hw4", [64, 1024]); I["hb"] = din("hb", [64, 4]); I["hbd"] = din("hbd", [128, 4])
    I["sink"] = din("sink", [128, 8])
    I["cosT"] = din("cosT", [128, TT]); I["sinT"] = din("sinT", [128, TT])
    I["bmask"] = din("bmask", [6, 128, 512], BF16)
    for tag, NA in (("L", 128), ("C", 4)):
        I["f1cs" + tag] = din("f1cs" + tag, [NA, 2 * NA], BF16)
        for nm in ("fC", "fS", "fnS"): I[nm + tag] = din(nm + tag, [128, 128], BF16)
        for nm in ("fCS", "fnSC"): I[nm + tag] = din(nm + tag, [128, 256], BF16)
        for nm in ("twA", "twB"): I[nm + tag] = din(nm + tag, [128, 512])
        for nm in ("twA2", "twB2"): I[nm + tag] = din(nm + tag, [NA, 1024])
        for nm in ("g3C", "g3nS"): I[nm + tag] = din(nm + tag, [NA, NA], BF16)
        if NA == 128:
            for nm in ("MC", "MS"): I[nm + tag] = din(nm + tag, [128, 128, 128], BF16)
        if NA == 4:
            for nm in ("twA2c", "twB2c"): I[nm + tag] = din(nm + tag, [128, 256])
            for nm in ("gbC", "gbnS"): I[nm + tag] = din(nm + tag, [128, 64], BF16)
        n = 64 * NA
        I["featsT" + tag] = din("featsT" + tag, [33, 2 * n]); I["t01b" + tag] = din("t01b" + tag, [128, 2 * n])
    I["ndel"] = din("ndel", [128, 4])
    OUT = nc.dram_tensor("out", [D, OWN], F32, kind="ExternalOutput").ap()

    S = {}
    for i in range(2):
        S[f"bw_in{i}"] = dscr(f"bw_in{i}", [D, 3328], BF16); S[f"bw_out{i}"] = dscr(f"bw_out{i}", [D, D], BF16)
        S[f"bw_up{i}"] = dscr(f"bw_up{i}", [D, 2 * DFF], BF16); S[f"bw_dn{i}"] = dscr(f"bw_dn{i}", [DFF, D], BF16)
    S["QT"] = dscr("QT", [8, 64, TT], BF16); S["KT"] = dscr("KT", [4, 64, TT], BF16); S["V"] = dscr("V", [TT, 256], BF16)
    S["UT"] = dscr("UT", [512, TT], BF16); S["X0T"] = dscr("X0T", [512, TT], F32)
    S["UF"] = dscr("UF", [512, TT], F32)
    S["UTA"] = dscr("UTA", [64, 512, 128], BF16); S["KNA"] = dscr("KNA", [128, 512, 128], BF16)
    S["YT"] = dscr("YT", [512, TT], F32)
    S["OAT"] = dscr("OAT", [512, TT], BF16); S["OCT"] = dscr("OCT", [512, TT], BF16)
    S["XM"] = dscr("XM", [D, TT], F32); S["X1"] = dscr("X1", [D, TT], F32); S["XM1"] = dscr("XM1", [D, TT], F32)
    S["KRAWL"] = dscr("KRAWL", [512, 2 * SEQ], F32); S["KNL"] = dscr("KNL", [512, 2 * SEQ], BF16)
    S["KRAWC"] = dscr("KRAWC", [512, 2 * CTX], F32); S["KNC"] = dscr("KNC", [512, 2 * CTX], BF16)

    MOD = k.sbuf([128, 2, 48, 2], F32, "mod", persist=True)
    AB = k.sbuf([128, 2, 2, 2, 8, 2], F32, "ab", persist=True)
    ONES = k.sbuf([128, 128], BF16, "ones", persist=True)
    BONES = k.sbuf([128, 128], BF16, "bones", persist=True)
    SEL = k.sbuf([128, 64], F32, "sel", persist=True)
    ESINK = k.sbuf([128, 8], F32, "esink", persist=True)
    RNORM = {"L": k.sbuf([128, 4], F32, "rnL", persist=True), "C": k.sbuf([128, 4], F32, "rnC", persist=True)}

    phases = []

    conv_items = []
    for i in range(2):
        for src, dst, rows, cols in ((f"w_in{i}", f"bw_in{i}", D, 3328), (f"w_out{i}", f"bw_out{i}", D, D),
                                     (f"w_up{i}", f"bw_up{i}", D, 2 * DFF), (f"w_dn{i}", f"bw_dn{i}", DFF, D)):
            for r0 in range(0, rows, 128):
                for c0 in range(0, cols, 2048):
                    conv_items.append((src, dst, r0, c0, min(2048, cols - c0)))
    CONV_EARLY = sum(1 for it in conv_items if it[0] in ("w_in0", "w_out0"))
    conv_pos = [0]

    class make_converter:
        def __init__(self):
            self.stg = [k.sbuf([128, 2048], F32, "stg") for _ in range(3)]
            self.stb = [k.sbuf([128, 2048], BF16, "stb") for _ in range(3)]
        def step(self, eng=None, q="sp"):
            n = conv_pos[0]
            if n >= len(conv_items): return False
            conv_pos[0] += 1
            src, dst, r0, c0, cw = conv_items[n]
            a = self.stg[n % 3]; bt = self.stb[n % 3]
            k.dma(q, a[:, 0:cw], I[src][r0:r0 + 128, c0:c0 + cw], writes=[a])
            k.copy(eng or ("dve" if n % 2 == 0 else "act"), bt[:, 0:cw], a[:, 0:cw], [a], [bt])
            k.dma(q, S[dst][r0:r0 + 128, c0:c0 + cw], bt[:, 0:cw], reads=[bt])
            return True

    def ph_setup():
        k.memset("dve", ONES[:], 1.0, [ONES])
        k.memset("dve", BONES[:], 0.0, [BONES])
        k.memset("dve", BONES[0:64, 0:64], 1.0, [BONES])
        k.memset("dve", BONES[64:128, 64:128], 1.0, [BONES])
        k.memset("dve", SEL[:], 0.0, [SEL])
        k.memset("dve", SEL[64:65, :], 1.0, [SEL])
        snk = k.sbuf([128, 8], F32)
        k.dma("sp", snk[:], I["sink"][:, :], writes=[snk])
        k.act(ESINK[:], snk[:], AF.Exp, [snk], [ESINK])
        cv_ = make_converter()
        for _ in range(CONV_EARLY):
            cv_.step()
        cv = k.sbuf([128, 8, 2], F32)
        k.dma("sp", cv[:], I["cvec"][:, :, :], writes=[cv])
        sc = k.sbuf([128, 8, 2], F32)
        k.act(sc[:], cv[:], AF.Silu, [cv], [sc])
        wst = [k.sbuf([128, 8, 512], F32, "wst") for _ in range(2)]
        ps = [k.psum([128, 512], F32) for _ in range(2)]
        n = 0
        for i in range(2):
            adb = k.sbuf([128, 48], F32)
            k.dma("sp", adb[:], I[f"ada_b{i}"][:, :], writes=[adb])
            for cb in range(12):
                wt = wst[n % 2]; n += 1
                k.dma("sp", wt[:], I[f"ada_w{i}"][:, cb * 512:(cb + 1) * 512].rearrange("(k p) f -> p k f", p=128), writes=[wt])
                for mi in range(4):
                    m = cb * 4 + mi
                    pt = ps[m % 2]
                    for kk in range(8):
                        k.mm(pt[:, 0:2], wt[:, kk, mi * 128:(mi + 1) * 128], sc[:, kk, :], kk == 0, kk == 7, [wt, sc], [pt])
                    k.ts("dve", MOD[:, i, m, :], pt[:, 0:2], adb[:, m:m + 1], None, ALU.add, None, [pt, adb], [MOD])
            for wh, (nm, sh0, sc0) in enumerate(((f"nmix{i}", 0, 8), (f"nffn{i}", 24, 32))):
                g = k.sbuf([128, 8], F32)
                k.dma("sp", g[:], I[nm][:, :], writes=[g])
                for j in range(2):
                    k.stt("dve", AB[:, i, wh, 0, :, j], MOD[:, i, sc0:sc0 + 8, j], 1.0, g[:], ALU.add, ALU.mult, [MOD, g], [AB])
                    k.copy("dve", AB[:, i, wh, 1, :, j], MOD[:, i, sh0:sh0 + 8, j], [MOD], [AB])
    phases.append(("setup", ph_setup))

    def norm_mod(xh, Wc, i, wh, j, sq, ssps, rstd, tmp, h):
        for kk in range(8):
            k.act(sq[:, kk, 0:Wc], xh[:, kk, 0:Wc], AF.Square, [xh], [sq])
        for (c0, c1) in ((0, min(512, Wc)), (512, Wc)):
            if c1 <= c0: continue
            for kk in range(8):
                k.mm(ssps[:, c0:c1], ONES[:, :], sq[:, kk, c0:c1], kk == 0, kk == 7, [ONES, sq], [ssps])
        k.act(rstd[:, 0:Wc], ssps[:, 0:Wc], AF.Sqrt, [ssps], [rstd], bias=1e-6, scale=1.0 / D)
        k.recip(rstd[:, 0:Wc], rstd[:, 0:Wc], [rstd], [rstd])
        for kk in range(8):
            t = tmp[kk % 2]
            k.stt("dve", t[:, 0:Wc], xh[:, kk, 0:Wc], AB[:, i, wh, 0, kk, j:j + 1], rstd[:, 0:Wc], ALU.mult, ALU.mult, [xh, AB, rstd], [t])
            k.act(h[:, kk, 0:Wc], t[:, 0:Wc], AF.Identity, [t, AB], [h], bias=AB[:, i, wh, 1, kk, j:j + 1], scale=1.0)

    CHUNKS_ALL = [(c * 512, 512, c == 0, c == 15, 0) for c in range(16)] + [(SEQ, CTX, True, True, 1)]
    CHUNKS_E = [(c * 512, 512, c == 0, False, 0) for c in range(9)]
    CTXCH = (SEQ, CTX, True, True, 1)

    def load_xh(xh, src, t0, W, ledge, redge, q="sp"):
        lo = 2 if ledge else 1
        hi = W + 2 if redge else W + 3
        if ledge: k.memset("pool", xh[:, :, 1:2], 0.0, [xh])
        if redge: k.memset("pool", xh[:, :, W + 2:W + 3], 0.0, [xh])
        k.dma(q, xh[:, :, lo:hi], src[:, t0 - 2 + lo:t0 - 2 + hi].rearrange("(k p) t -> p k t", p=128), writes=[xh])

    def make_inproj(i, XIN, chunks):
        def ph():
            w = k.sbuf([128, 8, 3328], BF16, "w_in")
            for kk in range(8):
                k.dma("sp", w[:, kk, :], S[f"bw_in{i}"][kk * 128:(kk + 1) * 128, :], writes=[w])
            qkg = k.sbuf([128, 4], F32); k.dma("sp", qkg[:], I[f"qkg{i}"][:, :], writes=[qkg])
            if i == 0:
                cw = k.sbuf([128, 12, 3], F32); k.dma("sp", cw[:], I["hcw"][:, :, :], writes=[cw])
                cb = k.sbuf([128, 12], F32); k.dma("sp", cb[:], I["hcb"][:, :], writes=[cb])
            else:
                cw = k.sbuf([128, 4, 3], F32); k.dma("sp", cw[:], I["scw"][:, :, :], writes=[cw])
            xhs = [k.sbuf([128, 8, 516], F32, "xh") for _ in range(2)]
            for t_ in xhs: k.memset("pool", t_[:], 0.0, [t_])
            sq = k.sbuf([128, 8, 516], BF16, "sq")
            h = k.sbuf([128, 8, 516], BF16, "h")
            rstd = k.sbuf([128, 516], F32, "rstd")
            tmp = [k.sbuf([128, 516], F32, "tmp") for _ in range(2)]
            ssps = k.psum([128, 1024], F32, "ssps")
            zps = [k.psum([128, 512], F32, "zps") for _ in range(4)]
            cps = k.psum([128, 1024], F32, "cps")
            cosb = k.sbuf([128, 512], F32, "cos"); sinb = k.sbuf([128, 512], F32, "sin")
            sq2 = [k.sbuf([128, 512], BF16, "sq2") for _ in range(2)]
            rs = [k.sbuf([128, 512], F32, "rs") for _ in range(2)]
            ta = [k.sbuf([128, 512], F32, "ta") for _ in range(2)]
            tb = [k.sbuf([128, 512], F32, "tb") for _ in range(2)]
            qo = [k.sbuf([128, 512], BF16, "qo") for _ in range(2)]
            vo = [k.sbuf([128, 256], BF16, "vo") for _ in range(2)]
            asb = [k.sbuf([128, 516], F32, "asb") for _ in range(3)]
            c1 = [k.sbuf([128, 512], F32, "c1") for _ in range(3)]
            uo = [k.sbuf([128, 512], F32, "uo") for _ in range(2)]
            ub = [k.sbuf([128, 512], BF16, "ub") for _ in range(2)]
            pp = [k.sbuf([128, 516], F32, "pp") for _ in range(2)]
            nq = [0]
            import os
            PARTS = os.environ.get("INPROJ_PARTS", "nqvc")
            NCHK = int(os.environ.get("INPROJ_NCH", "99"))
            for ci, (t0, W, le, re, j) in enumerate(chunks[:NCHK]):
                Wc = W + 4
                do_q = (t0 < E) or j == 1
                if i == 1 and j == 1: do_q = False
                xh = xhs[ci % 2]
                load_xh(xh, XIN, t0, W, le, re)
                norm_mod(xh, Wc, i, 0, j, sq, ssps, rstd, tmp, h)
                k.dma("sp", cosb[:, 0:W], I["cosT"][:, t0:t0 + W], writes=[cosb])
                k.dma("sp", sinb[:, 0:W], I["sinT"][:, t0:t0 + W], writes=[sinb])
                for pr in range(6):
                    if "q" not in PARTS: continue
                    if pr < 4 and not do_q: continue
                    n = nq[0]; nq[0] += 1
                    zp = zps[(2 * n) % 4]; zsp = zps[(2 * n + 1) % 4]
                    for kk in range(8):
                        k.mm(zp[:, 0:W], w[:, kk, pr * 128:(pr + 1) * 128], h[:, kk, 2:W + 2], kk == 0, kk == 7, [w, h], [zp])
                    for kk in range(8):
                        k.mm(zsp[:, 0:W], w[:, kk, 2560 + pr * 128:2560 + (pr + 1) * 128], h[:, kk, 2:W + 2], kk == 0, kk == 7, [w, h], [zsp])
                    s2 = sq2[n % 2]; r_ = rs[n % 2]; a_ = ta[n % 2]; b_ = tb[n % 2]; q_ = qo[n % 2]
                    gi = 0 if pr < 4 else 2
                    k.act(s2[:, 0:W], zp[:, 0:W], AF.Square, [zp], [s2])
                    k.stt("dve", a_[:, 0:W], zp[:, 0:W], qkg[:, gi:gi + 1], cosb[:, 0:W], ALU.mult, ALU.mult, [zp, qkg, cosb], [a_])
                    k.stt("dve", b_[:, 0:W], zsp[:, 0:W], qkg[:, gi + 1:gi + 2], sinb[:, 0:W], ALU.mult, ALU.mult, [zsp, qkg, sinb], [b_])
                    k.mm(zp[:, 0:W], BONES[:, :], s2[:, 0:W], True, True, [BONES, s2], [zp])
                    k.act(r_[:, 0:W], zp[:, 0:W], AF.Sqrt, [zp], [r_], bias=1e-6, scale=1.0 / 64)
                    k.recip(r_[:, 0:W], r_[:, 0:W], [r_], [r_])
                    QS = os.environ.get("QSKIP", "")
                    pe_ = "dve" if "pool" in QS else "pool"
                    k.tt(pe_, a_[:, 0:W], a_[:, 0:W], b_[:, 0:W], ALU.add, [a_, b_], [a_])
                    k.tt(pe_, q_[:, 0:W], a_[:, 0:W], r_[:, 0:W], ALU.mult, [a_, r_], [q_])
                    for hf in range(2):
                        if "dma" in QS: continue
                        if pr < 4:
                            dst = S["QT"][2 * pr + hf, :, t0:t0 + W]
                        else:
                            dst = S["KT"][2 * (pr - 4) + hf, :, t0:t0 + W]
                        k.dma("pool", dst, q_[hf * 64:(hf + 1) * 64, 0:W], reads=[q_])
                for tj in range(W // 128):
                    if "v" not in PARTS: continue
                    n = nq[0]; nq[0] += 1
                    vp = zps[n % 4]
                    for kk in range(8):
                        k.mm(vp[:, 0:256], h[:, kk, 2 + tj * 128:2 + (tj + 1) * 128], w[:, kk, 768:1024], kk == 0, kk == 7, [w, h], [vp])
                    v_ = vo[n % 2]
                    k.copy("act", v_[:, :], vp[:, 0:256], [vp], [v_])
                    k.dma("pool", S["V"][t0 + tj * 128:t0 + (tj + 1) * 128, :], v_[:, :], reads=[v_])
                def convproj(m, dst):
                    for (c0, c1_) in ((0, min(512, Wc)), (512, Wc)):
                        if c1_ <= c0: continue
                        for kk in range(8):
                            k.mm(cps[:, c0:c1_], w[:, kk, 1024 + m * 128:1024 + (m + 1) * 128], h[:, kk, c0:c1_], kk == 0, kk == 7, [w, h], [cps])
                    k.copy("act", dst[:, 0:Wc], cps[:, 0:Wc], [cps], [dst])
                    if le: k.memset("pool", dst[:, 1:2], 0.0, [dst])
                    if re: k.memset("pool", dst[:, W + 2:W + 3], 0.0, [dst])
                def conv3(out, a, wts, m, bias, eng="dve"):
                    if bias is not None:
                        k.ts(eng, out[:, 0:W], a[:, 2:W + 2], wts[:, m, 1:2], bias, ALU.mult, ALU.add, [a, wts, cb], [out])
                    else:
                        k.ts(eng, out[:, 0:W], a[:, 2:W + 2], wts[:, m, 1:2], None, ALU.mult, None, [a, wts], [out])
                    k.stt(eng, out[:, 0:W], a[:, 1:W + 1], wts[:, m, 0:1], out[:, 0:W], ALU.mult, ALU.add, [a, wts, out], [out])
                    k.stt(eng, out[:, 0:W], a[:, 3:W + 3], wts[:, m, 2:3], out[:, 0:W], ALU.mult, ALU.add, [a, wts, out], [out])
                for jc in range(4):
                    if "c" not in PARTS: continue
                    if i == 0:
                        convproj(4 + jc, asb[0]); conv3(c1[0], asb[0], cw, 4 + jc, cb[:, 4 + jc:5 + jc])
                        convproj(8 + jc, asb[1]); conv3(c1[1], asb[1], cw, 8 + jc, cb[:, 8 + jc:9 + jc])
                        u_ = uo[jc % 2]; ub_ = ub[jc % 2]
                        k.tt("dve", u_[:, 0:W], c1[0][:, 0:W], c1[1][:, 0:W], ALU.mult, [c1[0], c1[1]], [u_])
                        k.copy("pool", ub_[:, 0:W], u_[:, 0:W], [u_], [ub_])
                        k.dma("pool", S["UF"][jc * 128:(jc + 1) * 128, t0:t0 + W], u_[:, 0:W], reads=[u_])
                        if j == 0:
                            k.dma("pool", S["UTA"][t0 // 128:t0 // 128 + 4, jc * 128:(jc + 1) * 128, :].rearrange("a c p -> c a p"),
                                  ub_[:, 0:W].rearrange("c (a p) -> c a p", p=128), reads=[ub_])
                        else:
                            k.dma("pool", S["UT"][jc * 128:(jc + 1) * 128, t0:t0 + W], ub_[:, 0:W], reads=[ub_])
                        if do_q:
                            convproj(jc, asb[2]); conv3(c1[2], asb[2], cw, jc, cb[:, jc:jc + 1])
                            k.dma("pool", S["X0T"][jc * 128:(jc + 1) * 128, t0:t0 + W], c1[2][:, 0:W], reads=[c1[2]])
                    elif j == 0:
                        convproj(4 + jc, asb[0]); convproj(8 + jc, asb[1])
                        p_ = pp[jc % 2]
                        k.tt("dve", p_[:, 0:Wc], asb[0][:, 0:Wc], asb[1][:, 0:Wc], ALU.mult, [asb[0], asb[1]], [p_])
                        conv3(c1[0], p_, cw, jc, None)
                        convproj(jc, asb[2])
                        ub_ = ub[jc % 2]
                        k.tt("pool", ub_[:, 0:W], c1[0][:, 0:W], asb[2][:, 2:W + 2], ALU.mult, [c1[0], asb[2]], [ub_])
                        k.dma("pool", S["OCT"][jc * 128:(jc + 1) * 128, t0:t0 + W], ub_[:, 0:W], reads=[ub_])
        return ph

    def make_attn(i):
        def ph():
            NKT = TT // 128
            kT = k.sbuf([128, TT], BF16, "kT")
            k.memset("pool", kT[64:128, :], 0.0, [kT])
            va = k.sbuf([128, NKT, 65], BF16, "va")
            qT = [k.sbuf([128, 512], BF16, "qT") for _ in range(2)]
            for t_ in qT: k.memset("pool", t_[64:128, :], 0.0, [t_])
            sps = [k.psum([128, 512], F32, "sps") for _ in range(4)]
            ops_ = [k.psum([128, 512], F32, "ops") for _ in range(2)]
            bps = k.psum([128, 512], F32, "bps")
            pT = [k.sbuf([128, 512], BF16, "pT") for _ in range(4)]
            osb = [k.sbuf([65, 512], F32, "osb") for _ in range(2)]
            rb = [k.sbuf([64, 512], F32, "rb") for _ in range(2)]
            ob = [k.sbuf([64, 512], BF16, "ob") for _ in range(2)]
            if i == 1:
                bm = k.sbuf([128, 6, 512], BF16, "bm")
                for r in range(6):
                    k.dma("sp", bm[:, r, :], I["bmask"][r, :, :], writes=[bm])
            qch = []
            for c in range(9):
                t0 = c * 512
                Wq = 512 if c < 8 else 256
                if i == 0:
                    kts = [(kt, 0, Wq, None) for kt in range(NKT)]
                else:
                    kts = []
                    for r in range(-1, 5):
                        kt = 4 * c + r
                        if kt < 0 or kt >= E // 128: continue
                        f0 = max(0, 128 * (r - 1)); f1 = min(Wq, 128 * (r + 2))
                        if f1 <= f0: continue
                        kts.append((kt, f0, f1, r + 1))
                    kts += [(64, 0, Wq, None), (65, 0, Wq, None)]
                qch.append((t0, Wq, kts))
            if i == 0:
                qch.append((SEQ, CTX, [(64, 0, CTX, None), (65, 0, CTX, None)]))
            n = 0; nh = 0
            cvt = make_converter() if i == 0 else None
            for jkv in range(4):
                k.dma("sp", kT[0:64, :], S["KT"][jkv, :, :], writes=[kT])
                k.dma("sp", va[:, :, 0:64], S["V"][:, jkv * 64:(jkv + 1) * 64].rearrange("(n p) d -> p n d", p=128), writes=[va])
                k.memset("pool", va[:, :, 64:65], 1.0, [va])
                for g in range(2):
                    hq = 2 * jkv + g
                    for (t0, W, kts) in qch:
                        q_ = qT[nh % 2]; op_ = ops_[nh % 2]; o_ = osb[nh % 2]; r_ = rb[nh % 2]; b_ = ob[nh % 2]
                        nh += 1
                        k.dma("sp", q_[0:64, 0:W], S["QT"][hq, :, t0:t0 + W], writes=[q_])
                        if i == 1:
                            pass
                        nk = len(kts)
                        def smm(idx):
                            kt, f0, f1, mi = kts[idx]
                            sp_ = sps[(n + idx) % 4]
                            k.mm(sp_[:, f0:f1], kT[:, kt * 128:(kt + 1) * 128], q_[:, f0:f1], True, True, [kT, q_], [sp_])
                        smm(0)
                        if nk > 1: smm(1)
                        for idx in range(nk):
                            if idx + 2 < nk: smm(idx + 2)
                            kt, f0, f1, mi = kts[idx]
                            sp_ = sps[(n + idx) % 4]; p_ = pT[(n + idx) % 4]
                            if mi is not None and (f0 > 0 or f1 < W):
                                k.memset("dve", p_[:, 0:W], 0.0, [p_])
                            k.act(p_[:, f0:f1], sp_[:, f0:f1], AF.Exp, [sp_], [p_], scale=0.125)
                            if mi is not None:
                                k.tt("dve", p_[:, f0:f1], p_[:, f0:f1], bm[:, mi, f0:f1], ALU.mult, [p_, bm], [p_])
                            k.mm(op_[0:65, 0:W], va[:, kt, :], p_[:, 0:W], idx == 0, idx == nk - 1, [va, p_], [op_])
                        n += nk
                        k.copy("act", o_[:, 0:W], op_[0:65, 0:W], [op_], [o_])
                        k.mm(bps[0:64, 0:W], SEL[0:65, :], o_[0:65, 0:W], True, True, [SEL, o_], [bps])
                        if i == 1:
                            k.ts("dve", r_[:, 0:W], bps[0:64, 0:W], ESINK[0:64, hq:hq + 1], None, ALU.add, None, [bps, ESINK], [r_])
                            k.recip(r_[:, 0:W], r_[:, 0:W], [r_], [r_])
                        else:
                            k.recip(r_[:, 0:W], bps[0:64, 0:W], [bps], [r_])
                        k.tt("dve", b_[:, 0:W], o_[0:64, 0:W], r_[:, 0:W], ALU.mult, [o_, r_], [b_])
                        k.dma("pool", S["OAT"][hq * 64:(hq + 1) * 64, t0:t0 + W], b_[:, 0:W], reads=[b_])
                        if cvt is not None:
                            cvt.step("dve", "pool"); cvt.step("dve", "pool")
            if cvt is not None:
                while cvt.step("dve", "pool"): pass
        return ph

    def make_outproj(i, XIN, XOUT, chunks):
        def ph():
            wa = k.sbuf([128, 4, D], BF16, "woa")
            wc = k.sbuf([128, 4, D], BF16, "woc")
            k.dma("sp", wa[:], S[f"bw_out{i}"][0:512, :].rearrange("(c p) f -> p c f", p=128), writes=[wa])
            k.dma("sp", wc[:], S[f"bw_out{i}"][512:1024, :].rearrange("(c p) f -> p c f", p=128), writes=[wc])
            oa = [k.sbuf([128, 4, 512], BF16, "oa") for _ in range(2)]
            oc = [k.sbuf([128, 4, 512], BF16, "oc") for _ in range(2)]
            xs = [k.sbuf([128, 8, 512], F32, "xs") for _ in range(2)]
            xo = [k.sbuf([128, 8, 512], F32, "xo") for _ in range(2)]
            ps = [k.psum([128, 512], F32, "ps") for _ in range(4)]
            n = 0
            for ci, (t0, W, le, re, j) in enumerate(chunks):
                a_ = oa[ci % 2]; c_ = oc[ci % 2]; x_ = xs[ci % 2]; o_ = xo[ci % 2]
                k.dma("sp", a_[:, :, 0:W], S["OAT"][:, t0:t0 + W].rearrange("(c p) t -> p c t", p=128), writes=[a_])
                k.dma("sp", c_[:, :, 0:W], S["OCT"][:, t0:t0 + W].rearrange("(c p) t -> p c t", p=128), writes=[c_])
                k.dma("sp", x_[:, :, 0:W], XIN[:, t0:t0 + W].rearrange("(k p) t -> p k t", p=128), writes=[x_])
                for m in range(8):
                    p_ = ps[n % 4]; n += 1
                    for hh_ in range(4):
                        k.mm(p_[:, 0:W], wa[:, hh_, m * 128:(m + 1) * 128], a_[:, hh_, 0:W], hh_ == 0, False, [wa, a_], [p_])
                    for cc in range(4):
                        k.mm(p_[:, 0:W], wc[:, cc, m * 128:(m + 1) * 128], c_[:, cc, 0:W], False, cc == 3, [wc, c_], [p_])
                    k.stt("dve", o_[:, m, 0:W], p_[:, 0:W], MOD[:, i, 16 + m, j:j + 1], x_[:, m, 0:W], ALU.mult, ALU.add, [p_, MOD, x_], [o_])
                k.dma("pool", XOUT[:, t0:t0 + W].rearrange("(k p) t -> p k t", p=128), o_[:, :, 0:W], reads=[o_])
        return ph

    def make_ffn(i, XIN, XOUT, chunks, final=False):
        def ph():
            wu = k.sbuf([128, 8, 2 * DFF], BF16, "wu")
            for kk in range(8):
                k.dma("sp", wu[:, kk, :], S[f"bw_up{i}"][kk * 128:(kk + 1) * 128, :], writes=[wu])
            wd = [k.sbuf([128, NM, 128], BF16, "wd") for _ in range(2)]
            cw = k.sbuf([128, NM, 3], F32); k.dma("sp", cw[:], I[f"fcw{i}"][:, :, :], writes=[cw])
            cb = k.sbuf([128, NM], F32); k.dma("sp", cb[:], I[f"fcb{i}"][:, :], writes=[cb])
            xh = k.sbuf([128, 8, 516], F32, "xh")
            k.memset("pool", xh[:], 0.0, [xh])
            sq = k.sbuf([128, 8, 516], BF16, "sq")
            h = k.sbuf([128, 8, 516], BF16, "h")
            rstd = k.sbuf([128, 516], F32, "rstd")
            tmp = [k.sbuf([128, 516], F32, "tmp") for _ in range(2)]
            gg = k.sbuf([128, NM, 512], BF16, "gg")
            asb = [k.sbuf([128, 516], F32, "asb") for _ in range(2)]
            c1 = [k.sbuf([128, 512], F32, "c1") for _ in range(2)]
            xo = [k.sbuf([128, 512], F32, "xo") for _ in range(2)]
            ssps = k.psum([128, 1024], F32, "ssps")
            aps = [k.psum([128, 1024], F32, "aps") for _ in range(2)]
            vps = [k.psum([128, 512], F32, "vps") for _ in range(2)]
            n = 0; nd = 0
            for ci, (t0, W, le, re, j) in enumerate(chunks):
                Wc = W + 4
                load_xh(xh, XIN, t0, W, le, re)
                norm_mod(xh, Wc, i, 1, j, sq, ssps, rstd, tmp, h)
                for m in range(NM):
                    ap_ = aps[n % 2]; vp_ = vps[n % 2]; a_ = asb[n % 2]; c_ = c1[n % 2]; n += 1
                    for (c0, c1_) in ((0, min(512, Wc)), (512, Wc)):
                        if c1_ <= c0: continue
                        for kk in range(8):
                            k.mm(ap_[:, c0:c1_], wu[:, kk, m * 128:(m + 1) * 128], h[:, kk, c0:c1_], kk == 0, kk == 7, [wu, h], [ap_])
                    for kk in range(8):
                        k.mm(vp_[:, 0:W], wu[:, kk, DFF + m * 128:DFF + (m + 1) * 128], h[:, kk, 2:W + 2], kk == 0, kk == 7, [wu, h], [vp_])
                    k.copy("act", a_[:, 0:Wc], ap_[:, 0:Wc], [ap_], [a_])
                    if le: k.memset("pool", a_[:, 1:2], 0.0, [a_])
                    if re: k.memset("pool", a_[:, W + 2:W + 3], 0.0, [a_])
                    eng = "dve"
                    k.ts(eng, c_[:, 0:W], a_[:, 2:W + 2], cw[:, m, 1:2], cb[:, m:m + 1], ALU.mult, ALU.add, [a_, cw, cb], [c_])
                    k.stt(eng, c_[:, 0:W], a_[:, 1:W + 1], cw[:, m, 0:1], c_[:, 0:W], ALU.mult, ALU.add, [a_, cw, c_], [c_])
                    k.stt(eng, c_[:, 0:W], a_[:, 3:W + 3], cw[:, m, 2:3], c_[:, 0:W], ALU.mult, ALU.add, [a_, cw, c_], [c_])
                    k.act(c_[:, 0:W], c_[:, 0:W], AF.Gelu_apprx_tanh, [c_], [c_])
                    k.tt("dve", gg[:, m, 0:W], c_[:, 0:W], vp_[:, 0:W], ALU.mult, [c_, vp_], [gg])
                for mo in range(8):
                    wd_ = wd[nd % 2]; o_ = xo[nd % 2]; p_ = vps[nd % 2]; nd += 1
                    k.dma("sp", wd_[:], S[f"bw_dn{i}"][:, mo * 128:(mo + 1) * 128].rearrange("(m p) f -> p m f", p=128), writes=[wd_])
                    for m in range(NM):
                        k.mm(p_[:, 0:W], wd_[:, m, :], gg[:, m, 0:W], m == 0, m == NM - 1, [wd_, gg], [p_])
                    k.stt("dve", o_[:, 0:W], p_[:, 0:W], MOD[:, i, 40 + mo, j:j + 1], xh[:, mo, 2:W + 2], ALU.mult, ALU.add, [p_, MOD, xh], [o_])
                    k.dma("pool", XOUT[mo * 128:(mo + 1) * 128, t0:t0 + W], o_[:, 0:W], reads=[o_])
        return ph

    def make_filter(tag, n):
        KRAW = S["KRAW" + tag]; KN = S["KN" + tag]
        def ph():
            w1 = k.sbuf([33, 64], F32); k.dma("sp", w1[:], I["hw1"][:, :], writes=[w1])
            w2 = k.sbuf([64, 64], F32); k.dma("sp", w2[:], I["hw2"][:, :], writes=[w2])
            w3 = k.sbuf([64, 64], F32); k.dma("sp", w3[:], I["hw3"][:, :], writes=[w3])
            w4 = k.sbuf([64, 1024], F32); k.dma("sp", w4[:], I["hw4"][:, :], writes=[w4])
            hb = k.sbuf([64, 4], F32); k.dma("sp", hb[:], I["hb"][:, :], writes=[hb])
            ndel = k.sbuf([128, 4], F32); k.dma("sp", ndel[:], I["ndel"][:, :], writes=[ndel])
            asum = k.sbuf([128, 4, 40], F32, "asum")
            k.memset("dve", asum[:], 0.0, [asum])
            ft = [k.sbuf([33, 512], F32, "ft") for _ in range(2)]
            t01 = [k.sbuf([128, 512], F32, "t01") for _ in range(2)]
            hid2 = [[k.sbuf([64, 512], F32, "hid") for _ in range(3)] for _ in range(2)]
            ki2 = [k.sbuf([64, 512], I32, "ki") for _ in range(2)]
            win = [k.sbuf([128, 512], F32, "win") for _ in range(2)]
            kr = [k.sbuf([128, 512], F32, "kr") for _ in range(2)]
            junk = k.sbuf([128, 512], F32, "junk")
            krb = [k.sbuf([128, 512], BF16, "krb") for _ in range(2)]
            ps = [k.psum([128, 512], F32, "ps") for _ in range(4)]
            NCH = (2 * n) // 512
            n_ = 0
            for c in range(NCH):
                q0 = c * 512
                f_ = ft[c % 2]; t_ = t01[c % 2]
                k.dma("sp", f_[:, :], I["featsT" + tag][:, q0:q0 + 512], writes=[f_])
                k.dma("sp", t_[:, :], I["t01b" + tag][:, q0:q0 + 512], writes=[t_])
                src = f_; srcK = 33
                hid = hid2[c % 2]; ki = ki2[c % 2]
                for li, wl in enumerate((w1, w2, w3)):
                    p_ = ps[n_ % 4]; n_ += 1
                    k.mm(p_[0:64, :], wl[0:srcK, :], src[0:srcK, :], True, True, [wl, src], [p_])
                    hd = hid[li]
                    k.ts("dve", hd[:, :], p_[0:64, :], hb[:, li:li + 1], hb[:, 3:4], ALU.add, ALU.mult, [p_, hb], [hd])
                    k.ts("dve", ki[:, :], hd[:, :], float(1.0 / (2 * np.pi)), None, ALU.mult, None, [hd], [ki])
                    k.stt("dve", hd[:, :], ki[:, :], float(-2 * np.pi), hd[:, :], ALU.mult, ALU.add, [ki, hd], [hd])
                    k.act(hd[:, :], hd[:, :], AF.Sin, [hd], [hd])
                    src = hd; srcK = 64
                segs = []
                if q0 + 512 <= n: segs = [(0, 512, 0)]
                elif q0 >= n: segs = [(0, 512, 512)]
                else: segs = [(0, n - q0, 0), (n - q0, 512, 512)]
                for jc in range(4):
                    p_ = ps[n_ % 4]; n_ += 1
                    for (a0, a1, off) in segs:
                        k.mm(p_[:, a0:a1], w4[:, off + jc * 128:off + (jc + 1) * 128], hid[2][:, a0:a1], True, True, [w4, hid[2]], [p_])
                    wn = win[jc % 2]; kr_ = kr[jc % 2]
                    k.act(wn[:, :], t_[:, :], AF.Exp, [t_, ndel], [wn], scale=ndel[:, jc:jc + 1])
                    k.stt("dve", kr_[:, :], wn[:, :], 0.05, p_[:, :], ALU.add, ALU.mult, [wn, p_], [kr_])
                    if q0 <= n < q0 + 512:
                        k.memset("dve", kr_[:, n - q0:n - q0 + 1], 0.0, [kr_])
                    k.act(junk[:, :], kr_[:, :], AF.Abs, [kr_], [junk, asum], accum=asum[:, jc, c:c + 1])
                    kb_ = krb[jc % 2]
                    k.copy("pool", kb_[:, :], kr_[:, :], [kr_], [kb_])
                    if tag == "L":
                        k.dma("pool", S["KNA"][q0 // 128:q0 // 128 + 4, jc * 128:(jc + 1) * 128, :].rearrange("a c p -> c a p"),
                              kb_[:, :].rearrange("c (a p) -> c a p", p=128), reads=[kb_])
                    else:
                        k.dma("pool", KN[jc * 128:(jc + 1) * 128, q0:q0 + 512], kb_[:, :], reads=[kb_])
            rn = RNORM[tag]
            for jc in range(4):
                k.op("dve", lambda e, jc=jc: e.reduce_sum(out=rn[:, jc:jc + 1], in_=asum[:, jc, 0:NCH], axis=mybir.AxisListType.X), [asum], [rn])
            k.recip(rn[:, :], rn[:, :], [rn], [rn], force_self=True)
        return ph

    def make_fftconv(tag, NA, n, tok0, n_out_blocks):
        KN = S["KN" + tag]
        NR = NA // 2
        GF = 512 // NA
        GB = 512 // (2 * NA)
        def ph():
            cst = {}
            for nm, shp, dt in (("f1cs", [NA, 2 * NA], BF16), ("fC", [128, 128], BF16), ("fS", [128, 128], BF16), ("fnS", [128, 128], BF16),
                                ("fCS", [128, 256], BF16), ("fnSC", [128, 256], BF16), ("twA", [128, 512], F32), ("twB", [128, 512], F32),
                                ("twA2", [NA, 1024], F32), ("twB2", [NA, 1024], F32), ("g3C", [NA, NA], BF16), ("g3nS", [NA, NA], BF16)):
                cst[nm] = k.sbuf(shp, dt, nm)
                k.dma("sp", cst[nm][:], I[nm + tag][tuple(slice(None) for _ in shp)], writes=[cst[nm]])
            if NA == 4:
                for nm, shp, dt in (("twA2c", [128, 256], F32), ("twB2c", [128, 256], F32), ("gbC", [128, 64], BF16), ("gbnS", [128, 64], BF16)):
                    cst[nm] = k.sbuf(shp, dt, nm)
                    k.dma("sp", cst[nm][:], I[nm + tag][:, :], writes=[cst[nm]])
                c2a = [k.sbuf([128, 256], F32, "c2a") for _ in range(2)]; c2b = [k.sbuf([128, 256], F32, "c2b") for _ in range(2)]
                y3c = [k.sbuf([128, 2, 128], BF16, "y3c") for _ in range(2)]
                yoc = [k.sbuf([64, 128], F32, "yoc") for _ in range(2)]
            LG = 32 if NA == 128 else 128
            NXB = 2 if NA == 128 else 1
            xu = [k.sbuf([NA, LG, 128], BF16, "xu") for _ in range(NXB)]
            for t_ in xu: k.memset("pool", t_[:], 0.0, [t_])
            xk = [k.sbuf([NA, LG, 128], BF16, "xk") for _ in range(NXB)]
            s1 = [k.psum([128, 512], F32, "s1") for _ in range(2)]
            s2 = [k.psum([128, 512], F32, "s2") for _ in range(2)]
            s3 = [k.psum([128, 1024], F32, "s3") for _ in range(1)]
            s4 = [k.psum([128, 512], F32, "s4") for _ in range(2)]
            NB = 2
            ta = [k.sbuf([128, 512], F32, "ta") for _ in range(NB)]; tb = [k.sbuf([128, 512], F32, "tb") for _ in range(NB)]
            bu = [k.sbuf([128, 2, GF, NA], BF16, "bu") for _ in range(NB)]; bk = [k.sbuf([128, 2, GF, NA], BF16, "bk") for _ in range(NB)]
            kh = [k.sbuf([128, 2, 512], F32, "kh") for _ in range(NB)]
            m1 = [k.sbuf([128, 512], F32, "m1") for _ in range(NB)]; m2 = [k.sbuf([128, 512], F32, "m2") for _ in range(NB)]
            m3 = [k.sbuf([128, 512], F32, "m3") for _ in range(NB)]; m4 = [k.sbuf([128, 512], F32, "m4") for _ in range(NB)]
            yh = [k.sbuf([128, 2, GF, NA], BF16, "yh") for _ in range(NB)]
            t2a = [k.sbuf([NA, 1024], F32, "t2a") for _ in range(NB)]; t2b = [k.sbuf([NA, 1024], F32, "t2b") for _ in range(NB)]
            y3 = [k.sbuf([NA, 2, 4, 128], BF16, "y3") for _ in range(NB)]
            yo = [k.sbuf([n_out_blocks, 4, 128], F32, "yo") for _ in range(2)]
            MO = n_out_blocks
            cnt = {"tw": 0, "inv": 0, "g": 0}
            def fwd(x, cbase, bdst):
                for half in range(2):
                    bank = s1[half]
                    ta_ = ta[cnt["tw"] % NB]; tb_ = tb[cnt["tw"] % NB]; cnt["tw"] += 1
                    for cc in range(GB):
                        ch = cbase + half * GB + cc
                        k.mm(bank[:, cc * 2 * NA:(cc + 1) * 2 * NA], x[0:NA, ch, :], cst["f1cs"][0:NA, :], True, True, [x, cst["f1cs"]], [bank])
                    k.tt("dve", ta_[:, :], bank[:, :], cst["twA"][:, :], ALU.mult, [bank, cst["twA"]], [ta_])
                    k.tt("dve", tb_[:, :], bank[:, :], cst["twB"][:, :], ALU.mult, [bank, cst["twB"]], [tb_])
                    tav = ta_[:, :].rearrange("p (g r f) -> p g r f", g=GB, r=2)
                    tbv = tb_[:, :].rearrange("p (g r f) -> p g r f", g=GB, r=2)
                    k.tt("dve", bdst[:, 0, half * GB:(half + 1) * GB, :], tav[:, :, 0, :], tbv[:, :, 1, :], ALU.subtract, [ta_, tb_], [bdst])
                    k.tt("dve", bdst[:, 1, half * GB:(half + 1) * GB, :], tav[:, :, 1, :], tbv[:, :, 0, :], ALU.subtract, [ta_, tb_], [bdst])
                bre = bdst[:, 0, :, :].rearrange("p g f -> p (g f)"); bim = bdst[:, 1, :, :].rearrange("p g f -> p (g f)")
                k.mm(s2[0][:, :], cst["fC"][:, :], bre, True, False, [cst["fC"], bdst], [s2[0]])
                k.mm(s2[0][:, :], cst["fS"][:, :], bim, False, True, [cst["fS"], bdst], [s2[0]])
                k.mm(s2[1][:, :], cst["fC"][:, :], bim, True, False, [cst["fC"], bdst], [s2[1]])
                k.mm(s2[1][:, :], cst["fnS"][:, :], bre, False, True, [cst["fnS"], bdst], [s2[1]])
            for lg in range(512 // LG):
                xu_ = xu[lg % NXB]; xk_ = xk[lg % NXB]
                k.dma("sp", xu_[0:NR, :, :], S["UT"][lg * LG:(lg + 1) * LG, tok0:tok0 + n].rearrange("c (a p) -> a c p", p=128), writes=[xu_])
                k.dma("sp", xk_[:, :, :], KN[lg * LG:(lg + 1) * LG, :].rearrange("c (a p) -> a c p", p=128), writes=[xk_])
                for gf in range(LG // GF):
                    cbase = gf * GF
                    g_ = cnt["g"] % NB; cnt["g"] += 1
                    kh_ = kh[g_]; yh_ = yh[g_]
                    fwd(xk_, cbase, bk[g_])
                    k.copy("act", kh_[:, 0, :], s2[0][:, :], [s2[0]], [kh_])
                    k.copy("act", kh_[:, 1, :], s2[1][:, :], [s2[1]], [kh_])
                    fwd(xu_, cbase, bu[g_])
                    k.tt("dve", m1[g_][:, :], s2[0][:, :], kh_[:, 0, :], ALU.mult, [s2[0], kh_], [m1[g_]])
                    k.tt("dve", m3[g_][:, :], s2[0][:, :], kh_[:, 1, :], ALU.mult, [s2[0], kh_], [m3[g_]])
                    k.tt("dve", m2[g_][:, :], s2[1][:, :], kh_[:, 1, :], ALU.mult, [s2[1], kh_], [m2[g_]])
                    k.tt("dve", m4[g_][:, :], s2[1][:, :], kh_[:, 0, :], ALU.mult, [s2[1], kh_], [m4[g_]])
                    k.tt("pool", yh_[:, 0, :, :].rearrange("p g f -> p (g f)"), m1[g_][:, :], m2[g_][:, :], ALU.subtract, [m1[g_], m2[g_]], [yh_])
                    k.tt("pool", yh_[:, 1, :, :].rearrange("p g f -> p (g f)"), m3[g_][:, :], m4[g_][:, :], ALU.add, [m3[g_], m4[g_]], [yh_])
                    if NA == 4:
                        for sg in range(GF // 32):
                            b3 = s3[0]
                            iv = cnt["inv"] % 2; cnt["inv"] += 1
                            lre = yh_[:, 0, sg * 32:(sg + 1) * 32, :].rearrange("p g f -> p (g f)")
                            lim = yh_[:, 1, sg * 32:(sg + 1) * 32, :].rearrange("p g f -> p (g f)")
                            k.mm(b3[:, 0:256], lre, cst["fCS"][:, :], True, False, [yh_, cst["fCS"]], [b3])
                            k.mm(b3[:, 0:256], lim, cst["fnSC"][:, :], False, True, [yh_, cst["fnSC"]], [b3])
                            a_ = c2a[iv]; b_ = c2b[iv]; y_ = y3c[iv]; o_ = yoc[iv]
                            k.tt("dve", a_[:, :], b3[:, 0:256], cst["twA2c"][:, :], ALU.mult, [b3, cst["twA2c"]], [a_])
                            k.tt("dve", b_[:, :], b3[:, 0:256], cst["twB2c"][:, :], ALU.mult, [b3, cst["twB2c"]], [b_])
                            k.tt("pool", y_[:, 0, :], a_[:, 0:128], b_[:, 128:256], ALU.add, [a_, b_], [y_])
                            k.tt("pool", y_[:, 1, :], a_[:, 128:256], b_[:, 0:128], ALU.add, [a_, b_], [y_])
                            p4 = s4[iv]
                            k.mm(p4[0:64, 0:128], cst["gbC"][:, :], y_[:, 0, :], True, False, [cst["gbC"], y_], [p4])
                            k.mm(p4[0:64, 0:128], cst["gbnS"][:, :], y_[:, 1, :], False, True, [cst["gbnS"], y_], [p4])
                            k.copy("act", o_[:, :], p4[0:64, 0:128], [p4], [o_])
                            c0 = lg * LG + cbase + sg * 32
                            for a2 in range(2):
                                k.dma("sp", S["YT"][c0:c0 + 32, tok0 + a2 * 128:tok0 + (a2 + 1) * 128], o_[a2:64:2, :], reads=[o_])
                        continue
                    for sg in range(GF // 4):
                        b3 = s3[0]
                        iv = cnt["inv"] % NB; cnt["inv"] += 1
                        t2a_ = t2a[iv]; t2b_ = t2b[iv]; y3_ = y3[iv]
                        for cc in range(4):
                            ch = sg * 4 + cc
                            k.mm(b3[0:NA, cc * 256:(cc + 1) * 256], yh_[:, 0, ch, :], cst["fCS"][:, :], True, False, [yh_, cst["fCS"]], [b3])
                            k.mm(b3[0:NA, cc * 256:(cc + 1) * 256], yh_[:, 1, ch, :], cst["fnSC"][:, :], False, True, [yh_, cst["fnSC"]], [b3])
                        k.tt("dve", t2a_[:, :], b3[0:NA, :], cst["twA2"][:, :], ALU.mult, [b3, cst["twA2"]], [t2a_])
                        k.tt("dve", t2b_[:, :], b3[0:NA, :], cst["twB2"][:, :], ALU.mult, [b3, cst["twB2"]], [t2b_])
                        av = t2a_[:, :].rearrange("p (g r f) -> p g r f", g=4, r=2)
                        bv = t2b_[:, :].rearrange("p (g r f) -> p g r f", g=4, r=2)
                        k.tt("pool", y3_[:, 0, :, :], av[:, :, 0, :], bv[:, :, 1, :], ALU.add, [t2a_, t2b_], [y3_])
                        k.tt("pool", y3_[:, 1, :, :], av[:, :, 1, :], bv[:, :, 0, :], ALU.add, [t2a_, t2b_], [y3_])
                        p4 = s4[iv % 2]; yo_ = yo[iv % 2]
                        k.mm(p4[0:MO, :], cst["g3C"][:, 0:MO], y3_[:, 0, :, :].rearrange("p g f -> p (g f)"), True, False, [cst["g3C"], y3_], [p4])
                        k.mm(p4[0:MO, :], cst["g3nS"][:, 0:MO], y3_[:, 1, :, :].rearrange("p g f -> p (g f)"), False, True, [cst["g3nS"], y3_], [p4])
                        k.copy("act", yo_[:, :, :].rearrange("p g f -> p (g f)"), p4[0:MO, :], [p4], [yo_])
                        c0 = lg * LG + cbase + sg * 4
                        k.dma("sp", S["YT"][c0:c0 + 4, tok0:tok0 + MO * 128].rearrange("c (a p) -> a c p", p=128), yo_[:, :, :], reads=[yo_])
        return ph

    def make_fftconvL2(n_out_blocks):
        tag = "L"; NA = 128; NR = 64; n = SEQ; tok0 = 0; CB = 32; MO = n_out_blocks
        def ph():
            cst = {}
            for nm, shp, dt in (("f1cs", [128, 256], BF16), ("fCS", [128, 256], BF16), ("fnSC", [128, 256], BF16),
                                ("twA2", [128, 1024], F32), ("twB2", [128, 1024], F32), ("g3C", [128, 128], BF16), ("g3nS", [128, 128], BF16)):
                cst[nm] = k.sbuf(shp, dt, nm)
                k.dma("sp", cst[nm][:], I[nm + tag][:, :], writes=[cst[nm]])
            MC = k.sbuf([128, 128, 128], BF16, "MC"); MS = k.sbuf([128, 128, 128], BF16, "MS")
            for q4 in range(4):
                k.dma("sp", MC[:, q4 * 32:(q4 + 1) * 32, :], I["MCL"][:, q4 * 32:(q4 + 1) * 32, :], writes=[MC])
                k.dma("sp", MS[:, q4 * 32:(q4 + 1) * 32, :], I["MSL"][:, q4 * 32:(q4 + 1) * 32, :], writes=[MS])
            xu = k.sbuf([128, CB, 128], BF16, "xu"); k.memset("pool", xu[:], 0.0, [xu])
            xk = k.sbuf([128, CB, 128], BF16, "xk")
            ATs = [k.sbuf([128, CB, 3, 128], BF16, "AT") for _ in range(2)]
            KH = k.sbuf([128, 2, 128, CB], BF16, "KH")
            YH = k.sbuf([128, 2, CB, 128], BF16, "YH")
            m1 = k.sbuf([128, 512], F32, "m1"); m2 = k.sbuf([128, 512], F32, "m2")
            m3 = k.sbuf([128, 512], F32, "m3"); m4 = k.sbuf([128, 512], F32, "m4")
            t2a = [k.sbuf([128, 1024], BF16, "t2a") for _ in range(2)]; t2b = [k.sbuf([128, 1024], BF16, "t2b") for _ in range(2)]
            y3 = [k.sbuf([128, 2, 4, 128], BF16, "y3") for _ in range(2)]
            yo = [k.sbuf([MO, 4, 128], F32, "yo") for _ in range(2)]
            s1 = [k.psum([128, 512], F32, "s1") for _ in range(2)]
            s2 = [k.psum([128, 512], F32, "s2") for _ in range(4)]
            s3 = k.psum([128, 1024], F32, "s3")
            cnt = {"p": 0, "b": 0, "inv": 0}
            def step1_gen(bt, kind, AT):
                if kind == "F":
                    k.dma("sp", xk[:, :, :], S["KNA"][:, bt * CB:(bt + 1) * CB, :], writes=[xk]); x = xk
                else:
                    k.dma("sp", xu[0:NR, :, :], S["UTA"][:, bt * CB:(bt + 1) * CB, :], writes=[xu]); x = xu
                for pr in range(CB // 2):
                    bank = s1[cnt["p"] % 2]; cnt["p"] += 1
                    for cc in range(2):
                        k.mm(bank[:, cc * 256:(cc + 1) * 256], x[:, 2 * pr + cc, :], cst["f1cs"][:, :], True, True, [x, cst["f1cs"]], [bank])
                    bv = bank[:, :].rearrange("p (g r f) -> p g r f", g=2, r=2)
                    o1 = AT[:, 2 * pr:2 * pr + 2, 0:2, :]; o2 = AT[:, 2 * pr:2 * pr + 2, 2, :]; i2 = bv[:, :, 0, :]
                    k.op("act", lambda e, o1=o1, bv=bv: e.copy(out=o1, in_=bv), [bank], [AT])
                    k.op("act", lambda e, o2=o2, i2=i2: e.mul(out=o2, in_=i2, mul=-1.0), [bank], [AT])
                    yield
            def inverse_gen(bt):
                for sg in range(CB // 4):
                    iv = cnt["inv"] % 2; cnt["inv"] += 1
                    t2a_ = t2a[iv]; t2b_ = t2b[iv]; y3_ = y3[iv]
                    for cc in range(4):
                        ch = sg * 4 + cc
                        k.mm(s3[:, cc * 256:(cc + 1) * 256], YH[:, 0, ch, :], cst["fCS"][:, :], True, False, [YH, cst["fCS"]], [s3])
                        k.mm(s3[:, cc * 256:(cc + 1) * 256], YH[:, 1, ch, :], cst["fnSC"][:, :], False, True, [YH, cst["fnSC"]], [s3])
                    k.tt("dve", t2a_[:, :], s3[:, :], cst["twA2"][:, :], ALU.mult, [s3, cst["twA2"]], [t2a_])
                    k.tt("dve", t2b_[:, :], s3[:, :], cst["twB2"][:, :], ALU.mult, [s3, cst["twB2"]], [t2b_])
                    av = t2a_[:, :].rearrange("p (g r f) -> p g r f", g=4, r=2)
                    bv2 = t2b_[:, :].rearrange("p (g r f) -> p g r f", g=4, r=2)
                    k.tt("dve", y3_[:, 0, :, :], av[:, :, 0, :], bv2[:, :, 1, :], ALU.add, [t2a_, t2b_], [y3_])
                    k.tt("pool", y3_[:, 1, :, :], av[:, :, 1, :], bv2[:, :, 0, :], ALU.add, [t2a_, t2b_], [y3_])
                    yield
                    p4 = s1[cnt["p"] % 2]; cnt["p"] += 1
                    yo_ = yo[iv]
                    k.mm(p4[0:MO, :], cst["g3C"][:, 0:MO], y3_[:, 0, :, :].rearrange("p g f -> p (g f)"), True, False, [cst["g3C"], y3_], [p4])
                    k.mm(p4[0:MO, :], cst["g3nS"][:, 0:MO], y3_[:, 1, :, :].rearrange("p g f -> p (g f)"), False, True, [cst["g3nS"], y3_], [p4])
                    k.copy("act", yo_[:, :, :].rearrange("p g f -> p (g f)"), p4[0:MO, :], [p4], [yo_])
                    c0 = bt * CB + sg * 4
                    k.dma("sp", S["YT"][c0:c0 + 4, tok0:tok0 + MO * 128].rearrange("c (a p) -> a c p", p=128), yo_[:, :, :], reads=[yo_])
                    yield
            def filt_block(blk, bre, bim):
                k.copy("act", KH[:, 0, blk * 16:(blk + 1) * 16, :].rearrange("p a b -> p (a b)"), bre[:, :], [bre], [KH])
                k.copy("act", KH[:, 1, blk * 16:(blk + 1) * 16, :].rearrange("p a b -> p (a b)"), bim[:, :], [bim], [KH])
            def sig_block(blk, bre, bim):
                kre = KH[:, 0, blk * 16:(blk + 1) * 16, :].rearrange("p a b -> p (a b)")
                kim = KH[:, 1, blk * 16:(blk + 1) * 16, :].rearrange("p a b -> p (a b)")
                k.tt("dve", m1[:, :], bre[:, :], kre, ALU.mult, [bre, KH], [m1])
                k.tt("dve", m3[:, :], bre[:, :], kim, ALU.mult, [bre, KH], [m3])
                k.tt("dve", m2[:, :], bim[:, :], kim, ALU.mult, [bim, KH], [m2])
                k.tt("dve", m4[:, :], bim[:, :], kre, ALU.mult, [bim, KH], [m4])
                ore = YH[:, 0, :, blk * 16:(blk + 1) * 16].rearrange("p c f -> p f c")
                oim = YH[:, 1, :, blk * 16:(blk + 1) * 16].rearrange("p c f -> p f c")
                v = lambda t: t[:, :].rearrange("p (f c) -> p f c", c=CB)
                k.tt("pool", ore, v(m1), v(m2), ALU.subtract, [m1, m2], [YH])
                k.tt("pool", oim, v(m3), v(m4), ALU.add, [m3, m4], [YH])
            def step2(AT, on_block, g_next, g_inv):
                for f1 in range(128):
                    j = f1 % 16
                    if j == 0:
                        bre = s2[(cnt["b"] % 2) * 2]; bim = s2[(cnt["b"] % 2) * 2 + 1]; cnt["b"] += 1
                    cols = slice(j * CB, (j + 1) * CB)
                    k.mm(bre[:, cols], MC[:, f1, :], AT[:, :, 0, f1], True, False, [MC, AT], [bre])
                    k.mm(bre[:, cols], MS[:, f1, :], AT[:, :, 1, f1], False, True, [MS, AT], [bre])
                    k.mm(bim[:, cols], MC[:, f1, :], AT[:, :, 1, f1], True, False, [MC, AT], [bim])
                    k.mm(bim[:, cols], MS[:, f1, :], AT[:, :, 2, f1], False, True, [MS, AT], [bim])
                    if j == 15:
                        on_block(f1 // 16, bre, bim)
                    if f1 % 8 == 3 and g_next is not None: next(g_next, None)
                    if f1 % 8 == 7 and g_inv is not None: next(g_inv, None)
            jobs = [(bt, kind) for bt in range(512 // CB) for kind in ("F", "S")]
            g0 = step1_gen(jobs[0][0], jobs[0][1], ATs[0])
            for _ in g0: pass
            for ji, (bt, kind) in enumerate(jobs):
                g_next = step1_gen(jobs[ji + 1][0], jobs[ji + 1][1], ATs[(ji + 1) % 2]) if ji + 1 < len(jobs) else None
                g_inv = inverse_gen(bt - 1) if (kind == "F" and bt > 0) else None
                step2(ATs[ji % 2], filt_block if kind == "F" else sig_block, g_next, g_inv)
                if g_next is not None:
                    for _ in g_next: pass
                if g_inv is not None:
                    for _ in g_inv: pass
            for _ in inverse_gen(512 // CB - 1): pass
        return ph

    def make_hycombine(chunks):
        def ph():
            bd = k.sbuf([128, 4], F32); k.dma("sp", bd[:], I["hbd"][:, :], writes=[bd])
            yt = [k.sbuf([128, 512], F32, "yt") for _ in range(2)]
            ut = [k.sbuf([128, 512], F32, "ut") for _ in range(2)]
            x0 = [k.sbuf([128, 512], F32, "x0") for _ in range(2)]
            ob = [k.sbuf([128, 512], BF16, "ob") for _ in range(2)]
            n = 0
            for (t0, W, le, re, j) in chunks:
                rn = RNORM["C" if j == 1 else "L"]
                for jc in range(4):
                    y_ = yt[n % 2]; u_ = ut[n % 2]; x_ = x0[n % 2]; o_ = ob[n % 2]; n += 1
                    rows = slice(jc * 128, (jc + 1) * 128)
                    k.dma("sp", y_[:, 0:W], S["YT"][rows, t0:t0 + W], writes=[y_])
                    k.dma("sp", u_[:, 0:W], S["UF"][rows, t0:t0 + W], writes=[u_])
                    k.dma("sp", x_[:, 0:W], S["X0T"][rows, t0:t0 + W], writes=[x_])
                    k.ts("dve", y_[:, 0:W], y_[:, 0:W], rn[:, jc:jc + 1], None, ALU.mult, None, [y_, rn], [y_])
                    k.stt("dve", y_[:, 0:W], u_[:, 0:W], bd[:, jc:jc + 1], y_[:, 0:W], ALU.mult, ALU.add, [u_, bd, y_], [y_])
                    k.tt("dve", o_[:, 0:W], y_[:, 0:W], x_[:, 0:W], ALU.mult, [y_, x_], [o_])
                    k.dma("pool", S["OCT"][rows, t0:t0 + W], o_[:, 0:W], reads=[o_])
        return ph

    CH_E = [(c * 512, 512, c == 0, False, 0) for c in range(8)] + [(4096, 256, False, True, 0)]
    CH_OWN = [(c * 512, 512, c == 0, False, 0) for c in range(8)]
    phases.append(("inproj0", make_inproj(0, I["xt0"], CHUNKS_ALL)))
    phases.append(("filterL", make_filter("L", SEQ)))
    phases.append(("filterC", make_filter("C", CTX)))
    phases.append(("fftL", make_fftconvL2(E // 128)))
    phases.append(("fftC", make_fftconv("C", 4, CTX, SEQ, 2)))
    phases.append(("hycomb", make_hycombine(CH_E + [CTXCH])))
    phases.append(("attn0", make_attn(0)))
    phases.append(("outproj0", make_outproj(0, I["xt0"], S["XM"], CH_E + [CTXCH])))
    phases.append(("ffn0", make_ffn(0, S["XM"], S["X1"], CH_E + [CTXCH])))
    phases.append(("inproj1", make_inproj(1, S["X1"], CH_E + [CTXCH])))
    phases.append(("attn1", make_attn(1)))
    phases.append(("outproj1", make_outproj(1, S["X1"], S["XM1"], CH_E)))
    phases.append(("ffn1", make_ffn(1, S["XM1"], OUT, CH_OWN)))
    for nm, ph in phases:
        k.phase(ph)
        if stop_after == nm:
            break
    k.close()
    return nc


_CACHE = {}


def kernel(**inputs):
    inp = {kk: np.asarray(v) for kk, v in inputs.items()}
    if "C" not in _CACHE:
        C = _consts()
        C["fftL"] = _fft_consts(128); C["fftC"] = _fft_consts(4)
        C["filtL"] = _filter_consts(SEQ); C["filtC"] = _filter_consts(CTX)
        _CACHE["C"] = C
    C = _CACHE["C"]
    nc = _build()
    in_maps = []
    for core in range(8):
        b, hh = core // 2, core % 2
        in_maps.append(_host_prep(inp, b, hh, C))
    res = run_bass_kernel_spmd(nc, in_maps, core_ids=list(range(8)))
    out = np.empty((4, SEQ, D), np.float32)
    for core in range(8):
        b, hh = core // 2, core % 2
        o = np.asarray(res.results[core]["out"]).T
        if hh == 0:
            out[b, :OWN] = o
        else:
            out[b, OWN:] = o[::-1]
    return out
```

```python
import numpy as np
import ml_dtypes
from contextlib import ExitStack
import concourse.bass as bass
import concourse.mybir as mybir
from concourse.bass_utils import run_bass_kernel_spmd

F32 = mybir.dt.float32
BF16 = mybir.dt.bfloat16
I32 = mybir.dt.int32
AF = mybir.ActivationFunctionType
ALU = mybir.AluOpType
NPBF = ml_dtypes.bfloat16

D = 1024; SEQ = 8192; CTX = 256; TT = SEQ + CTX; E = 4352; OWN = 4096
DFF = 2816; NM = 22
SAME_ENGINE_SYNC = {"act", "pool"}


class Res:
    __slots__ = ("name", "last_w", "reads", "excl")
    def __init__(self, name, excl=False):
        self.name = name; self.last_w = None; self.reads = {}; self.excl = excl


class Tl:
    def __init__(self, t, r):
        self.t = t; self.r = r
    def __getitem__(self, idx):
        return self.t[idx]


class K:
    ENGS = ("pe", "act", "dve", "pool", "sp")

    def __init__(self, nc):
        self.nc = nc
        self.es = ExitStack()
        self.sem = {}; self.cnt = {}
        for e in self.ENGS:
            self.sem[e] = self.es.enter_context(nc.semaphore("s_" + e))
            self.cnt[e] = 0
        self.dma_sems = {}
        self.dma_key = {}
        self.dma_rr = {}
        self.NDMASEM = {"sp": 32, "pool": 24, "act": 8, "pe": 4, "dve": 4}
        self.seen = {e: {} for e in self.ENGS}
        self.ops = {e: [] for e in self.ENGS}
        self.phase_es = None
        self.nres = 0
        self.ndma = 0

    def sbuf(self, shape, dt, name=None, persist=False):
        self.nres += 1
        name = (name or "t") + "_%d" % self.nres
        es = self.es if persist else self.phase_es
        t = es.enter_context(self.nc.sbuf_tensor(name, list(shape), dt))
        return Tl(t, Res(name))

    def psum(self, shape, dt, name=None):
        self.nres += 1
        name = (name or "p") + "_%d" % self.nres
        t = self.phase_es.enter_context(self.nc.psum_tensor(name, list(shape), dt))
        return Tl(t, Res(name, excl=True))

    def _need(self, reads, writes, eng=None):
        evs = []
        for r in reads:
            if r.last_w is not None: evs.append(r.last_w)
            if r.excl:
                evs.extend((kk[0], kk[1], v) for kk, v in r.reads.items() if not (kk[0] == "eng" and kk[1] == eng))
        for w in writes:
            if w.last_w is not None: evs.append(w.last_w)
            evs.extend((kk[0], kk[1], v) for kk, v in w.reads.items())
        return evs

    def _emit_waits(self, eng, evs, force_self=False):
        need = {}
        for kind, key, val in evs:
            if kind == "eng":
                if key == eng and force_self == "never": continue
                if key == eng and eng not in SAME_ENGINE_SYNC and not force_self: continue
                v = val
            else:
                v = val
            if self.seen[eng].get((kind, key), 0) >= v: continue
            if need.get((kind, key), 0) < v: need[(kind, key)] = v
        for (kind, key), v in need.items():
            self.seen[eng][(kind, key)] = v
            sem = self.sem[key] if kind == "eng" else self.dma_sems[key][0]
            self.ops[eng].append(lambda e, sem=sem, v=v: e.wait_ge(sem, v))

    def _commit(self, ev, reads, writes):
        for r in reads:
            kk = (ev[0], ev[1])
            if r.reads.get(kk, 0) < ev[2]: r.reads[kk] = ev[2]
        for w in writes:
            w.last_w = ev; w.reads = {}

    def op(self, eng, fn, reads=(), writes=(), force_self=False):
        reads = [x.r if isinstance(x, Tl) else x for x in reads]
        writes = [x.r if isinstance(x, Tl) else x for x in writes]
        self._emit_waits(eng, self._need(reads, writes, eng), force_self)
        self.cnt[eng] += 1
        sem = self.sem[eng]
        self.ops[eng].append(lambda e, fn=fn, sem=sem: fn(e).then_inc(sem, 1))
        self._commit(("eng", eng, self.cnt[eng]), reads, writes)

    def dma(self, q, out, in_, reads=(), writes=(), **kw):
        reads = [x.r if isinstance(x, Tl) else x for x in reads]
        writes = [x.r if isinstance(x, Tl) else x for x in writes]
        npool = self.NDMASEM[q]
        idx = (q, self.dma_rr.get(q, 0) % npool)
        self.dma_rr[q] = self.dma_rr.get(q, 0) + 1
        if idx not in self.dma_sems:
            s_ = self.es.enter_context(self.nc.semaphore("d_%s_%d" % idx))
            self.dma_sems[idx] = [s_, 0]
        ent = self.dma_sems[idx]
        evs = self._need(reads, writes, q)
        if ent[1] > 0:
            evs.append(("dma", idx, ent[1] * 16))
        self._emit_waits(q, evs)
        ent[1] += 1
        sem = ent[0]
        self.ndma += 1
        self.ops[q].append(lambda e, out=out, in_=in_, sem=sem, kw=kw: e.dma_start(out=out, in_=in_, **kw).then_inc(sem, 16))
        self._commit(("dma", idx, ent[1] * 16), reads, writes)

    def barrier(self):
        evs = [("eng", e, self.cnt[e]) for e in self.ENGS if self.cnt[e] > 0]
        evs += [("dma", kk, v[1] * 16) for kk, v in self.dma_sems.items() if v[1] > 0]
        for e in self.ENGS:
            self._emit_waits(e, [ev for ev in evs if not (ev[0] == "eng" and ev[1] == e)])

    def phase(self, body):
        with ExitStack() as pes:
            self.phase_es = pes
            body()
            self.barrier()
            ops = self.ops
            self.ops = {e: [] for e in self.ENGS}
            with self.nc.Block() as block:
                @block.tensor
                def _(e):
                    for f in ops["pe"]: f(e)
                @block.scalar
                def _(e):
                    for f in ops["act"]: f(e)
                @block.vector
                def _(e):
                    for f in ops["dve"]: f(e)
                @block.gpsimd
                def _(e):
                    for f in ops["pool"]: f(e)
                @block.sync
                def _(e):
                    for f in ops["sp"]: f(e)
        self.phase_es = None

    def close(self):
        self.es.close()

    def ts(self, eng, out, in0, s1, s2, op0, op1, r, w, force_self=False):
        if s2 is None:
            s2 = 0.0; op1 = ALU.add
        self.op(eng, lambda e: e.tensor_scalar(out=out, in0=in0, scalar1=s1, scalar2=s2, op0=op0, op1=op1), r, w, force_self)
    def stt(self, eng, out, in0, sc, in1, op0, op1, r, w):
        self.op(eng, lambda e: e.scalar_tensor_tensor(out=out, in0=in0, scalar=sc, in1=in1, op0=op0, op1=op1), r, w)
    def tt(self, eng, out, in0, in1, op, r, w):
        self.op(eng, lambda e: e.tensor_tensor(out=out, in0=in0, in1=in1, op=op), r, w)
    def act(self, out, in_, func, r, w, bias=None, scale=None, accum=None):
        kw = {}
        if bias is not None: kw["bias"] = bias
        if scale is not None: kw["scale"] = scale
        if accum is not None: kw["accum_out"] = accum
        self.op("act", lambda e: e.activation(out=out, in_=in_, func=func, **kw), r, w)
    def mm(self, out, lhsT, rhs, start, stop, r, w):
        self.op("pe", lambda e: e.matmul(out, lhsT=lhsT, rhs=rhs, start=start, stop=stop), r, w)
    def copy(self, eng, out, in_, r, w):
        if eng == "act":
            self.op("act", lambda e: e.copy(out=out, in_=in_), r, w)
        else:
            self.op(eng, lambda e: e.tensor_copy(out=out, in_=in_), r, w)
    def memset(self, eng, ap, val, w):
        self.op(eng, lambda e: e.memset(ap, val), [], w)
    def recip(self, out, in_, r, w, force_self=False):
        self.op("dve", lambda e: e.reciprocal(out=out, in_=in_), r, w, force_self)

def _consts():
    c = {}
    nf = 16
    inv = 10000.0 ** (-np.arange(nf, dtype=np.float64) / nf)
    t = np.arange(SEQ)
    row = (t // 64).astype(np.float64); col = (t % 64).astype(np.float64)
    ar = row[None, :] * inv[:, None]; ac = col[None, :] * inv[:, None]
    cos64 = np.concatenate([np.cos(ar), np.cos(ar), np.cos(ac), np.cos(ac)], 0)
    sin64 = np.concatenate([-np.sin(ar), np.sin(ar), -np.sin(ac), np.sin(ac)], 0)
    c["cos64"] = cos64.astype(np.float32); c["sin64"] = sin64.astype(np.float32)
    perm = np.concatenate([np.arange(16) + 16, np.arange(16), np.arange(16) + 48, np.arange(16) + 32])
    c["perm64"] = perm
    p = np.arange(128)[:, None]; f = np.arange(512)[None, :]
    c["bmask"] = np.stack([(np.abs(128 * r + p - f) <= 128) for r in range(-1, 5)], 0).astype(NPBF)
    return c


def _fft_consts(NA):
    N = 128 * NA
    c = {}
    a = np.arange(NA)[:, None]; f1 = np.arange(NA)[None, :]
    th = 2 * np.pi * a * f1 / NA
    c["f1cs"] = np.concatenate([np.cos(th), -np.sin(th)], 1).astype(NPBF)
    p = np.arange(128)[:, None]; f2 = np.arange(128)[None, :]
    th2 = 2 * np.pi * p * f2 / 128
    C = np.cos(th2); S = np.sin(th2)
    c["fC"] = C.astype(NPBF); c["fS"] = S.astype(NPBF); c["fnS"] = (-S).astype(NPBF)
    c["fCS"] = np.concatenate([C, S], 1).astype(NPBF)
    c["fnSC"] = np.concatenate([-S, C], 1).astype(NPBF)
    tw = 2 * np.pi * np.arange(128)[:, None] * np.arange(NA)[None, :] / N
    G = 512 // (2 * NA)
    tc_ = np.cos(tw); ts_ = np.sin(tw)
    A = np.concatenate([tc_, tc_], 1)
    B = np.concatenate([ts_, -ts_], 1)
    c["twA"] = np.tile(A[:, None, :], (1, G, 1)).reshape(128, 512).astype(np.float32)
    c["twB"] = np.tile(B[:, None, :], (1, G, 1)).reshape(128, 512).astype(np.float32)
    tcT = np.cos(tw).T; tsT = np.sin(tw).T
    A2 = np.stack([tcT, tcT], 1)
    B2 = np.stack([tsT, -tsT], 1)
    c["twA2"] = np.tile(A2[:, None], (1, 4, 1, 1)).reshape(NA, 1024).astype(np.float32)
    c["twB2"] = np.tile(B2[:, None], (1, 4, 1, 1)).reshape(NA, 1024).astype(np.float32)
    th = 2 * np.pi * np.arange(NA)[:, None] * np.arange(NA)[None, :] / NA
    c["g3C"] = (np.cos(th) / N).astype(NPBF); c["g3nS"] = (-np.sin(th) / N).astype(NPBF)
    if NA == 128:
        pp_ = np.arange(128, dtype=np.int64)[:, None, None]; f1_ = np.arange(128, dtype=np.int64)[None, :, None]; f2_ = np.arange(128, dtype=np.int64)[None, None, :]
        ang = 2 * np.pi * ((pp_ * (f1_ + 128 * f2_)) % N).astype(np.float64) / N
        c["MC"] = np.cos(ang).astype(NPBF); c["MS"] = np.sin(ang).astype(NPBF)
    if NA == 4:
        f1i = np.arange(128) % 4
        twp = 2 * np.pi * f1i[:, None] * np.arange(128)[None, :] / N
        c["twA2c"] = np.concatenate([np.cos(twp), np.cos(twp)], 1).astype(np.float32)
        c["twB2c"] = np.concatenate([np.sin(twp), -np.sin(twp)], 1).astype(np.float32)
        gC = np.zeros((128, 64), np.float64); gS = np.zeros((128, 64), np.float64)
        for ch in range(32):
            for f1_ in range(4):
                for a_ in range(2):
                    gC[ch * 4 + f1_, ch * 2 + a_] = np.cos(2 * np.pi * f1_ * a_ / 4) / N
                    gS[ch * 4 + f1_, ch * 2 + a_] = -np.sin(2 * np.pi * f1_ * a_ / 4) / N
        c["gbC"] = gC.astype(NPBF); c["gbnS"] = gS.astype(NPBF)
    return c


def _filter_consts(n):
    q = np.arange(2 * n)
    tap = np.where(q < n, q, 2 * n - q).astype(np.int64)
    tap = np.minimum(tap, n - 1)
    t01 = np.linspace(0.0, 1.0, n, dtype=np.float32)
    bands = 16
    w = (2.0 * np.pi * np.arange(n, dtype=np.float32) / n).astype(np.float32)
    f = np.linspace(1e-4, bands - 1, bands, dtype=np.float32)[None, :]
    feats = np.concatenate([t01[:, None], np.cos(f * w[:, None]), -np.sin(f * w[:, None])], -1).astype(np.float32)
    featsT = np.ascontiguousarray(feats[tap].T)
    t01b = np.ascontiguousarray(np.tile(t01[tap][None, :], (128, 1)))
    deltas = np.abs(np.linspace(np.log(1e-2) / 1.5, np.log(1e-2) / 0.3, 512, dtype=np.float32))
    ndel = np.ascontiguousarray((-deltas).reshape(4, 128).T)
    return featsT.astype(np.float32), t01b.astype(np.float32), ndel.astype(np.float32)


def _pm(v, n):
    return np.ascontiguousarray(np.asarray(v, np.float32).reshape(n, 128).T)


def _host_prep(inp, b, hh, C):
    fl = (hh == 1)
    m = {}
    x = inp["x"][b]; cx = inp["ctx"][b]
    if fl: x = x[::-1]; cx = cx[::-1]
    m["xt0"] = np.ascontiguousarray(np.concatenate([x, cx], 0).T)
    cv = np.stack([inp["c"][b], inp["c_ctx"]], 1)
    m["cvec"] = np.ascontiguousarray(cv.reshape(8, 128, 2).transpose(1, 0, 2))
    perm = C["perm64"]
    for i in range(2):
        m[f"ada_w{i}"] = inp["ada_w"][i]
        m[f"ada_b{i}"] = _pm(inp["ada_b"][i], 48)
        m[f"nmix{i}"] = _pm(inp["norm_mix"][i], 8)
        m[f"nffn{i}"] = _pm(inp["norm_ffn"][i], 8)
        w = inp["mix_w_in"][i]
        qk = w[:, :768].reshape(D, 12, 64)[:, :, perm].reshape(D, 768)
        m[f"w_in{i}"] = np.ascontiguousarray(np.concatenate([w, qk], 1))
        m[f"w_out{i}"] = inp["mix_w_out"][i]
        gq = inp["attn_q_norm"][i]; gk = inp["attn_k_norm"][i]
        m[f"qkg{i}"] = np.ascontiguousarray(np.stack([np.tile(gq, 2), np.tile(gq[perm], 2), np.tile(gk, 2), np.tile(gk[perm], 2)], 1).astype(np.float32))
        m[f"w_up{i}"] = inp["ffn_w_up"][i]
        m[f"w_dn{i}"] = inp["ffn_w_down"][i]
        fw = inp["ffn_conv_w"][i]
        if fl: fw = fw[::-1]
        m[f"fcw{i}"] = np.ascontiguousarray(fw.reshape(3, NM, 128).transpose(2, 1, 0))
        m[f"fcb{i}"] = _pm(inp["ffn_conv_b"][i], NM)
    hw = inp["hy_conv_w"][0]
    if fl: hw = hw[::-1]
    m["hcw"] = np.ascontiguousarray(hw.reshape(3, 12, 128).transpose(2, 1, 0))
    m["hcb"] = _pm(inp["hy_conv_b"][0], 12)
    sw = inp["sc_conv_w"][0]
    if fl: sw = sw[::-1]
    m["scw"] = np.ascontiguousarray(sw.reshape(3, 4, 128).transpose(2, 1, 0))
    m["hw1"] = inp["hy_w1"][0]; m["hw2"] = inp["hy_w2"][0]; m["hw3"] = inp["hy_w3"][0]
    w4 = inp["hy_w4"][0]
    if fl: w4 = np.concatenate([w4[:, 512:], w4[:, :512]], 1)
    m["hw4"] = np.ascontiguousarray(w4)
    m["hb"] = np.ascontiguousarray(np.stack([inp["hy_b1"][0], inp["hy_b2"][0], inp["hy_b3"][0], inp["hy_freq"][0]], 1).astype(np.float32))
    m["hbd"] = _pm(inp["hy_bias_d"][0], 4)
    m["sink"] = np.ascontiguousarray(np.tile(inp["swa_sink"][0][None, :], (128, 1)).astype(np.float32))
    cos = C["cos64"]; sin = C["sin64"]
    if fl: cos = cos[:, ::-1]; sin = sin[:, ::-1]
    cosx = np.concatenate([cos, np.ones((64, CTX), np.float32)], 1)
    sinx = np.concatenate([sin, np.zeros((64, CTX), np.float32)], 1)
    m["cosT"] = np.ascontiguousarray(np.concatenate([cosx, cosx], 0))
    m["sinT"] = np.ascontiguousarray(np.concatenate([sinx, sinx], 0))
    m["bmask"] = C["bmask"]
    for tag, NA in (("L", 128), ("C", 4)):
        for kk, v in C["fft" + tag].items():
            m[kk + tag] = v
    for tag in ("L", "C"):
        ft, t01b, ndel = C["filt" + tag]
        m["featsT" + tag] = ft; m["t01b" + tag] = t01b
    m["ndel"] = C["filtL"][2]
    return m

def _build(stop_after=None, dbg=()):
    nc = bass.Bass("TRN2", target_bir_lowering=False)
    k = K(nc)
    def din(name, shape, dt=F32):
        return nc.dram_tensor(name, list(shape), dt, kind="ExternalInput").ap()
    def dscr(name, shape, dt):
        kind = "ExternalOutput" if name in dbg else "Internal"
        return nc.dram_tensor(name, list(shape), dt, kind=kind).ap()
    I = {}
    I["xt0"] = din("xt0", [D, TT]); I["cvec"] = din("cvec", [128, 8, 2])
    for i in range(2):
        I[f"ada_w{i}"] = din(f"ada_w{i}", [D, 6 * D]); I[f"ada_b{i}"] = din(f"ada_b{i}", [128, 48])
        I[f"nmix{i}"] = din(f"nmix{i}", [128, 8]); I[f"nffn{i}"] = din(f"nffn{i}", [128, 8])
        I[f"w_in{i}"] = din(f"w_in{i}", [D, 3328]); I[f"w_out{i}"] = din(f"w_out{i}", [D, D])
        I[f"qkg{i}"] = din(f"qkg{i}", [128, 4])
        I[f"w_up{i}"] = din(f"w_up{i}", [D, 2 * DFF]); I[f"w_dn{i}"] = din(f"w_dn{i}", [DFF, D])
        I[f"fcw{i}"] = din(f"fcw{i}", [128, NM, 3]); I[f"fcb{i}"] = din(f"fcb{i}", [128, NM])
    I["hcw"] = din("hcw", [128, 12, 3]); I["hcb"] = din("hcb", [128, 12]); I["scw"] = din("scw", [128, 4, 3])
    I["hw1"] = din("hw1", [33, 64]); I["hw2"] = din("hw2", [64, 64]); I["hw3"] = din("hw3", [64, 64])
    I["hw4"] = din("hw4", [64, 1024]); I["hb"] = din("hb", [64, 4]); I["hbd"] = din("hbd", [128, 4])
    I["sink"] = din("sink", [128, 8])
    I["cosT"] = din("cosT", [128, TT]); I["sinT"] = din("sinT", [128, TT])
    I["bmask"] = din("bmask", [6, 128, 512], BF16)
    for tag, NA in (("L", 128), ("C", 4)):
        I["f1cs" + tag] = din("f1cs" + tag, [NA, 2 * NA], BF16)
        for nm in ("fC", "fS", "fnS"): I[nm + tag] = din(nm + tag, [128, 128], BF16)
        for nm in ("fCS", "fnSC"): I[nm + tag] = din(nm + tag, [128, 256], BF16)
        for nm in ("twA", "twB"): I[nm + tag] = din(nm + tag, [128, 512])
        for nm in ("twA2", "twB2"): I[nm + tag] = din(nm + tag, [NA, 1024])
        for nm in ("g3C", "g3nS"): I[nm + tag] = din(nm + tag, [NA, NA], BF16)
        if NA == 128:
            for nm in ("MC", "MS"): I[nm + tag] = din(nm + tag, [128, 128, 128], BF16)
        if NA == 4:
            for nm in ("twA2c", "twB2c"): I[nm + tag] = din(nm + tag, [128, 256])
            for nm in ("gbC", "gbnS"): I[nm + tag] = din(nm + tag, [128, 64], BF16)
        n = 64 * NA
        I["featsT" + tag] = din("featsT" + tag, [33, 2 * n]); I["t01b" + tag] = din("t01b" + tag, [128, 2 * n])
    I["ndel"] = din("ndel", [128, 4])
    OUT = nc.dram_tensor("out", [D, OWN], F32, kind="ExternalOutput").ap()

    S = {}
    for i in range(2):
        S[f"bw_in{i}"] = dscr(f"bw_in{i}", [D, 3328], BF16); S[f"bw_out{i}"] = dscr(f"bw_out{i}", [D, D], BF16)
        S[f"bw_up{i}"] = dscr(f"bw_up{i}", [D, 2 * DFF], BF16); S[f"bw_dn{i}"] = dscr(f"bw_dn{i}", [DFF, D], BF16)
    S["QT"] = dscr("QT", [8, 64, TT], BF16); S["KT"] = dscr("KT", [4, 64, TT], BF16); S["V"] = dscr("V", [TT, 256], BF16)
    S["UT"] = dscr("UT", [512, TT], BF16); S["X0T"] = dscr("X0T", [512, TT], F32)
    S["UF"] = dscr("UF", [512, TT], F32)
    S["UTA"] = dscr("UTA", [64, 512, 128], BF16); S["KNA"] = dscr("KNA", [128, 512, 128], BF16)
    S["YT"] = dscr("YT", [512, TT], F32)
    S["OAT"] = dscr("OAT", [512, TT], BF16); S["OCT"] = dscr("OCT", [512, TT], BF16)
    S["XM"] = dscr("XM", [D, TT], F32); S["X1"] = dscr("X1", [D, TT], F32); S["XM1"] = dscr("XM1", [D, TT], F32)
    S["KRAWL"] = dscr("KRAWL", [512, 2 * SEQ], F32); S["KNL"] = dscr("KNL", [512, 2 * SEQ], BF16)
    S["KRAWC"] = dscr("KRAWC", [512, 2 * CTX], F32); S["KNC"] = dscr("KNC", [512, 2 * CTX], BF16)

    MOD = k.sbuf([128, 2, 48, 2], F32, "mod", persist=True)
    AB = k.sbuf([128, 2, 2, 2, 8, 2], F32, "ab", persist=True)
    ONES = k.sbuf([128, 128], BF16, "ones", persist=True)
    BONES = k.sbuf([128, 128], BF16, "bones", persist=True)
    SEL = k.sbuf([128, 64], F32, "sel", persist=True)
    ESINK = k.sbuf([128, 8], F32, "esink", persist=True)
    RNORM = {"L": k.sbuf([128, 4], F32, "rnL", persist=True), "C": k.sbuf([128, 4], F32, "rnC", persist=True)}

    phases = []

    conv_items = []
    for i in range(2):
        for src, dst, rows, cols in ((f"w_in{i}", f"bw_in{i}", D, 3328), (f"w_out{i}", f"bw_out{i}", D, D),
                                     (f"w_up{i}", f"bw_up{i}", D, 2 * DFF), (f"w_dn{i}", f"bw_dn{i}", DFF, D)):
            for r0 in range(0, rows, 128):
                for c0 in range(0, cols, 2048):
                    conv_items.append((src, dst, r0, c0, min(2048, cols - c0)))
    CONV_EARLY = sum(1 for it in conv_items if it[0] in ("w_in0", "w_out0"))
    conv_pos = [0]

    class make_converter:
        def __init__(self):
            self.stg = [k.sbuf([128, 2048], F32, "stg") for _ in range(3)]
            self.stb = [k.sbuf([128, 2048], BF16, "stb") for _ in range(3)]
        def step(self, eng=None, q="sp"):
            n = conv_pos[0]
            if n >= len(conv_items): return False
            conv_pos[0] += 1
            src, dst, r0, c0, cw = conv_items[n]
            a = self.stg[n % 3]; bt = self.stb[n % 3]
            k.dma(q, a[:, 0:cw], I[src][r0:r0 + 128, c0:c0 + cw], writes=[a])
            k.copy(eng or ("dve" if n % 2 == 0 else "act"), bt[:, 0:cw], a[:, 0:cw], [a], [bt])
            k.dma(q, S[dst][r0:r0 + 128, c0:c0 + cw], bt[:, 0:cw], reads=[bt])
            return True

    def ph_setup():
        k.memset("dve", ONES[:], 1.0, [ONES])
        k.memset("dve", BONES[:], 0.0, [BONES])
        k.memset("dve", BONES[0:64, 0:64], 1.0, [BONES])
        k.memset("dve", BONES[64:128, 64:128], 1.0, [BONES])
        k.memset("dve", SEL[:], 0.0, [SEL])
        k.memset("dve", SEL[64:65, :], 1.0, [SEL])
        snk = k.sbuf([128, 8], F32)
        k.dma("sp", snk[:], I["sink"][:, :], writes=[snk])
        k.act(ESINK[:], snk[:], AF.Exp, [snk], [ESINK])
        cv_ = make_converter()
        for _ in range(CONV_EARLY):
            cv_.step()
        cv = k.sbuf([128, 8, 2], F32)
        k.dma("sp", cv[:], I["cvec"][:, :, :], writes=[cv])
        sc = k.sbuf([128, 8, 2], F32)
        k.act(sc[:], cv[:], AF.Silu, [cv], [sc])
        wst = [k.sbuf([128, 8, 512], F32, "wst") for _ in range(2)]
        ps = [k.psum([128, 512], F32) for _ in range(2)]
        n = 0
        for i in range(2):
            adb = k.sbuf([128, 48], F32)
            k.dma("sp", adb[:], I[f"ada_b{i}"][:, :], writes=[adb])
            for cb in range(12):
                wt = wst[n % 2]; n += 1
                k.dma("sp", wt[:], I[f"ada_w{i}"][:, cb * 512:(cb + 1) * 512].rearrange("(k p) f -> p k f", p=128), writes=[wt])
                for mi in range(4):
                    m = cb * 4 + mi
                    pt = ps[m % 2]
                    for kk in range(8):
                        k.mm(pt[:, 0:2], wt[:, kk, mi * 128:(mi + 1) * 128], sc[:, kk, :], kk == 0, kk == 7, [wt, sc], [pt])
                    k.ts("dve", MOD[:, i, m, :], pt[:, 0:2], adb[:, m:m + 1], None, ALU.add, None, [pt, adb], [MOD])
            for wh, (nm, sh0, sc0) in enumerate(((f"nmix{i}", 0, 8), (f"nffn{i}", 24, 32))):
                g = k.sbuf([128, 8], F32)
                k.dma("sp", g[:], I[nm][:, :], writes=[g])
                for j in range(2):
                    k.stt("dve", AB[:, i, wh, 0, :, j], MOD[:, i, sc0:sc0 + 8, j], 1.0, g[:], ALU.add, ALU.mult, [MOD, g], [AB])
                    k.copy("dve", AB[:, i, wh, 1, :, j], MOD[:, i, sh0:sh0 + 8, j], [MOD], [AB])
    phases.append(("setup", ph_setup))

    def norm_mod(xh, Wc, i, wh, j, sq, ssps, rstd, tmp, h):
        for kk in range(8):
            k.act(sq[:, kk, 0:Wc], xh[:, kk, 0:Wc], AF.Square, [xh], [sq])
        for (c0, c1) in ((0, min(512, Wc)), (512, Wc)):
            if c1 <= c0: continue
            for kk in range(8):
                k.mm(ssps[:, c0:c1], ONES[:, :], sq[:, kk, c0:c1], kk == 0, kk == 7, [ONES, sq], [ssps])
        k.act(rstd[:, 0:Wc], ssps[:, 0:Wc], AF.Sqrt, [ssps], [rstd], bias=1e-6, scale=1.0 / D)
        k.recip(rstd[:, 0:Wc], rstd[:, 0:Wc], [rstd], [rstd])
        for kk in range(8):
            t = tmp[kk % 2]
            k.stt("dve", t[:, 0:Wc], xh[:, kk, 0:Wc], AB[:, i, wh, 0, kk, j:j + 1], rstd[:, 0:Wc], ALU.mult, ALU.mult, [xh, AB, rstd], [t])
            k.act(h[:, kk, 0:Wc], t[:, 0:Wc], AF.Identity, [t, AB], [h], bias=AB[:, i, wh, 1, kk, j:j + 1], scale=1.0)

    CHUNKS_ALL = [(c * 512, 512, c == 0, c == 15, 0) for c in range(16)] + [(SEQ, CTX, True, True, 1)]
    CHUNKS_E = [(c * 512, 512, c == 0, False, 0) for c in range(9)]
    CTXCH = (SEQ, CTX, True, True, 1)

    def load_xh(xh, src, t0, W, ledge, redge, q="sp"):
        lo = 2 if ledge else 1
        hi = W + 2 if redge else W + 3
        if ledge: k.memset("pool", xh[:, :, 1:2], 0.0, [xh])
        if redge: k.memset("pool", xh[:, :, W + 2:W + 3], 0.0, [xh])
        k.dma(q, xh[:, :, lo:hi], src[:, t0 - 2 + lo:t0 - 2 + hi].rearrange("(k p) t -> p k t", p=128), writes=[xh])

    def make_inproj(i, XIN, chunks):
        def ph():
            w = k.sbuf([128, 8, 3328], BF16, "w_in")
            for kk in range(8):
                k.dma("sp", w[:, kk, :], S[f"bw_in{i}"][kk * 128:(kk + 1) * 128, :], writes=[w])
            qkg = k.sbuf([128, 4], F32); k.dma("sp", qkg[:], I[f"qkg{i}"][:, :], writes=[qkg])
            if i == 0:
                cw = k.sbuf([128, 12, 3], F32); k.dma("sp", cw[:], I["hcw"][:, :, :], writes=[cw])
                cb = k.sbuf([128, 12], F32); k.dma("sp", cb[:], I["hcb"][:, :], writes=[cb])
            else:
                cw = k.sbuf([128, 4, 3], F32); k.dma("sp", cw[:], I["scw"][:, :, :], writes=[cw])
            xhs = [k.sbuf([128, 8, 516], F32, "xh") for _ in range(2)]
            for t_ in xhs: k.memset("pool", t_[:], 0.0, [t_])
            sq = k.sbuf([128, 8, 516], BF16, "sq")
            h = k.sbuf([128, 8, 516], BF16, "h")
            rstd = k.sbuf([128, 516], F32, "rstd")
            tmp = [k.sbuf([128, 516], F32, "tmp") for _ in range(2)]
            ssps = k.psum([128, 1024], F32, "ssps")
            zps = [k.psum([128, 512], F32, "zps") for _ in range(4)]
            cps = k.psum([128, 1024], F32, "cps")
            cosb = k.sbuf([128, 512], F32, "cos"); sinb = k.sbuf([128, 512], F32, "sin")
            sq2 = [k.sbuf([128, 512], BF16, "sq2") for _ in range(2)]
            rs = [k.sbuf([128, 512], F32, "rs") for _ in range(2)]
            ta = [k.sbuf([128, 512], F32, "ta") for _ in range(2)]
            tb = [k.sbuf([128, 512], F32, "tb") for _ in range(2)]
            qo = [k.sbuf([128, 512], BF16, "qo") for _ in range(2)]
            vo = [k.sbuf([128, 256], BF16, "vo") for _ in range(2)]
            asb = [k.sbuf([128, 516], F32, "asb") for _ in range(3)]
            c1 = [k.sbuf([128, 512], F32, "c1") for _ in range(3)]
            uo = [k.sbuf([128, 512], F32, "uo") for _ in range(2)]
            ub = [k.sbuf([128, 512], BF16, "ub") for _ in range(2)]
            pp = [k.sbuf([128, 516], F32, "pp") for _ in range(2)]
            nq = [0]
            import os
            PARTS = os.environ.get("INPROJ_PARTS", "nqvc")
            NCHK = int(os.environ.get("INPROJ_NCH", "99"))
            for ci, (t0, W, le, re, j) in enumerate(chunks[:NCHK]):
                Wc = W + 4
                do_q = (t0 < E) or j == 1
                if i == 1 and j == 1: do_q = False
                xh = xhs[ci % 2]
                load_xh(xh, XIN, t0, W, le, re)
                norm_mod(xh, Wc, i, 0, j, sq, ssps, rstd, tmp, h)
                k.dma("sp", cosb[:, 0:W], I["cosT"][:, t0:t0 + W], writes=[cosb])
                k.dma("sp", sinb[:, 0:W], I["sinT"][:, t0:t0 + W], writes=[sinb])
                for pr in range(6):
                    if "q" not in PARTS: continue
                    if pr < 4 and not do_q: continue
                    n = nq[0]; nq[0] += 1
                    zp = zps[(2 * n) % 4]; zsp = zps[(2 * n + 1) % 4]
                    for kk in range(8):
                        k.mm(zp[:, 0:W], w[:, kk, pr * 128:(pr + 1) * 128], h[:, kk, 2:W + 2], kk == 0, kk == 7, [w, h], [zp])
                    for kk in range(8):
                        k.mm(zsp[:, 0:W], w[:, kk, 2560 + pr * 128:2560 + (pr + 1) * 128], h[:, kk, 2:W + 2], kk == 0, kk == 7, [w, h], [zsp])
                    s2 = sq2[n % 2]; r_ = rs[n % 2]; a_ = ta[n % 2]; b_ = tb[n % 2]; q_ = qo[n % 2]
                    gi = 0 if pr < 4 else 2
                    k.act(s2[:, 0:W], zp[:, 0:W], AF.Square, [zp], [s2])
                    k.stt("dve", a_[:, 0:W], zp[:, 0:W], qkg[:, gi:gi + 1], cosb[:, 0:W], ALU.mult, ALU.mult, [zp, qkg, cosb], [a_])
                    k.stt("dve", b_[:, 0:W], zsp[:, 0:W], qkg[:, gi + 1:gi + 2], sinb[:, 0:W], ALU.mult, ALU.mult, [zsp, qkg, sinb], [b_])
                    k.mm(zp[:, 0:W], BONES[:, :], s2[:, 0:W], True, True, [BONES, s2], [zp])
                    k.act(r_[:, 0:W], zp[:, 0:W], AF.Sqrt, [zp], [r_], bias=1e-6, scale=1.0 / 64)
                    k.recip(r_[:, 0:W], r_[:, 0:W], [r_], [r_])
                    QS = os.environ.get("QSKIP", "")
                    pe_ = "dve" if "pool" in QS else "pool"
                    k.tt(pe_, a_[:, 0:W], a_[:, 0:W], b_[:, 0:W], ALU.add, [a_, b_], [a_])
                    k.tt(pe_, q_[:, 0:W], a_[:, 0:W], r_[:, 0:W], ALU.mult, [a_, r_], [q_])
                    for hf in range(2):
                        if "dma" in QS: continue
                        if pr < 4:
                            dst = S["QT"][2 * pr + hf, :, t0:t0 + W]
                        else:
                            dst = S["KT"][2 * (pr - 4) + hf, :, t0:t0 + W]
                        k.dma("pool", dst, q_[hf * 64:(hf + 1) * 64, 0:W], reads=[q_])
                for tj in range(W // 128):
                    if "v" not in PARTS: continue
                    n = nq[0]; nq[0] += 1
                    vp = zps[n % 4]
                    for kk in range(8):
                        k.mm(vp[:, 0:256], h[:, kk, 2 + tj * 128:2 + (tj + 1) * 128], w[:, kk, 768:1024], kk == 0, kk == 7, [w, h], [vp])
                    v_ = vo[n % 2]
                    k.copy("act", v_[:, :], vp[:, 0:256], [vp], [v_])
                    k.dma("pool", S["V"][t0 + tj * 128:t0 + (tj + 1) * 128, :], v_[:, :], reads=[v_])
                def convproj(m, dst):
                    for (c0, c1_) in ((0, min(512, Wc)), (512, Wc)):
                        if c1_ <= c0: continue
                        for kk in range(8):
                            k.mm(cps[:, c0:c1_], w[:, kk, 1024 + m * 128:1024 + (m + 1) * 128], h[:, kk, c0:c1_], kk == 0, kk == 7, [w, h], [cps])
                    k.copy("act", dst[:, 0:Wc], cps[:, 0:Wc], [cps], [dst])
                    if le: k.memset("pool", dst[:, 1:2], 0.0, [dst])
                    if re: k.memset("pool", dst[:, W + 2:W + 3], 0.0, [dst])
                def conv3(out, a, wts, m, bias, eng="dve"):
                    if bias is not None:
                        k.ts(eng, out[:, 0:W], a[:, 2:W + 2], wts[:, m, 1:2], bias, ALU.mult, ALU.add, [a, wts, cb], [out])
                    else:
                        k.ts(eng, out[:, 0:W], a[:, 2:W + 2], wts[:, m, 1:2], None, ALU.mult, None, [a, wts], [out])
                    k.stt(eng, out[:, 0:W], a[:, 1:W + 1], wts[:, m, 0:1], out[:, 0:W], ALU.mult, ALU.add, [a, wts, out], [out])
                    k.stt(eng, out[:, 0:W], a[:, 3:W + 3], wts[:, m, 2:3], out[:, 0:W], ALU.mult, ALU.add, [a, wts, out], [out])
                for jc in range(4):
                    if "c" not in PARTS: continue
                    if i == 0:
                        convproj(4 + jc, asb[0]); conv3(c1[0], asb[0], cw, 4 + jc, cb[:, 4 + jc:5 + jc])
                        convproj(8 + jc, asb[1]); conv3(c1[1], asb[1], cw, 8 + jc, cb[:, 8 + jc:9 + jc])
                        u_ = uo[jc % 2]; ub_ = ub[jc % 2]
                        k.tt("dve", u_[:, 0:W], c1[0][:, 0:W], c1[1][:, 0:W], ALU.mult, [c1[0], c1[1]], [u_])
                        k.copy("pool", ub_[:, 0:W], u_[:, 0:W], [u_], [ub_])
                        k.dma("pool", S["UF"][jc * 128:(jc + 1) * 128, t0:t0 + W], u_[:, 0:W], reads=[u_])
                        if j == 0:
                            k.dma("pool", S["UTA"][t0 // 128:t0 // 128 + 4, jc * 128:(jc + 1) * 128, :].rearrange("a c p -> c a p"),
                                  ub_[:, 0:W].rearrange("c (a p) -> c a p", p=128), reads=[ub_])
                        else:
                            k.dma("pool", S["UT"][jc * 128:(jc + 1) * 128, t0:t0 + W], ub_[:, 0:W], reads=[ub_])
                        if do_q:
                            convproj(jc, asb[2]); conv3(c1[2], asb[2], cw, jc, cb[:, jc:jc + 1])
                            k.dma("pool", S["X0T"][jc * 128:(jc + 1) * 128, t0:t0 + W], c1[2][:, 0:W], reads=[c1[2]])
                    elif j == 0:
                        convproj(4 + jc, asb[0]); convproj(8 + jc, asb[1])
                        p_ = pp[jc % 2]
                        k.tt("dve", p_[:, 0:Wc], asb[0][:, 0:Wc], asb[1][:, 0:Wc], ALU.mult, [asb[0], asb[1]], [p_])
                        conv3(c1[0], p_, cw, jc, None)
                        convproj(jc, asb[2])
                        ub_ = ub[jc % 2]
                        k.tt("pool", ub_[:, 0:W], c1[0][:, 0:W], asb[2][:, 2:W + 2], ALU.mult, [c1[0], asb[2]], [ub_])
                        k.dma("pool", S["OCT"][jc * 128:(jc + 1) * 128, t0:t0 + W], ub_[:, 0:W], reads=[ub_])
        return ph

    def make_attn(i):
        def ph():
            NKT = TT // 128
            kT = k.sbuf([128, TT], BF16, "kT")
            k.memset("pool", kT[64:128, :], 0.0, [kT])
            va = k.sbuf([128, NKT, 65], BF16, "va")
            qT = [k.sbuf([128, 512], BF16, "qT") for _ in range(2)]
            for t_ in qT: k.memset("pool", t_[64:128, :], 0.0, [t_])
            sps = [k.psum([128, 512], F32, "sps") for _ in range(4)]
            ops_ = [k.psum([128, 512], F32, "ops") for _ in range(2)]
            bps = k.psum([128, 512], F32, "bps")
            pT = [k.sbuf([128, 512], BF16, "pT") for _ in range(4)]
            osb = [k.sbuf([65, 512], F32, "osb") for _ in range(2)]
            rb = [k.sbuf([64, 512], F32, "rb") for _ in range(2)]
            ob = [k.sbuf([64, 512], BF16, "ob") for _ in range(2)]
            if i == 1:
                bm = k.sbuf([128, 6, 512], BF16, "bm")
                for r in range(6):
                    k.dma("sp", bm[:, r, :], I["bmask"][r, :, :], writes=[bm])
            qch = []
            for c in range(9):
                t0 = c * 512
                Wq = 512 if c < 8 else 256
                if i == 0:
                    kts = [(kt, 0, Wq, None) for kt in range(NKT)]
                else:
                    kts = []
                    for r in range(-1, 5):
                        kt = 4 * c + r
                        if kt < 0 or kt >= E // 128: continue
                        f0 = max(0, 128 * (r - 1)); f1 = min(Wq, 128 * (r + 2))
                        if f1 <= f0: continue
                        kts.append((kt, f0, f1, r + 1))
                    kts += [(64, 0, Wq, None), (65, 0, Wq, None)]
                qch.append((t0, Wq, kts))
            if i == 0:
                qch.append((SEQ, CTX, [(64, 0, CTX, None), (65, 0, CTX, None)]))
            n = 0; nh = 0
            cvt = make_converter() if i == 0 else None
            for jkv in range(4):
                k.dma("sp", kT[0:64, :], S["KT"][jkv, :, :], writes=[kT])
                k.dma("sp", va[:, :, 0:64], S["V"][:, jkv * 64:(jkv + 1) * 64].rearrange("(n p) d -> p n d", p=128), writes=[va])
                k.memset("pool", va[:, :, 64:65], 1.0, [va])
                for g in range(2):
                    hq = 2 * jkv + g
                    for (t0, W, kts) in qch:
                        q_ = qT[nh % 2]; op_ = ops_[nh % 2]; o_ = osb[nh % 2]; r_ = rb[nh % 2]; b_ = ob[nh % 2]
                        nh += 1
                        k.dma("sp", q_[0:64, 0:W], S["QT"][hq, :, t0:t0 + W], writes=[q_])
                        if i == 1:
                            pass
                        nk = len(kts)
                        def smm(idx):
                            kt, f0, f1, mi = kts[idx]
                            sp_ = sps[(n + idx) % 4]
                            k.mm(sp_[:, f0:f1], kT[:, kt * 128:(kt + 1) * 128], q_[:, f0:f1], True, True, [kT, q_], [sp_])
                        smm(0)
                        if nk > 1: smm(1)
                        for idx in range(nk):
                            if idx + 2 < nk: smm(idx + 2)
                            kt, f0, f1, mi = kts[idx]
                            sp_ = sps[(n + idx) % 4]; p_ = pT[(n + idx) % 4]
                            if mi is not None and (f0 > 0 or f1 < W):
                                k.memset("dve", p_[:, 0:W], 0.0, [p_])
                            k.act(p_[:, f0:f1], sp_[:, f0:f1], AF.Exp, [sp_], [p_], scale=0.125)
                            if mi is not None:
                                k.tt("dve", p_[:, f0:f1], p_[:, f0:f1], bm[:, mi, f0:f1], ALU.mult, [p_, bm], [p_])
                            k.mm(op_[0:65, 0:W], va[:, kt, :], p_[:, 0:W], idx == 0, idx == nk - 1, [va, p_], [op_])
                        n += nk
                        k.copy("act", o_[:, 0:W], op_[0:65, 0:W], [op_], [o_])
                        k.mm(bps[0:64, 0:W], SEL[0:65, :], o_[0:65, 0:W], True, True, [SEL, o_], [bps])
                        if i == 1:
                            k.ts("dve", r_[:, 0:W], bps[0:64, 0:W], ESINK[0:64, hq:hq + 1], None, ALU.add, None, [bps, ESINK], [r_])
                            k.recip(r_[:, 0:W], r_[:, 0:W], [r_], [r_])
                        else:
                            k.recip(r_[:, 0:W], bps[0:64, 0:W], [bps], [r_])
                        k.tt("dve", b_[:, 0:W], o_[0:64, 0:W], r_[:, 0:W], ALU.mult, [o_, r_], [b_])
                        k.dma("pool", S["OAT"][hq * 64:(hq + 1) * 64, t0:t0 + W], b_[:, 0:W], reads=[b_])
                        if cvt is not None:
                            cvt.step("dve", "pool"); cvt.step("dve", "pool")
            if cvt is not None:
                while cvt.step("dve", "pool"): pass
        return ph

    def make_outproj(i, XIN, XOUT, chunks):
        def ph():
            wa = k.sbuf([128, 4, D], BF16, "woa")
            wc = k.sbuf([128, 4, D], BF16, "woc")
            k.dma("sp", wa[:], S[f"bw_out{i}"][0:512, :].rearrange("(c p) f -> p c f", p=128), writes=[wa])
            k.dma("sp", wc[:], S[f"bw_out{i}"][512:1024, :].rearrange("(c p) f -> p c f", p=128), writes=[wc])
            oa = [k.sbuf([128, 4, 512], BF16, "oa") for _ in range(2)]
            oc = [k.sbuf([128, 4, 512], BF16, "oc") for _ in range(2)]
            xs = [k.sbuf([128, 8, 512], F32, "xs") for _ in range(2)]
            xo = [k.sbuf([128, 8, 512], F32, "xo") for _ in range(2)]
            ps = [k.psum([128, 512], F32, "ps") for _ in range(4)]
            n = 0
            for ci, (t0, W, le, re, j) in enumerate(chunks):
                a_ = oa[ci % 2]; c_ = oc[ci % 2]; x_ = xs[ci % 2]; o_ = xo[ci % 2]
                k.dma("sp", a_[:, :, 0:W], S["OAT"][:, t0:t0 + W].rearrange("(c p) t -> p c t", p=128), writes=[a_])
                k.dma("sp", c_[:, :, 0:W], S["OCT"][:, t0:t0 + W].rearrange("(c p) t -> p c t", p=128), writes=[c_])
                k.dma("sp", x_[:, :, 0:W], XIN[:, t0:t0 + W].rearrange("(k p) t -> p k t", p=128), writes=[x_])
                for m in range(8):
                    p_ = ps[n % 4]; n += 1
                    for hh_ in range(4):
                        k.mm(p_[:, 0:W], wa[:, hh_, m * 128:(m + 1) * 128], a_[:, hh_, 0:W], hh_ == 0, False, [wa, a_], [p_])
                    for cc in range(4):
                        k.mm(p_[:, 0:W], wc[:, cc, m * 128:(m + 1) * 128], c_[:, cc, 0:W], False, cc == 3, [wc, c_], [p_])
                    k.stt("dve", o_[:, m, 0:W], p_[:, 0:W], MOD[:, i, 16 + m, j:j + 1], x_[:, m, 0:W], ALU.mult, ALU.add, [p_, MOD, x_], [o_])
                k.dma("pool", XOUT[:, t0:t0 + W].rearrange("(k p) t -> p k t", p=128), o_[:, :, 0:W], reads=[o_])
        return ph

    def make_ffn(i, XIN, XOUT, chunks, final=False):
        def ph():
            wu = k.sbuf([128, 8, 2 * DFF], BF16, "wu")
            for kk in range(8):
                k.dma("sp", wu[:, kk, :], S[f"bw_up{i}"][kk * 128:(kk + 1) * 128, :], writes=[wu])
            wd = [k.sbuf([128, NM, 128], BF16, "wd") for _ in range(2)]
            cw = k.sbuf([128, NM, 3], F32); k.dma("sp", cw[:], I[f"fcw{i}"][:, :, :], writes=[cw])
            cb = k.sbuf([128, NM], F32); k.dma("sp", cb[:], I[f"fcb{i}"][:, :], writes=[cb])
            xh = k.sbuf([128, 8, 516], F32, "xh")
            k.memset("pool", xh[:], 0.0, [xh])
            sq = k.sbuf([128, 8, 516], BF16, "sq")
            h = k.sbuf([128, 8, 516], BF16, "h")
            rstd = k.sbuf([128, 516], F32, "rstd")
            tmp = [k.sbuf([128, 516], F32, "tmp") for _ in range(2)]
            gg = k.sbuf([128, NM, 512], BF16, "gg")
            asb = [k.sbuf([128, 516], F32, "asb") for _ in range(2)]
            c1 = [k.sbuf([128, 512], F32, "c1") for _ in range(2)]
            xo = [k.sbuf([128, 512], F32, "xo") for _ in range(2)]
            ssps = k.psum([128, 1024], F32, "ssps")
            aps = [k.psum([128, 1024], F32, "aps") for _ in range(2)]
            vps = [k.psum([128, 512], F32, "vps") for _ in range(2)]
            n = 0; nd = 0
            for ci, (t0, W, le, re, j) in enumerate(chunks):
                Wc = W + 4
                load_xh(xh, XIN, t0, W, le, re)
                norm_mod(xh, Wc, i, 1, j, sq, ssps, rstd, tmp, h)
                for m in range(NM):
                    ap_ = aps[n % 2]; vp_ = vps[n % 2]; a_ = asb[n % 2]; c_ = c1[n % 2]; n += 1
                    for (c0, c1_) in ((0, min(512, Wc)), (512, Wc)):
                        if c1_ <= c0: continue
                        for kk in range(8):
                            k.mm(ap_[:, c0:c1_], wu[:, kk, m * 128:(m + 1) * 128], h[:, kk, c0:c1_], kk == 0, kk == 7, [wu, h], [ap_])
                    for kk in range(8):
                        k.mm(vp_[:, 0:W], wu[:, kk, DFF + m * 128:DFF + (m + 1) * 128], h[:, kk, 2:W + 2], kk == 0, kk == 7, [wu, h], [vp_])
                    k.copy("act", a_[:, 0:Wc], ap_[:, 0:Wc], [ap_], [a_])
                    if le: k.memset("pool", a_[:, 1:2], 0.0, [a_])
                    if re: k.memset("pool", a_[:, W + 2:W + 3], 0.0, [a_])
                    eng = "dve"
                    k.ts(eng, c_[:, 0:W], a_[:, 2:W + 2], cw[:, m, 1:2], cb[:, m:m + 1], ALU.mult, ALU.add, [a_, cw, cb], [c_])
                    k.stt(eng, c_[:, 0:W], a_[:, 1:W + 1], cw[:, m, 0:1], c_[:, 0:W], ALU.mult, ALU.add, [a_, cw, c_], [c_])
                    k.stt(eng, c_[:, 0:W], a_[:, 3:W + 3], cw[:, m, 2:3], c_[:, 0:W], ALU.mult, ALU.add, [a_, cw, c_], [c_])
                    k.act(c_[:, 0:W], c_[:, 0:W], AF.Gelu_apprx_tanh, [c_], [c_])
                    k.tt("dve", gg[:, m, 0:W], c_[:, 0:W], vp_[:, 0:W], ALU.mult, [c_, vp_], [gg])
                for mo in range(8):
                    wd_ = wd[nd % 2]; o_ = xo[nd % 2]; p_ = vps[nd % 2]; nd += 1
                    k.dma("sp", wd_[:], S[f"bw_dn{i}"][:, mo * 128:(mo + 1) * 128].rearrange("(m p) f -> p m f", p=128), writes=[wd_])
                    for m in range(NM):
                        k.mm(p_[:, 0:W], wd_[:, m, :], gg[:, m, 0:W], m == 0, m == NM - 1, [wd_, gg], [p_])
                    k.stt("dve", o_[:, 0:W], p_[:, 0:W], MOD[:, i, 40 + mo, j:j + 1], xh[:, mo, 2:W + 2], ALU.mult, ALU.add, [p_, MOD, xh], [o_])
                    k.dma("pool", XOUT[mo * 128:(mo + 1) * 128, t0:t0 + W], o_[:, 0:W], reads=[o_])
        return ph

    def make_filter(tag, n):
        KRAW = S["KRAW" + tag]; KN = S["KN" + tag]
        def ph():
            w1 = k.sbuf([33, 64], F32); k.dma("sp", w1[:], I["hw1"][:, :], writes=[w1])
            w2 = k.sbuf([64, 64], F32); k.dma("sp", w2[:], I["hw2"][:, :], writes=[w2])
            w3 = k.sbuf([64, 64], F32); k.dma("sp", w3[:], I["hw3"][:, :], writes=[w3])
            w4 = k.sbuf([64, 1024], F32); k.dma("sp", w4[:], I["hw4"][:, :], writes=[w4])
            hb = k.sbuf([64, 4], F32); k.dma("sp", hb[:], I["hb"][:, :], writes=[hb])
            ndel = k.sbuf([128, 4], F32); k.dma("sp", ndel[:], I["ndel"][:, :], writes=[ndel])
            asum = k.sbuf([128, 4, 40], F32, "asum")
            k.memset("dve", asum[:], 0.0, [asum])
            ft = [k.sbuf([33, 512], F32, "ft") for _ in range(2)]
            t01 = [k.sbuf([128, 512], F32, "t01") for _ in range(2)]
            hid2 = [[k.sbuf([64, 512], F32, "hid") for _ in range(3)] for _ in range(2)]
            ki2 = [k.sbuf([64, 512], I32, "ki") for _ in range(2)]
            win = [k.sbuf([128, 512], F32, "win") for _ in range(2)]
            kr = [k.sbuf([128, 512], F32, "kr") for _ in range(2)]
            junk = k.sbuf([128, 512], F32, "junk")
            krb = [k.sbuf([128, 512], BF16, "krb") for _ in range(2)]
            ps = [k.psum([128, 512], F32, "ps") for _ in range(4)]
            NCH = (2 * n) // 512
            n_ = 0
            for c in range(NCH):
                q0 = c * 512
                f_ = ft[c % 2]; t_ = t01[c % 2]
                k.dma("sp", f_[:, :], I["featsT" + tag][:, q0:q0 + 512], writes=[f_])
                k.dma("sp", t_[:, :], I["t01b" + tag][:, q0:q0 + 512], writes=[t_])
                src = f_; srcK = 33
                hid = hid2[c % 2]; ki = ki2[c % 2]
                for li, wl in enumerate((w1, w2, w3)):
                    p_ = ps[n_ % 4]; n_ += 1
                    k.mm(p_[0:64, :], wl[0:srcK, :], src[0:srcK, :], True, True, [wl, src], [p_])
                    hd = hid[li]
                    k.ts("dve", hd[:, :], p_[0:64, :], hb[:, li:li + 1], hb[:, 3:4], ALU.add, ALU.mult, [p_, hb], [hd])
                    k.ts("dve", ki[:, :], hd[:, :], float(1.0 / (2 * np.pi)), None, ALU.mult, None, [hd], [ki])
                    k.stt("dve", hd[:, :], ki[:, :], float(-2 * np.pi), hd[:, :], ALU.mult, ALU.add, [ki, hd], [hd])
                    k.act(hd[:, :], hd[:, :], AF.Sin, [hd], [hd])
                    src = hd; srcK = 64
                segs = []
                if q0 + 512 <= n: segs = [(0, 512, 0)]
                elif q0 >= n: segs = [(0, 512, 512)]
                else: segs = [(0, n - q0, 0), (n - q0, 512, 512)]
                for jc in range(4):
                    p_ = ps[n_ % 4]; n_ += 1
                    for (a0, a1, off) in segs:
                        k.mm(p_[:, a0:a1], w4[:, off + jc * 128:off + (jc + 1) * 128], hid[2][:, a0:a1], True, True, [w4, hid[2]], [p_])
                    wn = win[jc % 2]; kr_ = kr[jc % 2]
                    k.act(wn[:, :], t_[:, :], AF.Exp, [t_, ndel], [wn], scale=ndel[:, jc:jc + 1])
                    k.stt("dve", kr_[:, :], wn[:, :], 0.05, p_[:, :], ALU.add, ALU.mult, [wn, p_], [kr_])
                    if q0 <= n < q0 + 512:
                        k.memset("dve", kr_[:, n - q0:n - q0 + 1], 0.0, [kr_])
                    k.act(junk[:, :], kr_[:, :], AF.Abs, [kr_], [junk, asum], accum=asum[:, jc, c:c + 1])
                    kb_ = krb[jc % 2]
                    k.copy("pool", kb_[:, :], kr_[:, :], [kr_], [kb_])
                    if tag == "L":
                        k.dma("pool", S["KNA"][q0 // 128:q0 // 128 + 4, jc * 128:(jc + 1) * 128, :].rearrange("a c p -> c a p"),
                              kb_[:, :].rearrange("c (a p) -> c a p", p=128), reads=[kb_])
                    else:
                        k.dma("pool", KN[jc * 128:(jc + 1) * 128, q0:q0 + 512], kb_[:, :], reads=[kb_])
            rn = RNORM[tag]
            for jc in range(4):
                k.op("dve", lambda e, jc=jc: e.reduce_sum(out=rn[:, jc:jc + 1], in_=asum[:, jc, 0:NCH], axis=mybir.AxisListType.X), [asum], [rn])
            k.recip(rn[:, :], rn[:, :], [rn], [rn], force_self=True)
        return ph

    def make_fftconv(tag, NA, n, tok0, n_out_blocks):
        KN = S["KN" + tag]
        NR = NA // 2
        GF = 512 // NA
        GB = 512 // (2 * NA)
        def ph():
            cst = {}
            for nm, shp, dt in (("f1cs", [NA, 2 * NA], BF16), ("fC", [128, 128], BF16), ("fS", [128, 128], BF16), ("fnS", [128, 128], BF16),
                                ("fCS", [128, 256], BF16), ("fnSC", [128, 256], BF16), ("twA", [128, 512], F32), ("twB", [128, 512], F32),
                                ("twA2", [NA, 1024], F32), ("twB2", [NA, 1024], F32), ("g3C", [NA, NA], BF16), ("g3nS", [NA, NA], BF16)):
                cst[nm] = k.sbuf(shp, dt, nm)
                k.dma("sp", cst[nm][:], I[nm + tag][tuple(slice(None) for _ in shp)], writes=[cst[nm]])
            if NA == 4:
                for nm, shp, dt in (("twA2c", [128, 256], F32), ("twB2c", [128, 256], F32), ("gbC", [128, 64], BF16), ("gbnS", [128, 64], BF16)):
                    cst[nm] = k.sbuf(shp, dt, nm)
                    k.dma("sp", cst[nm][:], I[nm + tag][:, :], writes=[cst[nm]])
                c2a = [k.sbuf([128, 256], F32, "c2a") for _ in range(2)]; c2b = [k.sbuf([128, 256], F32, "c2b") for _ in range(2)]
                y3c = [k.sbuf([128, 2, 128], BF16, "y3c") for _ in range(2)]
                yoc = [k.sbuf([64, 128], F32, "yoc") for _ in range(2)]
            LG = 32 if NA == 128 else 128
            NXB = 2 if NA == 128 else 1
            xu = [k.sbuf([NA, LG, 128], BF16, "xu") for _ in range(NXB)]
            for t_ in xu: k.memset("pool", t_[:], 0.0, [t_])
            xk = [k.sbuf([NA, LG, 128], BF16, "xk") for _ in range(NXB)]
            s1 = [k.psum([128, 512], F32, "s1") for _ in range(2)]
            s2 = [k.psum([128, 512], F32, "s2") for _ in range(2)]
            s3 = [k.psum([128, 1024], F32, "s3") for _ in range(1)]
            s4 = [k.psum([128, 512], F32, "s4") for _ in range(2)]
            NB = 2
            ta = [k.sbuf([128, 512], F32, "ta") for _ in range(NB)]; tb = [k.sbuf([128, 512], F32, "tb") for _ in range(NB)]
            bu = [k.sbuf([128, 2, GF, NA], BF16, "bu") for _ in range(NB)]; bk = [k.sbuf([128, 2, GF, NA], BF16, "bk") for _ in range(NB)]
            kh = [k.sbuf([128, 2, 512], F32, "kh") for _ in range(NB)]
            m1 = [k.sbuf([128, 512], F32, "m1") for _ in range(NB)]; m2 = [k.sbuf([128, 512], F32, "m2") for _ in range(NB)]
            m3 = [k.sbuf([128, 512], F32, "m3") for _ in range(NB)]; m4 = [k.sbuf([128, 512], F32, "m4") for _ in range(NB)]
            yh = [k.sbuf([128, 2, GF, NA], BF16, "yh") for _ in range(NB)]
            t2a = [k.sbuf([NA, 1024], F32, "t2a") for _ in range(NB)]; t2b = [k.sbuf([NA, 1024], F32, "t2b") for _ in range(NB)]
            y3 = [k.sbuf([NA, 2, 4, 128], BF16, "y3") for _ in range(NB)]
            yo = [k.sbuf([n_out_blocks, 4, 128], F32, "yo") for _ in range(2)]
            MO = n_out_blocks
            cnt = {"tw": 0, "inv": 0, "g": 0}
            def fwd(x, cbase, bdst):
                for half in range(2):
                    bank = s1[half]
                    ta_ = ta[cnt["tw"] % NB]; tb_ = tb[cnt["tw"] % NB]; cnt["tw"] += 1
                    for cc in range(GB):
                        ch = cbase + half * GB + cc
                        k.mm(bank[:, cc * 2 * NA:(cc + 1) * 2 * NA], x[0:NA, ch, :], cst["f1cs"][0:NA, :], True, True, [x, cst["f1cs"]], [bank])
                    k.tt("dve", ta_[:, :], bank[:, :], cst["twA"][:, :], ALU.mult, [bank, cst["twA"]], [ta_])
                    k.tt("dve", tb_[:, :], bank[:, :], cst["twB"][:, :], ALU.mult, [bank, cst["twB"]], [tb_])
                    tav = ta_[:, :].rearrange("p (g r f) -> p g r f", g=GB, r=2)
                    tbv = tb_[:, :].rearrange("p (g r f) -> p g r f", g=GB, r=2)
                    k.tt("dve", bdst[:, 0, half * GB:(half + 1) * GB, :], tav[:, :, 0, :], tbv[:, :, 1, :], ALU.subtract, [ta_, tb_], [bdst])
                    k.tt("dve", bdst[:, 1, half * GB:(half + 1) * GB, :], tav[:, :, 1, :], tbv[:, :, 0, :], ALU.subtract, [ta_, tb_], [bdst])
                bre = bdst[:, 0, :, :].rearrange("p g f -> p (g f)"); bim = bdst[:, 1, :, :].rearrange("p g f -> p (g f)")
                k.mm(s2[0][:, :], cst["fC"][:, :], bre, True, False, [cst["fC"], bdst], [s2[0]])
                k.mm(s2[0][:, :], cst["fS"][:, :], bim, False, True, [cst["fS"], bdst], [s2[0]])
                k.mm(s2[1][:, :], cst["fC"][:, :], bim, True, False, [cst["fC"], bdst], [s2[1]])
                k.mm(s2[1][:, :], cst["fnS"][:, :], bre, False, True, [cst["fnS"], bdst], [s2[1]])
            for lg in range(512 // LG):
                xu_ = xu[lg % NXB]; xk_ = xk[lg % NXB]
                k.dma("sp", xu_[0:NR, :, :], S["UT"][lg * LG:(lg + 1) * LG, tok0:tok0 + n].rearrange("c (a p) -> a c p", p=128), writes=[xu_])
                k.dma("sp", xk_[:, :, :], KN[lg * LG:(lg + 1) * LG, :].rearrange("c (a p) -> a c p", p=128), writes=[xk_])
                for gf in range(LG // GF):
                    cbase = gf * GF
                    g_ = cnt["g"] % NB; cnt["g"] += 1
                    kh_ = kh[g_]; yh_ = yh[g_]
                    fwd(xk_, cbase, bk[g_])
                    k.copy("act", kh_[:, 0, :], s2[0][:, :], [s2[0]], [kh_])
                    k.copy("act", kh_[:, 1, :], s2[1][:, :], [s2[1]], [kh_])
                    fwd(xu_, cbase, bu[g_])
                    k.tt("dve", m1[g_][:, :], s2[0][:, :], kh_[:, 0, :], ALU.mult, [s2[0], kh_], [m1[g_]])
                    k.tt("dve", m3[g_][:, :], s2[0][:, :], kh_[:, 1, :], ALU.mult, [s2[0], kh_], [m3[g_]])
                    k.tt("dve", m2[g_][:, :], s2[1][:, :], kh_[:, 1, :], ALU.mult, [s2[1], kh_], [m2[g_]])
                    k.tt("dve", m4[g_][:, :], s2[1][:, :], kh_[:, 0, :], ALU.mult, [s2[1], kh_], [m4[g_]])
                    k.tt("pool", yh_[:, 0, :, :].rearrange("p g f -> p (g f)"), m1[g_][:, :], m2[g_][:, :], ALU.subtract, [m1[g_], m2[g_]], [yh_])
                    k.tt("pool", yh_[:, 1, :, :].rearrange("p g f -> p (g f)"), m3[g_][:, :], m4[g_][:, :], ALU.add, [m3[g_], m4[g_]], [yh_])
                    if NA == 4:
                        for sg in range(GF // 32):
                            b3 = s3[0]
                            iv = cnt["inv"] % 2; cnt["inv"] += 1
                            lre = yh_[:, 0, sg * 32:(sg + 1) * 32, :].rearrange("p g f -> p (g f)")
                            lim = yh_[:, 1, sg * 32:(sg + 1) * 32, :].rearrange("p g f -> p (g f)")
                            k.mm(b3[:, 0:256], lre, cst["fCS"][:, :], True, False, [yh_, cst["fCS"]], [b3])
                            k.mm(b3[:, 0:256], lim, cst["fnSC"][:, :], False, True, [yh_, cst["fnSC"]], [b3])
                            a_ = c2a[iv]; b_ = c2b[iv]; y_ = y3c[iv]; o_ = yoc[iv]
                            k.tt("dve", a_[:, :], b3[:, 0:256], cst["twA2c"][:, :], ALU.mult, [b3, cst["twA2c"]], [a_])
                            k.tt("dve", b_[:, :], b3[:, 0:256], cst["twB2c"][:, :], ALU.mult, [b3, cst["twB2c"]], [b_])
                            k.tt("pool", y_[:, 0, :], a_[:, 0:128], b_[:, 128:256], ALU.add, [a_, b_], [y_])
                            k.tt("pool", y_[:, 1, :], a_[:, 128:256], b_[:, 0:128], ALU.add, [a_, b_], [y_])
                            p4 = s4[iv]
                            k.mm(p4[0:64, 0:128], cst["gbC"][:, :], y_[:, 0, :], True, False, [cst["gbC"], y_], [p4])
                            k.mm(p4[0:64, 0:128], cst["gbnS"][:, :], y_[:, 1, :], False, True, [cst["gbnS"], y_], [p4])
                            k.copy("act", o_[:, :], p4[0:64, 0:128], [p4], [o_])
                            c0 = lg * LG + cbase + sg * 32
                            for a2 in range(2):
                                k.dma("sp", S["YT"][c0:c0 + 32, tok0 + a2 * 128:tok0 + (a2 + 1) * 128], o_[a2:64:2, :], reads=[o_])
                        continue
                    for sg in range(GF // 4):
                        b3 = s3[0]
                        iv = cnt["inv"] % NB; cnt["inv"] += 1
                        t2a_ = t2a[iv]; t2b_ = t2b[iv]; y3_ = y3[iv]
                        for cc in range(4):
                            ch = sg * 4 + cc
                            k.mm(b3[0:NA, cc * 256:(cc + 1) * 256], yh_[:, 0, ch, :], cst["fCS"][:, :], True, False, [yh_, cst["fCS"]], [b3])
                            k.mm(b3[0:NA, cc * 256:(cc + 1) * 256], yh_[:, 1, ch, :], cst["fnSC"][:, :], False, True, [yh_, cst["fnSC"]], [b3])
                        k.tt("dve", t2a_[:, :], b3[0:NA, :], cst["twA2"][:, :], ALU.mult, [b3, cst["twA2"]], [t2a_])
                        k.tt("dve", t2b_[:, :], b3[0:NA, :], cst["twB2"][:, :], ALU.mult, [b3, cst["twB2"]], [t2b_])
                        av = t2a_[:, :].rearrange("p (g r f) -> p g r f", g=4, r=2)
                        bv = t2b_[:, :].rearrange("p (g r f) -> p g r f", g=4, r=2)
                        k.tt("pool", y3_[:, 0, :, :], av[:, :, 0, :], bv[:, :, 1, :], ALU.add, [t2a_, t2b_], [y3_])
                        k.tt("pool", y3_[:, 1, :, :], av[:, :, 1, :], bv[:, :, 0, :], ALU.add, [t2a_, t2b_], [y3_])
                        p4 = s4[iv % 2]; yo_ = yo[iv % 2]
                        k.mm(p4[0:MO, :], cst["g3C"][:, 0:MO], y3_[:, 0, :, :].rearrange("p g f -> p (g f)"), True, False, [cst["g3C"], y3_], [p4])
                        k.mm(p4[0:MO, :], cst["g3nS"][:, 0:MO], y3_[:, 1, :, :].rearrange("p g f -> p (g f)"), False, True, [cst["g3nS"], y3_], [p4])
                        k.copy("act", yo_[:, :, :].rearrange("p g f -> p (g f)"), p4[0:MO, :], [p4], [yo_])
                        c0 = lg * LG + cbase + sg * 4
                        k.dma("sp", S["YT"][c0:c0 + 4, tok0:tok0 + MO * 128].rearrange("c (a p) -> a c p", p=128), yo_[:, :, :], reads=[yo_])
        return ph

    def make_fftconvL2(n_out_blocks):
        tag = "L"; NA = 128; NR = 64; n = SEQ; tok0 = 0; CB = 32; MO = n_out_blocks
        def ph():
            cst = {}
            for nm, shp, dt in (("f1cs", [128, 256], BF16), ("fCS", [128, 256], BF16), ("fnSC", [128, 256], BF16),
                                ("twA2", [128, 1024], F32), ("twB2", [128, 1024], F32), ("g3C", [128, 128], BF16), ("g3nS", [128, 128], BF16)):
                cst[nm] = k.sbuf(shp, dt, nm)
                k.dma("sp", cst[nm][:], I[nm + tag][:, :], writes=[cst[nm]])
            MC = k.sbuf([128, 128, 128], BF16, "MC"); MS = k.sbuf([128, 128, 128], BF16, "MS")
            for q4 in range(4):
                k.dma("sp", MC[:, q4 * 32:(q4 + 1) * 32, :], I["MCL"][:, q4 * 32:(q4 + 1) * 32, :], writes=[MC])
                k.dma("sp", MS[:, q4 * 32:(q4 + 1) * 32, :], I["MSL"][:, q4 * 32:(q4 + 1) * 32, :], writes=[MS])
            xu = k.sbuf([128, CB, 128], BF16, "xu"); k.memset("pool", xu[:], 0.0, [xu])
            xk = k.sbuf([128, CB, 128], BF16, "xk")
            ATs = [k.sbuf([128, CB, 3, 128], BF16, "AT") for _ in range(2)]
            KH = k.sbuf([128, 2, 128, CB], BF16, "KH")
            YH = k.sbuf([128, 2, CB, 128], BF16, "YH")
            mm_ = [[k.sbuf([128, 512], F32, "m") for _ in range(4)] for _ in range(2)]
            t2a = [k.sbuf([128, 1024], BF16, "t2a") for _ in range(2)]; t2b = [k.sbuf([128, 1024], BF16, "t2b") for _ in range(2)]
            y3 = [k.sbuf([128, 2, 4, 128], BF16, "y3") for _ in range(2)]
            yo = [k.sbuf([MO, 4, 128], F32, "yo") for _ in range(2)]
            s1 = [k.psum([128, 512], F32, "s1") for _ in range(2)]
            s2 = [k.psum([128, 512], F32, "s2") for _ in range(4)]
            s3 = k.psum([128, 1024], F32, "s3")
            cnt = {"p": 0, "b": 0, "inv": 0}
            def step1_gen(bt, kind, AT):
                if kind == "F":
                    k.dma("sp", xk[:, :, :], S["KNA"][:, bt * CB:(bt + 1) * CB, :], writes=[xk]); x = xk
                else:
                    k.dma("sp", xu[0:NR, :, :], S["UTA"][:, bt * CB:(bt + 1) * CB, :], writes=[xu]); x = xu
                for pr in range(CB // 2):
                    bank = s1[cnt["p"] % 2]; cnt["p"] += 1
                    for cc in range(2):
                        k.mm(bank[:, cc * 256:(cc + 1) * 256], x[:, 2 * pr + cc, :], cst["f1cs"][:, :], True, True, [x, cst["f1cs"]], [bank])
                    bv = bank[:, :].rearrange("p (g r f) -> p g r f", g=2, r=2)
                    o1 = AT[:, 2 * pr:2 * pr + 2, 0:2, :]; o2 = AT[:, 2 * pr:2 * pr + 2, 2, :]; i2 = bv[:, :, 0, :]
                    k.op("act", lambda e, o1=o1, bv=bv: e.copy(out=o1, in_=bv), [bank], [AT], force_self="never")
                    k.op("act", lambda e, o2=o2, i2=i2: e.mul(out=o2, in_=i2, mul=-1.0), [bank], [AT], force_self="never")
                    yield
            def inverse_gen(bt):
                for sg in range(CB // 4):
                    iv = cnt["inv"] % 2; cnt["inv"] += 1
                    t2a_ = t2a[iv]; t2b_ = t2b[iv]; y3_ = y3[iv]
                    for cc in range(4):
                        ch = sg * 4 + cc
                        k.mm(s3[:, cc * 256:(cc + 1) * 256], YH[:, 0, ch, :], cst["fCS"][:, :], True, False, [YH, cst["fCS"]], [s3])
                        k.mm(s3[:, cc * 256:(cc + 1) * 256], YH[:, 1, ch, :], cst["fnSC"][:, :], False, True, [YH, cst["fnSC"]], [s3])
                    k.tt("dve", t2a_[:, :], s3[:, :], cst["twA2"][:, :], ALU.mult, [s3, cst["twA2"]], [t2a_])
                    k.tt("dve", t2b_[:, :], s3[:, :], cst["twB2"][:, :], ALU.mult, [s3, cst["twB2"]], [t2b_])
                    av = t2a_[:, :].rearrange("p (g r f) -> p g r f", g=4, r=2)
                    bv2 = t2b_[:, :].rearrange("p (g r f) -> p g r f", g=4, r=2)
                    k.tt("dve", y3_[:, 0, :, :], av[:, :, 0, :], bv2[:, :, 1, :], ALU.add, [t2a_, t2b_], [y3_])
                    k.tt("pool", y3_[:, 1, :, :], av[:, :, 1, :], bv2[:, :, 0, :], ALU.add, [t2a_, t2b_], [y3_])
                    yield
                    p4 = s1[cnt["p"] % 2]; cnt["p"] += 1
                    yo_ = yo[iv]
                    k.mm(p4[0:MO, :], cst["g3C"][:, 0:MO], y3_[:, 0, :, :].rearrange("p g f -> p (g f)"), True, False, [cst["g3C"], y3_], [p4])
                    k.mm(p4[0:MO, :], cst["g3nS"][:, 0:MO], y3_[:, 1, :, :].rearrange("p g f -> p (g f)"), False, True, [cst["g3nS"], y3_], [p4])
                    k.copy("act", yo_[:, :, :].rearrange("p g f -> p (g f)"), p4[0:MO, :], [p4], [yo_])
                    c0 = bt * CB + sg * 4
                    k.dma("sp", S["YT"][c0:c0 + 4, tok0:tok0 + MO * 128].rearrange("c (a p) -> a c p", p=128), yo_[:, :, :], reads=[yo_])
                    yield
            def filt_block(blk, bre, bim):
                o_a = KH[:, 0, blk * 16:(blk + 1) * 16, :].rearrange("p a b -> p (a b)"); o_b = KH[:, 1, blk * 16:(blk + 1) * 16, :].rearrange("p a b -> p (a b)")
                k.op("act", lambda e: e.copy(out=o_a, in_=bre[:, :]), [bre], [KH], force_self="never")
                k.op("act", lambda e: e.copy(out=o_b, in_=bim[:, :]), [bim], [KH], force_self="never")
            def sig_block(blk, bre, bim):
                kre = KH[:, 0, blk * 16:(blk + 1) * 16, :].rearrange("p a b -> p (a b)")
                kim = KH[:, 1, blk * 16:(blk + 1) * 16, :].rearrange("p a b -> p (a b)")
                m1, m2, m3, m4 = mm_[blk % 2]
                k.tt("dve", m1[:, :], bre[:, :], kre, ALU.mult, [bre, KH], [m1])
                k.tt("dve", m3[:, :], bre[:, :], kim, ALU.mult, [bre, KH], [m3])
                k.tt("dve", m2[:, :], bim[:, :], kim, ALU.mult, [bim, KH], [m2])
                k.tt("dve", m4[:, :], bim[:, :], kre, ALU.mult, [bim, KH], [m4])
                ore = YH[:, 0, :, blk * 16:(blk + 1) * 16].rearrange("p c f -> p f c")
                oim = YH[:, 1, :, blk * 16:(blk + 1) * 16].rearrange("p c f -> p f c")
                v = lambda t: t[:, :].rearrange("p (f c) -> p f c", c=CB)
                k.op("dve", lambda e: e.tensor_tensor(out=ore, in0=v(m1), in1=v(m2), op=ALU.subtract), [m1, m2], [YH])
                k.op("pool", lambda e: e.tensor_tensor(out=oim, in0=v(m3), in1=v(m4), op=ALU.add), [m3, m4], [YH], force_self="never")
            def step2(AT, on_block, g_next, g_inv):
                for f1 in range(128):
                    j = f1 % 16
                    if j == 0:
                        bre = s2[(cnt["b"] % 2) * 2]; bim = s2[(cnt["b"] % 2) * 2 + 1]; cnt["b"] += 1
                    cols = slice(j * CB, (j + 1) * CB)
                    k.mm(bre[:, cols], MC[:, f1, :], AT[:, :, 0, f1], True, False, [MC, AT], [bre])
                    k.mm(bre[:, cols], MS[:, f1, :], AT[:, :, 1, f1], False, True, [MS, AT], [bre])
                    k.mm(bim[:, cols], MC[:, f1, :], AT[:, :, 1, f1], True, False, [MC, AT], [bim])
                    k.mm(bim[:, cols], MS[:, f1, :], AT[:, :, 2, f1], False, True, [MS, AT], [bim])
                    if j == 15:
                        on_block(f1 // 16, bre, bim)
                    if f1 % 8 == 3 and g_next is not None: next(g_next, None)
                    if f1 % 8 == 7 and g_inv is not None: next(g_inv, None)
            jobs = [(bt, kind) for bt in range(512 // CB) for kind in ("F", "S")]
            g0 = step1_gen(jobs[0][0], jobs[0][1], ATs[0])
            for _ in g0: pass
            for ji, (bt, kind) in enumerate(jobs):
                g_next = step1_gen(jobs[ji + 1][0], jobs[ji + 1][1], ATs[(ji + 1) % 2]) if ji + 1 < len(jobs) else None
                g_inv = inverse_gen(bt - 1) if (kind == "F" and bt > 0) else None
                step2(ATs[ji % 2], filt_block if kind == "F" else sig_block, g_next, g_inv)
                if g_next is not None:
                    for _ in g_next: pass
                if g_inv is not None:
                    for _ in g_inv: pass
            for _ in inverse_gen(512 // CB - 1): pass
        return ph

    def make_hycombine(chunks):
        def ph():
            bd = k.sbuf([128, 4], F32); k.dma("sp", bd[:], I["hbd"][:, :], writes=[bd])
            yt = [k.sbuf([128, 512], F32, "yt") for _ in range(2)]
            ut = [k.sbuf([128, 512], F32, "ut") for _ in range(2)]
            x0 = [k.sbuf([128, 512], F32, "x0") for _ in range(2)]
            ob = [k.sbuf([128, 512], BF16, "ob") for _ in range(2)]
            n = 0
            for (t0, W, le, re, j) in chunks:
                rn = RNORM["C" if j == 1 else "L"]
                for jc in range(4):
                    y_ = yt[n % 2]; u_ = ut[n % 2]; x_ = x0[n % 2]; o_ = ob[n % 2]; n += 1
                    rows = slice(jc * 128, (jc + 1) * 128)
                    k.dma("sp", y_[:, 0:W], S["YT"][rows, t0:t0 + W], writes=[y_])
                    k.dma("sp", u_[:, 0:W], S["UF"][rows, t0:t0 + W], writes=[u_])
                    k.dma("sp", x_[:, 0:W], S["X0T"][rows, t0:t0 + W], writes=[x_])
                    k.ts("dve", y_[:, 0:W], y_[:, 0:W], rn[:, jc:jc + 1], None, ALU.mult, None, [y_, rn], [y_])
                    k.stt("dve", y_[:, 0:W], u_[:, 0:W], bd[:, jc:jc + 1], y_[:, 0:W], ALU.mult, ALU.add, [u_, bd, y_], [y_])
                    k.tt("dve", o_[:, 0:W], y_[:, 0:W], x_[:, 0:W], ALU.mult, [y_, x_], [o_])
                    k.dma("pool", S["OCT"][rows, t0:t0 + W], o_[:, 0:W], reads=[o_])
        return ph

    CH_E = [(c * 512, 512, c == 0, False, 0) for c in range(8)] + [(4096, 256, False, True, 0)]
    CH_OWN = [(c * 512, 512, c == 0, False, 0) for c in range(8)]
    phases.append(("inproj0", make_inproj(0, I["xt0"], CHUNKS_ALL)))
    phases.append(("filterL", make_filter("L", SEQ)))
    phases.append(("filterC", make_filter("C", CTX)))
    phases.append(("fftL", make_fftconvL2(E // 128)))
    phases.append(("fftC", make_fftconv("C", 4, CTX, SEQ, 2)))
    phases.append(("hycomb", make_hycombine(CH_E + [CTXCH])))
    phases.append(("attn0", make_attn(0)))
    phases.append(("outproj0", make_outproj(0, I["xt0"], S["XM"], CH_E + [CTXCH])))
    phases.append(("ffn0", make_ffn(0, S["XM"], S["X1"], CH_E + [CTXCH])))
    phases.append(("inproj1", make_inproj(1, S["X1"], CH_E + [CTXCH])))
    phases.append(("attn1", make_attn(1)))
    phases.append(("outproj1", make_outproj(1, S["X1"], S["XM1"], CH_E)))
    phases.append(("ffn1", make_ffn(1, S["XM1"], OUT, CH_OWN)))
    for nm, ph in phases:
        k.phase(ph)
        if stop_after == nm:
            break
    k.close()
    return nc


_CACHE = {}


def kernel(**inputs):
    inp = {kk: np.asarray(v) for kk, v in inputs.items()}
    if "C" not in _CACHE:
        C = _consts()
        C["fftL"] = _fft_consts(128); C["fftC"] = _fft_consts(4)
        C["filtL"] = _filter_consts(SEQ); C["filtC"] = _filter_consts(CTX)
        _CACHE["C"] = C
    C = _CACHE["C"]
    nc = _build()
    in_maps = []
    for core in range(8):
        b, hh = core // 2, core % 2
        in_maps.append(_host_prep(inp, b, hh, C))
    res = run_bass_kernel_spmd(nc, in_maps, core_ids=list(range(8)))
    out = np.empty((4, SEQ, D), np.float32)
    for core in range(8):
        b, hh = core // 2, core % 2
        o = np.asarray(res.results[core]["out"]).T
        if hh == 0:
            out[b, :OWN] = o
        else:
            out[b, OWN:] = o[::-1]
    return out
```

```python
import numpy as np
import ml_dtypes
from contextlib import ExitStack
import concourse.bass as bass
import concourse.mybir as mybir
from concourse.bass_utils import run_bass_kernel_spmd

F32 = mybir.dt.float32
BF16 = mybir.dt.bfloat16
I32 = mybir.dt.int32
AF = mybir.ActivationFunctionType
ALU = mybir.AluOpType
NPBF = ml_dtypes.bfloat16

D = 1024; SEQ = 8192; CTX = 256; TT = SEQ + CTX; E = 4352; OWN = 4096
DFF = 2816; NM = 22
SAME_ENGINE_SYNC = {"act", "pool"}


class Res:
    __slots__ = ("name", "last_w", "reads", "excl")
    def __init__(self, name, excl=False):
        self.name = name; self.last_w = None; self.reads = {}; self.excl = excl


class Tl:
    def __init__(self, t, r):
        self.t = t; self.r = r
    def __getitem__(self, idx):
        return self.t[idx]


class K:
    ENGS = ("pe", "act", "dve", "pool", "sp")

    def __init__(self, nc):
        self.nc = nc
        self.es = ExitStack()
        self.sem = {}; self.cnt = {}
        for e in self.ENGS:
            self.sem[e] = self.es.enter_context(nc.semaphore("s_" + e))
            self.cnt[e] = 0
        self.dma_sems = {}
        self.dma_key = {}
        self.dma_rr = {}
        self.NDMASEM = {"sp": 32, "pool": 24, "act": 8, "pe": 4, "dve": 4}
        self.seen = {e: {} for e in self.ENGS}
        self.ops = {e: [] for e in self.ENGS}
        self.phase_es = None
        self.nres = 0
        self.ndma = 0

    def sbuf(self, shape, dt, name=None, persist=False):
        self.nres += 1
        name = (name or "t") + "_%d" % self.nres
        es = self.es if persist else self.phase_es
        t = es.enter_context(self.nc.sbuf_tensor(name, list(shape), dt))
        return Tl(t, Res(name))

    def psum(self, shape, dt, name=None):
        self.nres += 1
        name = (name or "p") + "_%d" % self.nres
        t = self.phase_es.enter_context(self.nc.psum_tensor(name, list(shape), dt))
        return Tl(t, Res(name, excl=True))

    def _need(self, reads, writes, eng=None):
        evs = []
        for r in reads:
            if r.last_w is not None: evs.append(r.last_w)
            if r.excl:
                evs.extend((kk[0], kk[1], v) for kk, v in r.reads.items() if not (kk[0] == "eng" and kk[1] == eng))
        for w in writes:
            if w.last_w is not None: evs.append(w.last_w)
            evs.extend((kk[0], kk[1], v) for kk, v in w.reads.items())
        return evs

    def _emit_waits(self, eng, evs, force_self=False):
        need = {}
        for kind, key, val in evs:
            if kind == "eng":
                if force_self == "never": force_self = False
                if key == eng and eng not in SAME_ENGINE_SYNC and not force_self: continue
                v = val
            else:
                v = val
            if self.seen[eng].get((kind, key), 0) >= v: continue
            if need.get((kind, key), 0) < v: need[(kind, key)] = v
        for (kind, key), v in need.items():
            self.seen[eng][(kind, key)] = v
            sem = self.sem[key] if kind == "eng" else self.dma_sems[key][0]
            self.ops[eng].append(lambda e, sem=sem, v=v: e.wait_ge(sem, v))

    def _commit(self, ev, reads, writes):
        for r in reads:
            kk = (ev[0], ev[1])
            if r.reads.get(kk, 0) < ev[2]: r.reads[kk] = ev[2]
        for w in writes:
            w.last_w = ev; w.reads = {}

    def op(self, eng, fn, reads=(), writes=(), force_self=False):
        reads = [x.r if isinstance(x, Tl) else x for x in reads]
        writes = [x.r if isinstance(x, Tl) else x for x in writes]
        self._emit_waits(eng, self._need(reads, writes, eng), force_self)
        self.cnt[eng] += 1
        sem = self.sem[eng]
        self.ops[eng].append(lambda e, fn=fn, sem=sem: fn(e).then_inc(sem, 1))
        self._commit(("eng", eng, self.cnt[eng]), reads, writes)

    def dma(self, q, out, in_, reads=(), writes=(), **kw):
        reads = [x.r if isinstance(x, Tl) else x for x in reads]
        writes = [x.r if isinstance(x, Tl) else x for x in writes]
        npool = self.NDMASEM[q]
        idx = (q, self.dma_rr.get(q, 0) % npool)
        self.dma_rr[q] = self.dma_rr.get(q, 0) + 1
        if idx not in self.dma_sems:
            s_ = self.es.enter_context(self.nc.semaphore("d_%s_%d" % idx))
            self.dma_sems[idx] = [s_, 0]
        ent = self.dma_sems[idx]
        evs = self._need(reads, writes, q)
        if ent[1] > 0:
            evs.append(("dma", idx, ent[1] * 16))
        self._emit_waits(q, evs)
        ent[1] += 1
        sem = ent[0]
        self.ndma += 1
        self.ops[q].append(lambda e, out=out, in_=in_, sem=sem, kw=kw: e.dma_start(out=out, in_=in_, **kw).then_inc(sem, 16))
        self._commit(("dma", idx, ent[1] * 16), reads, writes)

    def barrier(self):
        evs = [("eng", e, self.cnt[e]) for e in self.ENGS if self.cnt[e] > 0]
        evs += [("dma", kk, v[1] * 16) for kk, v in self.dma_sems.items() if v[1] > 0]
        for e in self.ENGS:
            self._emit_waits(e, [ev for ev in evs if not (ev[0] == "eng" and ev[1] == e)])

    def phase(self, body):
        with ExitStack() as pes:
            self.phase_es = pes
            body()
            self.barrier()
            ops = self.ops
            self.ops = {e: [] for e in self.ENGS}
            with self.nc.Block() as block:
                @block.tensor
                def _(e):
                    for f in ops["pe"]: f(e)
                @block.scalar
                def _(e):
                    for f in ops["act"]: f(e)
                @block.vector
                def _(e):
                    for f in ops["dve"]: f(e)
                @block.gpsimd
                def _(e):
                    for f in ops["pool"]: f(e)
                @block.sync
                def _(e):
                    for f in ops["sp"]: f(e)
        self.phase_es = None

    def close(self):
        self.es.close()

    def ts(self, eng, out, in0, s1, s2, op0, op1, r, w, force_self=False):
        if s2 is None:
            s2 = 0.0; op1 = ALU.add
        self.op(eng, lambda e: e.tensor_scalar(out=out, in0=in0, scalar1=s1, scalar2=s2, op0=op0, op1=op1), r, w, force_self)
    def stt(self, eng, out, in0, sc, in1, op0, op1, r, w):
        self.op(eng, lambda e: e.scalar_tensor_tensor(out=out, in0=in0, scalar=sc, in1=in1, op0=op0, op1=op1), r, w)
    def tt(self, eng, out, in0, in1, op, r, w):
        self.op(eng, lambda e: e.tensor_tensor(out=out, in0=in0, in1=in1, op=op), r, w)
    def act(self, out, in_, func, r, w, bias=None, scale=None, accum=None):
        kw = {}
        if bias is not None: kw["bias"] = bias
        if scale is not None: kw["scale"] = scale
        if accum is not None: kw["accum_out"] = accum
        self.op("act", lambda e: e.activation(out=out, in_=in_, func=func, **kw), r, w)
    def mm(self, out, lhsT, rhs, start, stop, r, w):
        self.op("pe", lambda e: e.matmul(out, lhsT=lhsT, rhs=rhs, start=start, stop=stop), r, w)
    def copy(self, eng, out, in_, r, w):
        if eng == "act":
            self.op("act", lambda e: e.copy(out=out, in_=in_), r, w)
        else:
            self.op(eng, lambda e: e.tensor_copy(out=out, in_=in_), r, w)
    def memset(self, eng, ap, val, w):
        self.op(eng, lambda e: e.memset(ap, val), [], w)
    def recip(self, out, in_, r, w, force_self=False):
        self.op("dve", lambda e: e.reciprocal(out=out, in_=in_), r, w, force_self)

def _consts():
    c = {}
    nf = 16
    inv = 10000.0 ** (-np.arange(nf, dtype=np.float64) / nf)
    t = np.arange(SEQ)
    row = (t // 64).astype(np.float64); col = (t % 64).astype(np.float64)
    ar = row[None, :] * inv[:, None]; ac = col[None, :] * inv[:, None]
    cos64 = np.concatenate([np.cos(ar), np.cos(ar), np.cos(ac), np.cos(ac)], 0)
    sin64 = np.concatenate([-np.sin(ar), np.sin(ar), -np.sin(ac), np.sin(ac)], 0)
    c["cos64"] = cos64.astype(np.float32); c["sin64"] = sin64.astype(np.float32)
    perm = np.concatenate([np.arange(16) + 16, np.arange(16), np.arange(16) + 48, np.arange(16) + 32])
    c["perm64"] = perm
    p = np.arange(128)[:, None]; f = np.arange(512)[None, :]
    c["bmask"] = np.stack([(np.abs(128 * r + p - f) <= 128) for r in range(-1, 5)], 0).astype(NPBF)
    return c


def _fft_consts(NA):
    N = 128 * NA
    c = {}
    a = np.arange(NA)[:, None]; f1 = np.arange(NA)[None, :]
    th = 2 * np.pi * a * f1 / NA
    c["f1cs"] = np.concatenate([np.cos(th), -np.sin(th)], 1).astype(NPBF)
    p = np.arange(128)[:, None]; f2 = np.arange(128)[None, :]
    th2 = 2 * np.pi * p * f2 / 128
    C = np.cos(th2); S = np.sin(th2)
    c["fC"] = C.astype(NPBF); c["fS"] = S.astype(NPBF); c["fnS"] = (-S).astype(NPBF)
    c["fCS"] = np.concatenate([C, S], 1).astype(NPBF)
    c["fnSC"] = np.concatenate([-S, C], 1).astype(NPBF)
    tw = 2 * np.pi * np.arange(128)[:, None] * np.arange(NA)[None, :] / N
    G = 512 // (2 * NA)
    tc_ = np.cos(tw); ts_ = np.sin(tw)
    A = np.concatenate([tc_, tc_], 1)
    B = np.concatenate([ts_, -ts_], 1)
    c["twA"] = np.tile(A[:, None, :], (1, G, 1)).reshape(128, 512).astype(np.float32)
    c["twB"] = np.tile(B[:, None, :], (1, G, 1)).reshape(128, 512).astype(np.float32)
    tcT = np.cos(tw).T; tsT = np.sin(tw).T
    A2 = np.stack([tcT, tcT], 1)
    B2 = np.stack([tsT, -tsT], 1)
    c["twA2"] = np.tile(A2[:, None], (1, 4, 1, 1)).reshape(NA, 1024).astype(np.float32)
    c["twB2"] = np.tile(B2[:, None], (1, 4, 1, 1)).reshape(NA, 1024).astype(np.float32)
    th = 2 * np.pi * np.arange(NA)[:, None] * np.arange(NA)[None, :] / NA
    c["g3C"] = (np.cos(th) / N).astype(NPBF); c["g3nS"] = (-np.sin(th) / N).astype(NPBF)
    if NA == 128:
        pp_ = np.arange(128, dtype=np.int64)[:, None, None]; f1_ = np.arange(128, dtype=np.int64)[None, :, None]; f2_ = np.arange(128, dtype=np.int64)[None, None, :]
        ang = 2 * np.pi * ((pp_ * (f1_ + 128 * f2_)) % N).astype(np.float64) / N
        c["MC"] = np.cos(ang).astype(NPBF); c["MS"] = np.sin(ang).astype(NPBF)
    if NA == 4:
        f1i = np.arange(128) % 4
        twp = 2 * np.pi * f1i[:, None] * np.arange(128)[None, :] / N
        c["twA2c"] = np.concatenate([np.cos(twp), np.cos(twp)], 1).astype(np.float32)
        c["twB2c"] = np.concatenate([np.sin(twp), -np.sin(twp)], 1).astype(np.float32)
        gC = np.zeros((128, 64), np.float64); gS = np.zeros((128, 64), np.float64)
        for ch in range(32):
            for f1_ in range(4):
                for a_ in range(2):
                    gC[ch * 4 + f1_, ch * 2 + a_] = np.cos(2 * np.pi * f1_ * a_ / 4) / N
                    gS[ch * 4 + f1_, ch * 2 + a_] = -np.sin(2 * np.pi * f1_ * a_ / 4) / N
        c["gbC"] = gC.astype(NPBF); c["gbnS"] = gS.astype(NPBF)
    return c


def _filter_consts(n):
    q = np.arange(2 * n)
    tap = np.where(q < n, q, 2 * n - q).astype(np.int64)
    tap = np.minimum(tap, n - 1)
    t01 = np.linspace(0.0, 1.0, n, dtype=np.float32)
    bands = 16
    w = (2.0 * np.pi * np.arange(n, dtype=np.float32) / n).astype(np.float32)
    f = np.linspace(1e-4, bands - 1, bands, dtype=np.float32)[None, :]
    feats = np.concatenate([t01[:, None], np.cos(f * w[:, None]), -np.sin(f * w[:, None])], -1).astype(np.float32)
    featsT = np.ascontiguousarray(feats[tap].T)
    t01b = np.ascontiguousarray(np.tile(t01[tap][None, :], (128, 1)))
    deltas = np.abs(np.linspace(np.log(1e-2) / 1.5, np.log(1e-2) / 0.3, 512, dtype=np.float32))
    ndel = np.ascontiguousarray((-deltas).reshape(4, 128).T)
    return featsT.astype(np.float32), t01b.astype(np.float32), ndel.astype(np.float32)


def _pm(v, n):
    return np.ascontiguousarray(np.asarray(v, np.float32).reshape(n, 128).T)


def _host_prep(inp, b, hh, C):
    fl = (hh == 1)
    m = {}
    x = inp["x"][b]; cx = inp["ctx"][b]
    if fl: x = x[::-1]; cx = cx[::-1]
    m["xt0"] = np.ascontiguousarray(np.concatenate([x, cx], 0).T)
    cv = np.stack([inp["c"][b], inp["c_ctx"]], 1)
    m["cvec"] = np.ascontiguousarray(cv.reshape(8, 128, 2).transpose(1, 0, 2))
    perm = C["perm64"]
    for i in range(2):
        m[f"ada_w{i}"] = inp["ada_w"][i]
        m[f"ada_b{i}"] = _pm(inp["ada_b"][i], 48)
        m[f"nmix{i}"] = _pm(inp["norm_mix"][i], 8)
        m[f"nffn{i}"] = _pm(inp["norm_ffn"][i], 8)
        w = inp["mix_w_in"][i]
        qk = w[:, :768].reshape(D, 12, 64)[:, :, perm].reshape(D, 768)
        m[f"w_in{i}"] = np.ascontiguousarray(np.concatenate([w, qk], 1))
        m[f"w_out{i}"] = inp["mix_w_out"][i]
        gq = inp["attn_q_norm"][i]; gk = inp["attn_k_norm"][i]
        m[f"qkg{i}"] = np.ascontiguousarray(np.stack([np.tile(gq, 2), np.tile(gq[perm], 2), np.tile(gk, 2), np.tile(gk[perm], 2)], 1).astype(np.float32))
        m[f"w_up{i}"] = inp["ffn_w_up"][i]
        m[f"w_dn{i}"] = inp["ffn_w_down"][i]
        fw = inp["ffn_conv_w"][i]
        if fl: fw = fw[::-1]
        m[f"fcw{i}"] = np.ascontiguousarray(fw.reshape(3, NM, 128).transpose(2, 1, 0))
        m[f"fcb{i}"] = _pm(inp["ffn_conv_b"][i], NM)
    hw = inp["hy_conv_w"][0]
    if fl: hw = hw[::-1]
    m["hcw"] = np.ascontiguousarray(hw.reshape(3, 12, 128).transpose(2, 1, 0))
    m["hcb"] = _pm(inp["hy_conv_b"][0], 12)
    sw = inp["sc_conv_w"][0]
    if fl: sw = sw[::-1]
    m["scw"] = np.ascontiguousarray(sw.reshape(3, 4, 128).transpose(2, 1, 0))
    m["hw1"] = inp["hy_w1"][0]; m["hw2"] = inp["hy_w2"][0]; m["hw3"] = inp["hy_w3"][0]
    w4 = inp["hy_w4"][0]
    if fl: w4 = np.concatenate([w4[:, 512:], w4[:, :512]], 1)
    m["hw4"] = np.ascontiguousarray(w4)
    m["hb"] = np.ascontiguousarray(np.stack([inp["hy_b1"][0], inp["hy_b2"][0], inp["hy_b3"][0], inp["hy_freq"][0]], 1).astype(np.float32))
    m["hbd"] = _pm(inp["hy_bias_d"][0], 4)
    m["sink"] = np.ascontiguousarray(np.tile(inp["swa_sink"][0][None, :], (128, 1)).astype(np.float32))
    cos = C["cos64"]; sin = C["sin64"]
    if fl: cos = cos[:, ::-1]; sin = sin[:, ::-1]
    cosx = np.concatenate([cos, np.ones((64, CTX), np.float32)], 1)
    sinx = np.concatenate([sin, np.zeros((64, CTX), np.float32)], 1)
    m["cosT"] = np.ascontiguousarray(np.concatenate([cosx, cosx], 0))
    m["sinT"] = np.ascontiguousarray(np.concatenate([sinx, sinx], 0))
    m["bmask"] = C["bmask"]
    for tag, NA in (("L", 128), ("C", 4)):
        for kk, v in C["fft" + tag].items():
            m[kk + tag] = v
    for tag in ("L", "C"):
        ft, t01b, ndel = C["filt" + tag]
        m["featsT" + tag] = ft; m["t01b" + tag] = t01b
    m["ndel"] = C["filtL"][2]
    return m

def _build(stop_after=None, dbg=()):
    nc = bass.Bass("TRN2", target_bir_lowering=False)
    k = K(nc)
    def din(name, shape, dt=F32):
        return nc.dram_tensor(name, list(shape), dt, kind="ExternalInput").ap()
    def dscr(name, shape, dt):
        kind = "ExternalOutput" if name in dbg else "Internal"
        return nc.dram_tensor(name, list(shape), dt, kind=kind).ap()
    I = {}
    I["xt0"] = din("xt0", [D, TT]); I["cvec"] = din("cvec", [128, 8, 2])
    for i in range(2):
        I[f"ada_w{i}"] = din(f"ada_w{i}", [D, 6 * D]); I[f"ada_b{i}"] = din(f"ada_b{i}", [128, 48])
        I[f"nmix{i}"] = din(f"nmix{i}", [128, 8]); I[f"nffn{i}"] = din(f"nffn{i}", [128, 8])
        I[f"w_in{i}"] = din(f"w_in{i}", [D, 3328]); I[f"w_out{i}"] = din(f"w_out{i}", [D, D])
        I[f"qkg{i}"] = din(f"qkg{i}", [128, 4])
        I[f"w_up{i}"] = din(f"w_up{i}", [D, 2 * DFF]); I[f"w_dn{i}"] = din(f"w_dn{i}", [DFF, D])
        I[f"fcw{i}"] = din(f"fcw{i}", [128, NM, 3]); I[f"fcb{i}"] = din(f"fcb{i}", [128, NM])
    I["hcw"] = din("hcw", [128, 12, 3]); I["hcb"] = din("hcb", [128, 12]); I["scw"] = din("scw", [128, 4, 3])
    I["hw1"] = din("hw1", [33, 64]); I["hw2"] = din("hw2", [64, 64]); I["hw3"] = din("hw3", [64, 64])
    I["hw4"] = din("hw4", [64, 1024]); I["hb"] = din("hb", [64, 4]); I["hbd"] = din("hbd", [128, 4])
    I["sink"] = din("sink", [128, 8])
    I["cosT"] = din("cosT", [128, TT]); I["sinT"] = din("sinT", [128, TT])
    I["bmask"] = din("bmask", [6, 128, 512], BF16)
    for tag, NA in (("L", 128), ("C", 4)):
        I["f1cs" + tag] = din("f1cs" + tag, [NA, 2 * NA], BF16)
        for nm in ("fC", "fS", "fnS"): I[nm + tag] = din(nm + tag, [128, 128], BF16)
        for nm in ("fCS", "fnSC"): I[nm + tag] = din(nm + tag, [128, 256], BF16)
        for nm in ("twA", "twB"): I[nm + tag] = din(nm + tag, [128, 512])
        for nm in ("twA2", "twB2"): I[nm + tag] = din(nm + tag, [NA, 1024])
        for nm in ("g3C", "g3nS"): I[nm + tag] = din(nm + tag, [NA, NA], BF16)
        if NA == 128:
            for nm in ("MC", "MS"): I[nm + tag] = din(nm + tag, [128, 128, 128], BF16)
        if NA == 4:
            for nm in ("twA2c", "twB2c"): I[nm + tag] = din(nm + tag, [128, 256])
            for nm in ("gbC", "gbnS"): I[nm + tag] = din(nm + tag, [128, 64], BF16)
        n = 64 * NA
        I["featsT" + tag] = din("featsT" + tag, [33, 2 * n]); I["t01b" + tag] = din("t01b" + tag, [128, 2 * n])
    I["ndel"] = din("ndel", [128, 4])
    OUT = nc.dram_tensor("out", [D, OWN], F32, kind="ExternalOutput").ap()

    S = {}
    for i in range(2):
        S[f"bw_in{i}"] = dscr(f"bw_in{i}", [D, 3328], BF16); S[f"bw_out{i}"] = dscr(f"bw_out{i}", [D, D], BF16)
        S[f"bw_up{i}"] = dscr(f"bw_up{i}", [D, 2 * DFF], BF16); S[f"bw_dn{i}"] = dscr(f"bw_dn{i}", [DFF, D], BF16)
    S["QT"] = dscr("QT", [8, 64, TT], BF16); S["KT"] = dscr("KT", [4, 64, TT], BF16); S["V"] = dscr("V", [TT, 256], BF16)
    S["UT"] = dscr("UT", [512, TT], BF16); S["X0T"] = dscr("X0T", [512, TT], F32)
    S["UF"] = dscr("UF", [512, TT], F32)
    S["UTA"] = dscr("UTA", [64, 512, 128], BF16); S["KNA"] = dscr("KNA", [128, 512, 128], BF16)
    S["YT"] = dscr("YT", [512, TT], F32)
    S["OAT"] = dscr("OAT", [512, TT], BF16); S["OCT"] = dscr("OCT", [512, TT], BF16)
    S["XM"] = dscr("XM", [D, TT], F32); S["X1"] = dscr("X1", [D, TT], F32); S["XM1"] = dscr("XM1", [D, TT], F32)
    S["KRAWL"] = dscr("KRAWL", [512, 2 * SEQ], F32); S["KNL"] = dscr("KNL", [512, 2 * SEQ], BF16)
    S["KRAWC"] = dscr("KRAWC", [512, 2 * CTX], F32); S["KNC"] = dscr("KNC", [512, 2 * CTX], BF16)

    MOD = k.sbuf([128, 2, 48, 2], F32, "mod", persist=True)
    AB = k.sbuf([128, 2, 2, 2, 8, 2], F32, "ab", persist=True)
    ONES = k.sbuf([128, 128], BF16, "ones", persist=True)
    BONES = k.sbuf([128, 128], BF16, "bones", persist=True)
    SEL = k.sbuf([128, 64], F32, "sel", persist=True)
    ESINK = k.sbuf([128, 8], F32, "esink", persist=True)
    RNORM = {"L": k.sbuf([128, 4], F32, "rnL", persist=True), "C": k.sbuf([128, 4], F32, "rnC", persist=True)}

    phases = []

    conv_items = []
    for i in range(2):
        for src, dst, rows, cols in ((f"w_in{i}", f"bw_in{i}", D, 3328), (f"w_out{i}", f"bw_out{i}", D, D),
                                     (f"w_up{i}", f"bw_up{i}", D, 2 * DFF), (f"w_dn{i}", f"bw_dn{i}", DFF, D)):
            for r0 in range(0, rows, 128):
                for c0 in range(0, cols, 2048):
                    conv_items.append((src, dst, r0, c0, min(2048, cols - c0)))
    CONV_EARLY = sum(1 for it in conv_items if it[0] in ("w_in0", "w_out0"))
    conv_pos = [0]

    class make_converter:
        def __init__(self):
            self.stg = [k.sbuf([128, 2048], F32, "stg") for _ in range(3)]
            self.stb = [k.sbuf([128, 2048], BF16, "stb") for _ in range(3)]
        def step(self, eng=None, q="sp"):
            n = conv_pos[0]
            if n >= len(conv_items): return False
            conv_pos[0] += 1
            src, dst, r0, c0, cw = conv_items[n]
            a = self.stg[n % 3]; bt = self.stb[n % 3]
            k.dma(q, a[:, 0:cw], I[src][r0:r0 + 128, c0:c0 + cw], writes=[a])
            k.copy(eng or ("dve" if n % 2 == 0 else "act"), bt[:, 0:cw], a[:, 0:cw], [a], [bt])
            k.dma(q, S[dst][r0:r0 + 128, c0:c0 + cw], bt[:, 0:cw], reads=[bt])
            return True

    def ph_setup():
        k.memset("dve", ONES[:], 1.0, [ONES])
        k.memset("dve", BONES[:], 0.0, [BONES])
        k.memset("dve", BONES[0:64, 0:64], 1.0, [BONES])
        k.memset("dve", BONES[64:128, 64:128], 1.0, [BONES])
        k.memset("dve", SEL[:], 0.0, [SEL])
        k.memset("dve", SEL[64:65, :], 1.0, [SEL])
        snk = k.sbuf([128, 8], F32)
        k.dma("sp", snk[:], I["sink"][:, :], writes=[snk])
        k.act(ESINK[:], snk[:], AF.Exp, [snk], [ESINK])
        cv_ = make_converter()
        for _ in range(CONV_EARLY):
            cv_.step()
        cv = k.sbuf([128, 8, 2], F32)
        k.dma("sp", cv[:], I["cvec"][:, :, :], writes=[cv])
        sc = k.sbuf([128, 8, 2], F32)
        k.act(sc[:], cv[:], AF.Silu, [cv], [sc])
        wst = [k.sbuf([128, 8, 512], F32, "wst") for _ in range(2)]
        ps = [k.psum([128, 512], F32) for _ in range(2)]
        n = 0
        for i in range(2):
            adb = k.sbuf([128, 48], F32)
            k.dma("sp", adb[:], I[f"ada_b{i}"][:, :], writes=[adb])
            for cb in range(12):
                wt = wst[n % 2]; n += 1
                k.dma("sp", wt[:], I[f"ada_w{i}"][:, cb * 512:(cb + 1) * 512].rearrange("(k p) f -> p k f", p=128), writes=[wt])
                for mi in range(4):
                    m = cb * 4 + mi
                    pt = ps[m % 2]
                    for kk in range(8):
                        k.mm(pt[:, 0:2], wt[:, kk, mi * 128:(mi + 1) * 128], sc[:, kk, :], kk == 0, kk == 7, [wt, sc], [pt])
                    k.ts("dve", MOD[:, i, m, :], pt[:, 0:2], adb[:, m:m + 1], None, ALU.add, None, [pt, adb], [MOD])
            for wh, (nm, sh0, sc0) in enumerate(((f"nmix{i}", 0, 8), (f"nffn{i}", 24, 32))):
                g = k.sbuf([128, 8], F32)
                k.dma("sp", g[:], I[nm][:, :], writes=[g])
                for j in range(2):
                    k.stt("dve", AB[:, i, wh, 0, :, j], MOD[:, i, sc0:sc0 + 8, j], 1.0, g[:], ALU.add, ALU.mult, [MOD, g], [AB])
                    k.copy("dve", AB[:, i, wh, 1, :, j], MOD[:, i, sh0:sh0 + 8, j], [MOD], [AB])
    phases.append(("setup", ph_setup))

    def norm_mod(xh, Wc, i, wh, j, sq, ssps, rstd, tmp, h):
        for kk in range(8):
            k.act(sq[:, kk, 0:Wc], xh[:, kk, 0:Wc], AF.Square, [xh], [sq])
        for (c0, c1) in ((0, min(512, Wc)), (512, Wc)):
            if c1 <= c0: continue
            for kk in range(8):
                k.mm(ssps[:, c0:c1], ONES[:, :], sq[:, kk, c0:c1], kk == 0, kk == 7, [ONES, sq], [ssps])
        k.act(rstd[:, 0:Wc], ssps[:, 0:Wc], AF.Sqrt, [ssps], [rstd], bias=1e-6, scale=1.0 / D)
        k.recip(rstd[:, 0:Wc], rstd[:, 0:Wc], [rstd], [rstd])
        for kk in range(8):
            t = tmp[kk % 2]
            k.stt("dve", t[:, 0:Wc], xh[:, kk, 0:Wc], AB[:, i, wh, 0, kk, j:j + 1], rstd[:, 0:Wc], ALU.mult, ALU.mult, [xh, AB, rstd], [t])
            k.act(h[:, kk, 0:Wc], t[:, 0:Wc], AF.Identity, [t, AB], [h], bias=AB[:, i, wh, 1, kk, j:j + 1], scale=1.0)

    def norm_mod_gen(xh, Wc, i, wh, j, sq, ssps, rstd, tmp, h):
        for kk in range(8):
            k.act(sq[:, kk, 0:Wc], xh[:, kk, 0:Wc], AF.Square, [xh], [sq])
        yield
        yield
        for (c0, c1) in ((0, min(512, Wc)), (512, Wc)):
            if c1 <= c0: continue
            for kk in range(8):
                k.mm(ssps[:, c0:c1], ONES[:, :], sq[:, kk, c0:c1], kk == 0, kk == 7, [ONES, sq], [ssps])
        yield
        k.act(rstd[:, 0:Wc], ssps[:, 0:Wc], AF.Sqrt, [ssps], [rstd], bias=1e-6, scale=1.0 / D)
        k.recip(rstd[:, 0:Wc], rstd[:, 0:Wc], [rstd], [rstd])
        yield
        for kk in range(8):
            t = tmp[kk % 2]
            k.stt("dve", t[:, 0:Wc], xh[:, kk, 0:Wc], AB[:, i, wh, 0, kk, j:j + 1], rstd[:, 0:Wc], ALU.mult, ALU.mult, [xh, AB, rstd], [t])
            k.act(h[:, kk, 0:Wc], t[:, 0:Wc], AF.Identity, [t, AB], [h], bias=AB[:, i, wh, 1, kk, j:j + 1], scale=1.0)
            if kk % 2 == 1: yield

    CHUNKS_ALL = [(c * 512, 512, c == 0, c == 15, 0) for c in range(16)] + [(SEQ, CTX, True, True, 1)]
    CHUNKS_E = [(c * 512, 512, c == 0, False, 0) for c in range(9)]
    CTXCH = (SEQ, CTX, True, True, 1)

    def load_xh(xh, src, t0, W, ledge, redge, q="sp"):
        lo = 2 if ledge else 1
        hi = W + 2 if redge else W + 3
        if ledge: k.memset("pool", xh[:, :, 1:2], 0.0, [xh])
        if redge: k.memset("pool", xh[:, :, W + 2:W + 3], 0.0, [xh])
        k.dma(q, xh[:, :, lo:hi], src[:, t0 - 2 + lo:t0 - 2 + hi].rearrange("(k p) t -> p k t", p=128), writes=[xh])

    def make_inproj(i, XIN, chunks):
        def ph():
            w = k.sbuf([128, 8, 3328], BF16, "w_in")
            for kk in range(8):
                k.dma("sp", w[:, kk, :], S[f"bw_in{i}"][kk * 128:(kk + 1) * 128, :], writes=[w])
            qkg = k.sbuf([128, 4], F32); k.dma("sp", qkg[:], I[f"qkg{i}"][:, :], writes=[qkg])
            if i == 0:
                cw = k.sbuf([128, 12, 3], F32); k.dma("sp", cw[:], I["hcw"][:, :, :], writes=[cw])
                cb = k.sbuf([128, 12], F32); k.dma("sp", cb[:], I["hcb"][:, :], writes=[cb])
            else:
                cw = k.sbuf([128, 4, 3], F32); k.dma("sp", cw[:], I["scw"][:, :, :], writes=[cw])
            xhs = [k.sbuf([128, 8, 516], F32, "xh") for _ in range(2)]
            for t_ in xhs: k.memset("pool", t_[:], 0.0, [t_])
            sq = k.sbuf([128, 8, 516], BF16, "sq")
            hs = [k.sbuf([128, 8, 516], BF16, "h") for _ in range(2)]
            rstd = k.sbuf([128, 516], F32, "rstd")
            tmp = [k.sbuf([128, 516], F32, "tmp") for _ in range(2)]
            ssps = k.psum([128, 1024], F32, "ssps")
            zps = [k.psum([128, 512], F32, "zps") for _ in range(4)]
            cps = k.psum([128, 1024], F32, "cps")
            cosbs = [k.sbuf([128, 512], F32, "cos") for _ in range(2)]; sinbs = [k.sbuf([128, 512], F32, "sin") for _ in range(2)]
            sq2 = [k.sbuf([128, 512], BF16, "sq2") for _ in range(2)]
            rs = [k.sbuf([128, 512], F32, "rs") for _ in range(2)]
            ta = [k.sbuf([128, 512], F32, "ta") for _ in range(2)]
            tb = [k.sbuf([128, 512], F32, "tb") for _ in range(2)]
            qo = [k.sbuf([128, 512], BF16, "qo") for _ in range(2)]
            vo = [k.sbuf([128, 256], BF16, "vo") for _ in range(2)]
            asb = [k.sbuf([128, 516], F32, "asb") for _ in range(3)]
            c1 = [k.sbuf([128, 512], F32, "c1") for _ in range(3)]
            uo = [k.sbuf([128, 512], F32, "uo") for _ in range(2)]
            ub = [k.sbuf([128, 512], BF16, "ub") for _ in range(2)]
            pp = [k.sbuf([128, 516], F32, "pp") for _ in range(2)]
            nq = [0]
            import os
            PARTS = os.environ.get("INPROJ_PARTS", "nqvc")
            NCHK = int(os.environ.get("INPROJ_NCH", "99"))
            CHL = chunks[:NCHK]
            def issue_loads(cj):
                t0_, W_, le_, re_, j_ = CHL[cj]
                load_xh(xhs[cj % 2], XIN, t0_, W_, le_, re_)
                k.dma("sp", cosbs[cj % 2][:, 0:W_], I["cosT"][:, t0_:t0_ + W_], writes=[cosbs[cj % 2]])
                k.dma("sp", sinbs[cj % 2][:, 0:W_], I["sinT"][:, t0_:t0_ + W_], writes=[sinbs[cj % 2]])
            for ci, (t0, W, le, re, j) in enumerate(CHL):
                Wc = W + 4
                do_q = (t0 < E) or j == 1
                if i == 1 and j == 1: do_q = False
                xh = xhs[ci % 2]; cosb = cosbs[ci % 2]; sinb = sinbs[ci % 2]; h = hs[ci % 2]
                if ci == 0:
                    issue_loads(0)
                    for _ in norm_mod_gen(xh, Wc, i, 0, j, sq, ssps, rstd, tmp, h): pass
                gnx = None
                if ci + 1 < len(CHL):
                    issue_loads(ci + 1)
                    gnx = norm_mod_gen(xhs[(ci + 1) % 2], CHL[ci + 1][1] + 4, i, 0, CHL[ci + 1][4], sq, ssps, rstd, tmp, hs[(ci + 1) % 2])
                def tick():
                    if gnx is not None: next(gnx, None)
                for pr in range(6):
                    if "q" not in PARTS: continue
                    if pr < 4 and not do_q: continue
                    tick()
                    n = nq[0]; nq[0] += 1
                    zp = zps[(2 * n) % 4]; zsp = zps[(2 * n + 1) % 4]
                    for kk in range(8):
                        k.mm(zp[:, 0:W], w[:, kk, pr * 128:(pr + 1) * 128], h[:, kk, 2:W + 2], kk == 0, kk == 7, [w, h], [zp])
                    for kk in range(8):
                        k.mm(zsp[:, 0:W], w[:, kk, 2560 + pr * 128:2560 + (pr + 1) * 128], h[:, kk, 2:W + 2], kk == 0, kk == 7, [w, h], [zsp])
                    s2 = sq2[n % 2]; r_ = rs[n % 2]; a_ = ta[n % 2]; b_ = tb[n % 2]; q_ = qo[n % 2]
                    gi = 0 if pr < 4 else 2
                    k.act(s2[:, 0:W], zp[:, 0:W], AF.Square, [zp], [s2])
                    k.stt("dve", a_[:, 0:W], zp[:, 0:W], qkg[:, gi:gi + 1], cosb[:, 0:W], ALU.mult, ALU.mult, [zp, qkg, cosb], [a_])
                    k.stt("dve", b_[:, 0:W], zsp[:, 0:W], qkg[:, gi + 1:gi + 2], sinb[:, 0:W], ALU.mult, ALU.mult, [zsp, qkg, sinb], [b_])
                    k.mm(zp[:, 0:W], BONES[:, :], s2[:, 0:W], True, True, [BONES, s2], [zp])
                    k.act(r_[:, 0:W], zp[:, 0:W], AF.Sqrt, [zp], [r_], bias=1e-6, scale=1.0 / 64)
                    k.recip(r_[:, 0:W], r_[:, 0:W], [r_], [r_])
                    QS = os.environ.get("QSKIP", "")
                    pe_ = "dve" if "pool" in QS else "pool"
                    k.tt(pe_, a_[:, 0:W], a_[:, 0:W], b_[:, 0:W], ALU.add, [a_, b_], [a_])
                    k.tt(pe_, q_[:, 0:W], a_[:, 0:W], r_[:, 0:W], ALU.mult, [a_, r_], [q_])
                    for hf in range(2):
                        if "dma" in QS: continue
                        if pr < 4:
                            dst = S["QT"][2 * pr + hf, :, t0:t0 + W]
                        else:
                            dst = S["KT"][2 * (pr - 4) + hf, :, t0:t0 + W]
                        k.dma("sp", dst, q_[hf * 64:(hf + 1) * 64, 0:W], reads=[q_])
                for tj in range(W // 128):
                    if "v" not in PARTS: continue
                    tick()
                    n = nq[0]; nq[0] += 1
                    vp = zps[n % 4]
                    for kk in range(8):
                        k.mm(vp[:, 0:256], h[:, kk, 2 + tj * 128:2 + (tj + 1) * 128], w[:, kk, 768:1024], kk == 0, kk == 7, [w, h], [vp])
                    v_ = vo[n % 2]
                    k.copy("act", v_[:, :], vp[:, 0:256], [vp], [v_])
                    k.dma("sp", S["V"][t0 + tj * 128:t0 + (tj + 1) * 128, :], v_[:, :], reads=[v_])
                def convproj(m, dst):
                    for (c0, c1_) in ((0, min(512, Wc)), (512, Wc)):
                        if c1_ <= c0: continue
                        for kk in range(8):
                            k.mm(cps[:, c0:c1_], w[:, kk, 1024 + m * 128:1024 + (m + 1) * 128], h[:, kk, c0:c1_], kk == 0, kk == 7, [w, h], [cps])
                    k.copy("act", dst[:, 0:Wc], cps[:, 0:Wc], [cps], [dst])
                    if le: k.memset("pool", dst[:, 1:2], 0.0, [dst])
                    if re: k.memset("pool", dst[:, W + 2:W + 3], 0.0, [dst])
                def conv3(out, a, wts, m, bias, eng="dve"):
                    if bias is not None:
                        k.ts(eng, out[:, 0:W], a[:, 2:W + 2], wts[:, m, 1:2], bias, ALU.mult, ALU.add, [a, wts, cb], [out])
                    else:
                        k.ts(eng, out[:, 0:W], a[:, 2:W + 2], wts[:, m, 1:2], None, ALU.mult, None, [a, wts], [out])
                    k.stt(eng, out[:, 0:W], a[:, 1:W + 1], wts[:, m, 0:1], out[:, 0:W], ALU.mult, ALU.add, [a, wts, out], [out])
                    k.stt(eng, out[:, 0:W], a[:, 3:W + 3], wts[:, m, 2:3], out[:, 0:W], ALU.mult, ALU.add, [a, wts, out], [out])
                for jc in range(4):
                    if "c" not in PARTS: continue
                    tick(); tick()
                    if i == 0:
                        convproj(4 + jc, asb[0]); conv3(c1[0], asb[0], cw, 4 + jc, cb[:, 4 + jc:5 + jc])
                        convproj(8 + jc, asb[1]); conv3(c1[1], asb[1], cw, 8 + jc, cb[:, 8 + jc:9 + jc])
                        u_ = uo[jc % 2]; ub_ = ub[jc % 2]
                        k.tt("dve", u_[:, 0:W], c1[0][:, 0:W], c1[1][:, 0:W], ALU.mult, [c1[0], c1[1]], [u_])
                        k.copy("pool", ub_[:, 0:W], u_[:, 0:W], [u_], [ub_])
                        k.dma("sp", S["UF"][jc * 128:(jc + 1) * 128, t0:t0 + W], u_[:, 0:W], reads=[u_])
                        if j == 0:
                            k.dma("sp", S["UTA"][t0 // 128:t0 // 128 + 4, jc * 128:(jc + 1) * 128, :].rearrange("a c p -> c a p"),
                                  ub_[:, 0:W].rearrange("c (a p) -> c a p", p=128), reads=[ub_])
                        else:
                            k.dma("sp", S["UT"][jc * 128:(jc + 1) * 128, t0:t0 + W], ub_[:, 0:W], reads=[ub_])
                        if do_q:
                            convproj(jc, asb[2]); conv3(c1[2], asb[2], cw, jc, cb[:, jc:jc + 1])
                            k.dma("sp", S["X0T"][jc * 128:(jc + 1) * 128, t0:t0 + W], c1[2][:, 0:W], reads=[c1[2]])
                    elif j == 0:
                        convproj(4 + jc, asb[0]); convproj(8 + jc, asb[1])
                        p_ = pp[jc % 2]
                        k.tt("dve", p_[:, 0:Wc], asb[0][:, 0:Wc], asb[1][:, 0:Wc], ALU.mult, [asb[0], asb[1]], [p_])
                        conv3(c1[0], p_, cw, jc, None)
                        convproj(jc, asb[2])
                        ub_ = ub[jc % 2]
                        k.tt("pool", ub_[:, 0:W], c1[0][:, 0:W], asb[2][:, 2:W + 2], ALU.mult, [c1[0], asb[2]], [ub_])
                        k.dma("sp", S["OCT"][jc * 128:(jc + 1) * 128, t0:t0 + W], ub_[:, 0:W], reads=[ub_])
                if gnx is not None:
                    for _ in gnx: pass
        return ph

    def make_attn(i):
        def ph():
            NKT = TT // 128
            kT = k.sbuf([128, TT], BF16, "kT")
            k.memset("pool", kT[64:128, :], 0.0, [kT])
            va = k.sbuf([128, NKT, 65], BF16, "va")
            qT = [k.sbuf([128, 512], BF16, "qT") for _ in range(2)]
            for t_ in qT: k.memset("pool", t_[64:128, :], 0.0, [t_])
            sps = [k.psum([128, 512], F32, "sps") for _ in range(4)]
            ops_ = [k.psum([128, 512], F32, "ops") for _ in range(2)]
            bps = k.psum([128, 512], F32, "bps")
            pT = [k.sbuf([128, 512], BF16, "pT") for _ in range(4)]
            osb = [k.sbuf([65, 512], F32, "osb") for _ in range(2)]
            rb = [k.sbuf([64, 512], F32, "rb") for _ in range(2)]
            ob = [k.sbuf([64, 512], BF16, "ob") for _ in range(2)]
            if i == 1:
                bm = k.sbuf([128, 6, 512], BF16, "bm")
                for r in range(6):
                    k.dma("sp", bm[:, r, :], I["bmask"][r, :, :], writes=[bm])
            qch = []
            for c in range(9):
                t0 = c * 512
                Wq = 512 if c < 8 else 256
                if i == 0:
                    kts = [(kt, 0, Wq, None) for kt in range(NKT)]
                else:
                    kts = []
                    for r in range(-1, 5):
                        kt = 4 * c + r
                        if kt < 0 or kt >= E // 128: continue
                        f0 = max(0, 128 * (r - 1)); f1 = min(Wq, 128 * (r + 2))
                        if f1 <= f0: continue
                        kts.append((kt, f0, f1, r + 1))
                    kts += [(64, 0, Wq, None), (65, 0, Wq, None)]
                qch.append((t0, Wq, kts))
            if i == 0:
                qch.append((SEQ, CTX, [(64, 0, CTX, None), (65, 0, CTX, None)]))
            n = 0; nh = 0
            cvt = make_converter() if i == 0 else None
            for jkv in range(4):
                k.dma("sp", kT[0:64, :], S["KT"][jkv, :, :], writes=[kT])
                k.dma("sp", va[:, :, 0:64], S["V"][:, jkv * 64:(jkv + 1) * 64].rearrange("(n p) d -> p n d", p=128), writes=[va])
                k.memset("pool", va[:, :, 64:65], 1.0, [va])
                for g in range(2):
                    hq = 2 * jkv + g
                    for (t0, W, kts) in qch:
                        q_ = qT[nh % 2]; op_ = ops_[nh % 2]; o_ = osb[nh % 2]; r_ = rb[nh % 2]; b_ = ob[nh % 2]
                        nh += 1
                        k.dma("sp", q_[0:64, 0:W], S["QT"][hq, :, t0:t0 + W], writes=[q_])
                        if i == 1:
                            pass
                        nk = len(kts)
                        def smm(idx):
                            kt, f0, f1, mi = kts[idx]
                            sp_ = sps[(n + idx) % 4]
                            k.mm(sp_[:, f0:f1], kT[:, kt * 128:(kt + 1) * 128], q_[:, f0:f1], True, True, [kT, q_], [sp_])
                        smm(0)
                        if nk > 1: smm(1)
                        for idx in range(nk):
                            if idx + 2 < nk: smm(idx + 2)
                            kt, f0, f1, mi = kts[idx]
                            sp_ = sps[(n + idx) % 4]; p_ = pT[(n + idx) % 4]
                            if mi is not None and (f0 > 0 or f1 < W):
                                k.memset("dve", p_[:, 0:W], 0.0, [p_])
                            k.act(p_[:, f0:f1], sp_[:, f0:f1], AF.Exp, [sp_], [p_], scale=0.125)
                            if mi is not None:
                                k.tt("dve", p_[:, f0:f1], p_[:, f0:f1], bm[:, mi, f0:f1], ALU.mult, [p_, bm], [p_])
                            k.mm(op_[0:65, 0:W], va[:, kt, :], p_[:, 0:W], idx == 0, idx == nk - 1, [va, p_], [op_])
                        n += nk
                        k.copy("act", o_[:, 0:W], op_[0:65, 0:W], [op_], [o_])
                        k.mm(bps[0:64, 0:W], SEL[0:65, :], o_[0:65, 0:W], True, True, [SEL, o_], [bps])
                        if i == 1:
                            k.ts("dve", r_[:, 0:W], bps[0:64, 0:W], ESINK[0:64, hq:hq + 1], None, ALU.add, None, [bps, ESINK], [r_])
                            k.recip(r_[:, 0:W], r_[:, 0:W], [r_], [r_])
                        else:
                            k.recip(r_[:, 0:W], bps[0:64, 0:W], [bps], [r_])
                        k.tt("dve", b_[:, 0:W], o_[0:64, 0:W], r_[:, 0:W], ALU.mult, [o_, r_], [b_])
                        k.dma("pool", S["OAT"][hq * 64:(hq + 1) * 64, t0:t0 + W], b_[:, 0:W], reads=[b_])
                        if cvt is not None:
                            cvt.step("dve", "pool"); cvt.step("dve", "pool")
            if cvt is not None:
                while cvt.step("dve", "pool"): pass
        return ph

    def make_outproj(i, XIN, XOUT, chunks):
        def ph():
            wa = k.sbuf([128, 4, D], BF16, "woa")
            wc = k.sbuf([128, 4, D], BF16, "woc")
            k.dma("sp", wa[:], S[f"bw_out{i}"][0:512, :].rearrange("(c p) f -> p c f", p=128), writes=[wa])
            k.dma("sp", wc[:], S[f"bw_out{i}"][512:1024, :].rearrange("(c p) f -> p c f", p=128), writes=[wc])
            oa = [k.sbuf([128, 4, 512], BF16, "oa") for _ in range(2)]
            oc = [k.sbuf([128, 4, 512], BF16, "oc") for _ in range(2)]
            xs = [k.sbuf([128, 8, 512], F32, "xs") for _ in range(2)]
            xo = [k.sbuf([128, 8, 512], F32, "xo") for _ in range(2)]
            ps = [k.psum([128, 512], F32, "ps") for _ in range(4)]
            n = 0
            for ci, (t0, W, le, re, j) in enumerate(chunks):
                a_ = oa[ci % 2]; c_ = oc[ci % 2]; x_ = xs[ci % 2]; o_ = xo[ci % 2]
                k.dma("sp", a_[:, :, 0:W], S["OAT"][:, t0:t0 + W].rearrange("(c p) t -> p c t", p=128), writes=[a_])
                k.dma("sp", c_[:, :, 0:W], S["OCT"][:, t0:t0 + W].rearrange("(c p) t -> p c t", p=128), writes=[c_])
                k.dma("sp", x_[:, :, 0:W], XIN[:, t0:t0 + W].rearrange("(k p) t -> p k t", p=128), writes=[x_])
                for m in range(8):
                    p_ = ps[n % 4]; n += 1
                    for hh_ in range(4):
                        k.mm(p_[:, 0:W], wa[:, hh_, m * 128:(m + 1) * 128], a_[:, hh_, 0:W], hh_ == 0, False, [wa, a_], [p_])
                    for cc in range(4):
                        k.mm(p_[:, 0:W], wc[:, cc, m * 128:(m + 1) * 128], c_[:, cc, 0:W], False, cc == 3, [wc, c_], [p_])
                    k.stt("dve", o_[:, m, 0:W], p_[:, 0:W], MOD[:, i, 16 + m, j:j + 1], x_[:, m, 0:W], ALU.mult, ALU.add, [p_, MOD, x_], [o_])
                k.dma("pool", XOUT[:, t0:t0 + W].rearrange("(k p) t -> p k t", p=128), o_[:, :, 0:W], reads=[o_])
        return ph

    def make_ffn(i, XIN, XOUT, chunks, final=False):
        def ph():
            wu = k.sbuf([128, 8, 2 * DFF], BF16, "wu")
            for kk in range(8):
                k.dma("sp", wu[:, kk, :], S[f"bw_up{i}"][kk * 128:(kk + 1) * 128, :], writes=[wu])
            wd = [k.sbuf([128, NM, 128], BF16, "wd") for _ in range(2)]
            cw = k.sbuf([128, NM, 3], F32); k.dma("sp", cw[:], I[f"fcw{i}"][:, :, :], writes=[cw])
            cb = k.sbuf([128, NM], F32); k.dma("sp", cb[:], I[f"fcb{i}"][:, :], writes=[cb])
            xh = k.sbuf([128, 8, 516], F32, "xh")
            k.memset("pool", xh[:], 0.0, [xh])
            sq = k.sbuf([128, 8, 516], BF16, "sq")
            hs = [k.sbuf([128, 8, 516], BF16, "h") for _ in range(2)]
            xr = [k.sbuf([128, 512], F32, "xr") for _ in range(2)]
            rstd = k.sbuf([128, 516], F32, "rstd")
            tmp = [k.sbuf([128, 516], F32, "tmp") for _ in range(2)]
            gg = k.sbuf([128, NM, 512], BF16, "gg")
            asb = [k.sbuf([128, 516], F32, "asb") for _ in range(2)]
            c1 = [k.sbuf([128, 512], F32, "c1") for _ in range(2)]
            xo = [k.sbuf([128, 512], F32, "xo") for _ in range(2)]
            ssps = k.psum([128, 1024], F32, "ssps")
            aps = [k.psum([128, 1024], F32, "aps") for _ in range(2)]
            vps = [k.psum([128, 512], F32, "vps") for _ in range(2)]
            n = 0; nd = 0
            for ci, (t0, W, le, re, j) in enumerate(chunks):
                Wc = W + 4
                h = hs[ci % 2]
                if ci == 0:
                    load_xh(xh, XIN, t0, W, le, re)
                    for _ in norm_mod_gen(xh, Wc, i, 1, j, sq, ssps, rstd, tmp, h): pass
                gnx = None
                if ci + 1 < len(chunks):
                    t0n, Wn, len_, ren, jn = chunks[ci + 1]
                    load_xh(xh, XIN, t0n, Wn, len_, ren)
                    gnx = norm_mod_gen(xh, Wn + 4, i, 1, jn, sq, ssps, rstd, tmp, hs[(ci + 1) % 2])
                for m in range(NM):
                    if gnx is not None and m >= 2: next(gnx, None)
                    ap_ = aps[n % 2]; vp_ = vps[n % 2]; a_ = asb[n % 2]; c_ = c1[n % 2]; n += 1
                    for (c0, c1_) in ((0, min(512, Wc)), (512, Wc)):
                        if c1_ <= c0: continue
                        for kk in range(8):
                            k.mm(ap_[:, c0:c1_], wu[:, kk, m * 128:(m + 1) * 128], h[:, kk, c0:c1_], kk == 0, kk == 7, [wu, h], [ap_])
                    for kk in range(8):
                        k.mm(vp_[:, 0:W], wu[:, kk, DFF + m * 128:DFF + (m + 1) * 128], h[:, kk, 2:W + 2], kk == 0, kk == 7, [wu, h], [vp_])
                    k.copy("act", a_[:, 0:Wc], ap_[:, 0:Wc], [ap_], [a_])
                    if le: k.memset("pool", a_[:, 1:2], 0.0, [a_])
                    if re: k.memset("pool", a_[:, W + 2:W + 3], 0.0, [a_])
                    eng = "dve"
                    k.ts(eng, c_[:, 0:W], a_[:, 2:W + 2], cw[:, m, 1:2], cb[:, m:m + 1], ALU.mult, ALU.add, [a_, cw, cb], [c_])
                    k.stt(eng, c_[:, 0:W], a_[:, 1:W + 1], cw[:, m, 0:1], c_[:, 0:W], ALU.mult, ALU.add, [a_, cw, c_], [c_])
                    k.stt(eng, c_[:, 0:W], a_[:, 3:W + 3], cw[:, m, 2:3], c_[:, 0:W], ALU.mult, ALU.add, [a_, cw, c_], [c_])
                    k.act(c_[:, 0:W], c_[:, 0:W], AF.Gelu_apprx_tanh, [c_], [c_])
                    k.tt("dve", gg[:, m, 0:W], c_[:, 0:W], vp_[:, 0:W], ALU.mult, [c_, vp_], [gg])
                if gnx is not None:
                    for _ in gnx: pass
                for mo in range(8):
                    wd_ = wd[nd % 2]; o_ = xo[nd % 2]; p_ = vps[nd % 2]; xr_ = xr[nd % 2]; nd += 1
                    k.dma("sp", wd_[:], S[f"bw_dn{i}"][:, mo * 128:(mo + 1) * 128].rearrange("(m p) f -> p m f", p=128), writes=[wd_])
                    k.dma("sp", xr_[:, 0:W], XIN[mo * 128:(mo + 1) * 128, t0:t0 + W], writes=[xr_])
                    for m in range(NM):
                        k.mm(p_[:, 0:W], wd_[:, m, :], gg[:, m, 0:W], m == 0, m == NM - 1, [wd_, gg], [p_])
                    k.stt("dve", o_[:, 0:W], p_[:, 0:W], MOD[:, i, 40 + mo, j:j + 1], xr_[:, 0:W], ALU.mult, ALU.add, [p_, MOD, xr_], [o_])
                    k.dma("pool", XOUT[mo * 128:(mo + 1) * 128, t0:t0 + W], o_[:, 0:W], reads=[o_])
        return ph

    def make_filter(tag, n):
        KRAW = S["KRAW" + tag]; KN = S["KN" + tag]
        def ph():
            w1 = k.sbuf([33, 64], F32); k.dma("sp", w1[:], I["hw1"][:, :], writes=[w1])
            w2 = k.sbuf([64, 64], F32); k.dma("sp", w2[:], I["hw2"][:, :], writes=[w2])
            w3 = k.sbuf([64, 64], F32); k.dma("sp", w3[:], I["hw3"][:, :], writes=[w3])
            w4 = k.sbuf([64, 1024], F32); k.dma("sp", w4[:], I["hw4"][:, :], writes=[w4])
            hb = k.sbuf([64, 4], F32); k.dma("sp", hb[:], I["hb"][:, :], writes=[hb])
            ndel = k.sbuf([128, 4], F32); k.dma("sp", ndel[:], I["ndel"][:, :], writes=[ndel])
            asum = k.sbuf([128, 4, 40], F32, "asum")
            k.memset("dve", asum[:], 0.0, [asum])
            ft = [k.sbuf([33, 512], F32, "ft") for _ in range(2)]
            t01 = [k.sbuf([128, 512], F32, "t01") for _ in range(2)]
            hid2 = [[k.sbuf([64, 512], F32, "hid") for _ in range(3)] for _ in range(2)]
            ki2 = [k.sbuf([64, 512], I32, "ki") for _ in range(2)]
            win = [k.sbuf([128, 512], F32, "win") for _ in range(2)]
            kr = [k.sbuf([128, 512], F32, "kr") for _ in range(2)]
            junk = k.sbuf([128, 512], F32, "junk")
            krb = [k.sbuf([128, 512], BF16, "krb") for _ in range(2)]
            ps = [k.psum([128, 512], F32, "ps") for _ in range(4)]
            NCH = (2 * n) // 512
            n_ = 0
            for c in range(NCH):
                q0 = c * 512
                f_ = ft[c % 2]; t_ = t01[c % 2]
                def ld(cj):
                    k.dma("sp", ft[cj % 2][:, :], I["featsT" + tag][:, cj * 512:cj * 512 + 512], writes=[ft[cj % 2]])
                    k.dma("sp", t01[cj % 2][:, :], I["t01b" + tag][:, cj * 512:cj * 512 + 512], writes=[t01[cj % 2]])
                if c == 0: ld(0)
                if c + 1 < NCH: ld(c + 1)
                src = f_; srcK = 33
                hid = hid2[c % 2]; ki = ki2[c % 2]
                for li, wl in enumerate((w1, w2, w3)):
                    p_ = ps[n_ % 4]; n_ += 1
                    k.mm(p_[0:64, :], wl[0:srcK, :], src[0:srcK, :], True, True, [wl, src], [p_])
                    hd = hid[li]
                    k.ts("dve", hd[:, :], p_[0:64, :], hb[:, li:li + 1], hb[:, 3:4], ALU.add, ALU.mult, [p_, hb], [hd])
                    k.ts("dve", ki[:, :], hd[:, :], float(1.0 / (2 * np.pi)), None, ALU.mult, None, [hd], [ki])
                    k.stt("dve", hd[:, :], ki[:, :], float(-2 * np.pi), hd[:, :], ALU.mult, ALU.add, [ki, hd], [hd])
                    k.act(hd[:, :], hd[:, :], AF.Sin, [hd], [hd])
                    src = hd; srcK = 64
                segs = []
                if q0 + 512 <= n: segs = [(0, 512, 0)]
                elif q0 >= n: segs = [(0, 512, 512)]
                else: segs = [(0, n - q0, 0), (n - q0, 512, 512)]
                for jc in range(4):
                    p_ = ps[n_ % 4]; n_ += 1
                    for (a0, a1, off) in segs:
                        k.mm(p_[:, a0:a1], w4[:, off + jc * 128:off + (jc + 1) * 128], hid[2][:, a0:a1], True, True, [w4, hid[2]], [p_])
                    wn = win[jc % 2]; kr_ = kr[jc % 2]
                    k.act(wn[:, :], t_[:, :], AF.Exp, [t_, ndel], [wn], scale=ndel[:, jc:jc + 1])
                    kb_ = krb[jc % 2]
                    k.stt("dve", kb_[:, :], wn[:, :], 0.05, p_[:, :], ALU.add, ALU.mult, [wn, p_], [kb_])
                    if q0 <= n < q0 + 512:
                        k.memset("dve", kb_[:, n - q0:n - q0 + 1], 0.0, [kb_])
                    k.act(junk[:, :], kb_[:, :], AF.Abs, [kb_], [junk, asum], accum=asum[:, jc, c:c + 1])
                    if tag == "L":
                        k.dma("sp", S["KNA"][q0 // 128:q0 // 128 + 4, jc * 128:(jc + 1) * 128, :].rearrange("a c p -> c a p"),
                              kb_[:, :].rearrange("c (a p) -> c a p", p=128), reads=[kb_])
                    else:
                        k.dma("sp", KN[jc * 128:(jc + 1) * 128, q0:q0 + 512], kb_[:, :], reads=[kb_])
            rn = RNORM[tag]
            for jc in range(4):
                k.op("dve", lambda e, jc=jc: e.reduce_sum(out=rn[:, jc:jc + 1], in_=asum[:, jc, 0:NCH], axis=mybir.AxisListType.X), [asum], [rn])
            k.recip(rn[:, :], rn[:, :], [rn], [rn], force_self=True)
        return ph

    def make_fftconv(tag, NA, n, tok0, n_out_blocks):
        KN = S["KN" + tag]
        NR = NA // 2
        GF = 512 // NA
        GB = 512 // (2 * NA)
        def ph():
            cst = {}
            for nm, shp, dt in (("f1cs", [NA, 2 * NA], BF16), ("fC", [128, 128], BF16), ("fS", [128, 128], BF16), ("fnS", [128, 128], BF16),
                                ("fCS", [128, 256], BF16), ("fnSC", [128, 256], BF16), ("twA", [128, 512], F32), ("twB", [128, 512], F32),
                                ("twA2", [NA, 1024], F32), ("twB2", [NA, 1024], F32), ("g3C", [NA, NA], BF16), ("g3nS", [NA, NA], BF16)):
                cst[nm] = k.sbuf(shp, dt, nm)
                k.dma("sp", cst[nm][:], I[nm + tag][tuple(slice(None) for _ in shp)], writes=[cst[nm]])
            if NA == 4:
                for nm, shp, dt in (("twA2c", [128, 256], F32), ("twB2c", [128, 256], F32), ("gbC", [128, 64], BF16), ("gbnS", [128, 64], BF16)):
                    cst[nm] = k.sbuf(shp, dt, nm)
                    k.dma("sp", cst[nm][:], I[nm + tag][:, :], writes=[cst[nm]])
                c2a = [k.sbuf([128, 256], F32, "c2a") for _ in range(2)]; c2b = [k.sbuf([128, 256], F32, "c2b") for _ in range(2)]
                y3c = [k.sbuf([128, 2, 128], BF16, "y3c") for _ in range(2)]
                yoc = [k.sbuf([64, 128], F32, "yoc") for _ in range(2)]
            LG = 32 if NA == 128 else 128
            NXB = 2 if NA == 128 else 1
            xu = [k.sbuf([NA, LG, 128], BF16, "xu") for _ in range(NXB)]
            for t_ in xu: k.memset("pool", t_[:], 0.0, [t_])
            xk = [k.sbuf([NA, LG, 128], BF16, "xk") for _ in range(NXB)]
            s1 = [k.psum([128, 512], F32, "s1") for _ in range(2)]
            s2 = [k.psum([128, 512], F32, "s2") for _ in range(2)]
            s3 = [k.psum([128, 1024], F32, "s3") for _ in range(1)]
            s4 = [k.psum([128, 512], F32, "s4") for _ in range(2)]
            NB = 2
            ta = [k.sbuf([128, 512], F32, "ta") for _ in range(NB)]; tb = [k.sbuf([128, 512], F32, "tb") for _ in range(NB)]
            bu = [k.sbuf([128, 2, GF, NA], BF16, "bu") for _ in range(NB)]; bk = [k.sbuf([128, 2, GF, NA], BF16, "bk") for _ in range(NB)]
            kh = [k.sbuf([128, 2, 512], F32, "kh") for _ in range(NB)]
            m1 = [k.sbuf([128, 512], F32, "m1") for _ in range(NB)]; m2 = [k.sbuf([128, 512], F32, "m2") for _ in range(NB)]
            m3 = [k.sbuf([128, 512], F32, "m3") for _ in range(NB)]; m4 = [k.sbuf([128, 512], F32, "m4") for _ in range(NB)]
            yh = [k.sbuf([128, 2, GF, NA], BF16, "yh") for _ in range(NB)]
            t2a = [k.sbuf([NA, 1024], F32, "t2a") for _ in range(NB)]; t2b = [k.sbuf([NA, 1024], F32, "t2b") for _ in range(NB)]
            y3 = [k.sbuf([NA, 2, 4, 128], BF16, "y3") for _ in range(NB)]
            yo = [k.sbuf([n_out_blocks, 4, 128], F32, "yo") for _ in range(2)]
            MO = n_out_blocks
            cnt = {"tw": 0, "inv": 0, "g": 0}
            def fwd(x, cbase, bdst):
                for half in range(2):
                    bank = s1[half]
                    ta_ = ta[cnt["tw"] % NB]; tb_ = tb[cnt["tw"] % NB]; cnt["tw"] += 1
                    for cc in range(GB):
                        ch = cbase + half * GB + cc
                        k.mm(bank[:, cc * 2 * NA:(cc + 1) * 2 * NA], x[0:NA, ch, :], cst["f1cs"][0:NA, :], True, True, [x, cst["f1cs"]], [bank])
                    k.tt("dve", ta_[:, :], bank[:, :], cst["twA"][:, :], ALU.mult, [bank, cst["twA"]], [ta_])
                    k.tt("dve", tb_[:, :], bank[:, :], cst["twB"][:, :], ALU.mult, [bank, cst["twB"]], [tb_])
                    tav = ta_[:, :].rearrange("p (g r f) -> p g r f", g=GB, r=2)
                    tbv = tb_[:, :].rearrange("p (g r f) -> p g r f", g=GB, r=2)
                    k.tt("dve", bdst[:, 0, half * GB:(half + 1) * GB, :], tav[:, :, 0, :], tbv[:, :, 1, :], ALU.subtract, [ta_, tb_], [bdst])
                    k.tt("dve", bdst[:, 1, half * GB:(half + 1) * GB, :], tav[:, :, 1, :], tbv[:, :, 0, :], ALU.subtract, [ta_, tb_], [bdst])
                bre = bdst[:, 0, :, :].rearrange("p g f -> p (g f)"); bim = bdst[:, 1, :, :].rearrange("p g f -> p (g f)")
                k.mm(s2[0][:, :], cst["fC"][:, :], bre, True, False, [cst["fC"], bdst], [s2[0]])
                k.mm(s2[0][:, :], cst["fS"][:, :], bim, False, True, [cst["fS"], bdst], [s2[0]])
                k.mm(s2[1][:, :], cst["fC"][:, :], bim, True, False, [cst["fC"], bdst], [s2[1]])
                k.mm(s2[1][:, :], cst["fnS"][:, :], bre, False, True, [cst["fnS"], bdst], [s2[1]])
            for lg in range(512 // LG):
                xu_ = xu[lg % NXB]; xk_ = xk[lg % NXB]
                k.dma("sp", xu_[0:NR, :, :], S["UT"][lg * LG:(lg + 1) * LG, tok0:tok0 + n].rearrange("c (a p) -> a c p", p=128), writes=[xu_])
                k.dma("sp", xk_[:, :, :], KN[lg * LG:(lg + 1) * LG, :].rearrange("c (a p) -> a c p", p=128), writes=[xk_])
                for gf in range(LG // GF):
                    cbase = gf * GF
                    g_ = cnt["g"] % NB; cnt["g"] += 1
                    kh_ = kh[g_]; yh_ = yh[g_]
                    fwd(xk_, cbase, bk[g_])
                    k.copy("act", kh_[:, 0, :], s2[0][:, :], [s2[0]], [kh_])
                    k.copy("act", kh_[:, 1, :], s2[1][:, :], [s2[1]], [kh_])
                    fwd(xu_, cbase, bu[g_])
                    k.tt("dve", m1[g_][:, :], s2[0][:, :], kh_[:, 0, :], ALU.mult, [s2[0], kh_], [m1[g_]])
                    k.tt("dve", m3[g_][:, :], s2[0][:, :], kh_[:, 1, :], ALU.mult, [s2[0], kh_], [m3[g_]])
                    k.tt("dve", m2[g_][:, :], s2[1][:, :], kh_[:, 1, :], ALU.mult, [s2[1], kh_], [m2[g_]])
                    k.tt("dve", m4[g_][:, :], s2[1][:, :], kh_[:, 0, :], ALU.mult, [s2[1], kh_], [m4[g_]])
                    k.tt("pool", yh_[:, 0, :, :].rearrange("p g f -> p (g f)"), m1[g_][:, :], m2[g_][:, :], ALU.subtract, [m1[g_], m2[g_]], [yh_])
                    k.tt("pool", yh_[:, 1, :, :].rearrange("p g f -> p (g f)"), m3[g_][:, :], m4[g_][:, :], ALU.add, [m3[g_], m4[g_]], [yh_])
                    if NA == 4:
                        for sg in range(GF // 32):
                            b3 = s3[0]
                            iv = cnt["inv"] % 2; cnt["inv"] += 1
                            lre = yh_[:, 0, sg * 32:(sg + 1) * 32, :].rearrange("p g f -> p (g f)")
                            lim = yh_[:, 1, sg * 32:(sg + 1) * 32, :].rearrange("p g f -> p (g f)")
                            k.mm(b3[:, 0:256], lre, cst["fCS"][:, :], True, False, [yh_, cst["fCS"]], [b3])
                            k.mm(b3[:, 0:256], lim, cst["fnSC"][:, :], False, True, [yh_, cst["fnSC"]], [b3])
                            a_ = c2a[iv]; b_ = c2b[iv]; y_ = y3c[iv]; o_ = yoc[iv]
                            k.tt("dve", a_[:, :], b3[:, 0:256], cst["twA2c"][:, :], ALU.mult, [b3, cst["twA2c"]], [a_])
                            k.tt("dve", b_[:, :], b3[:, 0:256], cst["twB2c"][:, :], ALU.mult, [b3, cst["twB2c"]], [b_])
                            k.tt("pool", y_[:, 0, :], a_[:, 0:128], b_[:, 128:256], ALU.add, [a_, b_], [y_])
                            k.tt("pool", y_[:, 1, :], a_[:, 128:256], b_[:, 0:128], ALU.add, [a_, b_], [y_])
                            p4 = s4[iv]
                            k.mm(p4[0:64, 0:128], cst["gbC"][:, :], y_[:, 0, :], True, False, [cst["gbC"], y_], [p4])
                            k.mm(p4[0:64, 0:128], cst["gbnS"][:, :], y_[:, 1, :], False, True, [cst["gbnS"], y_], [p4])
                            k.copy("act", o_[:, :], p4[0:64, 0:128], [p4], [o_])
                            c0 = lg * LG + cbase + sg * 32
                            for a2 in range(2):
                                k.dma("sp", S["YT"][c0:c0 + 32, tok0 + a2 * 128:tok0 + (a2 + 1) * 128], o_[a2:64:2, :], reads=[o_])
                        continue
                    for sg in range(GF // 4):
                        b3 = s3[0]
                        iv = cnt["inv"] % NB; cnt["inv"] += 1
                        t2a_ = t2a[iv]; t2b_ = t2b[iv]; y3_ = y3[iv]
                        for cc in range(4):
                            ch = sg * 4 + cc
                            k.mm(b3[0:NA, cc * 256:(cc + 1) * 256], yh_[:, 0, ch, :], cst["fCS"][:, :], True, False, [yh_, cst["fCS"]], [b3])
                            k.mm(b3[0:NA, cc * 256:(cc + 1) * 256], yh_[:, 1, ch, :], cst["fnSC"][:, :], False, True, [yh_, cst["fnSC"]], [b3])
                        k.tt("dve", t2a_[:, :], b3[0:NA, :], cst["twA2"][:, :], ALU.mult, [b3, cst["twA2"]], [t2a_])
                        k.tt("dve", t2b_[:, :], b3[0:NA, :], cst["twB2"][:, :], ALU.mult, [b3, cst["twB2"]], [t2b_])
                        av = t2a_[:, :].rearrange("p (g r f) -> p g r f", g=4, r=2)
                        bv = t2b_[:, :].rearrange("p (g r f) -> p g r f", g=4, r=2)
                        k.tt("pool", y3_[:, 0, :, :], av[:, :, 0, :], bv[:, :, 1, :], ALU.add, [t2a_, t2b_], [y3_])
                        k.tt("pool", y3_[:, 1, :, :], av[:, :, 1, :], bv[:, :, 0, :], ALU.add, [t2a_, t2b_], [y3_])
                        p4 = s4[iv % 2]; yo_ = yo[iv % 2]
                        k.mm(p4[0:MO, :], cst["g3C"][:, 0:MO], y3_[:, 0, :, :].rearrange("p g f -> p (g f)"), True, False, [cst["g3C"], y3_], [p4])
                        k.mm(p4[0:MO, :], cst["g3nS"][:, 0:MO], y3_[:, 1, :, :].rearrange("p g f -> p (g f)"), False, True, [cst["g3nS"], y3_], [p4])
                        k.copy("act", yo_[:, :, :].rearrange("p g f -> p (g f)"), p4[0:MO, :], [p4], [yo_])
                        c0 = lg * LG + cbase + sg * 4
                        k.dma("sp", S["YT"][c0:c0 + 4, tok0:tok0 + MO * 128].rearrange("c (a p) -> a c p", p=128), yo_[:, :, :], reads=[yo_])
        return ph

    def make_fftconvL2(n_out_blocks):
        tag = "L"; NA = 128; NR = 64; n = SEQ; tok0 = 0; CB = 32; MO = n_out_blocks
        def ph():
            cst = {}
            for nm, shp, dt in (("f1cs", [128, 256], BF16), ("fCS", [128, 256], BF16), ("fnSC", [128, 256], BF16),
                                ("twA2", [128, 1024], F32), ("twB2", [128, 1024], F32), ("g3C", [128, 128], BF16), ("g3nS", [128, 128], BF16)):
                cst[nm] = k.sbuf(shp, dt, nm)
                k.dma("sp", cst[nm][:], I[nm + tag][:, :], writes=[cst[nm]])
            MC = k.sbuf([128, 128, 128], BF16, "MC"); MS = k.sbuf([128, 128, 128], BF16, "MS")
            for q4 in range(4):
                k.dma("sp", MC[:, q4 * 32:(q4 + 1) * 32, :], I["MCL"][:, q4 * 32:(q4 + 1) * 32, :], writes=[MC])
                k.dma("sp", MS[:, q4 * 32:(q4 + 1) * 32, :], I["MSL"][:, q4 * 32:(q4 + 1) * 32, :], writes=[MS])
            xu = k.sbuf([128, CB, 128], BF16, "xu"); k.memset("pool", xu[:], 0.0, [xu])
            xk = k.sbuf([128, CB, 128], BF16, "xk")
            ATs = [k.sbuf([128, CB, 3, 128], BF16, "AT") for _ in range(2)]
            KH = k.sbuf([128, 2, 128, CB], BF16, "KH")
            YH = k.sbuf([128, 2, CB, 128], BF16, "YH")
            mm_ = [[k.sbuf([128, 512], F32, "m") for _ in range(4)] for _ in range(2)]
            t2a = [k.sbuf([128, 1024], BF16, "t2a") for _ in range(2)]; t2b = [k.sbuf([128, 1024], BF16, "t2b") for _ in range(2)]
            y3 = [k.sbuf([128, 2, 4, 128], BF16, "y3") for _ in range(2)]
            yo = [k.sbuf([MO, 4, 128], F32, "yo") for _ in range(2)]
            s1 = [k.psum([128, 512], F32, "s1") for _ in range(2)]
            s2 = [k.psum([128, 512], F32, "s2") for _ in range(4)]
            s3 = k.psum([128, 1024], F32, "s3")
            cnt = {"p": 0, "b": 0, "inv": 0}
            def step1_gen(bt, kind, AT):
                if kind == "F":
                    k.dma("sp", xk[:, :, :], S["KNA"][:, bt * CB:(bt + 1) * CB, :], writes=[xk]); x = xk
                else:
                    k.dma("sp", xu[0:NR, :, :], S["UTA"][:, bt * CB:(bt + 1) * CB, :], writes=[xu]); x = xu
                for pr in range(CB // 2):
                    bank = s1[cnt["p"] % 2]; cnt["p"] += 1
                    for cc in range(2):
                        k.mm(bank[:, cc * 256:(cc + 1) * 256], x[:, 2 * pr + cc, :], cst["f1cs"][:, :], True, True, [x, cst["f1cs"]], [bank])
                    bv = bank[:, :].rearrange("p (g r f) -> p g r f", g=2, r=2)
                    o1 = AT[:, 2 * pr:2 * pr + 2, 0:2, :]; o2 = AT[:, 2 * pr:2 * pr + 2, 2, :]; i2 = bv[:, :, 0, :]
                    k.op("act", lambda e, o1=o1, bv=bv: e.copy(out=o1, in_=bv), [bank], [AT], force_self="never")
                    k.op("act", lambda e, o2=o2, i2=i2: e.mul(out=o2, in_=i2, mul=-1.0), [bank], [AT], force_self="never")
                    yield
            def inverse_gen(bt):
                for sg in range(CB // 4):
                    iv = cnt["inv"] % 2; cnt["inv"] += 1
                    t2a_ = t2a[iv]; t2b_ = t2b[iv]; y3_ = y3[iv]
                    for cc in range(4):
                        ch = sg * 4 + cc
                        k.mm(s3[:, cc * 256:(cc + 1) * 256], YH[:, 0, ch, :], cst["fCS"][:, :], True, False, [YH, cst["fCS"]], [s3])
                        k.mm(s3[:, cc * 256:(cc + 1) * 256], YH[:, 1, ch, :], cst["fnSC"][:, :], False, True, [YH, cst["fnSC"]], [s3])
                    k.tt("dve", t2a_[:, :], s3[:, :], cst["twA2"][:, :], ALU.mult, [s3, cst["twA2"]], [t2a_])
                    k.tt("dve", t2b_[:, :], s3[:, :], cst["twB2"][:, :], ALU.mult, [s3, cst["twB2"]], [t2b_])
                    av = t2a_[:, :].rearrange("p (g r f) -> p g r f", g=4, r=2)
                    bv2 = t2b_[:, :].rearrange("p (g r f) -> p g r f", g=4, r=2)
                    k.tt("dve", y3_[:, 0, :, :], av[:, :, 0, :], bv2[:, :, 1, :], ALU.add, [t2a_, t2b_], [y3_])
                    k.tt("pool", y3_[:, 1, :, :], av[:, :, 1, :], bv2[:, :, 0, :], ALU.add, [t2a_, t2b_], [y3_])
                    yield
                    p4 = s1[cnt["p"] % 2]; cnt["p"] += 1
                    yo_ = yo[iv]
                    k.mm(p4[0:MO, :], cst["g3C"][:, 0:MO], y3_[:, 0, :, :].rearrange("p g f -> p (g f)"), True, False, [cst["g3C"], y3_], [p4])
                    k.mm(p4[0:MO, :], cst["g3nS"][:, 0:MO], y3_[:, 1, :, :].rearrange("p g f -> p (g f)"), False, True, [cst["g3nS"], y3_], [p4])
                    k.copy("act", yo_[:, :, :].rearrange("p g f -> p (g f)"), p4[0:MO, :], [p4], [yo_])
                    c0 = bt * CB + sg * 4
                    k.dma("sp", S["YT"][c0:c0 + 4, tok0:tok0 + MO * 128].rearrange("c (a p) -> a c p", p=128), yo_[:, :, :], reads=[yo_])
                    yield
            def filt_block(blk, bre, bim):
                o_a = KH[:, 0, blk * 16:(blk + 1) * 16, :].rearrange("p a b -> p (a b)"); o_b = KH[:, 1, blk * 16:(blk + 1) * 16, :].rearrange("p a b -> p (a b)")
                k.op("act", lambda e: e.copy(out=o_a, in_=bre[:, :]), [bre], [KH], force_self="never")
                k.op("act", lambda e: e.copy(out=o_b, in_=bim[:, :]), [bim], [KH], force_self="never")
            def sig_block(blk, bre, bim):
                kre = KH[:, 0, blk * 16:(blk + 1) * 16, :].rearrange("p a b -> p (a b)")
                kim = KH[:, 1, blk * 16:(blk + 1) * 16, :].rearrange("p a b -> p (a b)")
                m1, m2, m3, m4 = mm_[blk % 2]
                k.tt("dve", m1[:, :], bre[:, :], kre, ALU.mult, [bre, KH], [m1])
                k.tt("dve", m3[:, :], bre[:, :], kim, ALU.mult, [bre, KH], [m3])
                k.tt("dve", m2[:, :], bim[:, :], kim, ALU.mult, [bim, KH], [m2])
                k.tt("dve", m4[:, :], bim[:, :], kre, ALU.mult, [bim, KH], [m4])
                ore = YH[:, 0, :, blk * 16:(blk + 1) * 16].rearrange("p c f -> p f c")
                oim = YH[:, 1, :, blk * 16:(blk + 1) * 16].rearrange("p c f -> p f c")
                v = lambda t: t[:, :].rearrange("p (f c) -> p f c", c=CB)
                k.op("dve", lambda e: e.tensor_tensor(out=ore, in0=v(m1), in1=v(m2), op=ALU.subtract), [m1, m2], [YH])
                k.op("pool", lambda e: e.tensor_tensor(out=oim, in0=v(m3), in1=v(m4), op=ALU.add), [m3, m4], [YH], force_self="never")
            def step2(AT, on_block, g_next, g_inv):
                for f1 in range(128):
                    j = f1 % 16
                    if j == 0:
                        bre = s2[(cnt["b"] % 2) * 2]; bim = s2[(cnt["b"] % 2) * 2 + 1]; cnt["b"] += 1
                    cols = slice(j * CB, (j + 1) * CB)
                    k.mm(bre[:, cols], MC[:, f1, :], AT[:, :, 0, f1], True, False, [MC, AT], [bre])
                    k.mm(bre[:, cols], MS[:, f1, :], AT[:, :, 1, f1], False, True, [MS, AT], [bre])
                    k.mm(bim[:, cols], MC[:, f1, :], AT[:, :, 1, f1], True, False, [MC, AT], [bim])
                    k.mm(bim[:, cols], MS[:, f1, :], AT[:, :, 2, f1], False, True, [MS, AT], [bim])
                    if j == 15:
                        on_block(f1 // 16, bre, bim)
                    if f1 % 8 == 3 and g_next is not None: next(g_next, None)
                    if f1 % 8 == 7 and g_inv is not None: next(g_inv, None)
            jobs = [(bt, kind) for bt in range(512 // CB) for kind in ("F", "S")]
            g0 = step1_gen(jobs[0][0], jobs[0][1], ATs[0])
            for _ in g0: pass
            for ji, (bt, kind) in enumerate(jobs):
                g_next = step1_gen(jobs[ji + 1][0], jobs[ji + 1][1], ATs[(ji + 1) % 2]) if ji + 1 < len(jobs) else None
                g_inv = inverse_gen(bt - 1) if (kind == "F" and bt > 0) else None
                step2(ATs[ji % 2], filt_block if kind == "F" else sig_block, g_next, g_inv)
                if g_next is not None:
                    for _ in g_next: pass
                if g_inv is not None:
                    for _ in g_inv: pass
            for _ in inverse_gen(512 // CB - 1): pass
        return ph

    def make_hycombine(chunks):
        def ph():
            bd = k.sbuf([128, 4], F32); k.dma("sp", bd[:], I["hbd"][:, :], writes=[bd])
            yt = [k.sbuf([128, 512], F32, "yt") for _ in range(2)]
            ut = [k.sbuf([128, 512], F32, "ut") for _ in range(2)]
            x0 = [k.sbuf([128, 512], F32, "x0") for _ in range(2)]
            ob = [k.sbuf([128, 512], BF16, "ob") for _ in range(2)]
            n = 0
            for (t0, W, le, re, j) in chunks:
                rn = RNORM["C" if j == 1 else "L"]
                for jc in range(4):
                    y_ = yt[n % 2]; u_ = ut[n % 2]; x_ = x0[n % 2]; o_ = ob[n % 2]; n += 1
                    rows = slice(jc * 128, (jc + 1) * 128)
                    k.dma("sp", y_[:, 0:W], S["YT"][rows, t0:t0 + W], writes=[y_])
                    k.dma("sp", u_[:, 0:W], S["UF"][rows, t0:t0 + W], writes=[u_])
                    k.dma("sp", x_[:, 0:W], S["X0T"][rows, t0:t0 + W], writes=[x_])
                    k.ts("dve", y_[:, 0:W], y_[:, 0:W], rn[:, jc:jc + 1], None, ALU.mult, None, [y_, rn], [y_])
                    k.stt("dve", y_[:, 0:W], u_[:, 0:W], bd[:, jc:jc + 1], y_[:, 0:W], ALU.mult, ALU.add, [u_, bd, y_], [y_])
                    k.tt("dve", o_[:, 0:W], y_[:, 0:W], x_[:, 0:W], ALU.mult, [y_, x_], [o_])
                    k.dma("pool", S["OCT"][rows, t0:t0 + W], o_[:, 0:W], reads=[o_])
        return ph

    CH_E = [(c * 512, 512, c == 0, False, 0) for c in range(8)] + [(4096, 256, False, True, 0)]
    CH_OWN = [(c * 512, 512, c == 0, False, 0) for c in range(8)]
    phases.append(("inproj0", make_inproj(0, I["xt0"], CHUNKS_ALL)))
    phases.append(("filterL", make_filter("L", SEQ)))
    phases.append(("filterC", make_filter("C", CTX)))
    phases.append(("fftL", make_fftconvL2(E // 128)))
    phases.append(("fftC", make_fftconv("C", 4, CTX, SEQ, 2)))
    phases.append(("hycomb", make_hycombine(CH_E + [CTXCH])))
    phases.append(("attn0", make_attn(0)))
    phases.append(("outproj0", make_outproj(0, I["xt0"], S["XM"], CH_E + [CTXCH])))
    phases.append(("ffn0", make_ffn(0, S["XM"], S["X1"], CH_E + [CTXCH])))
    phases.append(("inproj1", make_inproj(1, S["X1"], CH_E + [CTXCH])))
    phases.append(("attn1", make_attn(1)))
    phases.append(("outproj1", make_outproj(1, S["X1"], S["XM1"], CH_E)))
    phases.append(("ffn1", make_ffn(1, S["XM1"], OUT, CH_OWN)))
    for nm, ph in phases:
        k.phase(ph)
        if stop_after == nm:
            break
    k.close()
    return nc


_CACHE = {}


def kernel(**inputs):
    inp = {kk: np.asarray(v) for kk, v in inputs.items()}
    if "C" not in _CACHE:
        C = _consts()
        C["fftL"] = _fft_consts(128); C["fftC"] = _fft_consts(4)
        C["filtL"] = _filter_consts(SEQ); C["filtC"] = _filter_consts(CTX)
        _CACHE["C"] = C
    C = _CACHE["C"]
    nc = _build()
    in_maps = []
    for core in range(8):
        b, hh = core // 2, core % 2
        in_maps.append(_host_prep(inp, b, hh, C))
    res = run_bass_kernel_spmd(nc, in_maps, core_ids=list(range(8)))
    out = np.empty((4, SEQ, D), np.float32)
    for core in range(8):
        b, hh = core // 2, core % 2
        o = np.asarray(res.results[core]["out"]).T
        if hh == 0:
            out[b, :OWN] = o
        else:
            out[b, OWN:] = o[::-1]
    return out
```

```python
import numpy as np
import ml_dtypes
from contextlib import ExitStack
import concourse.bass as bass
import concourse.mybir as mybir
from concourse.bass_utils import run_bass_kernel_spmd

F32 = mybir.dt.float32
BF16 = mybir.dt.bfloat16
I32 = mybir.dt.int32
AF = mybir.ActivationFunctionType
ALU = mybir.AluOpType
NPBF = ml_dtypes.bfloat16

D = 1024; SEQ = 8192; CTX = 256; TT = SEQ + CTX; E = 4352; OWN = 4096
DFF = 2816; NM = 22
SAME_ENGINE_SYNC = {"act", "pool"}


class Res:
    __slots__ = ("name", "last_w", "reads", "excl")
    def __init__(self, name, excl=False):
        self.name = name; self.last_w = None; self.reads = {}; self.excl = excl


class Tl:
    def __init__(self, t, r):
        self.t = t; self.r = r
    def __getitem__(self, idx):
        return self.t[idx]


class K:
    ENGS = ("pe", "act", "dve", "pool", "sp")

    def __init__(self, nc):
        self.nc = nc
        self.es = ExitStack()
        self.sem = {}; self.cnt = {}
        for e in self.ENGS:
            self.sem[e] = self.es.enter_context(nc.semaphore("s_" + e))
            self.cnt[e] = 0
        self.dma_sems = {}
        self.dma_key = {}
        self.dma_rr = {}
        self.NDMASEM = {"sp": 32, "pool": 24, "act": 8, "pe": 4, "dve": 4}
        self.seen = {e: {} for e in self.ENGS}
        self.ops = {e: [] for e in self.ENGS}
        self.phase_es = None
        self.nres = 0
        self.ndma = 0

    def sbuf(self, shape, dt, name=None, persist=False):
        self.nres += 1
        name = (name or "t") + "_%d" % self.nres
        es = self.es if persist else self.phase_es
        t = es.enter_context(self.nc.sbuf_tensor(name, list(shape), dt))
        return Tl(t, Res(name))

    def psum(self, shape, dt, name=None):
        self.nres += 1
        name = (name or "p") + "_%d" % self.nres
        t = self.phase_es.enter_context(self.nc.psum_tensor(name, list(shape), dt))
        return Tl(t, Res(name, excl=True))

    def _need(self, reads, writes, eng=None):
        evs = []
        for r in reads:
            if r.last_w is not None: evs.append(r.last_w)
            if r.excl:
                evs.extend((kk[0], kk[1], v) for kk, v in r.reads.items() if not (kk[0] == "eng" and kk[1] == eng))
        for w in writes:
            if w.last_w is not None: evs.append(w.last_w)
            evs.extend((kk[0], kk[1], v) for kk, v in w.reads.items())
        return evs

    def _emit_waits(self, eng, evs, force_self=False):
        need = {}
        for kind, key, val in evs:
            if kind == "eng":
                if force_self == "never": force_self = False
                if key == eng and eng not in SAME_ENGINE_SYNC and not force_self: continue
                v = val
            else:
                v = val
            if self.seen[eng].get((kind, key), 0) >= v: continue
            if need.get((kind, key), 0) < v: need[(kind, key)] = v
        for (kind, key), v in need.items():
            self.seen[eng][(kind, key)] = v
            sem = self.sem[key] if kind == "eng" else self.dma_sems[key][0]
            self.ops[eng].append(lambda e, sem=sem, v=v: e.wait_ge(sem, v))

    def _commit(self, ev, reads, writes):
        for r in reads:
            kk = (ev[0], ev[1])
            if r.reads.get(kk, 0) < ev[2]: r.reads[kk] = ev[2]
        for w in writes:
            w.last_w = ev; w.reads = {}

    def op(self, eng, fn, reads=(), writes=(), force_self=False):
        reads = [x.r if isinstance(x, Tl) else x for x in reads]
        writes = [x.r if isinstance(x, Tl) else x for x in writes]
        self._emit_waits(eng, self._need(reads, writes, eng), force_self)
        self.cnt[eng] += 1
        sem = self.sem[eng]
        self.ops[eng].append(lambda e, fn=fn, sem=sem: fn(e).then_inc(sem, 1))
        self._commit(("eng", eng, self.cnt[eng]), reads, writes)

    def dma(self, q, out, in_, reads=(), writes=(), **kw):
        reads = [x.r if isinstance(x, Tl) else x for x in reads]
        writes = [x.r if isinstance(x, Tl) else x for x in writes]
        npool = self.NDMASEM[q]
        idx = (q, self.dma_rr.get(q, 0) % npool)
        self.dma_rr[q] = self.dma_rr.get(q, 0) + 1
        if idx not in self.dma_sems:
            s_ = self.es.enter_context(self.nc.semaphore("d_%s_%d" % idx))
            self.dma_sems[idx] = [s_, 0]
        ent = self.dma_sems[idx]
        evs = self._need(reads, writes, q)
        if ent[1] > 0:
            evs.append(("dma", idx, ent[1] * 16))
        self._emit_waits(q, evs)
        ent[1] += 1
        sem = ent[0]
        self.ndma += 1
        self.ops[q].append(lambda e, out=out, in_=in_, sem=sem, kw=kw: e.dma_start(out=out, in_=in_, **kw).then_inc(sem, 16))
        self._commit(("dma", idx, ent[1] * 16), reads, writes)

    def barrier(self):
        evs = [("eng", e, self.cnt[e]) for e in self.ENGS if self.cnt[e] > 0]
        evs += [("dma", kk, v[1] * 16) for kk, v in self.dma_sems.items() if v[1] > 0]
        for e in self.ENGS:
            self._emit_waits(e, [ev for ev in evs if not (ev[0] == "eng" and ev[1] == e)])

    def phase(self, body):
        with ExitStack() as pes:
            self.phase_es = pes
            body()
            self.barrier()
            ops = self.ops
            self.ops = {e: [] for e in self.ENGS}
            with self.nc.Block() as block:
                @block.tensor
                def _(e):
                    for f in ops["pe"]: f(e)
                @block.scalar
                def _(e):
                    for f in ops["act"]: f(e)
                @block.vector
                def _(e):
                    for f in ops["dve"]: f(e)
                @block.gpsimd
                def _(e):
                    for f in ops["pool"]: f(e)
                @block.sync
                def _(e):
                    for f in ops["sp"]: f(e)
        self.phase_es = None

    def close(self):
        self.es.close()

    def ts(self, eng, out, in0, s1, s2, op0, op1, r, w, force_self=False):
        if s2 is None:
            s2 = 0.0; op1 = ALU.add
        self.op(eng, lambda e: e.tensor_scalar(out=out, in0=in0, scalar1=s1, scalar2=s2, op0=op0, op1=op1), r, w, force_self)
    def stt(self, eng, out, in0, sc, in1, op0, op1, r, w):
        self.op(eng, lambda e: e.scalar_tensor_tensor(out=out, in0=in0, scalar=sc, in1=in1, op0=op0, op1=op1), r, w)
    def tt(self, eng, out, in0, in1, op, r, w):
        self.op(eng, lambda e: e.tensor_tensor(out=out, in0=in0, in1=in1, op=op), r, w)
    def act(self, out, in_, func, r, w, bias=None, scale=None, accum=None):
        kw = {}
        if bias is not None: kw["bias"] = bias
        if scale is not None: kw["scale"] = scale
        if accum is not None: kw["accum_out"] = accum
        self.op("act", lambda e: e.activation(out=out, in_=in_, func=func, **kw), r, w)
    def mm(self, out, lhsT, rhs, start, stop, r, w):
        self.op("pe", lambda e: e.matmul(out, lhsT=lhsT, rhs=rhs, start=start, stop=stop), r, w)
    def copy(self, eng, out, in_, r, w):
        if eng == "act":
            self.op("act", lambda e: e.copy(out=out, in_=in_), r, w)
        else:
            self.op(eng, lambda e: e.tensor_copy(out=out, in_=in_), r, w)
    def memset(self, eng, ap, val, w):
        self.op(eng, lambda e: e.memset(ap, val), [], w)
    def recip(self, out, in_, r, w, force_self=False):
        self.op("dve", lambda e: e.reciprocal(out=out, in_=in_), r, w, force_self)

def _consts():
    c = {}
    nf = 16
    inv = 10000.0 ** (-np.arange(nf, dtype=np.float64) / nf)
    t = np.arange(SEQ)
    row = (t // 64).astype(np.float64); col = (t % 64).astype(np.float64)
    ar = row[None, :] * inv[:, None]; ac = col[None, :] * inv[:, None]
    cos64 = np.concatenate([np.cos(ar), np.cos(ar), np.cos(ac), np.cos(ac)], 0)
    sin64 = np.concatenate([-np.sin(ar), np.sin(ar), -np.sin(ac), np.sin(ac)], 0)
    c["cos64"] = cos64.astype(np.float32); c["sin64"] = sin64.astype(np.float32)
    perm = np.concatenate([np.arange(16) + 16, np.arange(16), np.arange(16) + 48, np.arange(16) + 32])
    c["perm64"] = perm
    p = np.arange(128)[:, None]; f = np.arange(512)[None, :]
    c["bmask"] = np.stack([(np.abs(128 * r + p - f) <= 128) for r in range(-1, 5)], 0).astype(NPBF)
    return c


def _fft_consts(NA):
    N = 128 * NA
    c = {}
    a = np.arange(NA)[:, None]; f1 = np.arange(NA)[None, :]
    th = 2 * np.pi * a * f1 / NA
    c["f1cs"] = np.concatenate([np.cos(th), -np.sin(th)], 1).astype(NPBF)
    p = np.arange(128)[:, None]; f2 = np.arange(128)[None, :]
    th2 = 2 * np.pi * p * f2 / 128
    C = np.cos(th2); S = np.sin(th2)
    c["fC"] = C.astype(NPBF); c["fS"] = S.astype(NPBF); c["fnS"] = (-S).astype(NPBF)
    c["fCS"] = np.concatenate([C, S], 1).astype(NPBF)
    c["fnSC"] = np.concatenate([-S, C], 1).astype(NPBF)
    tw = 2 * np.pi * np.arange(128)[:, None] * np.arange(NA)[None, :] / N
    G = 512 // (2 * NA)
    tc_ = np.cos(tw); ts_ = np.sin(tw)
    A = np.concatenate([tc_, tc_], 1)
    B = np.concatenate([ts_, -ts_], 1)
    c["twA"] = np.tile(A[:, None, :], (1, G, 1)).reshape(128, 512).astype(np.float32)
    c["twB"] = np.tile(B[:, None, :], (1, G, 1)).reshape(128, 512).astype(np.float32)
    tcT = np.cos(tw).T; tsT = np.sin(tw).T
    A2 = np.stack([tcT, tcT], 1)
    B2 = np.stack([tsT, -tsT], 1)
    c["twA2"] = np.tile(A2[:, None], (1, 4, 1, 1)).reshape(NA, 1024).astype(np.float32)
    c["twB2"] = np.tile(B2[:, None], (1, 4, 1, 1)).reshape(NA, 1024).astype(np.float32)
    th = 2 * np.pi * np.arange(NA)[:, None] * np.arange(NA)[None, :] / NA
    c["g3C"] = (np.cos(th) / N).astype(NPBF); c["g3nS"] = (-np.sin(th) / N).astype(NPBF)
    if NA == 128:
        pp_ = np.arange(128, dtype=np.int64)[:, None, None]; f1_ = np.arange(128, dtype=np.int64)[None, :, None]; f2_ = np.arange(128, dtype=np.int64)[None, None, :]
        ang = 2 * np.pi * ((pp_ * (f1_ + 128 * f2_)) % N).astype(np.float64) / N
        c["MC"] = np.cos(ang).astype(NPBF); c["MS"] = np.sin(ang).astype(NPBF)
    if NA == 4:
        f1i = np.arange(128) % 4
        twp = 2 * np.pi * f1i[:, None] * np.arange(128)[None, :] / N
        c["twA2c"] = np.concatenate([np.cos(twp), np.cos(twp)], 1).astype(np.float32)
        c["twB2c"] = np.concatenate([np.sin(twp), -np.sin(twp)], 1).astype(np.float32)
        gC = np.zeros((128, 64), np.float64); gS = np.zeros((128, 64), np.float64)
        for ch in range(32):
            for f1_ in range(4):
                for a_ in range(2):
                    gC[ch * 4 + f1_, ch * 2 + a_] = np.cos(2 * np.pi * f1_ * a_ / 4) / N
                    gS[ch * 4 + f1_, ch * 2 + a_] = -np.sin(2 * np.pi * f1_ * a_ / 4) / N
        c["gbC"] = gC.astype(NPBF); c["gbnS"] = gS.astype(NPBF)
    return c


def _filter_consts(n):
    q = np.arange(2 * n)
    tap = np.where(q < n, q, 2 * n - q).astype(np.int64)
    tap = np.minimum(tap, n - 1)
    t01 = np.linspace(0.0, 1.0, n, dtype=np.float32)
    bands = 16
    w = (2.0 * np.pi * np.arange(n, dtype=np.float32) / n).astype(np.float32)
    f = np.linspace(1e-4, bands - 1, bands, dtype=np.float32)[None, :]
    feats = np.concatenate([t01[:, None], np.cos(f * w[:, None]), -np.sin(f * w[:, None])], -1).astype(np.float32)
    featsT = np.ascontiguousarray(feats[tap].T)
    t01b = np.ascontiguousarray(np.tile(t01[tap][None, :], (128, 1)))
    deltas = np.abs(np.linspace(np.log(1e-2) / 1.5, np.log(1e-2) / 0.3, 512, dtype=np.float32))
    ndel = np.ascontiguousarray((-deltas).reshape(4, 128).T)
    return featsT.astype(np.float32), t01b.astype(np.float32), ndel.astype(np.float32)


def _pm(v, n):
    return np.ascontiguousarray(np.asarray(v, np.float32).reshape(n, 128).T)


def _host_prep(inp, b, hh, C):
    fl = (hh == 1)
    m = {}
    x = inp["x"][b]; cx = inp["ctx"][b]
    if fl: x = x[::-1]; cx = cx[::-1]
    m["xt0"] = np.ascontiguousarray(np.concatenate([x, cx], 0).T)
    cv = np.stack([inp["c"][b], inp["c_ctx"]], 1)
    m["cvec"] = np.ascontiguousarray(cv.reshape(8, 128, 2).transpose(1, 0, 2))
    perm = C["perm64"]
    for i in range(2):
        m[f"ada_w{i}"] = inp["ada_w"][i]
        m[f"ada_b{i}"] = _pm(inp["ada_b"][i], 48)
        m[f"nmix{i}"] = _pm(inp["norm_mix"][i], 8)
        m[f"nffn{i}"] = _pm(inp["norm_ffn"][i], 8)
        w = inp["mix_w_in"][i]
        qk = w[:, :768].reshape(D, 12, 64)[:, :, perm].reshape(D, 768)
        m[f"w_in{i}"] = np.ascontiguousarray(np.concatenate([w, qk], 1))
        m[f"w_out{i}"] = inp["mix_w_out"][i]
        gq = inp["attn_q_norm"][i]; gk = inp["attn_k_norm"][i]
        m[f"qkg{i}"] = np.ascontiguousarray(np.stack([np.tile(gq, 2), np.tile(gq[perm], 2), np.tile(gk, 2), np.tile(gk[perm], 2)], 1).astype(np.float32))
        m[f"w_up{i}"] = inp["ffn_w_up"][i]
        m[f"w_dn{i}"] = inp["ffn_w_down"][i]
        fw = inp["ffn_conv_w"][i]
        if fl: fw = fw[::-1]
        m[f"fcw{i}"] = np.ascontiguousarray(fw.reshape(3, NM, 128).transpose(2, 1, 0))
        m[f"fcb{i}"] = _pm(inp["ffn_conv_b"][i], NM)
    hw = inp["hy_conv_w"][0]
    if fl: hw = hw[::-1]
    m["hcw"] = np.ascontiguousarray(hw.reshape(3, 12, 128).transpose(2, 1, 0))
    m["hcb"] = _pm(inp["hy_conv_b"][0], 12)
    sw = inp["sc_conv_w"][0]
    if fl: sw = sw[::-1]
    m["scw"] = np.ascontiguousarray(sw.reshape(3, 4, 128).transpose(2, 1, 0))
    m["hw1"] = inp["hy_w1"][0]; m["hw2"] = inp["hy_w2"][0]; m["hw3"] = inp["hy_w3"][0]
    w4 = inp["hy_w4"][0]
    if fl: w4 = np.concatenate([w4[:, 512:], w4[:, :512]], 1)
    m["hw4"] = np.ascontiguousarray(w4)
    m["hb"] = np.ascontiguousarray(np.stack([inp["hy_b1"][0], inp["hy_b2"][0], inp["hy_b3"][0], inp["hy_freq"][0]], 1).astype(np.float32))
    m["hbd"] = _pm(inp["hy_bias_d"][0], 4)
    m["sink"] = np.ascontiguousarray(np.tile(inp["swa_sink"][0][None, :], (128, 1)).astype(np.float32))
    cos = C["cos64"]; sin = C["sin64"]
    if fl: cos = cos[:, ::-1]; sin = sin[:, ::-1]
    cosx = np.concatenate([cos, np.ones((64, CTX), np.float32)], 1)
    sinx = np.concatenate([sin, np.zeros((64, CTX), np.float32)], 1)
    m["cosT"] = np.ascontiguousarray(np.concatenate([cosx, cosx], 0))
    m["sinT"] = np.ascontiguousarray(np.concatenate([sinx, sinx], 0))
    m["bmask"] = C["bmask"]
    for tag, NA in (("L", 128), ("C", 4)):
        for kk, v in C["fft" + tag].items():
            m[kk + tag] = v
    for tag in ("L", "C"):
        ft, t01b, ndel = C["filt" + tag]
        m["featsT" + tag] = ft; m["t01b" + tag] = t01b
    m["ndel"] = C["filtL"][2]
    return m

def _build(stop_after=None, dbg=()):
    nc = bass.Bass("TRN2", target_bir_lowering=False)
    k = K(nc)
    def din(name, shape, dt=F32):
        return nc.dram_tensor(name, list(shape), dt, kind="ExternalInput").ap()
    def dscr(name, shape, dt):
        kind = "ExternalOutput" if name in dbg else "Internal"
        return nc.dram_tensor(name, list(shape), dt, kind=kind).ap()
    I = {}
    I["xt0"] = din("xt0", [D, TT]); I["cvec"] = din("cvec", [128, 8, 2])
    for i in range(2):
        I[f"ada_w{i}"] = din(f"ada_w{i}", [D, 6 * D]); I[f"ada_b{i}"] = din(f"ada_b{i}", [128, 48])
        I[f"nmix{i}"] = din(f"nmix{i}", [128, 8]); I[f"nffn{i}"] = din(f"nffn{i}", [128, 8])
        I[f"w_in{i}"] = din(f"w_in{i}", [D, 3328]); I[f"w_out{i}"] = din(f"w_out{i}", [D, D])
        I[f"qkg{i}"] = din(f"qkg{i}", [128, 4])
        I[f"w_up{i}"] = din(f"w_up{i}", [D, 2 * DFF]); I[f"w_dn{i}"] = din(f"w_dn{i}", [DFF, D])
        I[f"fcw{i}"] = din(f"fcw{i}", [128, NM, 3]); I[f"fcb{i}"] = din(f"fcb{i}", [128, NM])
    I["hcw"] = din("hcw", [128, 12, 3]); I["hcb"] = din("hcb", [128, 12]); I["scw"] = din("scw", [128, 4, 3])
    I["hw1"] = din("hw1", [33, 64]); I["hw2"] = din("hw2", [64, 64]); I["hw3"] = din("hw3", [64, 64])
    I["hw4"] = din("hw4", [64, 1024]); I["hb"] = din("hb", [64, 4]); I["hbd"] = din("hbd", [128, 4])
    I["sink"] = din("sink", [128, 8])
    I["cosT"] = din("cosT", [128, TT]); I["sinT"] = din("sinT", [128, TT])
    I["bmask"] = din("bmask", [6, 128, 512], BF16)
    for tag, NA in (("L", 128), ("C", 4)):
        I["f1cs" + tag] = din("f1cs" + tag, [NA, 2 * NA], BF16)
        for nm in ("fC", "fS", "fnS"): I[nm + tag] = din(nm + tag, [128, 128], BF16)
        for nm in ("fCS", "fnSC"): I[nm + tag] = din(nm + tag, [128, 256], BF16)
        for nm in ("twA", "twB"): I[nm + tag] = din(nm + tag, [128, 512])
        for nm in ("twA2", "twB2"): I[nm + tag] = din(nm + tag, [NA, 1024])
        for nm in ("g3C", "g3nS"): I[nm + tag] = din(nm + tag, [NA, NA], BF16)
        if NA == 128:
            for nm in ("MC", "MS"): I[nm + tag] = din(nm + tag, [128, 128, 128], BF16)
        if NA == 4:
            for nm in ("twA2c", "twB2c"): I[nm + tag] = din(nm + tag, [128, 256])
            for nm in ("gbC", "gbnS"): I[nm + tag] = din(nm + tag, [128, 64], BF16)
        n = 64 * NA
        I["featsT" + tag] = din("featsT" + tag, [33, 2 * n]); I["t01b" + tag] = din("t01b" + tag, [128, 2 * n])
    I["ndel"] = din("ndel", [128, 4])
    OUT = nc.dram_tensor("out", [D, OWN], F32, kind="ExternalOutput").ap()

    S = {}
    for i in range(2):
        S[f"bw_in{i}"] = dscr(f"bw_in{i}", [D, 3328], BF16); S[f"bw_out{i}"] = dscr(f"bw_out{i}", [D, D], BF16)
        S[f"bw_up{i}"] = dscr(f"bw_up{i}", [D, 2 * DFF], BF16); S[f"bw_dn{i}"] = dscr(f"bw_dn{i}", [DFF, D], BF16)
    S["QT"] = dscr("QT", [8, 64, TT], BF16); S["KT"] = dscr("KT", [4, 64, TT], BF16); S["V"] = dscr("V", [TT, 256], BF16)
    S["UT"] = dscr("UT", [512, TT], BF16); S["X0T"] = dscr("X0T", [512, TT], F32)
    S["UF"] = dscr("UF", [512, TT], F32)
    S["UTA"] = dscr("UTA", [64, 512, 128], BF16); S["KNA"] = dscr("KNA", [128, 512, 128], BF16)
    S["YT"] = dscr("YT", [512, TT], F32)
    S["OAT"] = dscr("OAT", [512, TT], BF16); S["OCT"] = dscr("OCT", [512, TT], BF16)
    S["XM"] = dscr("XM", [D, TT], F32); S["X1"] = dscr("X1", [D, TT], F32); S["XM1"] = dscr("XM1", [D, TT], F32)
    S["KRAWL"] = dscr("KRAWL", [512, 2 * SEQ], F32); S["KNL"] = dscr("KNL", [512, 2 * SEQ], BF16)
    S["KRAWC"] = dscr("KRAWC", [512, 2 * CTX], F32); S["KNC"] = dscr("KNC", [512, 2 * CTX], BF16)

    MOD = k.sbuf([128, 2, 48, 2], F32, "mod", persist=True)
    AB = k.sbuf([128, 2, 2, 2, 8, 2], F32, "ab", persist=True)
    ONES = k.sbuf([128, 128], BF16, "ones", persist=True)
    BONES = k.sbuf([128, 128], BF16, "bones", persist=True)
    SEL = k.sbuf([128, 64], F32, "sel", persist=True)
    ESINK = k.sbuf([128, 8], F32, "esink", persist=True)
    RNORM = {"L": k.sbuf([128, 4], F32, "rnL", persist=True), "C": k.sbuf([128, 4], F32, "rnC", persist=True)}

    phases = []

    conv_items = []
    for i in range(2):
        for src, dst, rows, cols in ((f"w_in{i}", f"bw_in{i}", D, 3328), (f"w_out{i}", f"bw_out{i}", D, D),
                                     (f"w_up{i}", f"bw_up{i}", D, 2 * DFF), (f"w_dn{i}", f"bw_dn{i}", DFF, D)):
            for r0 in range(0, rows, 128):
                for c0 in range(0, cols, 2048):
                    conv_items.append((src, dst, r0, c0, min(2048, cols - c0)))
    CONV_EARLY = sum(1 for it in conv_items if it[0] in ("w_in0", "w_out0"))
    conv_pos = [0]

    class make_converter:
        def __init__(self):
            self.stg = [k.sbuf([128, 2048], F32, "stg") for _ in range(3)]
            self.stb = [k.sbuf([128, 2048], BF16, "stb") for _ in range(3)]
        def step(self, eng=None, q="sp"):
            n = conv_pos[0]
            if n >= len(conv_items): return False
            conv_pos[0] += 1
            src, dst, r0, c0, cw = conv_items[n]
            a = self.stg[n % 3]; bt = self.stb[n % 3]
            k.dma(q, a[:, 0:cw], I[src][r0:r0 + 128, c0:c0 + cw], writes=[a])
            k.copy(eng or ("dve" if n % 2 == 0 else "act"), bt[:, 0:cw], a[:, 0:cw], [a], [bt])
            k.dma(q, S[dst][r0:r0 + 128, c0:c0 + cw], bt[:, 0:cw], reads=[bt])
            return True

    def ph_setup():
        k.memset("dve", ONES[:], 1.0, [ONES])
        k.memset("dve", BONES[:], 0.0, [BONES])
        k.memset("dve", BONES[0:64, 0:64], 1.0, [BONES])
        k.memset("dve", BONES[64:128, 64:128], 1.0, [BONES])
        k.memset("dve", SEL[:], 0.0, [SEL])
        k.memset("dve", SEL[64:65, :], 1.0, [SEL])
        snk = k.sbuf([128, 8], F32)
        k.dma("sp", snk[:], I["sink"][:, :], writes=[snk])
        k.act(ESINK[:], snk[:], AF.Exp, [snk], [ESINK])
        cv_ = make_converter()
        for _ in range(CONV_EARLY):
            cv_.step()
        cv = k.sbuf([128, 8, 2], F32)
        k.dma("sp", cv[:], I["cvec"][:, :, :], writes=[cv])
        sc = k.sbuf([128, 8, 2], F32)
        k.act(sc[:], cv[:], AF.Silu, [cv], [sc])
        wst = [k.sbuf([128, 8, 512], F32, "wst") for _ in range(2)]
        ps = [k.psum([128, 512], F32) for _ in range(2)]
        n = 0
        for i in range(2):
            adb = k.sbuf([128, 48], F32)
            k.dma("sp", adb[:], I[f"ada_b{i}"][:, :], writes=[adb])
            for cb in range(12):
                wt = wst[n % 2]; n += 1
                k.dma("sp", wt[:], I[f"ada_w{i}"][:, cb * 512:(cb + 1) * 512].rearrange("(k p) f -> p k f", p=128), writes=[wt])
                for mi in range(4):
                    m = cb * 4 + mi
                    pt = ps[m % 2]
                    for kk in range(8):
                        k.mm(pt[:, 0:2], wt[:, kk, mi * 128:(mi + 1) * 128], sc[:, kk, :], kk == 0, kk == 7, [wt, sc], [pt])
                    k.ts("dve", MOD[:, i, m, :], pt[:, 0:2], adb[:, m:m + 1], None, ALU.add, None, [pt, adb], [MOD])
            for wh, (nm, sh0, sc0) in enumerate(((f"nmix{i}", 0, 8), (f"nffn{i}", 24, 32))):
                g = k.sbuf([128, 8], F32)
                k.dma("sp", g[:], I[nm][:, :], writes=[g])
                for j in range(2):
                    k.stt("dve", AB[:, i, wh, 0, :, j], MOD[:, i, sc0:sc0 + 8, j], 1.0, g[:], ALU.add, ALU.mult, [MOD, g], [AB])
                    k.copy("dve", AB[:, i, wh, 1, :, j], MOD[:, i, sh0:sh0 + 8, j], [MOD], [AB])
    phases.append(("setup", ph_setup))

    def norm_mod(xh, Wc, i, wh, j, sq, ssps, rstd, tmp, h):
        for kk in range(8):
            k.act(sq[:, kk, 0:Wc], xh[:, kk, 0:Wc], AF.Square, [xh], [sq])
        for (c0, c1) in ((0, min(512, Wc)), (512, Wc)):
            if c1 <= c0: continue
            for kk in range(8):
                k.mm(ssps[:, c0:c1], ONES[:, :], sq[:, kk, c0:c1], kk == 0, kk == 7, [ONES, sq], [ssps])
        k.act(rstd[:, 0:Wc], ssps[:, 0:Wc], AF.Sqrt, [ssps], [rstd], bias=1e-6, scale=1.0 / D)
        k.recip(rstd[:, 0:Wc], rstd[:, 0:Wc], [rstd], [rstd])
        for kk in range(8):
            t = tmp[kk % 2]
            k.stt("dve", t[:, 0:Wc], xh[:, kk, 0:Wc], AB[:, i, wh, 0, kk, j:j + 1], rstd[:, 0:Wc], ALU.mult, ALU.mult, [xh, AB, rstd], [t])
            k.act(h[:, kk, 0:Wc], t[:, 0:Wc], AF.Identity, [t, AB], [h], bias=AB[:, i, wh, 1, kk, j:j + 1], scale=1.0)

    def norm_mod_gen(xh, Wc, i, wh, j, sq, ssps, rstd, tmp, h):
        for kk in range(8):
            k.act(sq[:, kk, 0:Wc], xh[:, kk, 0:Wc], AF.Square, [xh], [sq])
        yield
        yield
        for (c0, c1) in ((0, min(512, Wc)), (512, Wc)):
            if c1 <= c0: continue
            for kk in range(8):
                k.mm(ssps[:, c0:c1], ONES[:, :], sq[:, kk, c0:c1], kk == 0, kk == 7, [ONES, sq], [ssps])
        yield
        k.act(rstd[:, 0:Wc], ssps[:, 0:Wc], AF.Sqrt, [ssps], [rstd], bias=1e-6, scale=1.0 / D)
        k.recip(rstd[:, 0:Wc], rstd[:, 0:Wc], [rstd], [rstd])
        yield
        for kk in range(8):
            t = tmp[kk % 2]
            k.stt("dve", t[:, 0:Wc], xh[:, kk, 0:Wc], AB[:, i, wh, 0, kk, j:j + 1], rstd[:, 0:Wc], ALU.mult, ALU.mult, [xh, AB, rstd], [t])
            k.act(h[:, kk, 0:Wc], t[:, 0:Wc], AF.Identity, [t, AB], [h], bias=AB[:, i, wh, 1, kk, j:j + 1], scale=1.0)
            if kk % 2 == 1: yield

    CHUNKS_ALL = [(c * 512, 512, c == 0, c == 15, 0) for c in range(16)] + [(SEQ, CTX, True, True, 1)]
    CHUNKS_E = [(c * 512, 512, c == 0, False, 0) for c in range(9)]
    CTXCH = (SEQ, CTX, True, True, 1)

    def load_xh(xh, src, t0, W, ledge, redge, q="sp"):
        lo = 2 if ledge else 1
        hi = W + 2 if redge else W + 3
        if ledge: k.memset("pool", xh[:, :, 1:2], 0.0, [xh])
        if redge: k.memset("pool", xh[:, :, W + 2:W + 3], 0.0, [xh])
        k.dma(q, xh[:, :, lo:hi], src[:, t0 - 2 + lo:t0 - 2 + hi].rearrange("(k p) t -> p k t", p=128), writes=[xh])

    def make_inproj(i, XIN, chunks):
        def ph():
            w = k.sbuf([128, 8, 3328], BF16, "w_in")
            for kk in range(8):
                k.dma("sp", w[:, kk, :], S[f"bw_in{i}"][kk * 128:(kk + 1) * 128, :], writes=[w])
            qkg = k.sbuf([128, 4], F32); k.dma("sp", qkg[:], I[f"qkg{i}"][:, :], writes=[qkg])
            if i == 0:
                cw = k.sbuf([128, 12, 3], F32); k.dma("sp", cw[:], I["hcw"][:, :, :], writes=[cw])
                cb = k.sbuf([128, 12], F32); k.dma("sp", cb[:], I["hcb"][:, :], writes=[cb])
            else:
                cw = k.sbuf([128, 4, 3], F32); k.dma("sp", cw[:], I["scw"][:, :, :], writes=[cw])
            xhs = [k.sbuf([128, 8, 516], F32, "xh") for _ in range(2)]
            for t_ in xhs: k.memset("pool", t_[:], 0.0, [t_])
            sq = k.sbuf([128, 8, 516], BF16, "sq")
            hs = [k.sbuf([128, 8, 516], BF16, "h") for _ in range(2)]
            rstd = k.sbuf([128, 516], F32, "rstd")
            tmp = [k.sbuf([128, 516], F32, "tmp") for _ in range(2)]
            ssps = k.psum([128, 1024], F32, "ssps")
            zps = [k.psum([128, 512], F32, "zps") for _ in range(4)]
            cps = k.psum([128, 1024], F32, "cps")
            cosbs = [k.sbuf([128, 512], F32, "cos") for _ in range(2)]; sinbs = [k.sbuf([128, 512], F32, "sin") for _ in range(2)]
            sq2 = [k.sbuf([128, 512], BF16, "sq2") for _ in range(2)]
            rs = [k.sbuf([128, 512], F32, "rs") for _ in range(2)]
            ta = [k.sbuf([128, 512], F32, "ta") for _ in range(2)]
            tb = [k.sbuf([128, 512], F32, "tb") for _ in range(2)]
            qo = [k.sbuf([128, 512], BF16, "qo") for _ in range(2)]
            vo = [k.sbuf([128, 256], BF16, "vo") for _ in range(2)]
            asb = [k.sbuf([128, 516], F32, "asb") for _ in range(3)]
            c1 = [k.sbuf([128, 512], F32, "c1") for _ in range(3)]
            uo = [k.sbuf([128, 512], F32, "uo") for _ in range(2)]
            ub = [k.sbuf([128, 512], BF16, "ub") for _ in range(2)]
            pp = [k.sbuf([128, 516], F32, "pp") for _ in range(2)]
            nq = [0]
            import os
            PARTS = os.environ.get("INPROJ_PARTS", "nqvc")
            NCHK = int(os.environ.get("INPROJ_NCH", "99"))
            CHL = chunks[:NCHK]
            def issue_loads(cj):
                t0_, W_, le_, re_, j_ = CHL[cj]
                load_xh(xhs[cj % 2], XIN, t0_, W_, le_, re_)
                k.dma("sp", cosbs[cj % 2][:, 0:W_], I["cosT"][:, t0_:t0_ + W_], writes=[cosbs[cj % 2]])
                k.dma("sp", sinbs[cj % 2][:, 0:W_], I["sinT"][:, t0_:t0_ + W_], writes=[sinbs[cj % 2]])
            for ci, (t0, W, le, re, j) in enumerate(CHL):
                Wc = W + 4
                do_q = (t0 < E) or j == 1
                if i == 1 and j == 1: do_q = False
                xh = xhs[ci % 2]; cosb = cosbs[ci % 2]; sinb = sinbs[ci % 2]; h = hs[ci % 2]
                if ci == 0:
                    issue_loads(0)
                    for _ in norm_mod_gen(xh, Wc, i, 0, j, sq, ssps, rstd, tmp, h): pass
                gnx = None
                if ci + 1 < len(CHL):
                    issue_loads(ci + 1)
                    gnx = norm_mod_gen(xhs[(ci + 1) % 2], CHL[ci + 1][1] + 4, i, 0, CHL[ci + 1][4], sq, ssps, rstd, tmp, hs[(ci + 1) % 2])
                def tick():
                    if gnx is not None: next(gnx, None)
                for pr in range(6):
                    if "q" not in PARTS: continue
                    if pr < 4 and not do_q: continue
                    tick()
                    n = nq[0]; nq[0] += 1
                    zp = zps[(2 * n) % 4]; zsp = zps[(2 * n + 1) % 4]
                    for kk in range(8):
                        k.mm(zp[:, 0:W], w[:, kk, pr * 128:(pr + 1) * 128], h[:, kk, 2:W + 2], kk == 0, kk == 7, [w, h], [zp])
                    for kk in range(8):
                        k.mm(zsp[:, 0:W], w[:, kk, 2560 + pr * 128:2560 + (pr + 1) * 128], h[:, kk, 2:W + 2], kk == 0, kk == 7, [w, h], [zsp])
                    s2 = sq2[n % 2]; r_ = rs[n % 2]; a_ = ta[n % 2]; b_ = tb[n % 2]; q_ = qo[n % 2]
                    gi = 0 if pr < 4 else 2
                    k.act(s2[:, 0:W], zp[:, 0:W], AF.Square, [zp], [s2])
                    k.stt("dve", a_[:, 0:W], zp[:, 0:W], qkg[:, gi:gi + 1], cosb[:, 0:W], ALU.mult, ALU.mult, [zp, qkg, cosb], [a_])
                    k.stt("dve", b_[:, 0:W], zsp[:, 0:W], qkg[:, gi + 1:gi + 2], sinb[:, 0:W], ALU.mult, ALU.mult, [zsp, qkg, sinb], [b_])
                    k.mm(zp[:, 0:W], BONES[:, :], s2[:, 0:W], True, True, [BONES, s2], [zp])
                    k.act(r_[:, 0:W], zp[:, 0:W], AF.Sqrt, [zp], [r_], bias=1e-6, scale=1.0 / 64)
                    k.recip(r_[:, 0:W], r_[:, 0:W], [r_], [r_])
                    QS = os.environ.get("QSKIP", "")
                    pe_ = "dve" if "pool" in QS else "pool"
                    k.tt(pe_, a_[:, 0:W], a_[:, 0:W], b_[:, 0:W], ALU.add, [a_, b_], [a_])
                    k.tt(pe_, q_[:, 0:W], a_[:, 0:W], r_[:, 0:W], ALU.mult, [a_, r_], [q_])
                    for hf in range(2):
                        if "dma" in QS: continue
                        if pr < 4:
                            dst = S["QT"][2 * pr + hf, :, t0:t0 + W]
                        else:
                            dst = S["KT"][2 * (pr - 4) + hf, :, t0:t0 + W]
                        k.dma("sp", dst, q_[hf * 64:(hf + 1) * 64, 0:W], reads=[q_])
                for tj in range(W // 128):
                    if "v" not in PARTS: continue
                    tick()
                    n = nq[0]; nq[0] += 1
                    vp = zps[n % 4]
                    for kk in range(8):
                        k.mm(vp[:, 0:256], h[:, kk, 2 + tj * 128:2 + (tj + 1) * 128], w[:, kk, 768:1024], kk == 0, kk == 7, [w, h], [vp])
                    v_ = vo[n % 2]
                    k.copy("act", v_[:, :], vp[:, 0:256], [vp], [v_])
                    k.dma("sp", S["V"][t0 + tj * 128:t0 + (tj + 1) * 128, :], v_[:, :], reads=[v_])
                def convproj(m, dst):
                    for (c0, c1_) in ((0, min(512, Wc)), (512, Wc)):
                        if c1_ <= c0: continue
                        for kk in range(8):
                            k.mm(cps[:, c0:c1_], w[:, kk, 1024 + m * 128:1024 + (m + 1) * 128], h[:, kk, c0:c1_], kk == 0, kk == 7, [w, h], [cps])
                    k.copy("act", dst[:, 0:Wc], cps[:, 0:Wc], [cps], [dst])
                    if le: k.memset("pool", dst[:, 1:2], 0.0, [dst])
                    if re: k.memset("pool", dst[:, W + 2:W + 3], 0.0, [dst])
                def conv3(out, a, wts, m, bias, eng="dve"):
                    if bias is not None:
                        k.ts(eng, out[:, 0:W], a[:, 2:W + 2], wts[:, m, 1:2], bias, ALU.mult, ALU.add, [a, wts, cb], [out])
                    else:
                        k.ts(eng, out[:, 0:W], a[:, 2:W + 2], wts[:, m, 1:2], None, ALU.mult, None, [a, wts], [out])
                    k.stt(eng, out[:, 0:W], a[:, 1:W + 1], wts[:, m, 0:1], out[:, 0:W], ALU.mult, ALU.add, [a, wts, out], [out])
                    k.stt(eng, out[:, 0:W], a[:, 3:W + 3], wts[:, m, 2:3], out[:, 0:W], ALU.mult, ALU.add, [a, wts, out], [out])
                for jc in range(4):
                    if "c" not in PARTS: continue
                    tick(); tick()
                    if i == 0:
                        convproj(4 + jc, asb[0]); conv3(c1[0], asb[0], cw, 4 + jc, cb[:, 4 + jc:5 + jc])
                        convproj(8 + jc, asb[1]); conv3(c1[1], asb[1], cw, 8 + jc, cb[:, 8 + jc:9 + jc])
                        u_ = uo[jc % 2]; ub_ = ub[jc % 2]
                        k.tt("dve", u_[:, 0:W], c1[0][:, 0:W], c1[1][:, 0:W], ALU.mult, [c1[0], c1[1]], [u_])
                        k.copy("pool", ub_[:, 0:W], u_[:, 0:W], [u_], [ub_])
                        k.dma("sp", S["UF"][jc * 128:(jc + 1) * 128, t0:t0 + W], u_[:, 0:W], reads=[u_])
                        if j == 0:
                            k.dma("sp", S["UTA"][t0 // 128:t0 // 128 + 4, jc * 128:(jc + 1) * 128, :].rearrange("a c p -> c a p"),
                                  ub_[:, 0:W].rearrange("c (a p) -> c a p", p=128), reads=[ub_])
                        else:
                            k.dma("sp", S["UT"][jc * 128:(jc + 1) * 128, t0:t0 + W], ub_[:, 0:W], reads=[ub_])
                        if do_q:
                            convproj(jc, asb[2]); conv3(c1[2], asb[2], cw, jc, cb[:, jc:jc + 1])
                            k.dma("sp", S["X0T"][jc * 128:(jc + 1) * 128, t0:t0 + W], c1[2][:, 0:W], reads=[c1[2]])
                    elif j == 0:
                        convproj(4 + jc, asb[0]); convproj(8 + jc, asb[1])
                        p_ = pp[jc % 2]
                        k.tt("dve", p_[:, 0:Wc], asb[0][:, 0:Wc], asb[1][:, 0:Wc], ALU.mult, [asb[0], asb[1]], [p_])
                        conv3(c1[0], p_, cw, jc, None)
                        convproj(jc, asb[2])
                        ub_ = ub[jc % 2]
                        k.tt("pool", ub_[:, 0:W], c1[0][:, 0:W], asb[2][:, 2:W + 2], ALU.mult, [c1[0], asb[2]], [ub_])
                        k.dma("sp", S["OCT"][jc * 128:(jc + 1) * 128, t0:t0 + W], ub_[:, 0:W], reads=[ub_])
                if gnx is not None:
                    for _ in gnx: pass
        return ph

    def make_attn(i):
        def ph():
            NKT = TT // 128
            kT = k.sbuf([128, TT], BF16, "kT")
            k.memset("pool", kT[64:128, :], 0.0, [kT])
            va = k.sbuf([128, NKT, 65], BF16, "va")
            qT = [k.sbuf([128, 512], BF16, "qT") for _ in range(2)]
            for t_ in qT: k.memset("pool", t_[64:128, :], 0.0, [t_])
            if i == 1:
                q4 = [[k.sbuf([128, 512], BF16, "q4") for _ in range(2)] for _ in range(2)]
                for g_ in range(2):
                    for t_ in q4[g_]: k.memset("pool", t_[64:128, :], 0.0, [t_])
            sps = [k.psum([128, 512], F32, "sps") for _ in range(4)]
            ops_ = [k.psum([128, 512], F32, "ops") for _ in range(2)]
            bps = k.psum([128, 512], F32, "bps")
            pT = [k.sbuf([128, 512], BF16, "pT") for _ in range(4)]
            osb = [k.sbuf([65, 512], F32, "osb") for _ in range(2)]
            rb = [k.sbuf([64, 512], F32, "rb") for _ in range(2)]
            ob = [k.sbuf([64, 512], BF16, "ob") for _ in range(2)]
            if i == 1:
                bm = k.sbuf([128, 6, 512], BF16, "bm")
                for r in range(6):
                    k.dma("sp", bm[:, r, :], I["bmask"][r, :, :], writes=[bm])
            qch = []
            for c in range(9):
                t0 = c * 512
                Wq = 512 if c < 8 else 256
                if i == 0:
                    kts = [(kt, 0, Wq, None) for kt in range(NKT)]
                else:
                    kts = []
                    for r in range(-1, 5):
                        kt = 4 * c + r
                        if kt < 0 or kt >= E // 128: continue
                        f0 = max(0, 128 * (r - 1)); f1 = min(Wq, 128 * (r + 2))
                        if f1 <= f0: continue
                        kts.append((kt, f0, f1, r + 1))
                    kts += [(64, 0, Wq, None), (65, 0, Wq, None)]
                qch.append((t0, Wq, kts))
            if i == 0:
                qch.append((SEQ, CTX, [(64, 0, CTX, None), (65, 0, CTX, None)]))
            n = 0; nh = 0
            cvt = make_converter() if i == 0 else None
            for jkv in range(4):
                k.dma("sp", kT[0:64, :], S["KT"][jkv, :, :], writes=[kT])
                k.dma("sp", va[:, :, 0:64], S["V"][:, jkv * 64:(jkv + 1) * 64].rearrange("(n p) d -> p n d", p=128), writes=[va])
                k.memset("pool", va[:, :, 64:65], 1.0, [va])
                if i == 1:
                    for ci_, (t0, W, kts) in enumerate(qch):
                        nk = len(kts)
                        qg = [q4[g][ci_ % 2] for g in range(2)]
                        for g in range(2):
                            k.dma("sp", qg[g][0:64, 0:W], S["QT"][2 * jkv + g, :, t0:t0 + W], writes=[qg[g]])
                        def smm2(g, idx):
                            kt, f0, f1, mi = kts[idx]
                            sp_ = sps[(2 * idx + g) % 4]
                            k.mm(sp_[:, f0:f1], kT[:, kt * 128:(kt + 1) * 128], qg[g][:, f0:f1], True, True, [kT, qg[g]], [sp_])
                        smm2(0, 0); smm2(1, 0)
                        for idx in range(nk):
                            if idx + 1 < nk:
                                smm2(0, idx + 1); smm2(1, idx + 1)
                            kt, f0, f1, mi = kts[idx]
                            for g in range(2):
                                sp_ = sps[(2 * idx + g) % 4]; p_ = pT[(2 * idx + g) % 4]
                                if mi is not None and (f0 > 0 or f1 < W):
                                    k.memset("dve", p_[:, 0:W], 0.0, [p_])
                                k.act(p_[:, f0:f1], sp_[:, f0:f1], AF.Exp, [sp_], [p_], scale=0.125)
                                if mi is not None:
                                    k.tt("dve", p_[:, f0:f1], p_[:, f0:f1], bm[:, mi, f0:f1], ALU.mult, [p_, bm], [p_])
                                k.mm(ops_[g][0:65, 0:W], va[:, kt, :], p_[:, 0:W], idx == 0, idx == nk - 1, [va, p_], [ops_[g]])
                        for g in range(2):
                            hq = 2 * jkv + g
                            o_ = osb[g]; r_ = rb[g]; b_ = ob[g]
                            k.copy("act", o_[:, 0:W], ops_[g][0:65, 0:W], [ops_[g]], [o_])
                            k.mm(bps[0:64, 0:W], SEL[0:65, :], o_[0:65, 0:W], True, True, [SEL, o_], [bps])
                            k.ts("dve", r_[:, 0:W], bps[0:64, 0:W], ESINK[0:64, hq:hq + 1], None, ALU.add, None, [bps, ESINK], [r_])
                            k.recip(r_[:, 0:W], r_[:, 0:W], [r_], [r_])
                            k.tt("dve", b_[:, 0:W], o_[0:64, 0:W], r_[:, 0:W], ALU.mult, [o_, r_], [b_])
                            k.dma("pool", S["OAT"][hq * 64:(hq + 1) * 64, t0:t0 + W], b_[:, 0:W], reads=[b_])
                    continue
                for g in range(2):
                    hq = 2 * jkv + g
                    for (t0, W, kts) in qch:
                        q_ = qT[nh % 2]; op_ = ops_[nh % 2]; o_ = osb[nh % 2]; r_ = rb[nh % 2]; b_ = ob[nh % 2]
                        nh += 1
                        k.dma("sp", q_[0:64, 0:W], S["QT"][hq, :, t0:t0 + W], writes=[q_])
                        if i == 1:
                            pass
                        nk = len(kts)
                        def smm(idx):
                            kt, f0, f1, mi = kts[idx]
                            sp_ = sps[(n + idx) % 4]
                            k.mm(sp_[:, f0:f1], kT[:, kt * 128:(kt + 1) * 128], q_[:, f0:f1], True, True, [kT, q_], [sp_])
                        smm(0)
                        if nk > 1: smm(1)
                        for idx in range(nk):
                            if idx + 2 < nk: smm(idx + 2)
                            kt, f0, f1, mi = kts[idx]
                            sp_ = sps[(n + idx) % 4]; p_ = pT[(n + idx) % 4]
                            if mi is not None and (f0 > 0 or f1 < W):
                                k.memset("dve", p_[:, 0:W], 0.0, [p_])
                            k.act(p_[:, f0:f1], sp_[:, f0:f1], AF.Exp, [sp_], [p_], scale=0.125)
                            if mi is not None:
                                k.tt("dve", p_[:, f0:f1], p_[:, f0:f1], bm[:, mi, f0:f1], ALU.mult, [p_, bm], [p_])
                            k.mm(op_[0:65, 0:W], va[:, kt, :], p_[:, 0:W], idx == 0, idx == nk - 1, [va, p_], [op_])
                        n += nk
                        k.copy("act", o_[:, 0:W], op_[0:65, 0:W], [op_], [o_])
                        k.mm(bps[0:64, 0:W], SEL[0:65, :], o_[0:65, 0:W], True, True, [SEL, o_], [bps])
                        if i == 1:
                            k.ts("dve", r_[:, 0:W], bps[0:64, 0:W], ESINK[0:64, hq:hq + 1], None, ALU.add, None, [bps, ESINK], [r_])
                            k.recip(r_[:, 0:W], r_[:, 0:W], [r_], [r_])
                        else:
                            k.recip(r_[:, 0:W], bps[0:64, 0:W], [bps], [r_])
                        k.tt("dve", b_[:, 0:W], o_[0:64, 0:W], r_[:, 0:W], ALU.mult, [o_, r_], [b_])
                        k.dma("pool", S["OAT"][hq * 64:(hq + 1) * 64, t0:t0 + W], b_[:, 0:W], reads=[b_])
                        if cvt is not None:
                            cvt.step("dve", "pool"); cvt.step("dve", "pool")
            if cvt is not None:
                while cvt.step("dve", "pool"): pass
        return ph

    def make_outproj(i, XIN, XOUT, chunks):
        def ph():
            wa = k.sbuf([128, 4, D], BF16, "woa")
            wc = k.sbuf([128, 4, D], BF16, "woc")
            k.dma("sp", wa[:], S[f"bw_out{i}"][0:512, :].rearrange("(c p) f -> p c f", p=128), writes=[wa])
            k.dma("sp", wc[:], S[f"bw_out{i}"][512:1024, :].rearrange("(c p) f -> p c f", p=128), writes=[wc])
            oa = [k.sbuf([128, 4, 512], BF16, "oa") for _ in range(2)]
            oc = [k.sbuf([128, 4, 512], BF16, "oc") for _ in range(2)]
            xs = [k.sbuf([128, 8, 512], F32, "xs") for _ in range(2)]
            xo = [k.sbuf([128, 8, 512], F32, "xo") for _ in range(2)]
            ps = [k.psum([128, 512], F32, "ps") for _ in range(4)]
            n = 0
            for ci, (t0, W, le, re, j) in enumerate(chunks):
                a_ = oa[ci % 2]; c_ = oc[ci % 2]; x_ = xs[ci % 2]; o_ = xo[ci % 2]
                k.dma("sp", a_[:, :, 0:W], S["OAT"][:, t0:t0 + W].rearrange("(c p) t -> p c t", p=128), writes=[a_])
                k.dma("sp", c_[:, :, 0:W], S["OCT"][:, t0:t0 + W].rearrange("(c p) t -> p c t", p=128), writes=[c_])
                k.dma("sp", x_[:, :, 0:W], XIN[:, t0:t0 + W].rearrange("(k p) t -> p k t", p=128), writes=[x_])
                for m in range(8):
                    p_ = ps[n % 4]; n += 1
                    for hh_ in range(4):
                        k.mm(p_[:, 0:W], wa[:, hh_, m * 128:(m + 1) * 128], a_[:, hh_, 0:W], hh_ == 0, False, [wa, a_], [p_])
                    for cc in range(4):
                        k.mm(p_[:, 0:W], wc[:, cc, m * 128:(m + 1) * 128], c_[:, cc, 0:W], False, cc == 3, [wc, c_], [p_])
                    k.stt("dve", o_[:, m, 0:W], p_[:, 0:W], MOD[:, i, 16 + m, j:j + 1], x_[:, m, 0:W], ALU.mult, ALU.add, [p_, MOD, x_], [o_])
                k.dma("pool", XOUT[:, t0:t0 + W].rearrange("(k p) t -> p k t", p=128), o_[:, :, 0:W], reads=[o_])
        return ph

    def make_ffn(i, XIN, XOUT, chunks, final=False):
        def ph():
            wu = k.sbuf([128, 8, 2 * DFF], BF16, "wu")
            for kk in range(8):
                k.dma("sp", wu[:, kk, :], S[f"bw_up{i}"][kk * 128:(kk + 1) * 128, :], writes=[wu])
            wd = [k.sbuf([128, NM, 128], BF16, "wd") for _ in range(2)]
            cw = k.sbuf([128, NM, 3], F32); k.dma("sp", cw[:], I[f"fcw{i}"][:, :, :], writes=[cw])
            cb = k.sbuf([128, NM], F32); k.dma("sp", cb[:], I[f"fcb{i}"][:, :], writes=[cb])
            xh = k.sbuf([128, 8, 516], F32, "xh")
            k.memset("pool", xh[:], 0.0, [xh])
            sq = k.sbuf([128, 8, 516], BF16, "sq")
            hs = [k.sbuf([128, 8, 516], BF16, "h") for _ in range(2)]
            xr = [k.sbuf([128, 512], F32, "xr") for _ in range(2)]
            rstd = k.sbuf([128, 516], F32, "rstd")
            tmp = [k.sbuf([128, 516], F32, "tmp") for _ in range(2)]
            gg = k.sbuf([128, NM, 512], BF16, "gg")
            asb = [k.sbuf([128, 516], F32, "asb") for _ in range(2)]
            c1 = [k.sbuf([128, 512], F32, "c1") for _ in range(2)]
            xo = [k.sbuf([128, 512], F32, "xo") for _ in range(2)]
            ssps = k.psum([128, 1024], F32, "ssps")
            aps = [k.psum([128, 1024], F32, "aps") for _ in range(2)]
            vps = [k.psum([128, 512], F32, "vps") for _ in range(2)]
            n = 0; nd = 0
            for ci, (t0, W, le, re, j) in enumerate(chunks):
                Wc = W + 4
                h = hs[ci % 2]
                if ci == 0:
                    load_xh(xh, XIN, t0, W, le, re)
                    for _ in norm_mod_gen(xh, Wc, i, 1, j, sq, ssps, rstd, tmp, h): pass
                gnx = None
                if ci + 1 < len(chunks):
                    t0n, Wn, len_, ren, jn = chunks[ci + 1]
                    load_xh(xh, XIN, t0n, Wn, len_, ren)
                    gnx = norm_mod_gen(xh, Wn + 4, i, 1, jn, sq, ssps, rstd, tmp, hs[(ci + 1) % 2])
                for m in range(NM):
                    if gnx is not None and m >= 2: next(gnx, None)
                    ap_ = aps[n % 2]; vp_ = vps[n % 2]; a_ = asb[n % 2]; c_ = c1[n % 2]; n += 1
                    for (c0, c1_) in ((0, min(512, Wc)), (512, Wc)):
                        if c1_ <= c0: continue
                        for kk in range(8):
                            k.mm(ap_[:, c0:c1_], wu[:, kk, m * 128:(m + 1) * 128], h[:, kk, c0:c1_], kk == 0, kk == 7, [wu, h], [ap_])
                    for kk in range(8):
                        k.mm(vp_[:, 0:W], wu[:, kk, DFF + m * 128:DFF + (m + 1) * 128], h[:, kk, 2:W + 2], kk == 0, kk == 7, [wu, h], [vp_])
                    k.copy("act", a_[:, 0:Wc], ap_[:, 0:Wc], [ap_], [a_])
                    if le: k.memset("pool", a_[:, 1:2], 0.0, [a_])
                    if re: k.memset("pool", a_[:, W + 2:W + 3], 0.0, [a_])
                    eng = "dve"
                    k.ts(eng, c_[:, 0:W], a_[:, 2:W + 2], cw[:, m, 1:2], cb[:, m:m + 1], ALU.mult, ALU.add, [a_, cw, cb], [c_])
                    k.stt(eng, c_[:, 0:W], a_[:, 1:W + 1], cw[:, m, 0:1], c_[:, 0:W], ALU.mult, ALU.add, [a_, cw, c_], [c_])
                    k.stt(eng, c_[:, 0:W], a_[:, 3:W + 3], cw[:, m, 2:3], c_[:, 0:W], ALU.mult, ALU.add, [a_, cw, c_], [c_])
                    k.act(c_[:, 0:W], c_[:, 0:W], AF.Gelu_apprx_tanh, [c_], [c_])
                    k.tt("dve", gg[:, m, 0:W], c_[:, 0:W], vp_[:, 0:W], ALU.mult, [c_, vp_], [gg])
                if gnx is not None:
                    for _ in gnx: pass
                for mo in range(8):
                    wd_ = wd[nd % 2]; o_ = xo[nd % 2]; p_ = vps[nd % 2]; xr_ = xr[nd % 2]; nd += 1
                    k.dma("sp", wd_[:], S[f"bw_dn{i}"][:, mo * 128:(mo + 1) * 128].rearrange("(m p) f -> p m f", p=128), writes=[wd_])
                    k.dma("sp", xr_[:, 0:W], XIN[mo * 128:(mo + 1) * 128, t0:t0 + W], writes=[xr_])
                    for m in range(NM):
                        k.mm(p_[:, 0:W], wd_[:, m, :], gg[:, m, 0:W], m == 0, m == NM - 1, [wd_, gg], [p_])
                    k.stt("dve", o_[:, 0:W], p_[:, 0:W], MOD[:, i, 40 + mo, j:j + 1], xr_[:, 0:W], ALU.mult, ALU.add, [p_, MOD, xr_], [o_])
                    k.dma("pool", XOUT[mo * 128:(mo + 1) * 128, t0:t0 + W], o_[:, 0:W], reads=[o_])
        return ph

    def make_filter(tag, n):
        KRAW = S["KRAW" + tag]; KN = S["KN" + tag]
        def ph():
            w1 = k.sbuf([33, 64], F32); k.dma("sp", w1[:], I["hw1"][:, :], writes=[w1])
            w2 = k.sbuf([64, 64], F32); k.dma("sp", w2[:], I["hw2"][:, :], writes=[w2])
            w3 = k.sbuf([64, 64], F32); k.dma("sp", w3[:], I["hw3"][:, :], writes=[w3])
            w4 = k.sbuf([64, 1024], F32); k.dma("sp", w4[:], I["hw4"][:, :], writes=[w4])
            hb = k.sbuf([64, 4], F32); k.dma("sp", hb[:], I["hb"][:, :], writes=[hb])
            ndel = k.sbuf([128, 4], F32); k.dma("sp", ndel[:], I["ndel"][:, :], writes=[ndel])
            asum = k.sbuf([128, 4, 40], F32, "asum")
            k.memset("dve", asum[:], 0.0, [asum])
            ft = [k.sbuf([33, 512], F32, "ft") for _ in range(2)]
            t01 = [k.sbuf([128, 512], F32, "t01") for _ in range(2)]
            hid2 = [[k.sbuf([64, 512], F32, "hid") for _ in range(3)] for _ in range(2)]
            ki2 = [k.sbuf([64, 512], I32, "ki") for _ in range(2)]
            win = [k.sbuf([128, 512], F32, "win") for _ in range(2)]
            kr = [k.sbuf([128, 512], F32, "kr") for _ in range(2)]
            junk = k.sbuf([128, 512], F32, "junk")
            krb = [k.sbuf([128, 512], BF16, "krb") for _ in range(2)]
            ps = [k.psum([128, 512], F32, "ps") for _ in range(4)]
            NCH = (2 * n) // 512
            n_ = 0
            for c in range(NCH):
                q0 = c * 512
                f_ = ft[c % 2]; t_ = t01[c % 2]
                def ld(cj):
                    k.dma("sp", ft[cj % 2][:, :], I["featsT" + tag][:, cj * 512:cj * 512 + 512], writes=[ft[cj % 2]])
                    k.dma("sp", t01[cj % 2][:, :], I["t01b" + tag][:, cj * 512:cj * 512 + 512], writes=[t01[cj % 2]])
                if c == 0: ld(0)
                if c + 1 < NCH: ld(c + 1)
                src = f_; srcK = 33
                hid = hid2[c % 2]; ki = ki2[c % 2]
                for li, wl in enumerate((w1, w2, w3)):
                    p_ = ps[n_ % 4]; n_ += 1
                    k.mm(p_[0:64, :], wl[0:srcK, :], src[0:srcK, :], True, True, [wl, src], [p_])
                    hd = hid[li]
                    k.ts("dve", hd[:, :], p_[0:64, :], hb[:, li:li + 1], hb[:, 3:4], ALU.add, ALU.mult, [p_, hb], [hd])
                    k.ts("dve", ki[:, :], hd[:, :], float(1.0 / (2 * np.pi)), None, ALU.mult, None, [hd], [ki])
                    k.stt("dve", hd[:, :], ki[:, :], float(-2 * np.pi), hd[:, :], ALU.mult, ALU.add, [ki, hd], [hd])
                    k.act(hd[:, :], hd[:, :], AF.Sin, [hd], [hd])
                    src = hd; srcK = 64
                segs = []
                if q0 + 512 <= n: segs = [(0, 512, 0)]
                elif q0 >= n: segs = [(0, 512, 512)]
                else: segs = [(0, n - q0, 0), (n - q0, 512, 512)]
                for jc in range(4):
                    p_ = ps[n_ % 4]; n_ += 1
                    for (a0, a1, off) in segs:
                        k.mm(p_[:, a0:a1], w4[:, off + jc * 128:off + (jc + 1) * 128], hid[2][:, a0:a1], True, True, [w4, hid[2]], [p_])
                    wn = win[jc % 2]; kr_ = kr[jc % 2]
                    k.act(wn[:, :], t_[:, :], AF.Exp, [t_, ndel], [wn], scale=ndel[:, jc:jc + 1])
                    kb_ = krb[jc % 2]
                    k.stt("dve", kb_[:, :], wn[:, :], 0.05, p_[:, :], ALU.add, ALU.mult, [wn, p_], [kb_])
                    if q0 <= n < q0 + 512:
                        k.memset("dve", kb_[:, n - q0:n - q0 + 1], 0.0, [kb_])
                    k.act(junk[:, :], kb_[:, :], AF.Abs, [kb_], [junk, asum], accum=asum[:, jc, c:c + 1])
                    if tag == "L":
                        k.dma("sp", S["KNA"][q0 // 128:q0 // 128 + 4, jc * 128:(jc + 1) * 128, :].rearrange("a c p -> c a p"),
                              kb_[:, :].rearrange("c (a p) -> c a p", p=128), reads=[kb_])
                    else:
                        k.dma("sp", KN[jc * 128:(jc + 1) * 128, q0:q0 + 512], kb_[:, :], reads=[kb_])
            rn = RNORM[tag]
            for jc in range(4):
                k.op("dve", lambda e, jc=jc: e.reduce_sum(out=rn[:, jc:jc + 1], in_=asum[:, jc, 0:NCH], axis=mybir.AxisListType.X), [asum], [rn])
            k.recip(rn[:, :], rn[:, :], [rn], [rn], force_self=True)
        return ph

    def make_fftconv(tag, NA, n, tok0, n_out_blocks):
        KN = S["KN" + tag]
        NR = NA // 2
        GF = 512 // NA
        GB = 512 // (2 * NA)
        def ph():
            cst = {}
            for nm, shp, dt in (("f1cs", [NA, 2 * NA], BF16), ("fC", [128, 128], BF16), ("fS", [128, 128], BF16), ("fnS", [128, 128], BF16),
                                ("fCS", [128, 256], BF16), ("fnSC", [128, 256], BF16), ("twA", [128, 512], F32), ("twB", [128, 512], F32),
                                ("twA2", [NA, 1024], F32), ("twB2", [NA, 1024], F32), ("g3C", [NA, NA], BF16), ("g3nS", [NA, NA], BF16)):
                cst[nm] = k.sbuf(shp, dt, nm)
                k.dma("sp", cst[nm][:], I[nm + tag][tuple(slice(None) for _ in shp)], writes=[cst[nm]])
            if NA == 4:
                for nm, shp, dt in (("twA2c", [128, 256], F32), ("twB2c", [128, 256], F32), ("gbC", [128, 64], BF16), ("gbnS", [128, 64], BF16)):
                    cst[nm] = k.sbuf(shp, dt, nm)
                    k.dma("sp", cst[nm][:], I[nm + tag][:, :], writes=[cst[nm]])
                c2a = [k.sbuf([128, 256], F32, "c2a") for _ in range(2)]; c2b = [k.sbuf([128, 256], F32, "c2b") for _ in range(2)]
                y3c = [k.sbuf([128, 2, 128], BF16, "y3c") for _ in range(2)]
                yoc = [k.sbuf([64, 128], F32, "yoc") for _ in range(2)]
            LG = 32 if NA == 128 else 128
            NXB = 2 if NA == 128 else 1
            xu = [k.sbuf([NA, LG, 128], BF16, "xu") for _ in range(NXB)]
            for t_ in xu: k.memset("pool", t_[:], 0.0, [t_])
            xk = [k.sbuf([NA, LG, 128], BF16, "xk") for _ in range(NXB)]
            s1 = [k.psum([128, 512], F32, "s1") for _ in range(2)]
            s2 = [k.psum([128, 512], F32, "s2") for _ in range(2)]
            s3 = [k.psum([128, 1024], F32, "s3") for _ in range(1)]
            s4 = [k.psum([128, 512], F32, "s4") for _ in range(2)]
            NB = 2
            ta = [k.sbuf([128, 512], F32, "ta") for _ in range(NB)]; tb = [k.sbuf([128, 512], F32, "tb") for _ in range(NB)]
            bu = [k.sbuf([128, 2, GF, NA], BF16, "bu") for _ in range(NB)]; bk = [k.sbuf([128, 2, GF, NA], BF16, "bk") for _ in range(NB)]
            kh = [k.sbuf([128, 2, 512], F32, "kh") for _ in range(NB)]
            m1 = [k.sbuf([128, 512], F32, "m1") for _ in range(NB)]; m2 = [k.sbuf([128, 512], F32, "m2") for _ in range(NB)]
            m3 = [k.sbuf([128, 512], F32, "m3") for _ in range(NB)]; m4 = [k.sbuf([128, 512], F32, "m4") for _ in range(NB)]
            yh = [k.sbuf([128, 2, GF, NA], BF16, "yh") for _ in range(NB)]
            t2a = [k.sbuf([NA, 1024], F32, "t2a") for _ in range(NB)]; t2b = [k.sbuf([NA, 1024], F32, "t2b") for _ in range(NB)]
            y3 = [k.sbuf([NA, 2, 4, 128], BF16, "y3") for _ in range(NB)]
            yo = [k.sbuf([n_out_blocks, 4, 128], F32, "yo") for _ in range(2)]
            MO = n_out_blocks
            cnt = {"tw": 0, "inv": 0, "g": 0}
            def fwd(x, cbase, bdst):
                for half in range(2):
                    bank = s1[half]
                    ta_ = ta[cnt["tw"] % NB]; tb_ = tb[cnt["tw"] % NB]; cnt["tw"] += 1
                    for cc in range(GB):
                        ch = cbase + half * GB + cc
                        k.mm(bank[:, cc * 2 * NA:(cc + 1) * 2 * NA], x[0:NA, ch, :], cst["f1cs"][0:NA, :], True, True, [x, cst["f1cs"]], [bank])
                    k.tt("dve", ta_[:, :], bank[:, :], cst["twA"][:, :], ALU.mult, [bank, cst["twA"]], [ta_])
                    k.tt("dve", tb_[:, :], bank[:, :], cst["twB"][:, :], ALU.mult, [bank, cst["twB"]], [tb_])
                    tav = ta_[:, :].rearrange("p (g r f) -> p g r f", g=GB, r=2)
                    tbv = tb_[:, :].rearrange("p (g r f) -> p g r f", g=GB, r=2)
                    k.tt("dve", bdst[:, 0, half * GB:(half + 1) * GB, :], tav[:, :, 0, :], tbv[:, :, 1, :], ALU.subtract, [ta_, tb_], [bdst])
                    k.tt("dve", bdst[:, 1, half * GB:(half + 1) * GB, :], tav[:, :, 1, :], tbv[:, :, 0, :], ALU.subtract, [ta_, tb_], [bdst])
                bre = bdst[:, 0, :, :].rearrange("p g f -> p (g f)"); bim = bdst[:, 1, :, :].rearrange("p g f -> p (g f)")
                k.mm(s2[0][:, :], cst["fC"][:, :], bre, True, False, [cst["fC"], bdst], [s2[0]])
                k.mm(s2[0][:, :], cst["fS"][:, :], bim, False, True, [cst["fS"], bdst], [s2[0]])
                k.mm(s2[1][:, :], cst["fC"][:, :], bim, True, False, [cst["fC"], bdst], [s2[1]])
                k.mm(s2[1][:, :], cst["fnS"][:, :], bre, False, True, [cst["fnS"], bdst], [s2[1]])
            for lg in range(512 // LG):
                xu_ = xu[lg % NXB]; xk_ = xk[lg % NXB]
                k.dma("sp", xu_[0:NR, :, :], S["UT"][lg * LG:(lg + 1) * LG, tok0:tok0 + n].rearrange("c (a p) -> a c p", p=128), writes=[xu_])
                k.dma("sp", xk_[:, :, :], KN[lg * LG:(lg + 1) * LG, :].rearrange("c (a p) -> a c p", p=128), writes=[xk_])
                for gf in range(LG // GF):
                    cbase = gf * GF
                    g_ = cnt["g"] % NB; cnt["g"] += 1
                    kh_ = kh[g_]; yh_ = yh[g_]
                    fwd(xk_, cbase, bk[g_])
                    k.copy("act", kh_[:, 0, :], s2[0][:, :], [s2[0]], [kh_])
                    k.copy("act", kh_[:, 1, :], s2[1][:, :], [s2[1]], [kh_])
                    fwd(xu_, cbase, bu[g_])
                    k.tt("dve", m1[g_][:, :], s2[0][:, :], kh_[:, 0, :], ALU.mult, [s2[0], kh_], [m1[g_]])
                    k.tt("dve", m3[g_][:, :], s2[0][:, :], kh_[:, 1, :], ALU.mult, [s2[0], kh_], [m3[g_]])
                    k.tt("dve", m2[g_][:, :], s2[1][:, :], kh_[:, 1, :], ALU.mult, [s2[1], kh_], [m2[g_]])
                    k.tt("dve", m4[g_][:, :], s2[1][:, :], kh_[:, 0, :], ALU.mult, [s2[1], kh_], [m4[g_]])
                    k.tt("pool", yh_[:, 0, :, :].rearrange("p g f -> p (g f)"), m1[g_][:, :], m2[g_][:, :], ALU.subtract, [m1[g_], m2[g_]], [yh_])
                    k.tt("pool", yh_[:, 1, :, :].rearrange("p g f -> p (g f)"), m3[g_][:, :], m4[g_][:, :], ALU.add, [m3[g_], m4[g_]], [yh_])
                    if NA == 4:
                        for sg in range(GF // 32):
                            b3 = s3[0]
                            iv = cnt["inv"] % 2; cnt["inv"] += 1
                            lre = yh_[:, 0, sg * 32:(sg + 1) * 32, :].rearrange("p g f -> p (g f)")
                            lim = yh_[:, 1, sg * 32:(sg + 1) * 32, :].rearrange("p g f -> p (g f)")
                            k.mm(b3[:, 0:256], lre, cst["fCS"][:, :], True, False, [yh_, cst["fCS"]], [b3])
                            k.mm(b3[:, 0:256], lim, cst["fnSC"][:, :], False, True, [yh_, cst["fnSC"]], [b3])
                            a_ = c2a[iv]; b_ = c2b[iv]; y_ = y3c[iv]; o_ = yoc[iv]
                            k.tt("dve", a_[:, :], b3[:, 0:256], cst["twA2c"][:, :], ALU.mult, [b3, cst["twA2c"]], [a_])
                            k.tt("dve", b_[:, :], b3[:, 0:256], cst["twB2c"][:, :], ALU.mult, [b3, cst["twB2c"]], [b_])
                            k.tt("pool", y_[:, 0, :], a_[:, 0:128], b_[:, 128:256], ALU.add, [a_, b_], [y_])
                            k.tt("pool", y_[:, 1, :], a_[:, 128:256], b_[:, 0:128], ALU.add, [a_, b_], [y_])
                            p4 = s4[iv]
                            k.mm(p4[0:64, 0:128], cst["gbC"][:, :], y_[:, 0, :], True, False, [cst["gbC"], y_], [p4])
                            k.mm(p4[0:64, 0:128], cst["gbnS"][:, :], y_[:, 1, :], False, True, [cst["gbnS"], y_], [p4])
                            k.copy("act", o_[:, :], p4[0:64, 0:128], [p4], [o_])
                            c0 = lg * LG + cbase + sg * 32
                            for a2 in range(2):
                                k.dma("sp", S["YT"][c0:c0 + 32, tok0 + a2 * 128:tok0 + (a2 + 1) * 128], o_[a2:64:2, :], reads=[o_])
                        continue
                    for sg in range(GF // 4):
                        b3 = s3[0]
                        iv = cnt["inv"] % NB; cnt["inv"] += 1
                        t2a_ = t2a[iv]; t2b_ = t2b[iv]; y3_ = y3[iv]
                        for cc in range(4):
                            ch = sg * 4 + cc
                            k.mm(b3[0:NA, cc * 256:(cc + 1) * 256], yh_[:, 0, ch, :], cst["fCS"][:, :], True, False, [yh_, cst["fCS"]], [b3])
                            k.mm(b3[0:NA, cc * 256:(cc + 1) * 256], yh_[:, 1, ch, :], cst["fnSC"][:, :], False, True, [yh_, cst["fnSC"]], [b3])
                        k.tt("dve", t2a_[:, :], b3[0:NA, :], cst["twA2"][:, :], ALU.mult, [b3, cst["twA2"]], [t2a_])
                        k.tt("dve", t2b_[:, :], b3[0:NA, :], cst["twB2"][:, :], ALU.mult, [b3, cst["twB2"]], [t2b_])
                        av = t2a_[:, :].rearrange("p (g r f) -> p g r f", g=4, r=2)
                        bv = t2b_[:, :].rearrange("p (g r f) -> p g r f", g=4, r=2)
                        k.tt("pool", y3_[:, 0, :, :], av[:, :, 0, :], bv[:, :, 1, :], ALU.add, [t2a_, t2b_], [y3_])
                        k.tt("pool", y3_[:, 1, :, :], av[:, :, 1, :], bv[:, :, 0, :], ALU.add, [t2a_, t2b_], [y3_])
                        p4 = s4[iv % 2]; yo_ = yo[iv % 2]
                        k.mm(p4[0:MO, :], cst["g3C"][:, 0:MO], y3_[:, 0, :, :].rearrange("p g f -> p (g f)"), True, False, [cst["g3C"], y3_], [p4])
                        k.mm(p4[0:MO, :], cst["g3nS"][:, 0:MO], y3_[:, 1, :, :].rearrange("p g f -> p (g f)"), False, True, [cst["g3nS"], y3_], [p4])
                        k.copy("act", yo_[:, :, :].rearrange("p g f -> p (g f)"), p4[0:MO, :], [p4], [yo_])
                        c0 = lg * LG + cbase + sg * 4
                        k.dma("sp", S["YT"][c0:c0 + 4, tok0:tok0 + MO * 128].rearrange("c (a p) -> a c p", p=128), yo_[:, :, :], reads=[yo_])
        return ph

    def make_fftconvL2(n_out_blocks):
        tag = "L"; NA = 128; NR = 64; n = SEQ; tok0 = 0; CB = 32; MO = n_out_blocks
        def ph():
            cst = {}
            for nm, shp, dt in (("f1cs", [128, 256], BF16), ("fCS", [128, 256], BF16), ("fnSC", [128, 256], BF16),
                                ("twA2", [128, 1024], F32), ("twB2", [128, 1024], F32), ("g3C", [128, 128], BF16), ("g3nS", [128, 128], BF16)):
                cst[nm] = k.sbuf(shp, dt, nm)
                k.dma("sp", cst[nm][:], I[nm + tag][:, :], writes=[cst[nm]])
            MC = k.sbuf([128, 128, 128], BF16, "MC"); MS = k.sbuf([128, 128, 128], BF16, "MS")
            for q4 in range(4):
                k.dma("sp", MC[:, q4 * 32:(q4 + 1) * 32, :], I["MCL"][:, q4 * 32:(q4 + 1) * 32, :], writes=[MC])
                k.dma("sp", MS[:, q4 * 32:(q4 + 1) * 32, :], I["MSL"][:, q4 * 32:(q4 + 1) * 32, :], writes=[MS])
            xu = k.sbuf([128, CB, 128], BF16, "xu"); k.memset("pool", xu[:], 0.0, [xu])
            xk = k.sbuf([128, CB, 128], BF16, "xk")
            ATs = [k.sbuf([128, CB, 3, 128], BF16, "AT") for _ in range(2)]
            KH = k.sbuf([128, 2, 128, CB], BF16, "KH")
            YH = k.sbuf([128, 2, CB, 128], BF16, "YH")
            mm_ = [[k.sbuf([128, 512], F32, "m") for _ in range(4)] for _ in range(2)]
            t2a = [k.sbuf([128, 1024], BF16, "t2a") for _ in range(2)]; t2b = [k.sbuf([128, 1024], BF16, "t2b") for _ in range(2)]
            y3 = [k.sbuf([128, 2, 4, 128], BF16, "y3") for _ in range(2)]
            yo = [k.sbuf([MO, 4, 128], F32, "yo") for _ in range(2)]
            s1 = [k.psum([128, 512], F32, "s1") for _ in range(2)]
            s2 = [k.psum([128, 512], F32, "s2") for _ in range(4)]
            s3 = k.psum([128, 1024], F32, "s3")
            cnt = {"p": 0, "b": 0, "inv": 0}
            def step1_gen(bt, kind, AT):
                if kind == "F":
                    k.dma("sp", xk[:, :, :], S["KNA"][:, bt * CB:(bt + 1) * CB, :], writes=[xk]); x = xk
                else:
                    k.dma("sp", xu[0:NR, :, :], S["UTA"][:, bt * CB:(bt + 1) * CB, :], writes=[xu]); x = xu
                for pr in range(CB // 2):
                    bank = s1[cnt["p"] % 2]; cnt["p"] += 1
                    for cc in range(2):
                        k.mm(bank[:, cc * 256:(cc + 1) * 256], x[:, 2 * pr + cc, :], cst["f1cs"][:, :], True, True, [x, cst["f1cs"]], [bank])
                    bv = bank[:, :].rearrange("p (g r f) -> p g r f", g=2, r=2)
                    o1 = AT[:, 2 * pr:2 * pr + 2, 0:2, :]; o2 = AT[:, 2 * pr:2 * pr + 2, 2, :]; i2 = bv[:, :, 0, :]
                    k.op("act", lambda e, o1=o1, bv=bv: e.copy(out=o1, in_=bv), [bank], [AT], force_self="never")
                    k.op("act", lambda e, o2=o2, i2=i2: e.mul(out=o2, in_=i2, mul=-1.0), [bank], [AT], force_self="never")
                    yield
            def inverse_gen(bt):
                for sg in range(CB // 4):
                    iv = cnt["inv"] % 2; cnt["inv"] += 1
                    t2a_ = t2a[iv]; t2b_ = t2b[iv]; y3_ = y3[iv]
                    for cc in range(4):
                        ch = sg * 4 + cc
                        k.mm(s3[:, cc * 256:(cc + 1) * 256], YH[:, 0, ch, :], cst["fCS"][:, :], True, False, [YH, cst["fCS"]], [s3])
                        k.mm(s3[:, cc * 256:(cc + 1) * 256], YH[:, 1, ch, :], cst["fnSC"][:, :], False, True, [YH, cst["fnSC"]], [s3])
                    k.tt("dve", t2a_[:, :], s3[:, :], cst["twA2"][:, :], ALU.mult, [s3, cst["twA2"]], [t2a_])
                    k.tt("dve", t2b_[:, :], s3[:, :], cst["twB2"][:, :], ALU.mult, [s3, cst["twB2"]], [t2b_])
                    av = t2a_[:, :].rearrange("p (g r f) -> p g r f", g=4, r=2)
                    bv2 = t2b_[:, :].rearrange("p (g r f) -> p g r f", g=4, r=2)
                    k.tt("dve", y3_[:, 0, :, :], av[:, :, 0, :], bv2[:, :, 1, :], ALU.add, [t2a_, t2b_], [y3_])
                    k.tt("pool", y3_[:, 1, :, :], av[:, :, 1, :], bv2[:, :, 0, :], ALU.add, [t2a_, t2b_], [y3_])
                    yield
                    p4 = s1[cnt["p"] % 2]; cnt["p"] += 1
                    yo_ = yo[iv]
                    k.mm(p4[0:MO, :], cst["g3C"][:, 0:MO], y3_[:, 0, :, :].rearrange("p g f -> p (g f)"), True, False, [cst["g3C"], y3_], [p4])
                    k.mm(p4[0:MO, :], cst["g3nS"][:, 0:MO], y3_[:, 1, :, :].rearrange("p g f -> p (g f)"), False, True, [cst["g3nS"], y3_], [p4])
                    k.copy("act", yo_[:, :, :].rearrange("p g f -> p (g f)"), p4[0:MO, :], [p4], [yo_])
                    c0 = bt * CB + sg * 4
                    k.dma("sp", S["YT"][c0:c0 + 4, tok0:tok0 + MO * 128].rearrange("c (a p) -> a c p", p=128), yo_[:, :, :], reads=[yo_])
                    yield
            def filt_block(blk, bre, bim):
                o_a = KH[:, 0, blk * 16:(blk + 1) * 16, :].rearrange("p a b -> p (a b)"); o_b = KH[:, 1, blk * 16:(blk + 1) * 16, :].rearrange("p a b -> p (a b)")
                k.op("act", lambda e: e.copy(out=o_a, in_=bre[:, :]), [bre], [KH], force_self="never")
                k.op("act", lambda e: e.copy(out=o_b, in_=bim[:, :]), [bim], [KH], force_self="never")
            def sig_block(blk, bre, bim):
                kre = KH[:, 0, blk * 16:(blk + 1) * 16, :].rearrange("p a b -> p (a b)")
                kim = KH[:, 1, blk * 16:(blk + 1) * 16, :].rearrange("p a b -> p (a b)")
                m1, m2, m3, m4 = mm_[blk % 2]
                k.tt("dve", m1[:, :], bre[:, :], kre, ALU.mult, [bre, KH], [m1])
                k.tt("dve", m3[:, :], bre[:, :], kim, ALU.mult, [bre, KH], [m3])
                k.tt("dve", m2[:, :], bim[:, :], kim, ALU.mult, [bim, KH], [m2])
                k.tt("dve", m4[:, :], bim[:, :], kre, ALU.mult, [bim, KH], [m4])
                ore = YH[:, 0, :, blk * 16:(blk + 1) * 16].rearrange("p c f -> p f c")
                oim = YH[:, 1, :, blk * 16:(blk + 1) * 16].rearrange("p c f -> p f c")
                v = lambda t: t[:, :].rearrange("p (f c) -> p f c", c=CB)
                k.op("dve", lambda e: e.tensor_tensor(out=ore, in0=v(m1), in1=v(m2), op=ALU.subtract), [m1, m2], [YH])
                k.op("pool", lambda e: e.tensor_tensor(out=oim, in0=v(m3), in1=v(m4), op=ALU.add), [m3, m4], [YH], force_self="never")
            def step2(AT, on_block, g_next, g_inv):
                for f1 in range(128):
                    j = f1 % 16
                    if j == 0:
                        bre = s2[(cnt["b"] % 2) * 2]; bim = s2[(cnt["b"] % 2) * 2 + 1]; cnt["b"] += 1
                    cols = slice(j * CB, (j + 1) * CB)
                    k.mm(bre[:, cols], MC[:, f1, :], AT[:, :, 0, f1], True, False, [MC, AT], [bre])
                    k.mm(bre[:, cols], MS[:, f1, :], AT[:, :, 1, f1], False, True, [MS, AT], [bre])
                    k.mm(bim[:, cols], MC[:, f1, :], AT[:, :, 1, f1], True, False, [MC, AT], [bim])
                    k.mm(bim[:, cols], MS[:, f1, :], AT[:, :, 2, f1], False, True, [MS, AT], [bim])
                    if j == 15:
                        on_block(f1 // 16, bre, bim)
                    if f1 % 8 == 3 and g_next is not None: next(g_next, None)
                    if f1 % 8 == 7 and g_inv is not None: next(g_inv, None)
            jobs = [(bt, kind) for bt in range(512 // CB) for kind in ("F", "S")]
            g0 = step1_gen(jobs[0][0], jobs[0][1], ATs[0])
            for _ in g0: pass
            for ji, (bt, kind) in enumerate(jobs):
                g_next = step1_gen(jobs[ji + 1][0], jobs[ji + 1][1], ATs[(ji + 1) % 2]) if ji + 1 < len(jobs) else None
                g_inv = inverse_gen(bt - 1) if (kind == "F" and bt > 0) else None
                step2(ATs[ji % 2], filt_block if kind == "F" else sig_block, g_next, g_inv)
                if g_next is not None:
                    for _ in g_next: pass
                if g_inv is not None:
                    for _ in g_inv: pass
            for _ in inverse_gen(512 // CB - 1): pass
        return ph

    def make_hycombine(chunks):
        def ph():
            bd = k.sbuf([128, 4], F32); k.dma("sp", bd[:], I["hbd"][:, :], writes=[bd])
            yt = [k.sbuf([128, 512], F32, "yt") for _ in range(2)]
            ut = [k.sbuf([128, 512], F32, "ut") for _ in range(2)]
            x0 = [k.sbuf([128, 512], F32, "x0") for _ in range(2)]
            ob = [k.sbuf([128, 512], BF16, "ob") for _ in range(2)]
            n = 0
            for (t0, W, le, re, j) in chunks:
                rn = RNORM["C" if j == 1 else "L"]
                for jc in range(4):
                    y_ = yt[n % 2]; u_ = ut[n % 2]; x_ = x0[n % 2]; o_ = ob[n % 2]; n += 1
                    rows = slice(jc * 128, (jc + 1) * 128)
                    k.dma("sp", y_[:, 0:W], S["YT"][rows, t0:t0 + W], writes=[y_])
                    k.dma("sp", u_[:, 0:W], S["UF"][rows, t0:t0 + W], writes=[u_])
                    k.dma("sp", x_[:, 0:W], S["X0T"][rows, t0:t0 + W], writes=[x_])
                    k.ts("dve", y_[:, 0:W], y_[:, 0:W], rn[:, jc:jc + 1], None, ALU.mult, None, [y_, rn], [y_])
                    k.stt("dve", y_[:, 0:W], u_[:, 0:W], bd[:, jc:jc + 1], y_[:, 0:W], ALU.mult, ALU.add, [u_, bd, y_], [y_])
                    k.tt("dve", o_[:, 0:W], y_[:, 0:W], x_[:, 0:W], ALU.mult, [y_, x_], [o_])
                    k.dma("pool", S["OCT"][rows, t0:t0 + W], o_[:, 0:W], reads=[o_])
        return ph

    CH_E = [(c * 512, 512, c == 0, False, 0) for c in range(8)] + [(4096, 256, False, True, 0)]
    CH_OWN = [(c * 512, 512, c == 0, False, 0) for c in range(8)]
    phases.append(("inproj0", make_inproj(0, I["xt0"], CHUNKS_ALL)))
    phases.append(("filterL", make_filter("L", SEQ)))
    phases.append(("filterC", make_filter("C", CTX)))
    phases.append(("fftL", make_fftconvL2(E // 128)))
    phases.append(("fftC", make_fftconv("C", 4, CTX, SEQ, 2)))
    phases.append(("hycomb", make_hycombine(CH_E + [CTXCH])))
    phases.append(("attn0", make_attn(0)))
    phases.append(("outproj0", make_outproj(0, I["xt0"], S["XM"], CH_E + [CTXCH])))
    phases.append(("ffn0", make_ffn(0, S["XM"], S["X1"], CH_E + [CTXCH])))
    phases.append(("inproj1", make_inproj(1, S["X1"], CH_E + [CTXCH])))
    phases.append(("attn1", make_attn(1)))
    phases.append(("outproj1", make_outproj(1, S["X1"], S["XM1"], CH_E)))
    phases.append(("ffn1", make_ffn(1, S["XM1"], OUT, CH_OWN)))
    for nm, ph in phases:
        k.phase(ph)
        if stop_after == nm:
            break
    k.close()
    return nc


_CACHE = {}


def kernel(**inputs):
    inp = {kk: np.asarray(v) for kk, v in inputs.items()}
    if "C" not in _CACHE:
        C = _consts()
        C["fftL"] = _fft_consts(128); C["fftC"] = _fft_consts(4)
        C["filtL"] = _filter_consts(SEQ); C["filtC"] = _filter_consts(CTX)
        _CACHE["C"] = C
    C = _CACHE["C"]
    nc = _build()
    in_maps = []
    for core in range(8):
        b, hh = core // 2, core % 2
        in_maps.append(_host_prep(inp, b, hh, C))
    res = run_bass_kernel_spmd(nc, in_maps, core_ids=list(range(8)))
    out = np.empty((4, SEQ, D), np.float32)
    for core in range(8):
        b, hh = core // 2, core % 2
        o = np.asarray(res.results[core]["out"]).T
        if hh == 0:
            out[b, :OWN] = o
        else:
            out[b, OWN:] = o[::-1]
    return out
```
